# Optimizing a Trainium2 kernel written in Bass

```python
import math
import jax, jax.numpy as jnp
from jax import lax
import numpy as np

D_MODEL = 4096
BATCH = 2
SEQ = 4096
DEPTH = 2

HEAD_DIM = 128
EPS = 1e-6
A_BRANCHES = ((128, 1), (512, 4), (2048, 16))
N_BRANCH = len(A_BRANCHES)
A_HEADS = D_MODEL // (2 * HEAD_DIM)
A_WIDTH = A_HEADS * HEAD_DIM
BAND_BLOCK = 128
POOL_WINDOWS = (2, 4, 8, 16)
B_WIDTH = D_MODEL - A_WIDTH
B_GROUP = B_WIDTH // len(POOL_WINDOWS)
EVEN_IN = 2 * N_BRANCH * A_WIDTH + A_WIDTH + B_WIDTH
C_HEADS = 8
C_KEY_WIDTH = D_MODEL // 2
C_VAL_WIDTH = D_MODEL
C_DK = C_KEY_WIDTH // C_HEADS
C_DV = C_VAL_WIDTH // C_HEADS
C_GATE_RANK = 16
C_GATE_TAU = 16.0
C_CHUNK = 64
ODD_IN = 2 * C_KEY_WIDTH + 2 * C_VAL_WIDTH + C_GATE_RANK
MEM_LEN = 256
X_HEADS = 4
X_WIDTH = X_HEADS * HEAD_DIM
D_FF = ((8 * D_MODEL // 3 + 255) // 256) * 256
CONV_WIDTH = 3
N_EVEN = (DEPTH + 1) // 2
N_ODD = DEPTH // 2

kernel_name = 'hybrid_dilated_pool_gla_memx_convffn'


def rmsnorm(x, g):
    xf = x.astype(jnp.float32)
    y = xf * lax.rsqrt(jnp.mean(xf * xf, axis=-1, keepdims=True) + EPS)
    return (y * g.astype(jnp.float32)).astype(x.dtype)


def alibi_slopes():
    n = N_BRANCH * A_HEADS
    s = np.power(np.float32(2.0), -8.0 * np.arange(1, n + 1, dtype=np.float32) / np.float32(n)).astype(np.float32)
    return jnp.asarray(s.reshape(N_BRANCH, A_HEADS))


def dilated_band_attention(q, k, v, window, dilation, slopes):
    bsz, seq, heads, dh = q.shape
    w_sub = window // dilation
    sub_len = seq // dilation
    n_blk = -(-sub_len // BAND_BLOCK)
    pad_len = n_blk * BAND_BLOCK

    def to_sub(t):
        t = t.reshape(bsz, sub_len, dilation, heads, dh).transpose(0, 2, 3, 1, 4)
        t = jnp.pad(t, ((0, 0), (0, 0), (0, 0), (0, pad_len - sub_len), (0, 0)))
        return t.reshape(bsz, dilation, heads, n_blk, BAND_BLOCK, dh)

    def band(t):
        prev = jnp.concatenate([jnp.zeros_like(t[:, :, :, :1]), t[:, :, :, :-1]], axis=3)
        return jnp.concatenate([prev, t], axis=4)

    qb = to_sub(q)
    kb = band(to_sub(k))
    vb = band(to_sub(v))
    s = jnp.einsum('brhnqe,brhnke->brhnqk', qb, kb, preferred_element_type=jnp.float32) * (dh ** -0.5)
    q_idx = jnp.arange(BAND_BLOCK)[:, None] + BAND_BLOCK
    k_idx = jnp.arange(2 * BAND_BLOCK)[None, :]
    rel = q_idx - k_idx
    key_sub = (jnp.arange(n_blk) * BAND_BLOCK - BAND_BLOCK)[:, None, None] + k_idx[None]
    valid = (rel >= 0) & (rel <= w_sub) & (key_sub >= 0)
    dist = (rel * dilation).astype(jnp.float32)
    s = s - slopes.astype(jnp.float32)[:, None, None, None] * dist
    s = jnp.where(valid, s, -jnp.inf)
    m = jnp.max(s, axis=-1, keepdims=True)
    p = jnp.exp(s - m)
    den = jnp.sum(p, axis=-1, keepdims=True)
    o = jnp.einsum('brhnqk,brhnke->brhnqe', p, vb.astype(jnp.float32)) / den
    lse = (m + jnp.log(den))[..., 0]
    o = o.reshape(bsz, dilation, heads, pad_len, dh)[:, :, :, :sub_len]
    o = o.transpose(0, 3, 1, 2, 4).reshape(bsz, seq, heads, dh)
    lse = lse.reshape(bsz, dilation, heads, pad_len)[..., :sub_len]
    lse = lse.transpose(0, 3, 1, 2).reshape(bsz, seq, heads)
    return o, lse


def multiscale_pool(u, pool_w, pool_scale):
    bsz, seq, _ = u.shape
    uf = u.astype(jnp.float32)
    count = jnp.arange(1, seq + 1, dtype=jnp.float32)[:, None]
    groups = []
    for g, win in enumerate(POOL_WINDOWS):
        seg = uf[..., g * B_GROUP:(g + 1) * B_GROUP]
        cs = jnp.cumsum(seg, axis=1)
        lag = jnp.pad(cs[:, :seq - win], ((0, 0), (win, 0), (0, 0)))
        groups.append((cs - lag) / jnp.minimum(count, win) - seg)
    pooled = jnp.stack(groups, axis=2).astype(u.dtype)
    y = jnp.einsum('bsgc,gcd->bsgd', pooled, pool_w).reshape(bsz, seq, B_WIDTH)
    return y * pool_scale


def dilated_pool_mixer(h, w_in, q_gain, k_gain, pool_w, pool_scale, w_out):
    bsz, seq, _ = h.shape
    qk = N_BRANCH * A_WIDTH
    z = h @ w_in
    q = rmsnorm(z[..., :qk].reshape(bsz, seq, N_BRANCH, A_HEADS, HEAD_DIM), q_gain)
    k = rmsnorm(z[..., qk:2 * qk].reshape(bsz, seq, N_BRANCH, A_HEADS, HEAD_DIM), k_gain)
    v = z[..., 2 * qk:2 * qk + A_WIDTH].reshape(bsz, seq, A_HEADS, HEAD_DIM)
    u = z[..., 2 * qk + A_WIDTH:]
    slopes = alibi_slopes()
    outs, lses = [], []
    for g, (window, dilation) in enumerate(A_BRANCHES):
        o, l = dilated_band_attention(q[:, :, g], k[:, :, g], v, window, dilation, slopes[g])
        outs.append(o)
        lses.append(l)
    wts = jax.nn.softmax(jnp.stack(lses), axis=0)
    a_out = jnp.sum(wts[..., None] * jnp.stack(outs), axis=0).reshape(bsz, seq, A_WIDTH).astype(h.dtype)
    b_out = multiscale_pool(u, pool_w, pool_scale)
    return jnp.concatenate([a_out, b_out], axis=-1) @ w_out


def gla_chunked(q, k, v, log_a):
    bsz, seq, heads, dk = q.shape
    dv = v.shape[-1]
    n_chunk = seq // C_CHUNK

    def chunks(t):
        return t.astype(jnp.float32).reshape(bsz, n_chunk, C_CHUNK, heads, t.shape[-1]).transpose(1, 0, 3, 2, 4)

    qc = chunks(q) * (dk ** -0.5)
    kc = chunks(k)
    vc = chunks(v)
    bc = jnp.cumsum(chunks(log_a), axis=3)
    b_last = bc[:, :, :, -1:]
    q_in = qc * jnp.exp(bc)
    k_in = kc * jnp.exp(-bc)
    k_st = kc * jnp.exp(b_last - bc)
    causal = jnp.tril(jnp.ones((C_CHUNK, C_CHUNK), dtype=bool))
    att = jnp.where(causal, jnp.einsum('nbhid,nbhjd->nbhij', q_in, k_in), 0.0)
    o_intra = jnp.einsum('nbhij,nbhje->nbhie', att, vc)
    decay = jnp.exp(b_last)

    def step(state, xs):
        q_n, k_n, v_n, dec_n = xs
        o_n = jnp.einsum('bhid,bhde->bhie', q_n, state)
        state = dec_n[:, :, 0, :, None] * state + jnp.einsum('bhjd,bhje->bhde', k_n, v_n)
        return state, o_n

    state0 = jnp.zeros((bsz, heads, dk, dv), jnp.float32)
    _, o_inter = lax.scan(step, state0, (q_in, k_st, vc, decay))
    o = o_inter + o_intra
    return o.transpose(1, 0, 3, 2, 4).reshape(bsz, seq, heads, dv)


def gla_mixer(h, w_in, w_a2, b_a, o_gain, w_out):
    bsz, seq, _ = h.shape
    kw, vw = C_KEY_WIDTH, C_VAL_WIDTH
    z = h @ w_in
    q = z[..., :kw].reshape(bsz, seq, C_HEADS, C_DK)
    k = z[..., kw:2 * kw].reshape(bsz, seq, C_HEADS, C_DK)
    v = z[..., 2 * kw:2 * kw + vw].reshape(bsz, seq, C_HEADS, C_DV)
    gate = z[..., 2 * kw + vw:2 * kw + 2 * vw].reshape(bsz, seq, C_HEADS, C_DV)
    r = z[..., 2 * kw + 2 * vw:]
    log_a = jax.nn.log_sigmoid((r @ w_a2 + b_a).astype(jnp.float32)) / C_GATE_TAU
    log_a = log_a.reshape(bsz, seq, C_HEADS, C_DK)
    o = gla_chunked(q, k, v, log_a)
    o = rmsnorm(o, o_gain) * jax.nn.silu(gate.astype(jnp.float32))
    return o.reshape(bsz, seq, vw).astype(h.dtype) @ w_out


def memory_cross_attention(h, mem_n, wq, wkv, q_gain, k_gain, wo):
    bsz, seq, _ = h.shape
    mlen = mem_n.shape[1]
    q = rmsnorm((h @ wq).reshape(bsz, seq, X_HEADS, HEAD_DIM), q_gain)
    kv = mem_n @ wkv
    k = rmsnorm(kv[..., :X_WIDTH].reshape(bsz, mlen, X_HEADS, HEAD_DIM), k_gain)
    v = kv[..., X_WIDTH:].reshape(bsz, mlen, X_HEADS, HEAD_DIM)
    s = jnp.einsum('bshe,bmhe->bhsm', q, k, preferred_element_type=jnp.float32) * (HEAD_DIM ** -0.5)
    p = jax.nn.softmax(s, axis=-1)
    o = jnp.einsum('bhsm,bmhe->bshe', p.astype(v.dtype), v).reshape(bsz, seq, X_WIDTH)
    return o @ wo


def conv_ffn(h, w_up, conv_w, conv_b, w_down):
    seq = h.shape[1]
    u = h @ w_up
    up = jnp.pad(u, ((0, 0), (CONV_WIDTH - 1, 0), (0, 0)))
    c = conv_b
    for i in range(CONV_WIDTH):
        c = c + conv_w[i] * up[:, i:i + seq]
    gate, val = c[..., :D_FF], c[..., D_FF:]
    return (jax.nn.silu(gate) * val) @ w_down


def setup_inputs(seed: int = 0) -> dict:
    key = jax.random.key(seed)
    ks = iter(jax.random.split(key, 40))

    def nrm(shape, scale):
        return jax.random.normal(next(ks), shape, jnp.float32) * scale

    def gain(shape):
        return 1.0 + 0.02 * jax.random.normal(next(ks), shape, jnp.float32)

    d = D_MODEL
    return {
        'x': nrm((BATCH, SEQ, d), 1.0),
        'mem': nrm((BATCH, MEM_LEN, d), 1.0),
        'mix_norm': gain((DEPTH, d)),
        'ab_w_in': nrm((N_EVEN, d, EVEN_IN), d ** -0.5),
        'ab_q_norm': gain((N_EVEN, HEAD_DIM)),
        'ab_k_norm': gain((N_EVEN, HEAD_DIM)),
        'ab_pool_w': nrm((N_EVEN, len(POOL_WINDOWS), B_GROUP, B_GROUP), B_GROUP ** -0.5),
        'ab_pool_scale': gain((N_EVEN, B_WIDTH)),
        'ab_w_out': nrm((N_EVEN, A_WIDTH + B_WIDTH, d), (A_WIDTH + B_WIDTH) ** -0.5),
        'c_w_in': nrm((N_ODD, d, ODD_IN), d ** -0.5),
        'c_w_a2': nrm((N_ODD, C_GATE_RANK, C_KEY_WIDTH), C_GATE_RANK ** -0.5),
        'c_b_a': nrm((N_ODD, C_KEY_WIDTH), 0.1),
        'c_o_norm': gain((N_ODD, C_DV)),
        'c_w_out': nrm((N_ODD, C_VAL_WIDTH, d), C_VAL_WIDTH ** -0.5),
        'x_norm': gain((DEPTH, d)),
        'x_mem_norm': gain((DEPTH, d)),
        'x_wq': nrm((DEPTH, d, X_WIDTH), d ** -0.5),
        'x_wkv': nrm((DEPTH, d, 2 * X_WIDTH), d ** -0.5),
        'x_q_norm': gain((DEPTH, HEAD_DIM)),
        'x_k_norm': gain((DEPTH, HEAD_DIM)),
        'x_wo': nrm((DEPTH, X_WIDTH, d), X_WIDTH ** -0.5),
        'f_norm': gain((DEPTH, d)),
        'f_w_up': nrm((DEPTH, d, 2 * D_FF), d ** -0.5),
        'f_conv_w': nrm((DEPTH, CONV_WIDTH, 2 * D_FF), CONV_WIDTH ** -0.5),
        'f_conv_b': nrm((DEPTH, 2 * D_FF), 0.02),
        'f_w_down': nrm((DEPTH, D_FF, d), D_FF ** -0.5),
    }


def reference(x, mem, mix_norm, ab_w_in, ab_q_norm, ab_k_norm, ab_pool_w, ab_pool_scale, ab_w_out,
              c_w_in, c_w_a2, c_b_a, c_o_norm, c_w_out,
              x_norm, x_mem_norm, x_wq, x_wkv, x_q_norm, x_k_norm, x_wo,
              f_norm, f_w_up, f_conv_w, f_conv_b, f_w_down):
    h = x
    for layer in range(DEPTH):
        j = layer // 2
        hn = rmsnorm(h, mix_norm[layer])
        if layer % 2 == 0:
            mix = dilated_pool_mixer(hn, ab_w_in[j], ab_q_norm[j], ab_k_norm[j], ab_pool_w[j],
                                     ab_pool_scale[j], ab_w_out[j])
        else:
            mix = gla_mixer(hn, c_w_in[j], c_w_a2[j], c_b_a[j], c_o_norm[j], c_w_out[j])
        h = h + mix
        h = h + memory_cross_attention(rmsnorm(h, x_norm[layer]), rmsnorm(mem, x_mem_norm[layer]),
                                       x_wq[layer], x_wkv[layer], x_q_norm[layer], x_k_norm[layer], x_wo[layer])
        h = h + conv_ffn(rmsnorm(h, f_norm[layer]), f_w_up[layer], f_conv_w[layer], f_conv_b[layer], f_w_down[layer])
    return h
```

```python
import math
import numpy as np
import concourse.bass as bass
import concourse.mybir as mybir
from concourse.bass_utils import run_bass_kernel_spmd

F32 = mybir.dt.float32
AF = mybir.ActivationFunctionType
ALU = mybir.AluOpType
EPS = 1e-6
BIG = 1.0e5


class Cfg:
    def __init__(self, D=4096, T=4096, CH=8):
        self.D = D; self.T = T; self.KC = D // 128
        self.AH = D // 256; self.AW = self.AH * 128
        self.BW = D - self.AW; self.BG = self.BW // 4; self.BGC = self.BG // 128
        self.EVEN_IN = 7 * self.AW + self.BW
        self.CH = CH; self.CKW = D // 2; self.CVW = D
        self.CDK = self.CKW // CH; self.CDV = self.CVW // CH
        self.DKC = self.CDK // 128; self.DVC = self.CDV // 128
        self.ODD_IN = 2 * self.CKW + 2 * self.CVW + 16
        self.DFF = ((8 * D // 3 + 255) // 256) * 256; self.FC = self.DFF // 128
        self.XH = 4; self.XW = 512; self.ML = 256
        self.TT = 512
        self.NT = T // self.TT


class Buf:
    __slots__ = ("name", "w", "r")

    def __init__(self, name):
        self.name = name; self.w = None; self.r = {}


class Sched:
    MAXC = 30000

    def __init__(self, nc):
        self.nc = nc
        self.E = {"pe": nc.tensor, "act": nc.scalar, "dve": nc.vector, "pool": nc.gpsimd, "sp": nc.sync}
        self.sems = []; self.owner = []
        self.cur = {}
        self.seen = {e: {} for e in self.E}
        self.dq = {}
        self.ninst = 0

    def newsem(self, name, owner):
        h = self.nc.alloc_semaphore(name)
        self.sems.append(h); self.owner.append(owner)
        return len(self.sems) - 1

    def _wait(self, e, si, val):
        if self.seen[e].get(si, 0) >= val:
            return
        self.E[e].wait_ge(self.sems[si], val)
        self.seen[e][si] = val

    def _sync(self, e, reads, writes):
        deps = {}
        for b in reads:
            if b.w is not None:
                si, v = b.w
                if not (e == "pe" and self.owner[si] == "pe"):
                    deps[si] = max(deps.get(si, 0), v)
        for b in writes:
            if b.w is not None:
                si, v = b.w
                if self.owner[si] != e:
                    deps[si] = max(deps.get(si, 0), v)
            for si, v in b.r.items():
                if self.owner[si] != e:
                    deps[si] = max(deps.get(si, 0), v)
        for si, v in deps.items():
            self._wait(e, si, v)

    def _tick(self, e):
        c = self.cur.get(e)
        if c is None or c[1] >= self.MAXC:
            c = [self.newsem("c_%s_%d" % (e, len(self.sems)), e), 0]
            self.cur[e] = c
        c[1] += 1
        return c[0], c[1]

    def op(self, e, emit, reads=(), writes=()):
        self._sync(e, reads, writes)
        inst = emit(self.E[e])
        si, v = self._tick(e)
        inst.then_inc(self.sems[si], 1)
        for b in reads:
            b.r[si] = v
        for b in writes:
            b.w = (si, v); b.r = {}
        self.ninst += 1
        return inst

    def dma(self, q, out, in_, reads=(), writes=()):
        self._sync(q, reads, writes)
        ring = self.dq.get(q)
        if ring is None:
            ring = {"s": [[self.newsem("d_%s_%d" % (q, i), "dma_" + q), 0] for i in range(8)], "p": 0}
            self.dq[q] = ring
        p = ring["p"]; ring["p"] = (p + 1) % 8
        ent = ring["s"][p]
        if ent[1] >= self.MAXC:
            self._wait(q, ent[0], ent[1])
            ent = [self.newsem("d_%s_%d" % (q, len(self.sems)), "dma_" + q), 0]
            ring["s"][p] = ent
        self._wait(q, ent[0], ent[1])
        inst = self.E[q].dma_start(out=out, in_=in_)
        ent[1] += 16
        inst.then_inc(self.sems[ent[0]], 16)
        for b in reads:
            b.r[ent[0]] = ent[1]
        for b in writes:
            b.w = (ent[0], ent[1]); b.r = {}
        self.ninst += 1

    def wait_all(self, e, bufs):
        self._sync(e, bufs, ())


class Prog:
    def __init__(self, cfg, stop_after=None):
        self.cfg = cfg
        self.stop_after = stop_after
        nc = bass.Bass("TRN2", target_bir_lowering=False)
        self.nc = nc
        self.s = Sched(nc)
        self.din = {}
        self._alloc()

    def dI(self, name, shape):
        ap = self.nc.dram_tensor(name, list(shape), F32, kind="ExternalInput").ap()
        self.din[name] = tuple(shape)
        return ap

    def dS(self, name, shape):
        return self.nc.dram_tensor(name, list(shape), F32, kind="Internal").ap(), Buf(name)

    def sb(self, name, shape):
        return self.nc.alloc_sbuf_tensor(name, list(shape), F32), Buf(name)

    def _alloc(self):
        c = self.cfg; nc = self.nc
        KC, T, FC = c.KC, c.T, c.FC
        self.xT = self.dI("xT", [KC, 128, T]); self.xTB = Buf("xT")
        self.memT = self.dI("memT", [KC, 128, c.ML])
        self.gains = self.dI("gains", [128, 8, KC])
        self.hgain = self.dI("hgain", [128, 6])
        self.consts = self.dI("consts", [128, 2240])
        self.invc = self.dI("invc", [4, T])
        self.ab_win = self.dI("ab_win", [c.EVEN_IN // 128, 128, KC, 128])
        self.pool_w = self.dI("pool_w", [4 * c.BGC, 128, c.BGC, 128])
        self.pool_sc = self.dI("pool_sc", [128, c.BW // 128])
        self.ab_wout = self.dI("ab_wout", [KC, 128, KC, 128])
        self.c_win = self.dI("c_win", [(c.ODD_IN + 127) // 128, 128, KC, 128])
        self.c_wa2 = self.dI("c_wa2", [16, c.CKW])
        self.c_ba = self.dI("c_ba", [128, c.CKW // 128])
        self.c_on = self.dI("c_on", [128, c.DVC])
        self.c_wout = self.dI("c_wout", [KC, 128, KC, 128])
        self.x_wq = self.dI("x_wq", [2, 4, 128, KC, 128])
        self.x_wk = self.dI("x_wk", [2, 4, 128, KC, 128])
        self.x_wv = self.dI("x_wv", [2, 128, KC, 512])
        self.x_wo = self.dI("x_wo", [2, KC, 128, 4, 128])
        self.f_wup = self.dI("f_wup", [2, 2 * FC, 128, KC, 128])
        self.f_cw = self.dI("f_cw", [128, 2, 2 * FC, 4])
        self.f_wdn = self.dI("f_wdn", [2, KC, 128, FC, 128])
        self.outT = nc.dram_tensor("outT", [KC, 128, T], F32, kind="ExternalOutput").ap()
        self.outTB = Buf("outT")
        self.hA, self.hAB = self.dS("hA", [KC, 128, T])
        self.hB, self.hBB = self.dS("hB", [KC, 128, T])
        nzq = max(3 * c.AH, c.CKW // 128)
        self.zq, self.zqB = self.dS("zq", [nzq, 128, T])
        self.zk, self.zkB = self.dS("zk", [nzq, 128, T])
        self.zv, self.zvB = self.dS("zv", [max(c.AH, c.CVW // 128), 128, T])
        self.zu, self.zuB = self.dS("zu", [max(c.BW // 128, c.CVW // 128), 128, T])
        self.zr, self.zrB = self.dS("zr", [1, 128, T])
        self.ab, self.abB = self.dS("ab", [KC, 128, T])
        self.ffa, self.ffaB = self.dS("ffa", [FC, 128, T])
        self.cst, self.cstB = self.sb("cst", [128, 2240])
        self.ones, self.onesB = self.sb("ones", [128, 128])
        self.gn, self.gnB = self.sb("gn", [128, 8, KC])
        self.hg, self.hgB = self.sb("hg", [128, 6])
        arena_elems = max(KC * 512, 4 * T, 16384)
        self.arena, self.arenaB = self.sb("arena", [128, arena_elems])
        self.wb = []
        for i in range(4):
            self.wb.append(self.sb("wb%d" % i, [128, 32 * 128]))
        self.wctr = 0
        self.sq = [self.sb("sq%d" % i, [128, 512]) for i in range(2)]
        self.t1 = [self.sb("t1_%d" % i, [128, 512]) for i in range(2)]
        self.t2 = [self.sb("t2_%d" % i, [128, 512]) for i in range(2)]
        self.stg = [self.sb("stg%d" % i, [128, 512]) for i in range(4)]
        self.stgc = 0
        self.res = [self.sb("res%d" % i, [128, 512]) for i in range(3)]
        self.resc = 0
        self.gp, self.gpB = self.sb("gp", [128, 8448])
        self.ps = []
        for i in range(8):
            self.ps.append((nc.alloc_psum_tensor("ps%d" % i, [128, 512], F32), Buf("ps%d" % i)))
        self.psGc = 0; self.psXc = 0

    def psG(self):
        p = self.ps[self.psGc % 4]; self.psGc += 1
        return p

    def psX(self):
        p = self.ps[4 + self.psXc % 4]; self.psXc += 1
        return p

    def nstg(self):
        p = self.stg[self.stgc % 4]; self.stgc += 1
        return p

    def nres(self):
        p = self.res[self.resc % 3]; self.resc += 1
        return p

    def act(self, out, in_, func, reads, writes, scale=None, bias=None):
        kw = {}
        if scale is not None:
            kw["scale"] = scale
        if bias is not None:
            kw["bias"] = bias
        return self.s.op("act", lambda E: E.activation(out=out, in_=in_, func=func, **kw), reads, writes)

    def mm(self, psB, out, lhsT, rhs, start, stop, reads):
        return self.s.op("pe", lambda E: E.matmul(out, lhsT=lhsT, rhs=rhs, start=start, stop=stop), reads, [psB])

    def tr(self, psB, out, in_, reads):
        ident = self.cst[:, 0:128]
        return self.s.op("pe", lambda E: E.transpose(out, in_, ident[: in_.shape[0], : in_.shape[0]]), list(reads) + [self.cstB], [psB])

    def stt(self, eng, out, in0, scalar, in1, op0, op1, reads, writes):
        return self.s.op(eng, lambda E: E.scalar_tensor_tensor(out=out, in0=in0, scalar=scalar, in1=in1, op0=op0, op1=op1), reads, writes)

    def tt(self, eng, out, in0, in1, op, reads, writes):
        return self.s.op(eng, lambda E: E.tensor_tensor(out=out, in0=in0, in1=in1, op=op), reads, writes)

    def ts(self, eng, out, in0, s1, op0, reads, writes, s2=None, op1=None):
        if op1 is None:
            return self.s.op(eng, lambda E: E.tensor_scalar(out=out, in0=in0, scalar1=s1, scalar2=None, op0=op0), reads, writes)
        return self.s.op(eng, lambda E: E.tensor_scalar(out=out, in0=in0, scalar1=s1, scalar2=s2, op0=op0, op1=op1), reads, writes)

    def cp(self, eng, out, in_, reads, writes):
        if eng == "act":
            return self.s.op("act", lambda E: E.copy(out=out, in_=in_), reads, writes)
        return self.s.op(eng, lambda E: E.tensor_copy(out=out, in_=in_), reads, writes)

    def recip(self, out, in_, reads, writes):
        return self.s.op("dve", lambda E: E.reciprocal(out=out, in_=in_), reads, writes)

    def rsqrt_from(self, ps_ap, psB, mul, add, P, N):
        t1, t1B = self.t1[0]; self.t1.reverse()
        t2, t2B = self.t2[0]; self.t2.reverse()
        bias_ap = self.epsap(add, P)
        self.act(t1[:P, :N], ps_ap, AF.Sqrt, [psB, self.epstB], [t1B], scale=mul, bias=bias_ap)
        self.recip(t2[:P, :N], t1[:P, :N], [t1B], [t2B])
        return t2, t2B

    def epsap(self, val, P):
        key = float(val)
        if not hasattr(self, "_epsmap"):
            self._epsmap = {}
            self.epst, self.epstB = self.sb("epst", [128, 16])
        if key not in self._epsmap:
            j = len(self._epsmap)
            self.s.op("pool", lambda E: E.memset(self.epst[:, j:j + 1], key), [], [self.epstB])
            self._epsmap[key] = j
        j = self._epsmap[key]
        return self.epst[:P, j:j + 1]

    def prologue(self):
        s = self.s
        s.dma("sp", self.cst[:], self.consts, [], [self.cstB])
        s.dma("sp", self.gn[:], self.gains, [], [self.gnB])
        s.dma("sp", self.hg[:], self.hgain, [], [self.hgB])
        s.op("pool", lambda E: E.memset(self.ones[:], 1.0), [], [self.onesB])

    def load_tile(self, src, srcB, t0, Tt, KCn, gain_idx=None):
        c = self.cfg
        tile = self.arena[:, 0:KCn * Tt].rearrange("p (k t) -> p k t", k=KCn)
        step = max(1, (KCn + 3) // 4)
        for k0 in range(0, KCn, step):
            k1 = min(KCn, k0 + step)
            self.s.dma("sp", tile[:, k0:k1, :], src[k0:k1, :, t0:t0 + Tt].rearrange("k p t -> p k t"), [srcB], [self.arenaB])
        if gain_idx is None:
            return tile
        psq, psqB = self.psX()
        for kc in range(KCn):
            sq, sqB = self.sq[kc % 2]
            self.act(sq[:, :Tt], tile[:, kc, :], AF.Square, [self.arenaB], [sqB])
            self.mm(psqB, psq[:, :Tt], self.ones[:], sq[:, :Tt], kc == 0, kc == KCn - 1, [sqB, self.onesB])
        rstd, rstdB = self.rsqrt_from(psq[:, :Tt], psqB, 1.0 / (KCn * 128), EPS, 128, Tt)
        for kc in range(KCn):
            self.stt("dve", tile[:, kc, :], tile[:, kc, :], self.gn[:, gain_idx, kc:kc + 1], rstd[:, :Tt], ALU.mult, ALU.mult,
                     [self.arenaB, self.gnB, rstdB], [self.arenaB])
        return tile

    def gemm(self, rhs_fn, rhsB, KCtot, Tt, w_dram, CC, m_last, epi, pre=None, cc_list=None):
        pieces = [(k0, min(k0 + 32, KCtot)) for k0 in range(0, KCtot, 32)]
        ccs = list(range(CC)) if cc_list is None else cc_list
        blocks = [(cc, k0, k1) for cc in ccs for (k0, k1) in pieces]
        PF = 3
        loaded = {}

        def load(i):
            cc, k0, k1 = blocks[i]
            wbt, wbB = self.wb[self.wctr % 4]; self.wctr += 1
            wv = wbt[:, 0:(k1 - k0) * 128].rearrange("p (k m) -> p k m", m=128)
            self.s.dma("sp", wv, w_dram[cc, :, k0:k1, :], [], [wbB])
            loaded[i] = (wv, wbB)

        for i in range(min(PF, len(blocks))):
            load(i)
        ps = psB = None
        for i, (cc, k0, k1) in enumerate(blocks):
            if i + PF < len(blocks):
                load(i + PF)
            if k0 == 0:
                ps, psB = self.psG()
                if pre is not None:
                    pre(cc)
            M = m_last if cc == CC - 1 else 128
            wv, wbB = loaded.pop(i)
            for kc in range(k0, k1):
                self.mm(psB, ps[:M, :Tt], wv[:, kc - k0, :M], rhs_fn(kc), kc == 0, kc == KCtot - 1, [wbB, rhsB])
            if k1 == KCtot:
                epi(cc, ps, psB, M)

    def store_chunk(self, dst, dstB, ci, t0, Tt, src_ap, srcB, P=128):
        self.s.dma("pool", dst[ci, 0:P, t0:t0 + Tt], src_ap, [srcB], [dstB])

    def epi_store(self, dst, dstB, t0, Tt, ci_fn=lambda cc: cc, scale=None, eng="act"):
        def epi(cc, ps, psB, M):
            st, stB = self.nstg()
            if scale is not None:
                self.act(st[:M, :Tt], ps[:M, :Tt], AF.Copy, [psB], [stB], scale=scale)
            elif eng == "act":
                self.cp("act", st[:M, :Tt], ps[:M, :Tt], [psB], [stB])
            else:
                self.cp("dve", st[:M, :Tt], ps[:M, :Tt], [psB], [stB])
            self.store_chunk(dst, dstB, ci_fn(cc), t0, Tt, st[:M, :Tt], stB, P=M)
        return epi

    def epi_residual(self, hin, hinB, hout, houtB, t0, Tt):
        pend = {}

        def pre(cc):
            r, rB = self.nres()
            self.s.dma("pool", r[:, :Tt], hin[cc, :, t0:t0 + Tt], [hinB], [rB])
            pend[cc] = (r, rB)

        def epi(cc, ps, psB, M):
            r, rB = pend.pop(cc)
            st, stB = self.nstg()
            self.tt("dve", st[:, :Tt], ps[:, :Tt], r[:, :Tt], ALU.add, [psB, rB], [stB])
            self.store_chunk(hout, houtB, cc, t0, Tt, st[:, :Tt], stB)
        return pre, epi

    def qknorm(self, ps, psB, Tt, gain_ap, gainB, fold, out_ap, outB):
        sq, sqB = self.sq[0]; self.sq.reverse()
        self.act(sq[:, :Tt], ps[:, :Tt], AF.Square, [psB], [sqB])
        p2, p2B = self.psX()
        self.mm(p2B, p2[:, :Tt], self.ones[:], sq[:, :Tt], True, True, [sqB, self.onesB])
        rstd, rstdB = self.rsqrt_from(p2[:, :Tt], p2B, 1.0 / (128.0 * fold * fold), EPS / (fold * fold), 128, Tt)
        self.stt("dve", out_ap, ps[:, :Tt], gain_ap, rstd[:, :Tt], ALU.mult, ALU.mult, [psB, gainB, rstdB], [outB])

    def build(self):
        c = self.cfg
        self.prologue()
        stages = [
            ("l0_inproj", lambda: self.l0_inproj(self.xT, self.xTB)),
            ("l0_attn", self.l0_attn),
            ("l0_pool", self.l0_pool),
            ("l0_out", lambda: self.outproj(self.ab_wout, self.xT, self.xTB, self.hA, self.hAB)),
            ("l0_x", lambda: self.xattn(0, self.hA, self.hAB, self.hB, self.hBB)),
            ("l0_f", lambda: self.ffn(0, self.hB, self.hBB, self.hA, self.hAB)),
            ("l1_inproj", lambda: self.l1_inproj(self.hA, self.hAB)),
            ("l1_gla", self.l1_gla),
            ("l1_out", lambda: self.outproj(self.c_wout, self.hA, self.hAB, self.hB, self.hBB)),
            ("l1_x", lambda: self.xattn(1, self.hB, self.hBB, self.hA, self.hAB)),
            ("l1_f", lambda: self.ffn(1, self.hA, self.hAB, self.hB, self.hBB)),
        ]
        final = (self.hB, self.hBB)
        dbg = {"l0_inproj": (self.zq, self.zqB), "l0_attn": (self.ab, self.abB), "l0_pool": (self.ab, self.abB),
               "l0_out": (self.hA, self.hAB), "l0_x": (self.hB, self.hBB), "l0_f": (self.hA, self.hAB),
               "l1_inproj": (self.zk, self.zkB), "l1_gla": (self.ab, self.abB), "l1_out": (self.hB, self.hBB),
               "l1_x": (self.hA, self.hAB), "l1_f": (self.hB, self.hBB)}
        for name, fn in stages:
            fn()
            if self.stop_after == name:
                final = dbg[name]
                break
        self.copy_out(*final)
        return self.nc

    def copy_out(self, src, srcB):
        c = self.cfg
        n = min(src.shape[0], c.KC)
        step = max(1, n // 8)
        for k0 in range(0, n, step):
            k1 = min(n, k0 + step)
            self.s.dma("sp", self.outT[k0:k1], src[k0:k1], [srcB], [self.outTB])
        for e in ("sp",):
            self.s.wait_all(e, [self.outTB])
        for q, ring in self.s.dq.items():
            for ent in ring["s"]:
                self.s._wait("sp", ent[0], ent[1])

    def l0_inproj(self, hin, hinB):
        c = self.cfg
        AH = c.AH
        nq = 3 * AH
        for ti in range(c.NT):
            t0 = ti * c.TT; Tt = c.TT
            tile = self.load_tile(hin, hinB, t0, Tt, c.KC, gain_idx=0)

            def epi(cc, ps, psB, M, t0=t0, Tt=Tt):
                st, stB = self.nstg()
                if cc < nq:
                    self.qknorm(ps, psB, Tt, self.hg[:, 0:1], self.hgB, 128.0 ** -0.25 * 128.0 ** -0.25, st[:, :Tt], stB)
                    self.store_chunk(self.zq, self.zqB, cc, t0, Tt, st[:, :Tt], stB)
                elif cc < 2 * nq:
                    self.qknorm(ps, psB, Tt, self.hg[:, 1:2], self.hgB, 1.0, st[:, :Tt], stB)
                    self.store_chunk(self.zk, self.zkB, cc - nq, t0, Tt, st[:, :Tt], stB)
                elif cc < 2 * nq + AH:
                    self.cp("act", st[:, :Tt], ps[:, :Tt], [psB], [stB])
                    self.store_chunk(self.zv, self.zvB, cc - 2 * nq, t0, Tt, st[:, :Tt], stB)
                else:
                    self.cp("act", st[:, :Tt], ps[:, :Tt], [psB], [stB])
                    self.store_chunk(self.zu, self.zuB, cc - 2 * nq - AH, t0, Tt, st[:, :Tt], stB)

            self.gemm(lambda kc: tile[:, kc, :], self.arenaB, c.KC, Tt, self.ab_win, c.EVEN_IN // 128, 128, epi)

    def l0_attn(self):
        c = self.cfg; T = c.T; AH = c.AH
        ar = self.arena
        assert ar.shape[1] >= 4 * T
        qb = ar[:, 0:T]; kb = ar[:, T:2 * T]; vT = ar[:, 2 * T:3 * T]; accO = ar[:, 3 * T:4 * T]
        gp = self.gp
        assert gp.shape[1] >= T + 32 * 128
        accD = gp[:, 0:T]
        vtok = gp[:, T:T + 32 * 128].rearrange("p (b e) -> p b e", e=128)
        qB, kB, vB, aOB, aDB, vtB = Buf("aq"), Buf("ak"), Buf("av"), Buf("aO"), Buf("aD"), Buf("avt")
        fenceR = [self.arenaB, self.gpB]
        relcur = self.cst[:, 128:640]; relprev = self.cst[:, 640:1152]; relprevF = self.cst[:, 1152:1664]
        n_sl = 3 * AH
        slopes = [2.0 ** (-8.0 * (i + 1) / n_sl) for i in range(n_sl)]
        branches = [(128, 1), (512, 4), (2048, 16)]
        self.fence(fenceR)
        for hd in range(AH):
            self.s.dma("sp", vT, self.zv[hd, :, :], [self.zvB], [vB])
            for g, (win, d) in enumerate(branches):
                cneg = -slopes[g * AH + hd] * d
                self.s.dma("sp", qb, self.zq[g * AH + hd, :, :], [self.zqB], [qB])
                self.s.dma("sp", kb, self.zk[g * AH + hd, :, :], [self.zkB], [kB])
                NB = T // d // 128
                G = min(4, NB)
                nblk = d * NB
                for b0 in range(0, nblk, 4):
                    pt, ptB = self.psX()
                    for j in range(4):
                        blk = b0 + j; r = blk // NB; n = blk % NB
                        st0 = r + d * 128 * n
                        self.tr(ptB, pt[:, j * 128:(j + 1) * 128], vT[:, st0:st0 + d * 127 + 1:d], [vB])
                    self.cp("act", vtok[:, b0:b0 + 4, :], pt[:, :].rearrange("p (b e) -> p b e", e=128), [ptB], [vtB])
                for r in range(d):
                    for n0 in range(0, NB, G):
                        W = G * 128
                        sc, scB = self.psX(); sp_, spB = self.psX()
                        for j in range(G):
                            n = n0 + j
                            st0 = r + d * 128 * n
                            qs = qb[:, st0:st0 + d * 127 + 1:d]
                            kcur = kb[:, st0:st0 + d * 127 + 1:d]
                            self.mm(scB, sc[:, j * 128:(j + 1) * 128], kcur, qs, True, True, [kB, qB])
                            if n > 0:
                                stp = r + d * 128 * (n - 1)
                                kprev = kb[:, stp:stp + d * 127 + 1:d]
                            else:
                                kprev = kcur
                            self.mm(spB, sp_[:, j * 128:(j + 1) * 128], kprev, qs, True, True, [kB, qB])
                        pc, pcB = self.nstg(); pp, ppB = self.nstg()
                        self.stt("dve", pc[:, :W], relcur[:, :W], cneg, sc[:, :W], ALU.mult, ALU.add, [self.cstB, scB], [pcB])
                        rp = relprevF if n0 == 0 else relprev
                        self.stt("dve", pp[:, :W], rp[:, :W], cneg, sp_[:, :W], ALU.mult, ALU.add, [self.cstB, spB], [ppB])
                        self.act(pc[:, :W], pc[:, :W], AF.Exp, [pcB], [pcB])
                        self.act(pp[:, :W], pp[:, :W], AF.Exp, [ppB], [ppB])
                        po, poB = self.psG(); pd, pdB = self.psG()
                        for j in range(G):
                            n = n0 + j
                            blk = r * NB + n
                            self.mm(poB, po[:, j * 128:(j + 1) * 128], vtok[:, blk, :], pc[:, j * 128:(j + 1) * 128], True, n == 0, [vtB, pcB])
                            if n > 0:
                                self.mm(poB, po[:, j * 128:(j + 1) * 128], vtok[:, blk - 1, :], pp[:, j * 128:(j + 1) * 128], False, True, [vtB, ppB])
                            self.mm(pdB, pd[:, j * 128:(j + 1) * 128], self.ones[:], pc[:, j * 128:(j + 1) * 128], True, False, [self.onesB, pcB])
                            self.mm(pdB, pd[:, j * 128:(j + 1) * 128], self.ones[:], pp[:, j * 128:(j + 1) * 128], False, True, [self.onesB, ppB])
                        st0 = r + d * 128 * n0
                        osl = accO[:, st0:st0 + d * (W - 1) + 1:d]; dsl = accD[:, st0:st0 + d * (W - 1) + 1:d]
                        if g == 0:
                            self.cp("act", osl, po[:, :W], [poB], [aOB])
                            self.cp("dve", dsl, pd[:, :W], [pdB], [aDB])
                        else:
                            self.tt("dve", osl, po[:, :W], osl, ALU.add, [poB, aOB], [aOB])
                            self.tt("dve", dsl, pd[:, :W], dsl, ALU.add, [pdB, aDB], [aDB])
            self.recip(accD, accD, [aDB], [aDB])
            self.tt("dve", accO, accO, accD, ALU.mult, [aOB, aDB], [aOB])
            self.s.dma("pool", self.ab[hd, :, :], accO, [aOB], [self.abB])
        self._fence_release([qB, kB, vB, aOB, aDB, vtB], [self.arenaB, self.gpB])

    def fence(self, wholes):
        for e in ("act", "dve", "pool", "sp", "pe"):
            self.s._sync(e, [], wholes)

    def _fence_release(self, subs, wholes):
        for wB in wholes:
            for b in subs:
                if b.w is not None:
                    si, v = b.w
                    wB.r[si] = max(wB.r.get(si, 0), v)
                for si, v in b.r.items():
                    wB.r[si] = max(wB.r.get(si, 0), v)

    def l0_pool(self):
        c = self.cfg; T = c.T
        PADL = 16
        L = min(T, 2048)
        NH = T // L
        W = PADL + L
        ar = self.arena; gp = self.gp
        assert gp.shape[1] >= 3 * W + L
        ub = [gp[:, i * W:(i + 1) * W] for i in range(3)]
        ubB = [Buf("pu%d" % i) for i in range(3)]
        invt = gp[:, 3 * W:3 * W + L]; invB = Buf("invt")
        pooled = ar[:, 0:c.BGC * T].rearrange("p (j t) -> p j t", j=c.BGC)
        pooledB = Buf("pooled")
        psc, pscB = self.sb("psc", [128, c.BW // 128])
        self.s.dma("sp", psc[:], self.pool_sc, [], [pscB])
        self.fence([self.arenaB, self.gpB])
        for gi, win in enumerate((2, 4, 8, 16)):
            for hh in range(NH):
                t0 = hh * L
                self.s.dma("sp", invt, self.invc[gi:gi + 1, t0:t0 + L].partition_broadcast(128), [], [invB])
                for j in range(c.BGC):
                    ci = gi * c.BGC + j
                    if hh == 0:
                        self.s.op("pool", lambda E: E.memset(ub[0][:, 0:PADL], 0.0), [], [ubB[0]])
                        self.s.dma("sp", ub[0][:, PADL:], self.zu[ci, :, t0:t0 + L], [self.zuB], [ubB[0]])
                    else:
                        self.s.dma("sp", ub[0][:, :], self.zu[ci, :, t0 - PADL:t0 + L], [self.zuB], [ubB[0]])
                    src_i = 0; span = 1; dst_i = 1
                    while span < win:
                        self.tt("dve", ub[dst_i][:, span:W], ub[src_i][:, span:W], ub[src_i][:, 0:W - span], ALU.add,
                                [ubB[src_i]], [ubB[dst_i]])
                        src_i = dst_i
                        dst_i = 2 if dst_i == 1 else 1
                        span *= 2
                    self.tt("dve", ub[src_i][:, PADL:], ub[src_i][:, PADL:], invt, ALU.mult, [ubB[src_i], invB], [ubB[src_i]])
                    self.tt("dve", pooled[:, j, t0:t0 + L], ub[src_i][:, PADL:], ub[0][:, PADL:], ALU.subtract, [ubB[src_i], ubB[0]],
                            [pooledB])
            for ti in range(c.NT):
                t0 = ti * c.TT; Tt = c.TT

                def epi(cc, ps, psB, M, t0=t0, Tt=Tt):
                    st, stB = self.nstg()
                    self.ts("dve", st[:, :Tt], ps[:, :Tt], psc[:, cc:cc + 1], ALU.mult, [psB, pscB], [stB])
                    self.store_chunk(self.ab, self.abB, c.AH + cc, t0, Tt, st[:, :Tt], stB)

                self.gemm(lambda kc, t0=t0, Tt=Tt: pooled[:, kc, t0:t0 + Tt], pooledB, c.BGC, Tt, self.pool_w, 4 * c.BGC, 128, epi,
                          cc_list=[gi * c.BGC + oc for oc in range(c.BGC)])
        self._fence_release(ubB + [pooledB, invB], [self.arenaB, self.gpB])

    def outproj(self, w, hin, hinB, hout, houtB):
        c = self.cfg
        for ti in range(c.NT):
            t0 = ti * c.TT; Tt = c.TT
            tile = self.load_tile(self.ab, self.abB, t0, Tt, c.KC)
            pre, epi = self.epi_residual(hin, hinB, hout, houtB, t0, Tt)
            self.gemm(lambda kc: tile[:, kc, :], self.arenaB, c.KC, Tt, w, c.KC, 128, epi, pre=pre)

    def xattn(self, l, hin, hinB, hout, houtB):
        c = self.cfg; ML = c.ML
        gp = self.gp
        o0 = 0
        kx = gp[:, o0:o0 + 4 * ML].rearrange("p (h m) -> p h m", h=4); o0 += 4 * ML
        vx = gp[:, o0:o0 + 2 * 512].rearrange("p (b n) -> p b n", b=2); o0 += 1024
        qx = gp[:, o0:o0 + 4 * 512].rearrange("p (h t) -> p h t", h=4); o0 += 2048
        ox = gp[:, o0:o0 + 4 * 512].rearrange("p (h t) -> p h t", h=4); o0 += 2048
        pb = gp[:, o0:o0 + 2 * 512].rearrange("p (b t) -> p b t", b=2); o0 += 1024
        assert gp.shape[1] >= o0
        kxB, vxB, qxB, oxB, pbB = Buf("kx"), Buf("vx"), Buf("qx"), Buf("ox"), Buf("pb")
        fence = [self.gpB]
        mt = self.load_tile(self.memT, Buf("memT"), 0, ML, c.KC, gain_idx=4 + l)
        self.fence(fence)

        def epik(cc, ps, psB, M):
            self.qknorm(ps, psB, ML, self.hg[:, 4 + l:5 + l], self.hgB, 1.0, kx[:, cc, :], kxB)
        self.gemm(lambda kc: mt[:, kc, :], self.arenaB, c.KC, ML, self.x_wk[l], 4, 128, epik)
        for mb in range(ML // 128):
            ps, psB = self.psG()
            for k0 in range(0, c.KC, 8):
                k1 = min(c.KC, k0 + 8)
                wbt, wbB = self.wb[self.wctr % 4]; self.wctr += 1
                wv = wbt[:, 0:(k1 - k0) * 512].rearrange("p (k n) -> p k n", n=512)
                self.s.dma("sp", wv, self.x_wv[l, :, k0:k1, :], [], [wbB])
                for kc in range(k0, k1):
                    self.mm(psB, ps[:, :], mt[:, kc, mb * 128:(mb + 1) * 128], wv[:, kc - k0, :], kc == 0, kc == c.KC - 1, [wbB, self.arenaB])
            self.cp("act", vx[:, mb, :], ps[:, :], [psB], [vxB])
        for ti in range(c.NT):
            t0 = ti * c.TT; Tt = c.TT
            tile = self.load_tile(hin, hinB, t0, Tt, c.KC, gain_idx=2 + l)

            def epiq(cc, ps, psB, M):
                self.qknorm(ps, psB, Tt, self.hg[:, 2 + l:3 + l], self.hgB, 128.0 ** -0.5, qx[:, cc, :], qxB)
            self.gemm(lambda kc: tile[:, kc, :], self.arenaB, c.KC, Tt, self.x_wq[l], 4, 128, epiq)
            for h in range(4):
                pd, pdB = self.psX(); po, poB = self.psX()
                for mb in range(2):
                    ps, psB = self.psX()
                    self.mm(psB, ps[:, :Tt], kx[:, h, mb * 128:(mb + 1) * 128], qx[:, h, :], True, True, [kxB, qxB])
                    self.act(pb[:, mb, :], ps[:, :Tt], AF.Exp, [psB], [pbB])
                for mb in range(2):
                    self.mm(pdB, pd[:, :Tt], self.ones[:], pb[:, mb, :], mb == 0, mb == 1, [self.onesB, pbB])
                for mb in range(2):
                    self.mm(poB, po[:, :Tt], vx[:, mb, h * 128:(h + 1) * 128], pb[:, mb, :], mb == 0, mb == 1, [vxB, pbB])
                rc, rcB = self.nstg()
                self.recip(rc[:, :Tt], pd[:, :Tt], [pdB], [rcB])
                self.tt("dve", ox[:, h, :], po[:, :Tt], rc[:, :Tt], ALU.mult, [poB, rcB], [oxB])
            pre, epi = self.epi_residual(hin, hinB, hout, houtB, t0, Tt)
            self.gemm(lambda kc: ox[:, kc, :], oxB, 4, Tt, self.x_wo[l], c.KC, 128, epi, pre=pre)
        self._fence_release([kxB, vxB, qxB, oxB, pbB], [self.gpB])

    def ffn(self, l, hin, hinB, hout, houtB):
        c = self.cfg; FC = c.FC
        gp = self.gp
        o0 = 0
        cw = gp[:, o0:o0 + 2 * FC * 4].rearrange("p (c f) -> p c f", f=4); o0 += 2 * FC * 4
        tails = gp[:, o0:o0 + 2 * FC * 2].rearrange("p (c f) -> p c f", f=2); o0 += 2 * FC * 2
        ub = []
        for i in range(2):
            ub.append(gp[:, o0:o0 + 514]); o0 += 516
        cg = gp[:, o0:o0 + 512]; o0 += 512
        cv = gp[:, o0:o0 + 512]; o0 += 512
        assert gp.shape[1] >= o0
        cwB, tlB, cgB, cvB = Buf("cw"), Buf("tails"), Buf("cg"), Buf("cv")
        ubB = [Buf("fub0"), Buf("fub1")]
        fence = [self.gpB]
        self.fence(fence)
        self.s.dma("sp", cw, self.f_cw[:, l, :, :], [], [cwB])
        self.s.op("pool", lambda E: E.memset(tails, 0.0), [], [tlB])
        ucnt = [0]
        for ti in range(c.NT):
            t0 = ti * c.TT; Tt = c.TT
            tile = self.load_tile(hin, hinB, t0, Tt, c.KC, gain_idx=6 + l)

            def epi(cc, ps, psB, M, t0=t0, Tt=Tt):
                u = ub[ucnt[0] % 2]; uB = ubB[ucnt[0] % 2]; ucnt[0] += 1
                self.cp("act", u[:, 2:2 + Tt], ps[:, :Tt], [psB], [uB])
                self.cp("pool", u[:, 0:2], tails[:, cc, :], [tlB], [uB])
                self.cp("pool", tails[:, cc, :], u[:, Tt:Tt + 2], [uB], [tlB])
                isg = (cc % 2 == 0)
                dst, dstB = (cg, cgB) if isg else (cv, cvB)
                self.act(dst[:, :Tt], u[:, 2:2 + Tt], AF.Identity, [uB, cwB], [dstB], scale=cw[:, cc, 2:3], bias=cw[:, cc, 3:4])
                self.stt("dve", dst[:, :Tt], u[:, 1:1 + Tt], cw[:, cc, 1:2], dst[:, :Tt], ALU.mult, ALU.add, [uB, cwB, dstB], [dstB])
                self.stt("dve", dst[:, :Tt], u[:, 0:Tt], cw[:, cc, 0:1], dst[:, :Tt], ALU.mult, ALU.add, [uB, cwB, dstB], [dstB])
                if not isg:
                    st, stB = self.nstg()
                    self.act(cg[:, :Tt], cg[:, :Tt], AF.Silu, [cgB], [cgB])
                    self.tt("dve", st[:, :Tt], cg[:, :Tt], cv[:, :Tt], ALU.mult, [cgB, cvB], [stB])
                    self.store_chunk(self.ffa, self.ffaB, cc // 2, t0, Tt, st[:, :Tt], stB)

            self.gemm(lambda kc: tile[:, kc, :], self.arenaB, c.KC, Tt, self.f_wup[l], 2 * FC, 128, epi)
        self._fence_release([cwB, tlB, cgB, cvB] + ubB, [self.gpB])
        Td = 256
        halves = [(0, FC // 2), (FC // 2, FC)]
        route = [(hin, hinB, self.ab, self.abB), (self.ab, self.abB, hout, houtB)]
        for (k0, k1), (src_h, src_hB, dst_h, dst_hB) in zip(halves, route):
            for ti in range(c.T // Td):
                t0 = ti * Td
                tile = self.load_tile(self.ffa[k0:k1], self.ffaB, t0, Td, k1 - k0)
                pre, epi = self.epi_residual(src_h, src_hB, dst_h, dst_hB, t0, Td)
                self.gemm(lambda kc: tile[:, kc, :], self.arenaB, k1 - k0, Td, self.f_wdn[l][:, :, k0:k1, :], c.KC, 128, epi, pre=pre)

    def l1_inproj(self, hin, hinB):
        c = self.cfg
        nq = c.CKW // 128; nv = c.CVW // 128
        CC = (c.ODD_IN + 127) // 128
        for ti in range(c.NT):
            t0 = ti * c.TT; Tt = c.TT
            tile = self.load_tile(hin, hinB, t0, Tt, c.KC, gain_idx=1)

            def epi(cc, ps, psB, M, t0=t0, Tt=Tt):
                st, stB = self.nstg()
                if cc < nq:
                    self.act(st[:, :Tt], ps[:, :Tt], AF.Copy, [psB], [stB], scale=float(c.CDK) ** -0.5)
                    self.store_chunk(self.zq, self.zqB, cc, t0, Tt, st[:, :Tt], stB)
                elif cc < 2 * nq:
                    self.cp("act", st[:, :Tt], ps[:, :Tt], [psB], [stB])
                    self.store_chunk(self.zk, self.zkB, cc - nq, t0, Tt, st[:, :Tt], stB)
                elif cc < 2 * nq + nv:
                    self.cp("dve", st[:, :Tt], ps[:, :Tt], [psB], [stB])
                    self.store_chunk(self.zv, self.zvB, cc - 2 * nq, t0, Tt, st[:, :Tt], stB)
                elif cc < 2 * nq + 2 * nv:
                    self.act(st[:, :Tt], ps[:, :Tt], AF.Silu, [psB], [stB])
                    self.store_chunk(self.zu, self.zuB, cc - 2 * nq - nv, t0, Tt, st[:, :Tt], stB)
                else:
                    self.cp("act", st[:16, :Tt], ps[:16, :Tt], [psB], [stB])
                    self.store_chunk(self.zr, self.zrB, 0, t0, Tt, st[:16, :Tt], stB, P=16)

            self.gemm(lambda kc: tile[:, kc, :], self.arenaB, c.KC, Tt, self.c_win, CC, 16, epi)

    def l1_gla(self):
        c = self.cfg; T = c.T; DKC = c.DKC; DVC = c.DVC; Tt = c.TT
        NCH = Tt // 64
        ar = self.arena
        o0 = 0

        def carve(n):
            nonlocal o0
            a = ar[:, o0:o0 + n]; o0 += n
            return a
        wa2 = carve(c.CKW)
        ba = carve(c.CKW // 128)
        on = carve(DVC)
        rT = carve(Tt)
        qt = carve(DKC * Tt).rearrange("p (k t) -> p k t", k=DKC)
        kt = carve(DKC * Tt).rearrange("p (k t) -> p k t", k=DKC)
        vt = carve(DVC * Tt).rearrange("p (k t) -> p k t", k=DVC)
        gt = carve(DVC * Tt).rearrange("p (k t) -> p k t", k=DVC)
        bc = carve(DKC * Tt).rearrange("p (k t) -> p k t", k=DKC)
        ebc = carve(DKC * Tt).rearrange("p (k t) -> p k t", k=DKC)
        dec = carve(DKC * NCH).rearrange("p (k n) -> p k n", k=DKC)
        state = carve(DKC * c.CDV).rearrange("p (k e) -> p k e", k=DKC)
        ot = carve(DVC * Tt).rearrange("p (k t) -> p k t", k=DVC)
        kst = carve(DKC * 64).rearrange("p (k t) -> p k t", k=DKC)
        attm = carve(64)
        vtok = carve(c.CDV)
        ksttok = carve(c.CDK)
        assert ar.shape[1] >= o0, (ar.shape, o0)
        names = ["wa2", "ba", "on", "rT", "qt", "kt", "vt", "gt", "bc", "ebc", "dec", "state", "ot", "kst", "attm", "vtok", "ksttok"]
        B = {n: Buf("g_" + n) for n in names}
        fence = [self.arenaB]
        scanm = self.cst[:, 1728:2240]
        mask01 = self.cst[0:64, 1664:1728]
        self.fence(fence)
        self.s.dma("sp", wa2[0:16, :], self.c_wa2, [], [B["wa2"]])
        self.s.dma("sp", ba, self.c_ba, [], [B["ba"]])
        self.s.dma("sp", on, self.c_on, [], [B["on"]])
        self.ts("dve", ba, ba, -1.0, ALU.mult, [B["ba"]], [B["ba"]])
        for hd in range(c.CH):
            for dc in range(DKC):
                self.s.op("pool", lambda E, dc=dc: E.memset(state[:, dc, :], 0.0), [], [B["state"]])
            for ti in range(c.NT):
                t0 = ti * Tt
                self.s.dma("sp", rT[0:16, :], self.zr[0, 0:16, t0:t0 + Tt], [self.zrB], [B["rT"]])
                self.s.dma("sp", qt, self.zq[hd * DKC:(hd + 1) * DKC, :, t0:t0 + Tt].rearrange("k p t -> p k t"), [self.zqB], [B["qt"]])
                self.s.dma("sp", kt, self.zk[hd * DKC:(hd + 1) * DKC, :, t0:t0 + Tt].rearrange("k p t -> p k t"), [self.zkB], [B["kt"]])
                self.s.dma("sp", vt, self.zv[hd * DVC:(hd + 1) * DVC, :, t0:t0 + Tt].rearrange("k p t -> p k t"), [self.zvB], [B["vt"]])
                self.s.dma("sp", gt, self.zu[hd * DVC:(hd + 1) * DVC, :, t0:t0 + Tt].rearrange("k p t -> p k t"), [self.zuB], [B["gt"]])
                for dc in range(DKC):
                    col = hd * DKC + dc
                    ps, psB = self.psX()
                    self.mm(psB, ps[:, :Tt], wa2[0:16, col * 128:(col + 1) * 128], rT[0:16, :], True, True, [B["wa2"], B["rT"]])
                    self.act(ebc[:, dc, :], ps[:, :Tt], AF.Exp, [psB, B["ba"]], [B["ebc"]], scale=-1.0, bias=ba[:, col:col + 1])
                    one_ap = self.epsap(1.0, 128)
                    self.act(ebc[:, dc, :], ebc[:, dc, :], AF.Ln, [B["ebc"], self.epstB], [B["ebc"]], bias=one_ap)
                    self.ts("dve", ebc[:, dc, :], ebc[:, dc, :], -1.0 / 16.0, ALU.mult, [B["ebc"]], [B["ebc"]])
                    self.s.op("dve", lambda E, dc=dc: E.tensor_tensor_scan(out=bc[:, dc, :], data0=scanm[:, :Tt], data1=ebc[:, dc, :], initial=0.0,
                                                                            op0=ALU.mult, op1=ALU.add), [self.cstB, B["ebc"]], [B["bc"]])
                    self.act(dec[:, dc, :], bc[:, dc, 63:Tt:64], AF.Exp, [B["bc"]], [B["dec"]])
                    self.act(ebc[:, dc, :], bc[:, dc, :], AF.Exp, [B["bc"]], [B["ebc"]])
                    self.tt("dve", qt[:, dc, :], qt[:, dc, :], ebc[:, dc, :], ALU.mult, [B["qt"], B["ebc"]], [B["qt"]])
                    self.act(ebc[:, dc, :], bc[:, dc, :], AF.Exp, [B["bc"]], [B["ebc"]], scale=-1.0)
                    self.tt("dve", kt[:, dc, :], kt[:, dc, :], ebc[:, dc, :], ALU.mult, [B["kt"], B["ebc"]], [B["kt"]])
                for ch in range(NCH):
                    cs = slice(ch * 64, (ch + 1) * 64)
                    pa, paB = self.psX()
                    for dc in range(DKC):
                        self.mm(paB, pa[0:64, 0:64], kt[:, dc, cs], qt[:, dc, cs], dc == 0, dc == DKC - 1, [B["kt"], B["qt"]])
                    self.tt("dve", attm[0:64, :], pa[0:64, 0:64], mask01, ALU.mult, [paB, self.cstB], [B["attm"]])
                    pv, pvB = self.psX()
                    for ec in range(DVC):
                        self.tr(pvB, pv[0:64, ec * 128:(ec + 1) * 128], vt[:, ec, cs], [B["vt"]])
                    self.cp("act", vtok[0:64, :], pv[0:64, 0:c.CDV], [pvB], [B["vtok"]])
                    for dc in range(DKC):
                        self.ts("dve", kst[:, dc, :], kt[:, dc, cs], dec[:, dc, ch:ch + 1], ALU.mult, [B["kt"], B["dec"]], [B["kst"]])
                    pk, pkB = self.psX()
                    for dc in range(DKC):
                        self.tr(pkB, pk[0:64, dc * 128:(dc + 1) * 128], kst[:, dc, :], [B["kst"]])
                    self.cp("act", ksttok[0:64, :], pk[0:64, 0:c.CDK], [pkB], [B["ksttok"]])
                    po, poB = self.psG()
                    for ec in range(DVC):
                        osl = po[:, ec * 64:(ec + 1) * 64]
                        self.mm(poB, osl, vtok[0:64, ec * 128:(ec + 1) * 128], attm[0:64, :], True, False, [B["vtok"], B["attm"]])
                        for dc in range(DKC):
                            self.mm(poB, osl, state[:, dc, ec * 128:(ec + 1) * 128], qt[:, dc, cs], False, dc == DKC - 1, [B["state"], B["qt"]])
                    self.cp("act", ot[:, :, cs], po[:, 0:DVC * 64].rearrange("p (k t) -> p k t", k=DVC), [poB], [B["ot"]])
                    for dc in range(DKC):
                        pkv, pkvB = self.psG()
                        self.mm(pkvB, pkv[:, 0:c.CDV], ksttok[0:64, dc * 128:(dc + 1) * 128], vtok[0:64, :], True, True, [B["ksttok"], B["vtok"]])
                        self.stt("dve", state[:, dc, :], state[:, dc, :], dec[:, dc, ch:ch + 1], pkv[:, 0:c.CDV], ALU.mult, ALU.add,
                                 [B["state"], B["dec"], pkvB], [B["state"]])
                psq, psqB = self.psX()
                for ec in range(DVC):
                    sq, sqB = self.sq[ec % 2]
                    self.act(sq[:, :Tt], ot[:, ec, :], AF.Square, [B["ot"]], [sqB])
                    self.mm(psqB, psq[:, :Tt], self.ones[:], sq[:, :Tt], ec == 0, ec == DVC - 1, [sqB, self.onesB])
                rstd, rstdB = self.rsqrt_from(psq[:, :Tt], psqB, 1.0 / c.CDV, EPS, 128, Tt)
                for ec in range(DVC):
                    st, stB = self.nstg()
                    self.stt("dve", st[:, :Tt], ot[:, ec, :], on[:, ec:ec + 1], rstd[:, :Tt], ALU.mult, ALU.mult, [B["ot"], B["on"], rstdB], [stB])
                    self.tt("dve", st[:, :Tt], st[:, :Tt], gt[:, ec, :], ALU.mult, [stB, B["gt"]], [stB])
                    self.store_chunk(self.ab, self.abB, hd * DVC + ec, t0, Tt, st[:, :Tt], stB)
        self._fence_release(list(B.values()), [self.arenaB])


def _wblocks(W):
    K, N = W.shape
    CC = (N + 127) // 128
    if CC * 128 != N:
        W = np.concatenate([W, np.zeros((K, CC * 128 - N), W.dtype)], axis=1)
    return np.ascontiguousarray(W.reshape(K // 128, 128, CC, 128).transpose(2, 1, 0, 3))


def _fm(v):
    return np.ascontiguousarray(v.reshape(-1, 128).T)


def _consts():
    cst = np.zeros((128, 2240), np.float32)
    cst[:, 0:128] = np.eye(128, dtype=np.float32)
    i = np.arange(128)[:, None]; j = np.arange(128)[None, :]
    cur = np.where(j >= i, (j - i).astype(np.float32), BIG).astype(np.float32)
    prev = np.where(i >= j, (j + 128 - i).astype(np.float32), BIG).astype(np.float32)
    for r in range(4):
        cst[:, 128 + r * 128:128 + (r + 1) * 128] = cur
        cst[:, 640 + r * 128:640 + (r + 1) * 128] = prev
        cst[:, 1152 + r * 128:1152 + (r + 1) * 128] = prev if r > 0 else BIG
    jj = np.arange(64)[:, None]; ii = np.arange(64)[None, :]
    cst[0:64, 1664:1728] = (ii >= jj).astype(np.float32)
    m = np.ones(512, np.float32); m[::64] = 0.0
    cst[:, 1728:2240] = m[None, :]
    return cst


def prep_shared(cfg, inp):
    c = cfg
    f = lambda a: np.asarray(a, dtype=np.float32)
    d = {}
    gains = np.zeros((128, 8, c.KC), np.float32)
    for i, (nm, l) in enumerate([("mix_norm", 0), ("mix_norm", 1), ("x_norm", 0), ("x_norm", 1), ("x_mem_norm", 0), ("x_mem_norm", 1),
                                 ("f_norm", 0), ("f_norm", 1)]):
        gains[:, i, :] = _fm(f(inp[nm])[l])
    d["gains"] = gains
    hg = np.zeros((128, 6), np.float32)
    hg[:, 0] = f(inp["ab_q_norm"])[0]; hg[:, 1] = f(inp["ab_k_norm"])[0]
    hg[:, 2] = f(inp["x_q_norm"])[0]; hg[:, 3] = f(inp["x_q_norm"])[1]
    hg[:, 4] = f(inp["x_k_norm"])[0]; hg[:, 5] = f(inp["x_k_norm"])[1]
    d["hgain"] = hg
    d["consts"] = _consts()
    t = np.arange(1, c.T + 1, dtype=np.float32)
    d["invc"] = np.stack([np.float32(1.0) / np.minimum(t, np.float32(w)) for w in (2, 4, 8, 16)]).astype(np.float32)
    d["ab_win"] = _wblocks(f(inp["ab_w_in"])[0])
    pw = f(inp["ab_pool_w"])[0]
    d["pool_w"] = np.concatenate([_wblocks(pw[g]) for g in range(4)], axis=0)
    d["pool_sc"] = _fm(f(inp["ab_pool_scale"])[0])
    d["ab_wout"] = _wblocks(f(inp["ab_w_out"])[0])
    d["c_win"] = _wblocks(f(inp["c_w_in"])[0])
    d["c_wa2"] = np.ascontiguousarray(f(inp["c_w_a2"])[0])
    d["c_ba"] = _fm(f(inp["c_b_a"])[0])
    d["c_on"] = _fm(f(inp["c_o_norm"])[0])
    d["c_wout"] = _wblocks(f(inp["c_w_out"])[0])
    wq = f(inp["x_wq"]); wkv = f(inp["x_wkv"]); wo = f(inp["x_wo"])
    d["x_wq"] = np.stack([_wblocks(wq[l]) for l in range(2)])
    d["x_wk"] = np.stack([_wblocks(wkv[l][:, :512]) for l in range(2)])
    d["x_wv"] = np.stack([np.ascontiguousarray(wkv[l][:, 512:].reshape(c.KC, 128, 512).transpose(1, 0, 2)) for l in range(2)])
    d["x_wo"] = np.stack([_wblocks(wo[l]) for l in range(2)])
    wup = f(inp["f_w_up"]); cw = f(inp["f_conv_w"]); cb = f(inp["f_conv_b"]); wdn = f(inp["f_w_down"])
    idx = np.concatenate([np.concatenate([np.arange(j * 128, (j + 1) * 128), c.DFF + np.arange(j * 128, (j + 1) * 128)]) for j in range(c.FC)])
    d["f_wup"] = np.stack([_wblocks(wup[l][:, idx]) for l in range(2)])
    fcw = np.zeros((128, 2, 2 * c.FC, 4), np.float32)
    for l in range(2):
        for k in range(3):
            fcw[:, l, :, k] = _fm(cw[l, k][idx])
        fcw[:, l, :, 3] = _fm(cb[l][idx])
    d["f_cw"] = fcw
    d["f_wdn"] = np.stack([_wblocks(wdn[l]) for l in range(2)])
    return d


def _tfm(a):
    T, D = a.shape
    return np.ascontiguousarray(a.T.reshape(D // 128, 128, T))


_PROG_CACHE = {}


def run(cfg, inputs, stop_after=None, cores=None):
    key = (cfg.D, cfg.T, cfg.CH, stop_after)
    if key not in _PROG_CACHE:
        p = Prog(cfg, stop_after=stop_after)
        p.build()
        _PROG_CACHE[key] = p
    p = _PROG_CACHE[key]
    shared = prep_shared(cfg, inputs)
    x = np.asarray(inputs["x"], dtype=np.float32); mem = np.asarray(inputs["mem"], dtype=np.float32)
    nb = x.shape[0]
    in_maps = []
    for b in range(nb):
        m = dict(shared)
        m["xT"] = _tfm(x[b]); m["memT"] = _tfm(mem[b])
        in_maps.append(m)
    res = run_bass_kernel_spmd(p.nc, in_maps, core_ids=list(range(nb)))
    outs = []
    for b in range(nb):
        o = res.results[b]["outT"]
        outs.append(np.ascontiguousarray(o.reshape(cfg.D, cfg.T).T))
    return np.stack(outs)


def kernel(**inputs):
    cfg = Cfg(4096, 4096, 8)
    return run(cfg, inputs).astype(np.float32)
```

```python
import math
import numpy as np
import concourse.bass as bass
import concourse.mybir as mybir
from concourse.bass_utils import run_bass_kernel_spmd

F32 = mybir.dt.float32
BF16 = mybir.dt.bfloat16
AF = mybir.ActivationFunctionType
ALU = mybir.AluOpType
EPS = 1e-6
BIG = 1.0e5


class Cfg:
    def __init__(self, D=4096, T=4096, CH=8):
        self.D = D; self.T = T; self.KC = D // 128
        self.AH = D // 256; self.AW = self.AH * 128
        self.BW = D - self.AW; self.BG = self.BW // 4; self.BGC = self.BG // 128
        self.EVEN_IN = 7 * self.AW + self.BW
        self.CH = CH; self.CKW = D // 2; self.CVW = D
        self.CDK = self.CKW // CH; self.CDV = self.CVW // CH
        self.DKC = self.CDK // 128; self.DVC = self.CDV // 128
        self.ODD_IN = 2 * self.CKW + 2 * self.CVW + 16
        self.DFF = ((8 * D // 3 + 255) // 256) * 256; self.FC = self.DFF // 128
        self.XH = 4; self.XW = 512; self.ML = 256
        self.TT = 512
        self.NT = T // self.TT
        self.GT = 1024
        self.NGT = T // self.GT


class Buf:
    __slots__ = ("name", "w", "r")

    def __init__(self, name):
        self.name = name; self.w = None; self.r = {}


class Sched:
    MAXC = 30000

    def __init__(self, nc):
        self.nc = nc
        self.E = {"pe": nc.tensor, "act": nc.scalar, "dve": nc.vector, "pool": nc.gpsimd, "sp": nc.sync}
        self.sems = []; self.owner = []
        self.cur = {}
        self.seen = {e: {} for e in self.E}
        self.dq = {}
        self.ninst = 0

    def newsem(self, name, owner):
        h = self.nc.alloc_semaphore(name)
        self.sems.append(h); self.owner.append(owner)
        return len(self.sems) - 1

    def _wait(self, e, si, val):
        if self.seen[e].get(si, 0) >= val:
            return
        self.E[e].wait_ge(self.sems[si], val)
        self.seen[e][si] = val

    def _sync(self, e, reads, writes):
        deps = {}
        for b in reads:
            if b.w is not None:
                si, v = b.w
                if not (e == "pe" and self.owner[si] == "pe"):
                    deps[si] = max(deps.get(si, 0), v)
        for b in writes:
            if b.w is not None:
                si, v = b.w
                if self.owner[si] != e:
                    deps[si] = max(deps.get(si, 0), v)
            for si, v in b.r.items():
                if self.owner[si] != e:
                    deps[si] = max(deps.get(si, 0), v)
        for si, v in deps.items():
            self._wait(e, si, v)

    def _tick(self, e):
        c = self.cur.get(e)
        if c is None or c[1] >= self.MAXC:
            c = [self.newsem("c_%s_%d" % (e, len(self.sems)), e), 0]
            self.cur[e] = c
        c[1] += 1
        return c[0], c[1]

    def op(self, e, emit, reads=(), writes=()):
        self._sync(e, reads, writes)
        inst = emit(self.E[e])
        si, v = self._tick(e)
        inst.then_inc(self.sems[si], 1)
        for b in reads:
            b.r[si] = v
        for b in writes:
            b.w = (si, v); b.r = {}
        self.ninst += 1
        return inst

    def dma(self, q, out, in_, reads=(), writes=()):
        self._sync(q, reads, writes)
        ring = self.dq.get(q)
        if ring is None:
            ring = {"s": [[self.newsem("d_%s_%d" % (q, i), "dma_" + q), 0] for i in range(8)], "p": 0}
            self.dq[q] = ring
        p = ring["p"]; ring["p"] = (p + 1) % 8
        ent = ring["s"][p]
        if ent[1] >= self.MAXC:
            self._wait(q, ent[0], ent[1])
            ent = [self.newsem("d_%s_%d" % (q, len(self.sems)), "dma_" + q), 0]
            ring["s"][p] = ent
        self._wait(q, ent[0], ent[1])
        inst = self.E[q].dma_start(out=out, in_=in_)
        ent[1] += 16
        inst.then_inc(self.sems[ent[0]], 16)
        for b in reads:
            b.r[ent[0]] = ent[1]
        for b in writes:
            b.w = (ent[0], ent[1]); b.r = {}
        self.ninst += 1

    def wait_all(self, e, bufs):
        self._sync(e, bufs, ())


class Prog:
    def __init__(self, cfg, stop_after=None):
        self.cfg = cfg
        self.stop_after = stop_after
        nc = bass.Bass("TRN2", target_bir_lowering=False)
        self.nc = nc
        self.s = Sched(nc)
        self.din = {}
        self._alloc()

    def dI(self, name, shape):
        ap = self.nc.dram_tensor(name, list(shape), F32, kind="ExternalInput").ap()
        self.din[name] = tuple(shape)
        return ap

    def dS(self, name, shape, dt=F32):
        return self.nc.dram_tensor(name, list(shape), dt, kind="Internal").ap(), Buf(name)

    def sb(self, name, shape):
        return self.nc.alloc_sbuf_tensor(name, list(shape), F32), Buf(name)

    def _alloc(self):
        c = self.cfg; nc = self.nc
        KC, T, FC = c.KC, c.T, c.FC
        self.xT = self.dI("xT", [KC, 128, T]); self.xTB = Buf("xT")
        self.memT = self.dI("memT", [KC, 128, c.ML])
        self.gains = self.dI("gains", [128, 8, KC])
        self.hgain = self.dI("hgain", [128, 6])
        self.consts = self.dI("consts", [128, 2240])
        self.invc = self.dI("invc", [4, T])
        self.ab_win = self.dI("ab_win", [c.EVEN_IN // 128, 128, KC, 128])
        self.pool_w = self.dI("pool_w", [4 * c.BGC, 128, c.BGC, 128])
        self.pool_sc = self.dI("pool_sc", [128, c.BW // 128])
        self.ab_wout = self.dI("ab_wout", [KC, 128, KC, 128])
        self.c_win = self.dI("c_win", [(c.ODD_IN + 127) // 128, 128, KC, 128])
        self.c_wa2 = self.dI("c_wa2", [16, c.CKW])
        self.c_ba = self.dI("c_ba", [128, c.CKW // 128])
        self.c_on = self.dI("c_on", [128, c.DVC])
        self.c_wout = self.dI("c_wout", [KC, 128, KC, 128])
        self.x_wq = self.dI("x_wq", [2, 4, 128, KC, 128])
        self.x_wk = self.dI("x_wk", [2, 4, 128, KC, 128])
        self.x_wv = self.dI("x_wv", [2, 128, KC, 512])
        self.x_wo = self.dI("x_wo", [2, KC, 128, 4, 128])
        self.f_wup = self.dI("f_wup", [2, 2 * FC, 128, KC, 128])
        self.f_cw = self.dI("f_cw", [128, 2, 2 * FC, 4])
        self.f_wdn = self.dI("f_wdn", [2, KC, 128, FC, 128])
        self.outT = nc.dram_tensor("outT", [KC, 128, T], F32, kind="ExternalOutput").ap()
        self.outTB = Buf("outT")
        self.hA, self.hAB = self.dS("hA", [KC, 128, T])
        self.hB, self.hBB = self.dS("hB", [KC, 128, T])
        nzq = max(3 * c.AH, c.CKW // 128)
        self.zq, self.zqB = self.dS("zq", [nzq, 128, T])
        self.zk, self.zkB = self.dS("zk", [nzq, 128, T])
        self.zv, self.zvB = self.dS("zv", [max(c.AH, c.CVW // 128), 128, T])
        self.zu, self.zuB = self.dS("zu", [max(c.BW // 128, c.CVW // 128), 128, T])
        self.zr, self.zrB = self.dS("zr", [1, 128, T])
        self.ab, self.abB = self.dS("ab", [KC, 128, T], BF16)
        self.ffa, self.ffaB = self.dS("ffa", [FC, 128, T], BF16)
        self.hT1, self.hT1B = self.dS("hT1", [KC, 128, T])
        self.hT2, self.hT2B = self.dS("hT2", [KC, 128, T])
        self.cst, self.cstB = self.sb("cst", [128, 2240])
        self.ones, self.onesB = self.sb("ones", [128, 128])
        self.gn, self.gnB = self.sb("gn", [128, 8, KC])
        self.hg, self.hgB = self.sb("hg", [128, 6])
        arena_elems = max(KC * 512, 4 * T, 16384)
        self.arena, self.arenaB = self.sb("arena", [128, arena_elems])
        self.KP = 16
        self.wb = []
        for i in range(3):
            self.wb.append(self.sb("wb%d" % i, [128, self.KP * 128]))
        self.wctr = 0
        self.wb16 = []
        for i in range(3):
            t = self.nc.alloc_sbuf_tensor("wbh%d" % i, [128, self.KP * 128], BF16)
            self.wb16.append((t, Buf("wbh%d" % i)))
        self.w16ctr = 0
        self.ld = [self.sb("ld%d" % i, [128, 1024]) for i in range(4)]
        self.ldc = 0
        self.ones16 = self.nc.alloc_sbuf_tensor("ones16", [128, 128], BF16); self.ones16B = Buf("ones16")
        self.sq = [self.sb("sq%d" % i, [128, 1024]) for i in range(2)]
        self.t1 = [self.sb("t1_%d" % i, [128, 1024]) for i in range(2)]
        self.t2 = [self.sb("t2_%d" % i, [128, 1024]) for i in range(2)]
        self.stg = [self.sb("stg%d" % i, [128, 512]) for i in range(4)]
        self.stgc = 0
        self.res = [self.sb("res%d" % i, [128, 512]) for i in range(4)]
        self.resc = 0
        self.gp, self.gpB = self.sb("gp", [128, 8448])
        self.ps = []
        for i in range(8):
            self.ps.append((nc.alloc_psum_tensor("ps%d" % i, [128, 512], F32), Buf("ps%d" % i)))
        self.psGc = 0; self.psXc = 0

    def psG(self):
        p = self.ps[self.psGc % 4]; self.psGc += 1
        return p

    def psX(self):
        p = self.ps[4 + self.psXc % 4]; self.psXc += 1
        return p

    def nstg(self):
        p = self.stg[self.stgc % 4]; self.stgc += 1
        return p

    def nres(self):
        p = self.res[self.resc % 4]; self.resc += 1
        return p

    def act(self, out, in_, func, reads, writes, scale=None, bias=None):
        kw = {}
        if scale is not None:
            kw["scale"] = scale
        if bias is not None:
            kw["bias"] = bias
        return self.s.op("act", lambda E: E.activation(out=out, in_=in_, func=func, **kw), reads, writes)

    def mm(self, psB, out, lhsT, rhs, start, stop, reads):
        return self.s.op("pe", lambda E: E.matmul(out, lhsT=lhsT, rhs=rhs, start=start, stop=stop), reads, [psB])

    def tr(self, psB, out, in_, reads):
        ident = self.cst[:, 0:128]
        return self.s.op("pe", lambda E: E.transpose(out, in_, ident[: in_.shape[0], : in_.shape[0]]), list(reads) + [self.cstB], [psB])

    def stt(self, eng, out, in0, scalar, in1, op0, op1, reads, writes):
        return self.s.op(eng, lambda E: E.scalar_tensor_tensor(out=out, in0=in0, scalar=scalar, in1=in1, op0=op0, op1=op1), reads, writes)

    def tt(self, eng, out, in0, in1, op, reads, writes):
        return self.s.op(eng, lambda E: E.tensor_tensor(out=out, in0=in0, in1=in1, op=op), reads, writes)

    def ts(self, eng, out, in0, s1, op0, reads, writes, s2=None, op1=None):
        if op1 is None:
            return self.s.op(eng, lambda E: E.tensor_scalar(out=out, in0=in0, scalar1=s1, scalar2=None, op0=op0), reads, writes)
        return self.s.op(eng, lambda E: E.tensor_scalar(out=out, in0=in0, scalar1=s1, scalar2=s2, op0=op0, op1=op1), reads, writes)

    def cp(self, eng, out, in_, reads, writes):
        if eng == "act":
            return self.s.op("act", lambda E: E.copy(out=out, in_=in_), reads, writes)
        return self.s.op(eng, lambda E: E.tensor_copy(out=out, in_=in_), reads, writes)

    def recip(self, out, in_, reads, writes):
        return self.s.op("dve", lambda E: E.reciprocal(out=out, in_=in_), reads, writes)

    def rsqrt_from(self, ps_ap, psB, mul, add, P, N):
        t1, t1B = self.t1[0]; self.t1.reverse()
        t2, t2B = self.t2[0]; self.t2.reverse()
        bias_ap = self.epsap(add, P)
        self.act(t1[:P, :N], ps_ap, AF.Sqrt, [psB, self.epstB], [t1B], scale=mul, bias=bias_ap)
        self.recip(t2[:P, :N], t1[:P, :N], [t1B], [t2B])
        return t2, t2B

    def epsap(self, val, P):
        key = float(val)
        if not hasattr(self, "_epsmap"):
            self._epsmap = {}
            self.epst, self.epstB = self.sb("epst", [128, 16])
        if key not in self._epsmap:
            j = len(self._epsmap)
            self.s.op("pool", lambda E: E.memset(self.epst[:, j:j + 1], key), [], [self.epstB])
            self._epsmap[key] = j
        j = self._epsmap[key]
        return self.epst[:P, j:j + 1]

    def prologue(self):
        s = self.s
        s.dma("sp", self.cst[:], self.consts, [], [self.cstB])
        s.dma("sp", self.gn[:], self.gains, [], [self.gnB])
        s.dma("sp", self.hg[:], self.hgain, [], [self.hgB])
        s.op("pool", lambda E: E.memset(self.ones[:], 1.0), [], [self.onesB])

    def nld(self):
        p = self.ld[self.ldc % 4]; self.ldc += 1
        return p

    def load_tile(self, src, srcB, t0, Tt, KCn, gain_idx=None, src_bf16=False):
        a16 = self.arena[:].bitcast(BF16)
        tile = a16[:, 0:KCn * Tt].rearrange("p (k t) -> p k t", k=KCn)
        if src_bf16:
            step = max(1, (KCn + 3) // 4)
            for k0 in range(0, KCn, step):
                k1 = min(KCn, k0 + step)
                self.s.dma("sp", tile[:, k0:k1, :], src[k0:k1, :, t0:t0 + Tt].rearrange("k p t -> p k t"), [srcB], [self.arenaB])
            return tile
        if gain_idx is None:
            for kc in range(KCn):
                l, lB = self.nld()
                self.s.dma("sp", l[:, :Tt], src[kc, :, t0:t0 + Tt], [srcB], [lB])
                self.cp("act" if kc % 2 else "pool", tile[:, kc, :], l[:, :Tt], [lB], [self.arenaB])
            return tile
        acc, accB = self.t1[0]; self.t1.reverse()
        for kc in range(KCn):
            l, lB = self.nld()
            self.s.dma("sp", l[:, :Tt], src[kc, :, t0:t0 + Tt], [srcB], [lB])
            if kc == 0:
                self.act(acc[:, :Tt], l[:, :Tt], AF.Square, [lB], [accB])
            else:
                sq, sqB = self.sq[kc % 2]
                self.act(sq[:, :Tt], l[:, :Tt], AF.Square, [lB], [sqB])
                self.tt("pool", acc[:, :Tt], acc[:, :Tt], sq[:, :Tt], ALU.add, [accB, sqB], [accB])
        rstd, rstdB = self.t2[0]; self.t2.reverse()
        for s0 in range(0, Tt, 512):
            wd = min(512, Tt - s0)
            psq, psqB = self.psX()
            self.mm(psqB, psq[:, :wd], self.ones[:], acc[:, s0:s0 + wd], True, True, [accB, self.onesB])
            bias_ap = self.epsap(EPS, 128)
            sq, sqB = self.sq[0]; self.sq.reverse()
            self.act(sq[:, :wd], psq[:, :wd], AF.Sqrt, [psqB, self.epstB], [sqB], scale=1.0 / (KCn * 128), bias=bias_ap)
            self.recip(rstd[:, s0:s0 + wd], sq[:, :wd], [sqB], [rstdB])
        for kc in range(KCn):
            l, lB = self.nld()
            self.s.dma("sp", l[:, :Tt], src[kc, :, t0:t0 + Tt], [srcB], [lB])
            self.stt("dve", tile[:, kc, :], l[:, :Tt], self.gn[:, gain_idx, kc:kc + 1], rstd[:, :Tt], ALU.mult, ALU.mult,
                     [lB, self.gnB, rstdB], [self.arenaB])
        return tile

    def gemm(self, rhs_fn, rhsB, KCtot, nsub, w_dram, CC, m_last, epi, pre=None, cc_list=None, lowp=True, sw=512):
        KP = self.KP
        pieces = [(k0, min(k0 + KP, KCtot)) for k0 in range(0, KCtot, KP)]
        ccs = list(range(CC)) if cc_list is None else cc_list
        blocks = [(cc, k0, k1) for cc in ccs for (k0, k1) in pieces]
        PF = 2
        loaded = {}

        def load(i):
            cc, k0, k1 = blocks[i]
            wbt, wbB = self.wb[self.wctr % 3]; self.wctr += 1
            wv = wbt[:, 0:(k1 - k0) * 128].rearrange("p (k m) -> p k m", m=128)
            self.s.dma("sp", wv, w_dram[cc, :, k0:k1, :], [], [wbB])
            loaded[i] = (wv, wbB)

        for i in range(min(PF, len(blocks))):
            load(i)
        pss = None
        for i, (cc, k0, k1) in enumerate(blocks):
            if i + PF < len(blocks):
                load(i + PF)
            if k0 == 0:
                pss = [self.psG() for _ in range(nsub)]
                if pre is not None:
                    for sub in range(nsub):
                        pre(cc, sub)
            M = m_last if cc == CC - 1 else 128
            wv, wbB = loaded.pop(i)
            if lowp:
                w16t, w16B = self.wb16[self.w16ctr % 3]; self.w16ctr += 1
                w16 = w16t[:, 0:(k1 - k0) * 128].rearrange("p (k m) -> p k m", m=128)
                self.cp("pool" if (self.w16ctr % 2) else "act", w16, wv, [wbB], [w16B])
                wv, wbB = w16, w16B
            for kc in range(k0, k1):
                for sub in range(nsub):
                    ps, psB = pss[sub]
                    self.mm(psB, ps[:M, :sw], wv[:, kc - k0, :M], rhs_fn(kc, sub), kc == 0, kc == KCtot - 1, [wbB, rhsB])
            if k1 == KCtot:
                for sub in range(nsub):
                    ps, psB = pss[sub]
                    epi(cc, ps, psB, M, sub)

    def store_chunk(self, dst, dstB, ci, t0, Tt, src_ap, srcB, P=128):
        self.s.dma("pool", dst[ci, 0:P, t0:t0 + Tt], src_ap, [srcB], [dstB])

    def epi_store(self, dst, dstB, t0, Tt, ci_fn=lambda cc: cc, scale=None, eng="act"):
        def epi(cc, ps, psB, M):
            st, stB = self.nstg()
            if scale is not None:
                self.act(st[:M, :Tt], ps[:M, :Tt], AF.Copy, [psB], [stB], scale=scale)
            elif eng == "act":
                self.cp("act", st[:M, :Tt], ps[:M, :Tt], [psB], [stB])
            else:
                self.cp("dve", st[:M, :Tt], ps[:M, :Tt], [psB], [stB])
            self.store_chunk(dst, dstB, ci_fn(cc), t0, Tt, st[:M, :Tt], stB, P=M)
        return epi

    def epi_residual(self, hin, hinB, hout, houtB, t0):
        pend = {}

        def pre(cc, sub):
            r, rB = self.nres()
            ts0 = t0 + sub * 512
            self.s.dma("pool", r[:, :512], hin[cc, :, ts0:ts0 + 512], [hinB], [rB])
            pend[(cc, sub)] = (r, rB)

        def epi(cc, ps, psB, M, sub):
            r, rB = pend.pop((cc, sub))
            st, stB = self.nstg()
            ts0 = t0 + sub * 512
            self.tt("dve", st[:, :512], ps[:, :512], r[:, :512], ALU.add, [psB, rB], [stB])
            self.store_chunk(hout, houtB, cc, ts0, 512, st[:, :512], stB)
        return pre, epi

    def qknorm(self, ps, psB, Tt, gain_ap, gainB, fold, out_ap, outB):
        sq, sqB = self.sq[0]; self.sq.reverse()
        self.act(sq[:, :Tt], ps[:, :Tt], AF.Square, [psB], [sqB])
        p2, p2B = self.psX()
        self.mm(p2B, p2[:, :Tt], self.ones[:], sq[:, :Tt], True, True, [sqB, self.onesB])
        rstd, rstdB = self.rsqrt_from(p2[:, :Tt], p2B, 1.0 / (128.0 * fold * fold), EPS / (fold * fold), 128, Tt)
        self.stt("dve", out_ap, ps[:, :Tt], gain_ap, rstd[:, :Tt], ALU.mult, ALU.mult, [psB, gainB, rstdB], [outB])

    def build(self):
        c = self.cfg
        self.prologue()
        stages = [
            ("l0_inproj", lambda: self.l0_inproj(self.xT, self.xTB)),
            ("l0_attn", self.l0_attn),
            ("l0_pool", self.l0_pool),
            ("l0_out", lambda: self.outproj(self.ab_wout, self.xT, self.xTB, self.hA, self.hAB)),
            ("l0_x", lambda: self.xattn(0, self.hA, self.hAB, self.hB, self.hBB)),
            ("l0_f", lambda: self.ffn(0, self.hB, self.hBB, self.hA, self.hAB)),
            ("l1_inproj", lambda: self.l1_inproj(self.hA, self.hAB)),
            ("l1_gla", self.l1_gla),
            ("l1_out", lambda: self.outproj(self.c_wout, self.hA, self.hAB, self.hB, self.hBB)),
            ("l1_x", lambda: self.xattn(1, self.hB, self.hBB, self.hA, self.hAB)),
            ("l1_f", lambda: self.ffn(1, self.hA, self.hAB, self.hB, self.hBB)),
        ]
        final = (self.hB, self.hBB)
        dbg = {"l0_inproj": (self.zq, self.zqB), "l0_attn": (self.ab, self.abB), "l0_pool": (self.ab, self.abB),
               "l0_out": (self.hA, self.hAB), "l0_x": (self.hB, self.hBB), "l0_f": (self.hA, self.hAB),
               "l1_inproj": (self.zk, self.zkB), "l1_gla": (self.ab, self.abB), "l1_out": (self.hB, self.hBB),
               "l1_x": (self.hA, self.hAB), "l1_f": (self.hB, self.hBB)}
        for name, fn in stages:
            fn()
            if self.stop_after == name:
                final = dbg[name]
                break
        self.copy_out(*final)
        return self.nc

    def copy_out(self, src, srcB):
        c = self.cfg
        n = min(src.shape[0], c.KC)
        step = max(1, n // 8)
        for k0 in range(0, n, step):
            k1 = min(n, k0 + step)
            self.s.dma("sp", self.outT[k0:k1], src[k0:k1], [srcB], [self.outTB])
        for e in ("sp",):
            self.s.wait_all(e, [self.outTB])
        for q, ring in self.s.dq.items():
            for ent in ring["s"]:
                self.s._wait("sp", ent[0], ent[1])

    def l0_inproj(self, hin, hinB):
        c = self.cfg
        AH = c.AH
        nq = 3 * AH
        for ti in range(c.NGT):
            t0 = ti * c.GT
            tile = self.load_tile(hin, hinB, t0, c.GT, c.KC, gain_idx=0)

            def epi(cc, ps, psB, M, sub, t0=t0):
                ts0 = t0 + sub * 512; Tt = 512
                st, stB = self.nstg()
                if cc < nq:
                    self.qknorm(ps, psB, Tt, self.hg[:, 0:1], self.hgB, 128.0 ** -0.5, st[:, :Tt], stB)
                    self.store_chunk(self.zq, self.zqB, cc, ts0, Tt, st[:, :Tt], stB)
                elif cc < 2 * nq:
                    self.qknorm(ps, psB, Tt, self.hg[:, 1:2], self.hgB, 1.0, st[:, :Tt], stB)
                    self.store_chunk(self.zk, self.zkB, cc - nq, ts0, Tt, st[:, :Tt], stB)
                elif cc < 2 * nq + AH:
                    self.cp("act", st[:, :Tt], ps[:, :Tt], [psB], [stB])
                    self.store_chunk(self.zv, self.zvB, cc - 2 * nq, ts0, Tt, st[:, :Tt], stB)
                else:
                    self.cp("dve", st[:, :Tt], ps[:, :Tt], [psB], [stB])
                    self.store_chunk(self.zu, self.zuB, cc - 2 * nq - AH, ts0, Tt, st[:, :Tt], stB)

            self.gemm(lambda kc, sub: tile[:, kc, sub * 512:(sub + 1) * 512], self.arenaB, c.KC, c.GT // 512, self.ab_win,
                      c.EVEN_IN // 128, 128, epi)

    def l0_attn(self):
        c = self.cfg; T = c.T; AH = c.AH
        ar = self.arena
        assert ar.shape[1] >= 4 * T
        qb = ar[:, 0:T]; kb = ar[:, T:2 * T]; vT = ar[:, 2 * T:3 * T]; accO = ar[:, 3 * T:4 * T]
        gp = self.gp
        assert gp.shape[1] >= T + 32 * 128
        accD = gp[:, 0:T]
        vtok = gp[:, T:T + 32 * 128].rearrange("p (b e) -> p b e", e=128)
        qB, kB, vB, aOB, aDB, vtB = Buf("aq"), Buf("ak"), Buf("av"), Buf("aO"), Buf("aD"), Buf("avt")
        fenceR = [self.arenaB, self.gpB]
        relcur = self.cst[:, 128:640]; relprev = self.cst[:, 640:1152]; relprevF = self.cst[:, 1152:1664]
        n_sl = 3 * AH
        slopes = [2.0 ** (-8.0 * (i + 1) / n_sl) for i in range(n_sl)]
        branches = [(128, 1), (512, 4), (2048, 16)]
        self.fence(fenceR)
        for hd in range(AH):
            self.s.dma("sp", vT, self.zv[hd, :, :], [self.zvB], [vB])
            for g, (win, d) in enumerate(branches):
                cneg = -slopes[g * AH + hd] * d
                self.s.dma("sp", qb, self.zq[g * AH + hd, :, :], [self.zqB], [qB])
                self.s.dma("sp", kb, self.zk[g * AH + hd, :, :], [self.zkB], [kB])
                NB = T // d // 128
                G = min(4, NB)
                nblk = d * NB
                for b0 in range(0, nblk, 4):
                    pt, ptB = self.psX()
                    for j in range(4):
                        blk = b0 + j; r = blk // NB; n = blk % NB
                        st0 = r + d * 128 * n
                        self.tr(ptB, pt[:, j * 128:(j + 1) * 128], vT[:, st0:st0 + d * 127 + 1:d], [vB])
                    self.cp("act", vtok[:, b0:b0 + 4, :], pt[:, :].rearrange("p (b e) -> p b e", e=128), [ptB], [vtB])
                for r in range(d):
                    for n0 in range(0, NB, G):
                        W = G * 128
                        sc, scB = self.psX(); sp_, spB = self.psX()
                        for j in range(G):
                            n = n0 + j
                            st0 = r + d * 128 * n
                            qs = qb[:, st0:st0 + d * 127 + 1:d]
                            kcur = kb[:, st0:st0 + d * 127 + 1:d]
                            self.mm(scB, sc[:, j * 128:(j + 1) * 128], kcur, qs, True, True, [kB, qB])
                            if n > 0:
                                stp = r + d * 128 * (n - 1)
                                kprev = kb[:, stp:stp + d * 127 + 1:d]
                            else:
                                kprev = kcur
                            self.mm(spB, sp_[:, j * 128:(j + 1) * 128], kprev, qs, True, True, [kB, qB])
                        pc, pcB = self.nstg(); pp, ppB = self.nstg()
                        self.stt("dve", pc[:, :W], relcur[:, :W], cneg, sc[:, :W], ALU.mult, ALU.add, [self.cstB, scB], [pcB])
                        rp = relprevF if n0 == 0 else relprev
                        self.stt("dve", pp[:, :W], rp[:, :W], cneg, sp_[:, :W], ALU.mult, ALU.add, [self.cstB, spB], [ppB])
                        self.act(pc[:, :W], pc[:, :W], AF.Exp, [pcB], [pcB])
                        self.act(pp[:, :W], pp[:, :W], AF.Exp, [ppB], [ppB])
                        po, poB = self.psG(); pd, pdB = self.psG()
                        for j in range(G):
                            n = n0 + j
                            blk = r * NB + n
                            self.mm(poB, po[:, j * 128:(j + 1) * 128], vtok[:, blk, :], pc[:, j * 128:(j + 1) * 128], True, n == 0, [vtB, pcB])
                            if n > 0:
                                self.mm(poB, po[:, j * 128:(j + 1) * 128], vtok[:, blk - 1, :], pp[:, j * 128:(j + 1) * 128], False, True, [vtB, ppB])
                            self.mm(pdB, pd[:, j * 128:(j + 1) * 128], self.ones[:], pc[:, j * 128:(j + 1) * 128], True, False, [self.onesB, pcB])
                            self.mm(pdB, pd[:, j * 128:(j + 1) * 128], self.ones[:], pp[:, j * 128:(j + 1) * 128], False, True, [self.onesB, ppB])
                        st0 = r + d * 128 * n0
                        osl = accO[:, st0:st0 + d * (W - 1) + 1:d]; dsl = accD[:, st0:st0 + d * (W - 1) + 1:d]
                        if g == 0:
                            self.cp("act", osl, po[:, :W], [poB], [aOB])
                            self.cp("dve", dsl, pd[:, :W], [pdB], [aDB])
                        else:
                            self.tt("dve", osl, po[:, :W], osl, ALU.add, [poB, aOB], [aOB])
                            self.tt("dve", dsl, pd[:, :W], dsl, ALU.add, [pdB, aDB], [aDB])
            self.recip(accD, accD, [aDB], [aDB])
            o16 = qb.bitcast(BF16)[:, 0:T]
            self.tt("dve", o16, accO, accD, ALU.mult, [aOB, aDB], [qB])
            self.s.dma("pool", self.ab[hd, :, :], o16, [qB], [self.abB])
        self._fence_release([qB, kB, vB, aOB, aDB, vtB], [self.arenaB, self.gpB])

    def fence(self, wholes):
        for e in ("act", "dve", "pool", "sp", "pe"):
            self.s._sync(e, [], wholes)

    def _fence_release(self, subs, wholes):
        for wB in wholes:
            for b in subs:
                if b.w is not None:
                    si, v = b.w
                    wB.r[si] = max(wB.r.get(si, 0), v)
                for si, v in b.r.items():
                    wB.r[si] = max(wB.r.get(si, 0), v)

    def l0_pool(self):
        c = self.cfg; T = c.T
        PADL = 16
        L = min(T, 2048)
        NH = T // L
        W = PADL + L
        ar = self.arena; gp = self.gp
        assert gp.shape[1] >= 3 * W + L
        ub = [gp[:, i * W:(i + 1) * W] for i in range(3)]
        ubB = [Buf("pu%d" % i) for i in range(3)]
        invt = gp[:, 3 * W:3 * W + L]; invB = Buf("invt")
        pooled = ar[:, 0:c.BGC * T].rearrange("p (j t) -> p j t", j=c.BGC)
        pooledB = Buf("pooled")
        psc, pscB = self.sb("psc", [128, c.BW // 128])
        self.s.dma("sp", psc[:], self.pool_sc, [], [pscB])
        self.fence([self.arenaB, self.gpB])
        for gi, win in enumerate((2, 4, 8, 16)):
            for hh in range(NH):
                t0 = hh * L
                self.s.dma("sp", invt, self.invc[gi:gi + 1, t0:t0 + L].partition_broadcast(128), [], [invB])
                for j in range(c.BGC):
                    ci = gi * c.BGC + j
                    if hh == 0:
                        self.s.op("pool", lambda E: E.memset(ub[0][:, 0:PADL], 0.0), [], [ubB[0]])
                        self.s.dma("sp", ub[0][:, PADL:], self.zu[ci, :, t0:t0 + L], [self.zuB], [ubB[0]])
                    else:
                        self.s.dma("sp", ub[0][:, :], self.zu[ci, :, t0 - PADL:t0 + L], [self.zuB], [ubB[0]])
                    src_i = 0; span = 1; dst_i = 1
                    while span < win:
                        self.tt("dve", ub[dst_i][:, span:W], ub[src_i][:, span:W], ub[src_i][:, 0:W - span], ALU.add,
                                [ubB[src_i]], [ubB[dst_i]])
                        src_i = dst_i
                        dst_i = 2 if dst_i == 1 else 1
                        span *= 2
                    self.tt("dve", ub[src_i][:, PADL:], ub[src_i][:, PADL:], invt, ALU.mult, [ubB[src_i], invB], [ubB[src_i]])
                    self.tt("dve", pooled[:, j, t0:t0 + L], ub[src_i][:, PADL:], ub[0][:, PADL:], ALU.subtract, [ubB[src_i], ubB[0]],
                            [pooledB])
            for ti in range(c.NT):
                t0 = ti * c.TT; Tt = c.TT

                def epi(cc, ps, psB, M, sub, t0=t0, Tt=Tt):
                    st, stB = self.nstg()
                    s16 = st[:].bitcast(BF16)
                    self.ts("dve", s16[:, :Tt], ps[:, :Tt], psc[:, cc:cc + 1], ALU.mult, [psB, pscB], [stB])
                    self.store_chunk(self.ab, self.abB, c.AH + cc, t0, Tt, s16[:, :Tt], stB)

                self.gemm(lambda kc, sub, t0=t0, Tt=Tt: pooled[:, kc, t0:t0 + Tt], pooledB, c.BGC, 1, self.pool_w, 4 * c.BGC, 128, epi,
                          cc_list=[gi * c.BGC + oc for oc in range(c.BGC)], lowp=False)
        self._fence_release(ubB + [pooledB, invB], [self.arenaB, self.gpB])

    def outproj(self, w, hin, hinB, hout, houtB):
        c = self.cfg
        for ti in range(c.NGT):
            t0 = ti * c.GT
            tile = self.load_tile(self.ab, self.abB, t0, c.GT, c.KC, src_bf16=True)
            pre, epi = self.epi_residual(hin, hinB, hout, houtB, t0)
            self.gemm(lambda kc, sub: tile[:, kc, sub * 512:(sub + 1) * 512], self.arenaB, c.KC, c.GT // 512, w, c.KC, 128, epi, pre=pre)

    def xattn(self, l, hin, hinB, hout, houtB):
        c = self.cfg; ML = c.ML
        gp = self.gp
        o0 = 0
        kx = gp[:, o0:o0 + 4 * ML].rearrange("p (h m) -> p h m", h=4); o0 += 4 * ML
        vx = gp[:, o0:o0 + 2 * 512].rearrange("p (b n) -> p b n", b=2); o0 += 1024
        qx = gp[:, o0:o0 + 4 * 512].rearrange("p (h t) -> p h t", h=4); o0 += 2048
        ox16 = gp[:, o0:o0 + 1024].bitcast(BF16).rearrange("p (h t) -> p h t", h=4); o0 += 1024
        pb = gp[:, o0:o0 + 2 * 512].rearrange("p (b t) -> p b t", b=2); o0 += 1024
        assert gp.shape[1] >= o0
        kxB, vxB, qxB, oxB, pbB = Buf("kx"), Buf("vx"), Buf("qx"), Buf("ox"), Buf("pb")
        fence = [self.gpB]
        mt = self.load_tile(self.memT, Buf("memT"), 0, ML, c.KC, gain_idx=4 + l)
        self.fence(fence)

        def epik(cc, ps, psB, M, sub):
            self.qknorm(ps, psB, ML, self.hg[:, 4 + l:5 + l], self.hgB, 1.0, kx[:, cc, :], kxB)
        self.gemm(lambda kc, sub: mt[:, kc, :], self.arenaB, c.KC, 1, self.x_wk[l], 4, 128, epik, sw=ML)
        for mb in range(ML // 128):
            ps, psB = self.psG()
            for k0 in range(0, c.KC, 4):
                k1 = min(c.KC, k0 + 4)
                wbt, wbB = self.wb[self.wctr % 3]; self.wctr += 1
                wv = wbt[:, 0:(k1 - k0) * 512].rearrange("p (k n) -> p k n", n=512)
                self.s.dma("sp", wv, self.x_wv[l, :, k0:k1, :], [], [wbB])
                w16t, w16B = self.wb16[self.w16ctr % 3]; self.w16ctr += 1
                w16 = w16t[:, 0:(k1 - k0) * 512].rearrange("p (k n) -> p k n", n=512)
                self.cp("pool", w16, wv, [wbB], [w16B])
                for kc in range(k0, k1):
                    self.mm(psB, ps[:, :], mt[:, kc, mb * 128:(mb + 1) * 128], w16[:, kc - k0, :], kc == 0, kc == c.KC - 1, [w16B, self.arenaB])
            self.cp("act", vx[:, mb, :], ps[:, :], [psB], [vxB])
        for ti in range(c.NT):
            t0 = ti * c.TT; Tt = c.TT
            tile = self.load_tile(hin, hinB, t0, Tt, c.KC, gain_idx=2 + l)

            def epiq(cc, ps, psB, M, sub):
                self.qknorm(ps, psB, Tt, self.hg[:, 2 + l:3 + l], self.hgB, 128.0 ** -0.5, qx[:, cc, :], qxB)
            self.gemm(lambda kc, sub: tile[:, kc, :], self.arenaB, c.KC, 1, self.x_wq[l], 4, 128, epiq)
            for h in range(4):
                pd, pdB = self.psX(); po, poB = self.psX()
                for mb in range(2):
                    ps, psB = self.psX()
                    self.mm(psB, ps[:, :Tt], kx[:, h, mb * 128:(mb + 1) * 128], qx[:, h, :], True, True, [kxB, qxB])
                    self.act(pb[:, mb, :], ps[:, :Tt], AF.Exp, [psB], [pbB])
                for mb in range(2):
                    self.mm(pdB, pd[:, :Tt], self.ones[:], pb[:, mb, :], mb == 0, mb == 1, [self.onesB, pbB])
                for mb in range(2):
                    self.mm(poB, po[:, :Tt], vx[:, mb, h * 128:(h + 1) * 128], pb[:, mb, :], mb == 0, mb == 1, [vxB, pbB])
                rc, rcB = self.nstg()
                self.recip(rc[:, :Tt], pd[:, :Tt], [pdB], [rcB])
                self.tt("dve", ox16[:, h, :], po[:, :Tt], rc[:, :Tt], ALU.mult, [poB, rcB], [oxB])
            pre, epi = self.epi_residual(hin, hinB, hout, houtB, t0)
            self.gemm(lambda kc, sub: ox16[:, kc, :], oxB, 4, 1, self.x_wo[l], c.KC, 128, epi, pre=pre)
        self._fence_release([kxB, vxB, qxB, oxB, pbB], [self.gpB])

    def ffn(self, l, hin, hinB, hout, houtB):
        c = self.cfg; FC = c.FC
        gp = self.gp
        o0 = 0
        cw = gp[:, o0:o0 + 2 * FC * 4].rearrange("p (c f) -> p c f", f=4); o0 += 2 * FC * 4
        tails = gp[:, o0:o0 + 2 * FC * 2].rearrange("p (c f) -> p c f", f=2); o0 += 2 * FC * 2
        ub = []
        for i in range(2):
            ub.append(gp[:, o0:o0 + 514]); o0 += 516
        cg = []
        for i in range(2):
            cg.append(gp[:, o0:o0 + 512]); o0 += 512
        cv = gp[:, o0:o0 + 512]; o0 += 512
        assert gp.shape[1] >= o0
        cwB, tlB, cvB = Buf("cw"), Buf("tails"), Buf("cv")
        cgB = [Buf("cg0"), Buf("cg1")]
        ubB = [Buf("fub0"), Buf("fub1")]
        fence = [self.gpB]
        self.fence(fence)
        self.s.dma("sp", cw, self.f_cw[:, l, :, :], [], [cwB])
        self.s.op("pool", lambda E: E.memset(tails, 0.0), [], [tlB])
        ucnt = [0]
        for ti in range(c.NGT):
            t0 = ti * c.GT
            tile = self.load_tile(hin, hinB, t0, c.GT, c.KC, gain_idx=6 + l)

            def epi(cc, ps, psB, M, sub, t0=t0):
                Tt = 512; ts0 = t0 + sub * 512
                u = ub[ucnt[0] % 2]; uB = ubB[ucnt[0] % 2]; ucnt[0] += 1
                self.cp("act", u[:, 2:2 + Tt], ps[:, :Tt], [psB], [uB])
                self.cp("pool", u[:, 0:2], tails[:, cc, :], [tlB], [uB])
                self.cp("pool", tails[:, cc, :], u[:, Tt:Tt + 2], [uB], [tlB])
                isg = (cc % 2 == 0)
                dst, dstB = (cg[sub], cgB[sub]) if isg else (cv, cvB)
                self.act(dst[:, :Tt], u[:, 2:2 + Tt], AF.Identity, [uB, cwB], [dstB], scale=cw[:, cc, 2:3], bias=cw[:, cc, 3:4])
                self.stt("dve", dst[:, :Tt], u[:, 1:1 + Tt], cw[:, cc, 1:2], dst[:, :Tt], ALU.mult, ALU.add, [uB, cwB, dstB], [dstB])
                self.stt("dve", dst[:, :Tt], u[:, 0:Tt], cw[:, cc, 0:1], dst[:, :Tt], ALU.mult, ALU.add, [uB, cwB, dstB], [dstB])
                if not isg:
                    st, stB = self.nstg()
                    s16 = st[:].bitcast(BF16)
                    self.act(cg[sub][:, :Tt], cg[sub][:, :Tt], AF.Silu, [cgB[sub]], [cgB[sub]])
                    self.tt("dve", s16[:, :Tt], cg[sub][:, :Tt], cv[:, :Tt], ALU.mult, [cgB[sub], cvB], [stB])
                    self.store_chunk(self.ffa, self.ffaB, cc // 2, ts0, Tt, s16[:, :Tt], stB)

            self.gemm(lambda kc, sub: tile[:, kc, sub * 512:(sub + 1) * 512], self.arenaB, c.KC, c.GT // 512, self.f_wup[l], 2 * FC, 128, epi)
        self._fence_release([cwB, tlB, cvB] + cgB + ubB, [self.gpB])
        n3 = (FC + 2) // 3
        parts = [(k0, min(FC, k0 + n3)) for k0 in range(0, FC, n3)]
        temps = [(self.hT1, self.hT1B), (self.hT2, self.hT2B)]
        chain = [(hin, hinB)] + temps[:len(parts) - 1] + [(hout, houtB)]
        for pi, (k0, k1) in enumerate(parts):
            src_h, src_hB = chain[pi]; dst_h, dst_hB = chain[pi + 1]
            for ti in range(c.NGT):
                t0 = ti * c.GT
                tile = self.load_tile(self.ffa[k0:k1], self.ffaB, t0, c.GT, k1 - k0, src_bf16=True)
                pre, epi = self.epi_residual(src_h, src_hB, dst_h, dst_hB, t0)
                self.gemm(lambda kc, sub: tile[:, kc, sub * 512:(sub + 1) * 512], self.arenaB, k1 - k0, c.GT // 512,
                          self.f_wdn[l][:, :, k0:k1, :], c.KC, 128, epi, pre=pre)

    def l1_inproj(self, hin, hinB):
        c = self.cfg
        nq = c.CKW // 128; nv = c.CVW // 128
        CC = (c.ODD_IN + 127) // 128
        for ti in range(c.NGT):
            t0 = ti * c.GT
            tile = self.load_tile(hin, hinB, t0, c.GT, c.KC, gain_idx=1)

            def epi(cc, ps, psB, M, sub, t0=t0):
                Tt = 512; ts0 = t0 + sub * 512
                st, stB = self.nstg()
                if cc < nq:
                    self.act(st[:, :Tt], ps[:, :Tt], AF.Copy, [psB], [stB], scale=float(c.CDK) ** -0.5)
                    self.store_chunk(self.zq, self.zqB, cc, ts0, Tt, st[:, :Tt], stB)
                elif cc < 2 * nq:
                    self.cp("act", st[:, :Tt], ps[:, :Tt], [psB], [stB])
                    self.store_chunk(self.zk, self.zkB, cc - nq, ts0, Tt, st[:, :Tt], stB)
                elif cc < 2 * nq + nv:
                    self.cp("dve", st[:, :Tt], ps[:, :Tt], [psB], [stB])
                    self.store_chunk(self.zv, self.zvB, cc - 2 * nq, ts0, Tt, st[:, :Tt], stB)
                elif cc < 2 * nq + 2 * nv:
                    self.act(st[:, :Tt], ps[:, :Tt], AF.Silu, [psB], [stB])
                    self.store_chunk(self.zu, self.zuB, cc - 2 * nq - nv, ts0, Tt, st[:, :Tt], stB)
                else:
                    self.cp("act", st[:16, :Tt], ps[:16, :Tt], [psB], [stB])
                    self.store_chunk(self.zr, self.zrB, 0, ts0, Tt, st[:16, :Tt], stB, P=16)

            self.gemm(lambda kc, sub: tile[:, kc, sub * 512:(sub + 1) * 512], self.arenaB, c.KC, c.GT // 512, self.c_win, CC, 128, epi)

    def l1_gla(self):
        c = self.cfg; T = c.T; DKC = c.DKC; DVC = c.DVC; Tt = c.TT
        NCH = Tt // 64
        ar = self.arena
        o0 = 0

        def carve(n):
            nonlocal o0
            a = ar[:, o0:o0 + n]; o0 += n
            return a
        wa2 = carve(c.CKW)
        ba = carve(c.CKW // 128)
        on = carve(DVC)
        rT = carve(Tt)
        qt = carve(DKC * Tt).rearrange("p (k t) -> p k t", k=DKC)
        kt = carve(DKC * Tt).rearrange("p (k t) -> p k t", k=DKC)
        vt = carve(DVC * Tt).rearrange("p (k t) -> p k t", k=DVC)
        gt = carve(DVC * Tt).rearrange("p (k t) -> p k t", k=DVC)
        bc = carve(DKC * Tt).rearrange("p (k t) -> p k t", k=DKC)
        ebc = carve(DKC * Tt).rearrange("p (k t) -> p k t", k=DKC)
        dec = carve(DKC * NCH).rearrange("p (k n) -> p k n", k=DKC)
        state = carve(DKC * c.CDV).rearrange("p (k e) -> p k e", k=DKC)
        ot = carve(DVC * Tt).rearrange("p (k t) -> p k t", k=DVC)
        kst = carve(DKC * 64).rearrange("p (k t) -> p k t", k=DKC)
        attm = carve(64)
        vtok = carve(c.CDV)
        ksttok = carve(c.CDK)
        assert ar.shape[1] >= o0, (ar.shape, o0)
        names = ["wa2", "ba", "on", "rT", "qt", "kt", "vt", "gt", "bc", "ebc", "dec", "state", "ot", "kst", "attm", "vtok", "ksttok"]
        B = {n: Buf("g_" + n) for n in names}
        fence = [self.arenaB]
        scanm = self.cst[:, 1728:2240]
        mask01 = self.cst[0:64, 1664:1728]
        self.fence(fence)
        self.s.dma("sp", wa2[0:16, :], self.c_wa2, [], [B["wa2"]])
        self.s.dma("sp", ba, self.c_ba, [], [B["ba"]])
        self.s.dma("sp", on, self.c_on, [], [B["on"]])
        self.ts("dve", ba, ba, -1.0, ALU.mult, [B["ba"]], [B["ba"]])
        for hd in range(c.CH):
            for dc in range(DKC):
                self.s.op("pool", lambda E, dc=dc: E.memset(state[:, dc, :], 0.0), [], [B["state"]])
            for ti in range(c.NT):
                t0 = ti * Tt
                self.s.dma("sp", rT[0:16, :], self.zr[0, 0:16, t0:t0 + Tt], [self.zrB], [B["rT"]])
                self.s.dma("sp", qt, self.zq[hd * DKC:(hd + 1) * DKC, :, t0:t0 + Tt].rearrange("k p t -> p k t"), [self.zqB], [B["qt"]])
                self.s.dma("sp", kt, self.zk[hd * DKC:(hd + 1) * DKC, :, t0:t0 + Tt].rearrange("k p t -> p k t"), [self.zkB], [B["kt"]])
                self.s.dma("sp", vt, self.zv[hd * DVC:(hd + 1) * DVC, :, t0:t0 + Tt].rearrange("k p t -> p k t"), [self.zvB], [B["vt"]])
                self.s.dma("sp", gt, self.zu[hd * DVC:(hd + 1) * DVC, :, t0:t0 + Tt].rearrange("k p t -> p k t"), [self.zuB], [B["gt"]])
                for dc in range(DKC):
                    col = hd * DKC + dc
                    ps, psB = self.psX()
                    self.mm(psB, ps[:, :Tt], wa2[0:16, col * 128:(col + 1) * 128], rT[0:16, :], True, True, [B["wa2"], B["rT"]])
                    self.act(ebc[:, dc, :], ps[:, :Tt], AF.Exp, [psB, B["ba"]], [B["ebc"]], scale=-1.0, bias=ba[:, col:col + 1])
                    one_ap = self.epsap(1.0, 128)
                    self.act(ebc[:, dc, :], ebc[:, dc, :], AF.Ln, [B["ebc"], self.epstB], [B["ebc"]], bias=one_ap)
                    self.ts("dve", ebc[:, dc, :], ebc[:, dc, :], -1.0 / 16.0, ALU.mult, [B["ebc"]], [B["ebc"]])
                    self.s.op("dve", lambda E, dc=dc: E.tensor_tensor_scan(out=bc[:, dc, :], data0=scanm[:, :Tt], data1=ebc[:, dc, :], initial=0.0,
                                                                            op0=ALU.mult, op1=ALU.add), [self.cstB, B["ebc"]], [B["bc"]])
                    self.act(dec[:, dc, :], bc[:, dc, 63:Tt:64], AF.Exp, [B["bc"]], [B["dec"]])
                    self.act(ebc[:, dc, :], bc[:, dc, :], AF.Exp, [B["bc"]], [B["ebc"]])
                    self.tt("dve", qt[:, dc, :], qt[:, dc, :], ebc[:, dc, :], ALU.mult, [B["qt"], B["ebc"]], [B["qt"]])
                    self.act(ebc[:, dc, :], bc[:, dc, :], AF.Exp, [B["bc"]], [B["ebc"]], scale=-1.0)
                    self.tt("dve", kt[:, dc, :], kt[:, dc, :], ebc[:, dc, :], ALU.mult, [B["kt"], B["ebc"]], [B["kt"]])
                for ch in range(NCH):
                    cs = slice(ch * 64, (ch + 1) * 64)
                    pa, paB = self.psX()
                    for dc in range(DKC):
                        self.mm(paB, pa[0:64, 0:64], kt[:, dc, cs], qt[:, dc, cs], dc == 0, dc == DKC - 1, [B["kt"], B["qt"]])
                    self.tt("dve", attm[0:64, :], pa[0:64, 0:64], mask01, ALU.mult, [paB, self.cstB], [B["attm"]])
                    pv, pvB = self.psX()
                    for ec in range(DVC):
                        self.tr(pvB, pv[0:64, ec * 128:(ec + 1) * 128], vt[:, ec, cs], [B["vt"]])
                    self.cp("act", vtok[0:64, :], pv[0:64, 0:c.CDV], [pvB], [B["vtok"]])
                    for dc in range(DKC):
                        self.ts("dve", kst[:, dc, :], kt[:, dc, cs], dec[:, dc, ch:ch + 1], ALU.mult, [B["kt"], B["dec"]], [B["kst"]])
                    pk, pkB = self.psX()
                    for dc in range(DKC):
                        self.tr(pkB, pk[0:64, dc * 128:(dc + 1) * 128], kst[:, dc, :], [B["kst"]])
                    self.cp("act", ksttok[0:64, :], pk[0:64, 0:c.CDK], [pkB], [B["ksttok"]])
                    po, poB = self.psG()
                    for ec in range(DVC):
                        osl = po[:, ec * 64:(ec + 1) * 64]
                        self.mm(poB, osl, vtok[0:64, ec * 128:(ec + 1) * 128], attm[0:64, :], True, False, [B["vtok"], B["attm"]])
                        for dc in range(DKC):
                            self.mm(poB, osl, state[:, dc, ec * 128:(ec + 1) * 128], qt[:, dc, cs], False, dc == DKC - 1, [B["state"], B["qt"]])
                    self.cp("act", ot[:, :, cs], po[:, 0:DVC * 64].rearrange("p (k t) -> p k t", k=DVC), [poB], [B["ot"]])
                    for dc in range(DKC):
                        pkv, pkvB = self.psG()
                        self.mm(pkvB, pkv[:, 0:c.CDV], ksttok[0:64, dc * 128:(dc + 1) * 128], vtok[0:64, :], True, True, [B["ksttok"], B["vtok"]])
                        self.stt("dve", state[:, dc, :], state[:, dc, :], dec[:, dc, ch:ch + 1], pkv[:, 0:c.CDV], ALU.mult, ALU.add,
                                 [B["state"], B["dec"], pkvB], [B["state"]])
                psq, psqB = self.psX()
                for ec in range(DVC):
                    sq, sqB = self.sq[ec % 2]
                    self.act(sq[:, :Tt], ot[:, ec, :], AF.Square, [B["ot"]], [sqB])
                    self.mm(psqB, psq[:, :Tt], self.ones[:], sq[:, :Tt], ec == 0, ec == DVC - 1, [sqB, self.onesB])
                rstd, rstdB = self.rsqrt_from(psq[:, :Tt], psqB, 1.0 / c.CDV, EPS, 128, Tt)
                for ec in range(DVC):
                    st, stB = self.nstg()
                    st2, st2B = self.nstg()
                    s16 = st2[:].bitcast(BF16)
                    self.stt("dve", st[:, :Tt], ot[:, ec, :], on[:, ec:ec + 1], rstd[:, :Tt], ALU.mult, ALU.mult, [B["ot"], B["on"], rstdB], [stB])
                    self.tt("dve", s16[:, :Tt], st[:, :Tt], gt[:, ec, :], ALU.mult, [stB, B["gt"]], [st2B])
                    self.store_chunk(self.ab, self.abB, hd * DVC + ec, t0, Tt, s16[:, :Tt], st2B)
        self._fence_release(list(B.values()), [self.arenaB])


def _wblocks(W):
    K, N = W.shape
    CC = (N + 127) // 128
    if CC * 128 != N:
        W = np.concatenate([W, np.zeros((K, CC * 128 - N), W.dtype)], axis=1)
    return np.ascontiguousarray(W.reshape(K // 128, 128, CC, 128).transpose(2, 1, 0, 3))


def _fm(v):
    return np.ascontiguousarray(v.reshape(-1, 128).T)


def _consts():
    cst = np.zeros((128, 2240), np.float32)
    cst[:, 0:128] = np.eye(128, dtype=np.float32)
    i = np.arange(128)[:, None]; j = np.arange(128)[None, :]
    cur = np.where(j >= i, (j - i).astype(np.float32), BIG).astype(np.float32)
    prev = np.where(i >= j, (j + 128 - i).astype(np.float32), BIG).astype(np.float32)
    for r in range(4):
        cst[:, 128 + r * 128:128 + (r + 1) * 128] = cur
        cst[:, 640 + r * 128:640 + (r + 1) * 128] = prev
        cst[:, 1152 + r * 128:1152 + (r + 1) * 128] = prev if r > 0 else BIG
    jj = np.arange(64)[:, None]; ii = np.arange(64)[None, :]
    cst[0:64, 1664:1728] = (ii >= jj).astype(np.float32)
    m = np.ones(512, np.float32); m[::64] = 0.0
    cst[:, 1728:2240] = m[None, :]
    return cst


def prep_shared(cfg, inp):
    c = cfg
    f = lambda a: np.asarray(a, dtype=np.float32)
    d = {}
    gains = np.zeros((128, 8, c.KC), np.float32)
    for i, (nm, l) in enumerate([("mix_norm", 0), ("mix_norm", 1), ("x_norm", 0), ("x_norm", 1), ("x_mem_norm", 0), ("x_mem_norm", 1),
                                 ("f_norm", 0), ("f_norm", 1)]):
        gains[:, i, :] = _fm(f(inp[nm])[l])
    d["gains"] = gains
    hg = np.zeros((128, 6), np.float32)
    hg[:, 0] = f(inp["ab_q_norm"])[0]; hg[:, 1] = f(inp["ab_k_norm"])[0]
    hg[:, 2] = f(inp["x_q_norm"])[0]; hg[:, 3] = f(inp["x_q_norm"])[1]
    hg[:, 4] = f(inp["x_k_norm"])[0]; hg[:, 5] = f(inp["x_k_norm"])[1]
    d["hgain"] = hg
    d["consts"] = _consts()
    t = np.arange(1, c.T + 1, dtype=np.float32)
    d["invc"] = np.stack([np.float32(1.0) / np.minimum(t, np.float32(w)) for w in (2, 4, 8, 16)]).astype(np.float32)
    d["ab_win"] = _wblocks(f(inp["ab_w_in"])[0])
    pw = f(inp["ab_pool_w"])[0]
    d["pool_w"] = np.concatenate([_wblocks(pw[g]) for g in range(4)], axis=0)
    d["pool_sc"] = _fm(f(inp["ab_pool_scale"])[0])
    d["ab_wout"] = _wblocks(f(inp["ab_w_out"])[0])
    d["c_win"] = _wblocks(f(inp["c_w_in"])[0])
    d["c_wa2"] = np.ascontiguousarray(f(inp["c_w_a2"])[0])
    d["c_ba"] = _fm(f(inp["c_b_a"])[0])
    d["c_on"] = _fm(f(inp["c_o_norm"])[0])
    d["c_wout"] = _wblocks(f(inp["c_w_out"])[0])
    wq = f(inp["x_wq"]); wkv = f(inp["x_wkv"]); wo = f(inp["x_wo"])
    d["x_wq"] = np.stack([_wblocks(wq[l]) for l in range(2)])
    d["x_wk"] = np.stack([_wblocks(wkv[l][:, :512]) for l in range(2)])
    d["x_wv"] = np.stack([np.ascontiguousarray(wkv[l][:, 512:].reshape(c.KC, 128, 512).transpose(1, 0, 2)) for l in range(2)])
    d["x_wo"] = np.stack([_wblocks(wo[l]) for l in range(2)])
    wup = f(inp["f_w_up"]); cw = f(inp["f_conv_w"]); cb = f(inp["f_conv_b"]); wdn = f(inp["f_w_down"])
    idx = np.concatenate([np.concatenate([np.arange(j * 128, (j + 1) * 128), c.DFF + np.arange(j * 128, (j + 1) * 128)]) for j in range(c.FC)])
    d["f_wup"] = np.stack([_wblocks(wup[l][:, idx]) for l in range(2)])
    fcw = np.zeros((128, 2, 2 * c.FC, 4), np.float32)
    for l in range(2):
        for k in range(3):
            fcw[:, l, :, k] = _fm(cw[l, k][idx])
        fcw[:, l, :, 3] = _fm(cb[l][idx])
    d["f_cw"] = fcw
    d["f_wdn"] = np.stack([_wblocks(wdn[l]) for l in range(2)])
    return d


def _tfm(a):
    T, D = a.shape
    return np.ascontiguousarray(a.T.reshape(D // 128, 128, T))


_PROG_CACHE = {}


def run(cfg, inputs, stop_after=None, cores=None):
    key = (cfg.D, cfg.T, cfg.CH, stop_after)
    if key not in _PROG_CACHE:
        p = Prog(cfg, stop_after=stop_after)
        p.build()
        _PROG_CACHE[key] = p
    p = _PROG_CACHE[key]
    shared = prep_shared(cfg, inputs)
    x = np.asarray(inputs["x"], dtype=np.float32); mem = np.asarray(inputs["mem"], dtype=np.float32)
    nb = x.shape[0]
    in_maps = []
    for b in range(nb):
        m = dict(shared)
        m["xT"] = _tfm(x[b]); m["memT"] = _tfm(mem[b])
        in_maps.append(m)
    res = run_bass_kernel_spmd(p.nc, in_maps, core_ids=list(range(nb)))
    outs = []
    for b in range(nb):
        o = res.results[b]["outT"]
        outs.append(np.ascontiguousarray(o.reshape(cfg.D, cfg.T).T))
    return np.stack(outs)


def kernel(**inputs):
    cfg = Cfg(4096, 4096, 8)
    return run(cfg, inputs).astype(np.float32)
```

```python
import math
import numpy as np
import concourse.bass as bass
import concourse.mybir as mybir
from concourse.bass_utils import run_bass_kernel_spmd

F32 = mybir.dt.float32
BF16 = mybir.dt.bfloat16
AF = mybir.ActivationFunctionType
ALU = mybir.AluOpType
EPS = 1e-6
BIG = 1.0e5


class Cfg:
    def __init__(self, D=4096, T=4096, CH=8):
        self.D = D; self.T = T; self.KC = D // 128
        self.AH = D // 256; self.AW = self.AH * 128
        self.BW = D - self.AW; self.BG = self.BW // 4; self.BGC = self.BG // 128
        self.EVEN_IN = 7 * self.AW + self.BW
        self.CH = CH; self.CKW = D // 2; self.CVW = D
        self.CDK = self.CKW // CH; self.CDV = self.CVW // CH
        self.DKC = self.CDK // 128; self.DVC = self.CDV // 128
        self.ODD_IN = 2 * self.CKW + 2 * self.CVW + 16
        self.DFF = ((8 * D // 3 + 255) // 256) * 256; self.FC = self.DFF // 128
        self.XH = 4; self.XW = 512; self.ML = 256
        self.TT = 512
        self.NT = T // self.TT
        self.GT = 1024
        self.NGT = T // self.GT


class Buf:
    __slots__ = ("name", "w", "r")

    def __init__(self, name):
        self.name = name; self.w = None; self.r = {}


class Sched:
    MAXC = 30000

    def __init__(self, nc):
        self.nc = nc
        self.E = {"pe": nc.tensor, "act": nc.scalar, "dve": nc.vector, "pool": nc.gpsimd, "sp": nc.sync}
        self.sems = []; self.owner = []
        self.cur = {}
        self.seen = {e: {} for e in self.E}
        self.dq = {}
        self.ninst = 0

    def newsem(self, name, owner):
        h = self.nc.alloc_semaphore(name)
        self.sems.append(h); self.owner.append(owner)
        return len(self.sems) - 1

    def _wait(self, e, si, val):
        if self.seen[e].get(si, 0) >= val:
            return
        self.E[e].wait_ge(self.sems[si], val)
        self.seen[e][si] = val

    def _sync(self, e, reads, writes):
        deps = {}
        for b in reads:
            if b.w is not None:
                si, v = b.w
                if not (e == "pe" and self.owner[si] == "pe"):
                    deps[si] = max(deps.get(si, 0), v)
        for b in writes:
            if b.w is not None:
                si, v = b.w
                if self.owner[si] != e:
                    deps[si] = max(deps.get(si, 0), v)
            for si, v in b.r.items():
                if self.owner[si] != e:
                    deps[si] = max(deps.get(si, 0), v)
        for si, v in deps.items():
            self._wait(e, si, v)

    def _tick(self, e):
        c = self.cur.get(e)
        if c is None or c[1] >= self.MAXC:
            c = [self.newsem("c_%s_%d" % (e, len(self.sems)), e), 0]
            self.cur[e] = c
        c[1] += 1
        return c[0], c[1]

    def op(self, e, emit, reads=(), writes=()):
        self._sync(e, reads, writes)
        inst = emit(self.E[e])
        si, v = self._tick(e)
        inst.then_inc(self.sems[si], 1)
        for b in reads:
            b.r[si] = v
        for b in writes:
            b.w = (si, v); b.r = {}
        self.ninst += 1
        return inst

    def dma(self, q, out, in_, reads=(), writes=()):
        self._sync(q, reads, writes)
        ring = self.dq.get(q)
        if ring is None:
            ring = {"s": [[self.newsem("d_%s_%d" % (q, i), "dma_" + q), 0] for i in range(8)], "p": 0}
            self.dq[q] = ring
        p = ring["p"]; ring["p"] = (p + 1) % 8
        ent = ring["s"][p]
        if ent[1] >= self.MAXC:
            self._wait(q, ent[0], ent[1])
            ent = [self.newsem("d_%s_%d" % (q, len(self.sems)), "dma_" + q), 0]
            ring["s"][p] = ent
        self._wait(q, ent[0], ent[1])
        inst = self.E[q].dma_start(out=out, in_=in_)
        ent[1] += 16
        inst.then_inc(self.sems[ent[0]], 16)
        for b in reads:
            b.r[ent[0]] = ent[1]
        for b in writes:
            b.w = (ent[0], ent[1]); b.r = {}
        self.ninst += 1

    def wait_all(self, e, bufs):
        self._sync(e, bufs, ())


class Prog:
    def __init__(self, cfg, stop_after=None):
        self.cfg = cfg
        self.stop_after = stop_after
        nc = bass.Bass("TRN2", target_bir_lowering=False)
        self.nc = nc
        self.s = Sched(nc)
        self.din = {}
        self._alloc()

    def dI(self, name, shape):
        ap = self.nc.dram_tensor(name, list(shape), F32, kind="ExternalInput").ap()
        self.din[name] = tuple(shape)
        return ap

    def dS(self, name, shape, dt=F32):
        return self.nc.dram_tensor(name, list(shape), dt, kind="Internal").ap(), Buf(name)

    def sb(self, name, shape):
        return self.nc.alloc_sbuf_tensor(name, list(shape), F32), Buf(name)

    def _alloc(self):
        c = self.cfg; nc = self.nc
        KC, T, FC = c.KC, c.T, c.FC
        self.xT = self.dI("xT", [KC, 128, T]); self.xTB = Buf("xT")
        self.memT = self.dI("memT", [KC, 128, c.ML])
        self.gains = self.dI("gains", [128, 8, KC])
        self.hgain = self.dI("hgain", [128, 6])
        self.consts = self.dI("consts", [128, 2240])
        self.invc = self.dI("invc", [4, T])
        self.ab_win = self.dI("ab_win", [c.EVEN_IN // 128, 128, KC, 128])
        self.pool_w = self.dI("pool_w", [4 * c.BGC, 128, c.BGC, 128])
        self.pool_sc = self.dI("pool_sc", [128, c.BW // 128])
        self.ab_wout = self.dI("ab_wout", [KC, 128, KC, 128])
        self.c_win = self.dI("c_win", [(c.ODD_IN + 127) // 128, 128, KC, 128])
        self.c_wa2 = self.dI("c_wa2", [16, c.CKW])
        self.c_ba = self.dI("c_ba", [128, c.CKW // 128])
        self.c_on = self.dI("c_on", [128, c.DVC])
        self.c_wout = self.dI("c_wout", [KC, 128, KC, 128])
        self.x_wq = self.dI("x_wq", [2, 4, 128, KC, 128])
        self.x_wk = self.dI("x_wk", [2, 4, 128, KC, 128])
        self.x_wv = self.dI("x_wv", [2, 128, KC, 512])
        self.x_wo = self.dI("x_wo", [2, KC, 128, 4, 128])
        self.f_wup = self.dI("f_wup", [2, 2 * FC, 128, KC, 128])
        self.f_cw = self.dI("f_cw", [128, 2, 2 * FC, 4])
        self.f_wdn = self.dI("f_wdn", [2, KC, 128, FC, 128])
        self.outT = nc.dram_tensor("outT", [KC, 128, T], F32, kind="ExternalOutput").ap()
        self.outTB = Buf("outT")
        self.hA, self.hAB = self.dS("hA", [KC, 128, T])
        self.hB, self.hBB = self.dS("hB", [KC, 128, T])
        nzq = max(3 * c.AH, c.CKW // 128)
        self.zq, self.zqB = self.dS("zq", [nzq, 128, T])
        self.zk, self.zkB = self.dS("zk", [nzq, 128, T])
        self.zv, self.zvB = self.dS("zv", [max(c.AH, c.CVW // 128), 128, T])
        self.zu, self.zuB = self.dS("zu", [max(c.BW // 128, c.CVW // 128), 128, T])
        self.zr, self.zrB = self.dS("zr", [1, 128, T])
        self.ab, self.abB = self.dS("ab", [KC, 128, T], BF16)
        self.ffa, self.ffaB = self.dS("ffa", [FC, 128, T], BF16)
        self.hT1, self.hT1B = self.dS("hT1", [KC, 128, T])
        self.hT2, self.hT2B = self.dS("hT2", [KC, 128, T])
        self.cst, self.cstB = self.sb("cst", [128, 2240])
        self.ones, self.onesB = self.sb("ones", [128, 128])
        self.gn, self.gnB = self.sb("gn", [128, 8, KC])
        self.hg, self.hgB = self.sb("hg", [128, 6])
        arena_elems = max(KC * 512, 4 * T, 16384)
        self.arena, self.arenaB = self.sb("arena", [128, arena_elems])
        self.KP = 16
        self.wb = []
        for i in range(2):
            self.wb.append(self.sb("wb%d" % i, [128, self.KP * 128]))
        self.wctr = 0
        self.NW16 = 6
        self.wb16 = []
        for i in range(self.NW16):
            t = self.nc.alloc_sbuf_tensor("wbh%d" % i, [128, self.KP * 128], BF16)
            self.wb16.append((t, Buf("wbh%d" % i)))
        self.w16ctr = 0
        self.ld = [self.sb("ld%d" % i, [128, 1024]) for i in range(4)]
        self.ldc = 0
        self.ones16 = self.nc.alloc_sbuf_tensor("ones16", [128, 128], BF16); self.ones16B = Buf("ones16")
        self.sq = [self.sb("sq%d" % i, [128, 1024]) for i in range(2)]
        self.t1 = [self.sb("t1_%d" % i, [128, 1024]) for i in range(2)]
        self.t2 = [self.sb("t2_%d" % i, [128, 1024]) for i in range(2)]
        self.stg = [self.sb("stg%d" % i, [128, 512]) for i in range(4)]
        self.stgc = 0
        self.res = [self.sb("res%d" % i, [128, 512]) for i in range(4)]
        self.resc = 0
        self.gp, self.gpB = self.sb("gp", [128, 8448])
        self.ps = []
        for i in range(8):
            self.ps.append((nc.alloc_psum_tensor("ps%d" % i, [128, 512], F32), Buf("ps%d" % i)))
        self.psGc = 0; self.psXc = 0

    def psG(self):
        p = self.ps[self.psGc % 4]; self.psGc += 1
        return p

    def psX(self):
        p = self.ps[4 + self.psXc % 4]; self.psXc += 1
        return p

    def nstg(self):
        p = self.stg[self.stgc % 4]; self.stgc += 1
        return p

    def nres(self):
        p = self.res[self.resc % 4]; self.resc += 1
        return p

    def act(self, out, in_, func, reads, writes, scale=None, bias=None):
        kw = {}
        if scale is not None:
            kw["scale"] = scale
        if bias is not None:
            kw["bias"] = bias
        return self.s.op("act", lambda E: E.activation(out=out, in_=in_, func=func, **kw), reads, writes)

    def mm(self, psB, out, lhsT, rhs, start, stop, reads):
        return self.s.op("pe", lambda E: E.matmul(out, lhsT=lhsT, rhs=rhs, start=start, stop=stop), reads, [psB])

    def tr(self, psB, out, in_, reads):
        ident = self.cst[:, 0:128]
        return self.s.op("pe", lambda E: E.transpose(out, in_, ident[: in_.shape[0], : in_.shape[0]]), list(reads) + [self.cstB], [psB])

    def stt(self, eng, out, in0, scalar, in1, op0, op1, reads, writes):
        return self.s.op(eng, lambda E: E.scalar_tensor_tensor(out=out, in0=in0, scalar=scalar, in1=in1, op0=op0, op1=op1), reads, writes)

    def tt(self, eng, out, in0, in1, op, reads, writes):
        return self.s.op(eng, lambda E: E.tensor_tensor(out=out, in0=in0, in1=in1, op=op), reads, writes)

    def ts(self, eng, out, in0, s1, op0, reads, writes, s2=None, op1=None):
        if op1 is None:
            return self.s.op(eng, lambda E: E.tensor_scalar(out=out, in0=in0, scalar1=s1, scalar2=None, op0=op0), reads, writes)
        return self.s.op(eng, lambda E: E.tensor_scalar(out=out, in0=in0, scalar1=s1, scalar2=s2, op0=op0, op1=op1), reads, writes)

    def cp(self, eng, out, in_, reads, writes):
        if eng == "act":
            return self.s.op("act", lambda E: E.copy(out=out, in_=in_), reads, writes)
        return self.s.op(eng, lambda E: E.tensor_copy(out=out, in_=in_), reads, writes)

    def recip(self, out, in_, reads, writes):
        return self.s.op("dve", lambda E: E.reciprocal(out=out, in_=in_), reads, writes)

    def rsqrt_from(self, ps_ap, psB, mul, add, P, N):
        t1, t1B = self.t1[0]; self.t1.reverse()
        t2, t2B = self.t2[0]; self.t2.reverse()
        bias_ap = self.epsap(add, P)
        self.act(t1[:P, :N], ps_ap, AF.Sqrt, [psB, self.epstB], [t1B], scale=mul, bias=bias_ap)
        self.recip(t2[:P, :N], t1[:P, :N], [t1B], [t2B])
        return t2, t2B

    def epsap(self, val, P):
        key = float(val)
        if not hasattr(self, "_epsmap"):
            self._epsmap = {}
            self.epst, self.epstB = self.sb("epst", [128, 16])
        if key not in self._epsmap:
            j = len(self._epsmap)
            self.s.op("pool", lambda E: E.memset(self.epst[:, j:j + 1], key), [], [self.epstB])
            self._epsmap[key] = j
        j = self._epsmap[key]
        return self.epst[:P, j:j + 1]

    def prologue(self):
        s = self.s
        s.dma("sp", self.cst[:], self.consts, [], [self.cstB])
        s.dma("sp", self.gn[:], self.gains, [], [self.gnB])
        s.dma("sp", self.hg[:], self.hgain, [], [self.hgB])
        s.op("pool", lambda E: E.memset(self.ones[:], 1.0), [], [self.onesB])

    def nld(self):
        p = self.ld[self.ldc % 4]; self.ldc += 1
        return p

    def load_tile(self, src, srcB, t0, Tt, KCn, gain_idx=None, src_bf16=False):
        a16 = self.arena[:].bitcast(BF16)
        tile = a16[:, 0:KCn * Tt].rearrange("p (k t) -> p k t", k=KCn)
        if src_bf16:
            step = max(1, (KCn + 3) // 4)
            for k0 in range(0, KCn, step):
                k1 = min(KCn, k0 + step)
                self.s.dma("sp", tile[:, k0:k1, :], src[k0:k1, :, t0:t0 + Tt].rearrange("k p t -> p k t"), [srcB], [self.arenaB])
            return tile
        if gain_idx is None:
            for kc in range(KCn):
                l, lB = self.nld()
                self.s.dma("sp", l[:, :Tt], src[kc, :, t0:t0 + Tt], [srcB], [lB])
                self.cp("act" if kc % 2 else "pool", tile[:, kc, :], l[:, :Tt], [lB], [self.arenaB])
            return tile
        acc, accB = self.t1[0]; self.t1.reverse()
        for kc in range(KCn):
            l, lB = self.nld()
            self.s.dma("sp", l[:, :Tt], src[kc, :, t0:t0 + Tt], [srcB], [lB])
            if kc == 0:
                self.act(acc[:, :Tt], l[:, :Tt], AF.Square, [lB], [accB])
            else:
                sq, sqB = self.sq[kc % 2]
                self.act(sq[:, :Tt], l[:, :Tt], AF.Square, [lB], [sqB])
                self.tt("pool", acc[:, :Tt], acc[:, :Tt], sq[:, :Tt], ALU.add, [accB, sqB], [accB])
        rstd, rstdB = self.t2[0]; self.t2.reverse()
        for s0 in range(0, Tt, 512):
            wd = min(512, Tt - s0)
            psq, psqB = self.psX()
            self.mm(psqB, psq[:, :wd], self.ones[:], acc[:, s0:s0 + wd], True, True, [accB, self.onesB])
            bias_ap = self.epsap(EPS, 128)
            sq, sqB = self.sq[0]; self.sq.reverse()
            self.act(sq[:, :wd], psq[:, :wd], AF.Sqrt, [psqB, self.epstB], [sqB], scale=1.0 / (KCn * 128), bias=bias_ap)
            self.recip(rstd[:, s0:s0 + wd], sq[:, :wd], [sqB], [rstdB])
        for kc in range(KCn):
            l, lB = self.nld()
            self.s.dma("sp", l[:, :Tt], src[kc, :, t0:t0 + Tt], [srcB], [lB])
            self.stt("dve", tile[:, kc, :], l[:, :Tt], self.gn[:, gain_idx, kc:kc + 1], rstd[:, :Tt], ALU.mult, ALU.mult,
                     [lB, self.gnB, rstdB], [self.arenaB])
        return tile

    def gemm(self, rhs_fn, rhsB, KCtot, nsub, w_dram, CC, m_last, epi, pre=None, cc_list=None, lowp=True, sw=512):
        KP = self.KP
        pieces = [(k0, min(k0 + KP, KCtot)) for k0 in range(0, KCtot, KP)]
        ccs = list(range(CC)) if cc_list is None else cc_list
        blocks = [(cc, k0, k1) for cc in ccs for (k0, k1) in pieces]
        PF = (self.NW16 - 1) if lowp else 1
        loaded = {}

        def load(i):
            cc, k0, k1 = blocks[i]
            if lowp:
                w16t, w16B = self.wb16[self.w16ctr % self.NW16]; self.w16ctr += 1
                wv = w16t[:, 0:(k1 - k0) * 128].rearrange("p (k m) -> p k m", m=128)
                self.s.dma("pool", wv, w_dram[cc, :, k0:k1, :], [], [w16B])
                loaded[i] = (wv, w16B)
            else:
                wbt, wbB = self.wb[self.wctr % 2]; self.wctr += 1
                wv = wbt[:, 0:(k1 - k0) * 128].rearrange("p (k m) -> p k m", m=128)
                self.s.dma("sp", wv, w_dram[cc, :, k0:k1, :], [], [wbB])
                loaded[i] = (wv, wbB)

        for i in range(min(PF, len(blocks))):
            load(i)
        pss = None
        pend_epi = None
        for i, (cc, k0, k1) in enumerate(blocks):
            if i + PF < len(blocks):
                load(i + PF)
            if k0 == 0:
                pss = [self.psG() for _ in range(nsub)]
                if pre is not None:
                    for sub in range(nsub):
                        pre(cc, sub)
            M = m_last if cc == CC - 1 else 128
            wv, wbB = loaded.pop(i)
            for kc in range(k0, k1):
                for sub in range(nsub):
                    ps, psB = pss[sub]
                    self.mm(psB, ps[:M, :sw], wv[:, kc - k0, :M], rhs_fn(kc, sub), kc == 0, kc == KCtot - 1, [wbB, rhsB])
            if k1 == KCtot:
                if pend_epi is not None:
                    pend_epi()
                cur = (cc, list(pss), M)

                def run_epi(cur=cur):
                    cc_, pss_, M_ = cur
                    for sub in range(nsub):
                        ps, psB = pss_[sub]
                        epi(cc_, ps, psB, M_, sub)
                pend_epi = run_epi
        if pend_epi is not None:
            pend_epi()

    def store_chunk(self, dst, dstB, ci, t0, Tt, src_ap, srcB, P=128):
        self.s.dma("sp", dst[ci, 0:P, t0:t0 + Tt], src_ap, [srcB], [dstB])

    def epi_store(self, dst, dstB, t0, Tt, ci_fn=lambda cc: cc, scale=None, eng="act"):
        def epi(cc, ps, psB, M):
            st, stB = self.nstg()
            if scale is not None:
                self.act(st[:M, :Tt], ps[:M, :Tt], AF.Copy, [psB], [stB], scale=scale)
            elif eng == "act":
                self.cp("act", st[:M, :Tt], ps[:M, :Tt], [psB], [stB])
            else:
                self.cp("dve", st[:M, :Tt], ps[:M, :Tt], [psB], [stB])
            self.store_chunk(dst, dstB, ci_fn(cc), t0, Tt, st[:M, :Tt], stB, P=M)
        return epi

    def epi_residual(self, hin, hinB, hout, houtB, t0):
        pend = {}

        def pre(cc, sub):
            r, rB = self.nres()
            ts0 = t0 + sub * 512
            self.s.dma("sp", r[:, :512], hin[cc, :, ts0:ts0 + 512], [hinB], [rB])
            pend[(cc, sub)] = (r, rB)

        def epi(cc, ps, psB, M, sub):
            r, rB = pend.pop((cc, sub))
            st, stB = self.nstg()
            ts0 = t0 + sub * 512
            self.tt("dve", st[:, :512], ps[:, :512], r[:, :512], ALU.add, [psB, rB], [stB])
            self.store_chunk(hout, houtB, cc, ts0, 512, st[:, :512], stB)
        return pre, epi

    def qknorm(self, ps, psB, Tt, gain_ap, gainB, fold, out_ap, outB):
        sq, sqB = self.sq[0]; self.sq.reverse()
        self.act(sq[:, :Tt], ps[:, :Tt], AF.Square, [psB], [sqB])
        p2, p2B = self.psX()
        self.mm(p2B, p2[:, :Tt], self.ones[:], sq[:, :Tt], True, True, [sqB, self.onesB])
        rstd, rstdB = self.rsqrt_from(p2[:, :Tt], p2B, 1.0 / (128.0 * fold * fold), EPS / (fold * fold), 128, Tt)
        self.stt("dve", out_ap, ps[:, :Tt], gain_ap, rstd[:, :Tt], ALU.mult, ALU.mult, [psB, gainB, rstdB], [outB])

    def build(self):
        c = self.cfg
        self.prologue()
        stages = [
            ("l0_inproj", lambda: self.l0_inproj(self.xT, self.xTB)),
            ("l0_attn", self.l0_attn),
            ("l0_pool", self.l0_pool),
            ("l0_out", lambda: self.outproj(self.ab_wout, self.xT, self.xTB, self.hA, self.hAB)),
            ("l0_x", lambda: self.xattn(0, self.hA, self.hAB, self.hB, self.hBB)),
            ("l0_f", lambda: self.ffn(0, self.hB, self.hBB, self.hA, self.hAB)),
            ("l1_inproj", lambda: self.l1_inproj(self.hA, self.hAB)),
            ("l1_gla", self.l1_gla),
            ("l1_out", lambda: self.outproj(self.c_wout, self.hA, self.hAB, self.hB, self.hBB)),
            ("l1_x", lambda: self.xattn(1, self.hB, self.hBB, self.hA, self.hAB)),
            ("l1_f", lambda: self.ffn(1, self.hA, self.hAB, self.hB, self.hBB)),
        ]
        final = (self.hB, self.hBB)
        dbg = {"l0_inproj": (self.zq, self.zqB), "l0_attn": (self.ab, self.abB), "l0_pool": (self.ab, self.abB),
               "l0_out": (self.hA, self.hAB), "l0_x": (self.hB, self.hBB), "l0_f": (self.hA, self.hAB),
               "l1_inproj": (self.zk, self.zkB), "l1_gla": (self.ab, self.abB), "l1_out": (self.hB, self.hBB),
               "l1_x": (self.hA, self.hAB), "l1_f": (self.hB, self.hBB)}
        for name, fn in stages:
            fn()
            if self.stop_after == name:
                final = dbg[name]
                break
        self.copy_out(*final)
        return self.nc

    def copy_out(self, src, srcB):
        c = self.cfg
        n = min(src.shape[0], c.KC)
        step = max(1, n // 8)
        for k0 in range(0, n, step):
            k1 = min(n, k0 + step)
            self.s.dma("sp", self.outT[k0:k1], src[k0:k1], [srcB], [self.outTB])
        for e in ("sp",):
            self.s.wait_all(e, [self.outTB])
        for q, ring in self.s.dq.items():
            for ent in ring["s"]:
                self.s._wait("sp", ent[0], ent[1])

    def l0_inproj(self, hin, hinB):
        c = self.cfg
        AH = c.AH
        nq = 3 * AH
        for ti in range(c.NGT):
            t0 = ti * c.GT
            tile = self.load_tile(hin, hinB, t0, c.GT, c.KC, gain_idx=0)

            def epi(cc, ps, psB, M, sub, t0=t0):
                ts0 = t0 + sub * 512; Tt = 512
                st, stB = self.nstg()
                if cc < nq:
                    self.qknorm(ps, psB, Tt, self.hg[:, 0:1], self.hgB, 128.0 ** -0.5, st[:, :Tt], stB)
                    self.store_chunk(self.zq, self.zqB, cc, ts0, Tt, st[:, :Tt], stB)
                elif cc < 2 * nq:
                    self.qknorm(ps, psB, Tt, self.hg[:, 1:2], self.hgB, 1.0, st[:, :Tt], stB)
                    self.store_chunk(self.zk, self.zkB, cc - nq, ts0, Tt, st[:, :Tt], stB)
                elif cc < 2 * nq + AH:
                    self.cp("act", st[:, :Tt], ps[:, :Tt], [psB], [stB])
                    self.store_chunk(self.zv, self.zvB, cc - 2 * nq, ts0, Tt, st[:, :Tt], stB)
                else:
                    self.cp("dve", st[:, :Tt], ps[:, :Tt], [psB], [stB])
                    self.store_chunk(self.zu, self.zuB, cc - 2 * nq - AH, ts0, Tt, st[:, :Tt], stB)

            self.gemm(lambda kc, sub: tile[:, kc, sub * 512:(sub + 1) * 512], self.arenaB, c.KC, c.GT // 512, self.ab_win,
                      c.EVEN_IN // 128, 128, epi)

    def l0_attn(self):
        c = self.cfg; T = c.T; AH = c.AH
        ar = self.arena
        assert ar.shape[1] >= 4 * T
        qb = ar[:, 0:T]; kb = ar[:, T:2 * T]; vT = ar[:, 2 * T:3 * T]; accO = ar[:, 3 * T:4 * T]
        gp = self.gp
        assert gp.shape[1] >= T + 32 * 128
        accD = gp[:, 0:T]
        vtok = gp[:, T:T + 32 * 128].rearrange("p (b e) -> p b e", e=128)
        qB, kB, vB, aOB, aDB, vtB = Buf("aq"), Buf("ak"), Buf("av"), Buf("aO"), Buf("aD"), Buf("avt")
        fenceR = [self.arenaB, self.gpB]
        relcur = self.cst[:, 128:640]; relprev = self.cst[:, 640:1152]; relprevF = self.cst[:, 1152:1664]
        n_sl = 3 * AH
        slopes = [2.0 ** (-8.0 * (i + 1) / n_sl) for i in range(n_sl)]
        branches = [(128, 1), (512, 4), (2048, 16)]
        self.fence(fenceR)
        for hd in range(AH):
            self.s.dma("sp", vT, self.zv[hd, :, :], [self.zvB], [vB])
            for g, (win, d) in enumerate(branches):
                cneg = -slopes[g * AH + hd] * d
                self.s.dma("sp", qb, self.zq[g * AH + hd, :, :], [self.zqB], [qB])
                self.s.dma("sp", kb, self.zk[g * AH + hd, :, :], [self.zkB], [kB])
                NB = T // d // 128
                G = min(4, NB)
                nblk = d * NB
                for b0 in range(0, nblk, 4):
                    pt, ptB = self.psX()
                    for j in range(4):
                        blk = b0 + j; r = blk // NB; n = blk % NB
                        st0 = r + d * 128 * n
                        self.tr(ptB, pt[:, j * 128:(j + 1) * 128], vT[:, st0:st0 + d * 127 + 1:d], [vB])
                    self.cp("act", vtok[:, b0:b0 + 4, :], pt[:, :].rearrange("p (b e) -> p b e", e=128), [ptB], [vtB])
                for r in range(d):
                    for n0 in range(0, NB, G):
                        W = G * 128
                        sc, scB = self.psX(); sp_, spB = self.psX()
                        for j in range(G):
                            n = n0 + j
                            st0 = r + d * 128 * n
                            qs = qb[:, st0:st0 + d * 127 + 1:d]
                            kcur = kb[:, st0:st0 + d * 127 + 1:d]
                            self.mm(scB, sc[:, j * 128:(j + 1) * 128], kcur, qs, True, True, [kB, qB])
                            if n > 0:
                                stp = r + d * 128 * (n - 1)
                                kprev = kb[:, stp:stp + d * 127 + 1:d]
                            else:
                                kprev = kcur
                            self.mm(spB, sp_[:, j * 128:(j + 1) * 128], kprev, qs, True, True, [kB, qB])
                        pc, pcB = self.nstg(); pp, ppB = self.nstg()
                        self.stt("dve", pc[:, :W], relcur[:, :W], cneg, sc[:, :W], ALU.mult, ALU.add, [self.cstB, scB], [pcB])
                        rp = relprevF if n0 == 0 else relprev
                        self.stt("dve", pp[:, :W], rp[:, :W], cneg, sp_[:, :W], ALU.mult, ALU.add, [self.cstB, spB], [ppB])
                        self.act(pc[:, :W], pc[:, :W], AF.Exp, [pcB], [pcB])
                        self.act(pp[:, :W], pp[:, :W], AF.Exp, [ppB], [ppB])
                        po, poB = self.psG(); pd, pdB = self.psG()
                        for j in range(G):
                            n = n0 + j
                            blk = r * NB + n
                            self.mm(poB, po[:, j * 128:(j + 1) * 128], vtok[:, blk, :], pc[:, j * 128:(j + 1) * 128], True, n == 0, [vtB, pcB])
                            if n > 0:
                                self.mm(poB, po[:, j * 128:(j + 1) * 128], vtok[:, blk - 1, :], pp[:, j * 128:(j + 1) * 128], False, True, [vtB, ppB])
                            self.mm(pdB, pd[:, j * 128:(j + 1) * 128], self.ones[:], pc[:, j * 128:(j + 1) * 128], True, False, [self.onesB, pcB])
                            self.mm(pdB, pd[:, j * 128:(j + 1) * 128], self.ones[:], pp[:, j * 128:(j + 1) * 128], False, True, [self.onesB, ppB])
                        st0 = r + d * 128 * n0
                        osl = accO[:, st0:st0 + d * (W - 1) + 1:d]; dsl = accD[:, st0:st0 + d * (W - 1) + 1:d]
                        if g == 0:
                            self.cp("act", osl, po[:, :W], [poB], [aOB])
                            self.cp("dve", dsl, pd[:, :W], [pdB], [aDB])
                        else:
                            self.tt("dve", osl, po[:, :W], osl, ALU.add, [poB, aOB], [aOB])
                            self.tt("dve", dsl, pd[:, :W], dsl, ALU.add, [pdB, aDB], [aDB])
            self.recip(accD, accD, [aDB], [aDB])
            o16 = qb.bitcast(BF16)[:, 0:T]
            self.tt("dve", o16, accO, accD, ALU.mult, [aOB, aDB], [qB])
            self.s.dma("sp", self.ab[hd, :, :], o16, [qB], [self.abB])
        self._fence_release([qB, kB, vB, aOB, aDB, vtB], [self.arenaB, self.gpB])

    def fence(self, wholes):
        for e in ("act", "dve", "pool", "sp", "pe"):
            self.s._sync(e, [], wholes)

    def _fence_release(self, subs, wholes):
        for wB in wholes:
            for b in subs:
                if b.w is not None:
                    si, v = b.w
                    wB.r[si] = max(wB.r.get(si, 0), v)
                for si, v in b.r.items():
                    wB.r[si] = max(wB.r.get(si, 0), v)

    def l0_pool(self):
        c = self.cfg; T = c.T
        PADL = 16
        L = min(T, 2048)
        NH = T // L
        W = PADL + L
        ar = self.arena; gp = self.gp
        assert gp.shape[1] >= 3 * W + L
        ub = [gp[:, i * W:(i + 1) * W] for i in range(3)]
        ubB = [Buf("pu%d" % i) for i in range(3)]
        invt = gp[:, 3 * W:3 * W + L]; invB = Buf("invt")
        pooled = ar[:, 0:c.BGC * T].rearrange("p (j t) -> p j t", j=c.BGC)
        pooledB = Buf("pooled")
        psc, pscB = self.sb("psc", [128, c.BW // 128])
        self.s.dma("sp", psc[:], self.pool_sc, [], [pscB])
        self.fence([self.arenaB, self.gpB])
        for gi, win in enumerate((2, 4, 8, 16)):
            for hh in range(NH):
                t0 = hh * L
                self.s.dma("sp", invt, self.invc[gi:gi + 1, t0:t0 + L].partition_broadcast(128), [], [invB])
                for j in range(c.BGC):
                    ci = gi * c.BGC + j
                    if hh == 0:
                        self.s.op("pool", lambda E: E.memset(ub[0][:, 0:PADL], 0.0), [], [ubB[0]])
                        self.s.dma("sp", ub[0][:, PADL:], self.zu[ci, :, t0:t0 + L], [self.zuB], [ubB[0]])
                    else:
                        self.s.dma("sp", ub[0][:, :], self.zu[ci, :, t0 - PADL:t0 + L], [self.zuB], [ubB[0]])
                    src_i = 0; span = 1; dst_i = 1
                    while span < win:
                        self.tt("dve", ub[dst_i][:, span:W], ub[src_i][:, span:W], ub[src_i][:, 0:W - span], ALU.add,
                                [ubB[src_i]], [ubB[dst_i]])
                        src_i = dst_i
                        dst_i = 2 if dst_i == 1 else 1
                        span *= 2
                    self.tt("dve", ub[src_i][:, PADL:], ub[src_i][:, PADL:], invt, ALU.mult, [ubB[src_i], invB], [ubB[src_i]])
                    self.tt("dve", pooled[:, j, t0:t0 + L], ub[src_i][:, PADL:], ub[0][:, PADL:], ALU.subtract, [ubB[src_i], ubB[0]],
                            [pooledB])
            for ti in range(c.NT):
                t0 = ti * c.TT; Tt = c.TT

                def epi(cc, ps, psB, M, sub, t0=t0, Tt=Tt):
                    st, stB = self.nstg()
                    s16 = st[:].bitcast(BF16)
                    self.ts("dve", s16[:, :Tt], ps[:, :Tt], psc[:, cc:cc + 1], ALU.mult, [psB, pscB], [stB])
                    self.store_chunk(self.ab, self.abB, c.AH + cc, t0, Tt, s16[:, :Tt], stB)

                self.gemm(lambda kc, sub, t0=t0, Tt=Tt: pooled[:, kc, t0:t0 + Tt], pooledB, c.BGC, 1, self.pool_w, 4 * c.BGC, 128, epi,
                          cc_list=[gi * c.BGC + oc for oc in range(c.BGC)], lowp=False)
        self._fence_release(ubB + [pooledB, invB], [self.arenaB, self.gpB])

    def outproj(self, w, hin, hinB, hout, houtB):
        c = self.cfg
        for ti in range(c.NGT):
            t0 = ti * c.GT
            tile = self.load_tile(self.ab, self.abB, t0, c.GT, c.KC, src_bf16=True)
            pre, epi = self.epi_residual(hin, hinB, hout, houtB, t0)
            self.gemm(lambda kc, sub: tile[:, kc, sub * 512:(sub + 1) * 512], self.arenaB, c.KC, c.GT // 512, w, c.KC, 128, epi, pre=pre)

    def xattn(self, l, hin, hinB, hout, houtB):
        c = self.cfg; ML = c.ML
        gp = self.gp
        o0 = 0
        kx = gp[:, o0:o0 + 4 * ML].rearrange("p (h m) -> p h m", h=4); o0 += 4 * ML
        vx = gp[:, o0:o0 + 2 * 512].rearrange("p (b n) -> p b n", b=2); o0 += 1024
        qx = gp[:, o0:o0 + 4 * 512].rearrange("p (h t) -> p h t", h=4); o0 += 2048
        ox16 = gp[:, o0:o0 + 1024].bitcast(BF16).rearrange("p (h t) -> p h t", h=4); o0 += 1024
        pb = gp[:, o0:o0 + 2 * 512].rearrange("p (b t) -> p b t", b=2); o0 += 1024
        assert gp.shape[1] >= o0
        kxB, vxB, qxB, oxB, pbB = Buf("kx"), Buf("vx"), Buf("qx"), Buf("ox"), Buf("pb")
        fence = [self.gpB]
        mt = self.load_tile(self.memT, Buf("memT"), 0, ML, c.KC, gain_idx=4 + l)
        self.fence(fence)

        def epik(cc, ps, psB, M, sub):
            self.qknorm(ps, psB, ML, self.hg[:, 4 + l:5 + l], self.hgB, 1.0, kx[:, cc, :], kxB)
        self.gemm(lambda kc, sub: mt[:, kc, :], self.arenaB, c.KC, 1, self.x_wk[l], 4, 128, epik, sw=ML)
        for mb in range(ML // 128):
            ps, psB = self.psG()
            for k0 in range(0, c.KC, 4):
                k1 = min(c.KC, k0 + 4)
                w16t, w16B = self.wb16[self.w16ctr % self.NW16]; self.w16ctr += 1
                w16 = w16t[:, 0:(k1 - k0) * 512].rearrange("p (k n) -> p k n", n=512)
                self.s.dma("pool", w16, self.x_wv[l, :, k0:k1, :], [], [w16B])
                for kc in range(k0, k1):
                    self.mm(psB, ps[:, :], mt[:, kc, mb * 128:(mb + 1) * 128], w16[:, kc - k0, :], kc == 0, kc == c.KC - 1, [w16B, self.arenaB])
            self.cp("act", vx[:, mb, :], ps[:, :], [psB], [vxB])
        for ti in range(c.NT):
            t0 = ti * c.TT; Tt = c.TT
            tile = self.load_tile(hin, hinB, t0, Tt, c.KC, gain_idx=2 + l)

            def epiq(cc, ps, psB, M, sub):
                self.qknorm(ps, psB, Tt, self.hg[:, 2 + l:3 + l], self.hgB, 128.0 ** -0.5, qx[:, cc, :], qxB)
            self.gemm(lambda kc, sub: tile[:, kc, :], self.arenaB, c.KC, 1, self.x_wq[l], 4, 128, epiq)
            for h in range(4):
                pd, pdB = self.psX(); po, poB = self.psX()
                for mb in range(2):
                    ps, psB = self.psX()
                    self.mm(psB, ps[:, :Tt], kx[:, h, mb * 128:(mb + 1) * 128], qx[:, h, :], True, True, [kxB, qxB])
                    self.act(pb[:, mb, :], ps[:, :Tt], AF.Exp, [psB], [pbB])
                for mb in range(2):
                    self.mm(pdB, pd[:, :Tt], self.ones[:], pb[:, mb, :], mb == 0, mb == 1, [self.onesB, pbB])
                for mb in range(2):
                    self.mm(poB, po[:, :Tt], vx[:, mb, h * 128:(h + 1) * 128], pb[:, mb, :], mb == 0, mb == 1, [vxB, pbB])
                rc, rcB = self.nstg()
                self.recip(rc[:, :Tt], pd[:, :Tt], [pdB], [rcB])
                self.tt("dve", ox16[:, h, :], po[:, :Tt], rc[:, :Tt], ALU.mult, [poB, rcB], [oxB])
            pre, epi = self.epi_residual(hin, hinB, hout, houtB, t0)
            self.gemm(lambda kc, sub: ox16[:, kc, :], oxB, 4, 1, self.x_wo[l], c.KC, 128, epi, pre=pre)
        self._fence_release([kxB, vxB, qxB, oxB, pbB], [self.gpB])

    def ffn(self, l, hin, hinB, hout, houtB):
        c = self.cfg; FC = c.FC
        gp = self.gp
        o0 = 0
        cw = gp[:, o0:o0 + 2 * FC * 4].rearrange("p (c f) -> p c f", f=4); o0 += 2 * FC * 4
        tails = gp[:, o0:o0 + 2 * FC * 2].rearrange("p (c f) -> p c f", f=2); o0 += 2 * FC * 2
        ub = []
        for i in range(2):
            ub.append(gp[:, o0:o0 + 514]); o0 += 516
        cg = []
        for i in range(2):
            cg.append(gp[:, o0:o0 + 512]); o0 += 512
        cv = gp[:, o0:o0 + 512]; o0 += 512
        assert gp.shape[1] >= o0
        cwB, tlB, cvB = Buf("cw"), Buf("tails"), Buf("cv")
        cgB = [Buf("cg0"), Buf("cg1")]
        ubB = [Buf("fub0"), Buf("fub1")]
        fence = [self.gpB]
        self.fence(fence)
        self.s.dma("sp", cw, self.f_cw[:, l, :, :], [], [cwB])
        self.s.op("pool", lambda E: E.memset(tails, 0.0), [], [tlB])
        ucnt = [0]
        for ti in range(c.NGT):
            t0 = ti * c.GT
            tile = self.load_tile(hin, hinB, t0, c.GT, c.KC, gain_idx=6 + l)

            def epi(cc, ps, psB, M, sub, t0=t0):
                Tt = 512; ts0 = t0 + sub * 512
                u = ub[ucnt[0] % 2]; uB = ubB[ucnt[0] % 2]; ucnt[0] += 1
                self.cp("act", u[:, 2:2 + Tt], ps[:, :Tt], [psB], [uB])
                self.cp("dve", u[:, 0:2], tails[:, cc, :], [tlB], [uB])
                self.cp("dve", tails[:, cc, :], u[:, Tt:Tt + 2], [uB], [tlB])
                isg = (cc % 2 == 0)
                dst, dstB = (cg[sub], cgB[sub]) if isg else (cv, cvB)
                self.act(dst[:, :Tt], u[:, 2:2 + Tt], AF.Identity, [uB, cwB], [dstB], scale=cw[:, cc, 2:3], bias=cw[:, cc, 3:4])
                self.stt("dve", dst[:, :Tt], u[:, 1:1 + Tt], cw[:, cc, 1:2], dst[:, :Tt], ALU.mult, ALU.add, [uB, cwB, dstB], [dstB])
                self.stt("dve", dst[:, :Tt], u[:, 0:Tt], cw[:, cc, 0:1], dst[:, :Tt], ALU.mult, ALU.add, [uB, cwB, dstB], [dstB])
                if not isg:
                    st, stB = self.nstg()
                    s16 = st[:].bitcast(BF16)
                    self.act(cg[sub][:, :Tt], cg[sub][:, :Tt], AF.Silu, [cgB[sub]], [cgB[sub]])
                    self.tt("dve", s16[:, :Tt], cg[sub][:, :Tt], cv[:, :Tt], ALU.mult, [cgB[sub], cvB], [stB])
                    self.store_chunk(self.ffa, self.ffaB, cc // 2, ts0, Tt, s16[:, :Tt], stB)

            self.gemm(lambda kc, sub: tile[:, kc, sub * 512:(sub + 1) * 512], self.arenaB, c.KC, c.GT // 512, self.f_wup[l], 2 * FC, 128, epi)
        self._fence_release([cwB, tlB, cvB] + cgB + ubB, [self.gpB])
        n3 = (FC + 2) // 3
        parts = [(k0, min(FC, k0 + n3)) for k0 in range(0, FC, n3)]
        temps = [(self.hT1, self.hT1B), (self.hT2, self.hT2B)]
        chain = [(hin, hinB)] + temps[:len(parts) - 1] + [(hout, houtB)]
        for pi, (k0, k1) in enumerate(parts):
            src_h, src_hB = chain[pi]; dst_h, dst_hB = chain[pi + 1]
            for ti in range(c.NGT):
                t0 = ti * c.GT
                tile = self.load_tile(self.ffa[k0:k1], self.ffaB, t0, c.GT, k1 - k0, src_bf16=True)
                pre, epi = self.epi_residual(src_h, src_hB, dst_h, dst_hB, t0)
                self.gemm(lambda kc, sub: tile[:, kc, sub * 512:(sub + 1) * 512], self.arenaB, k1 - k0, c.GT // 512,
                          self.f_wdn[l][:, :, k0:k1, :], c.KC, 128, epi, pre=pre)

    def l1_inproj(self, hin, hinB):
        c = self.cfg
        nq = c.CKW // 128; nv = c.CVW // 128
        CC = (c.ODD_IN + 127) // 128
        for ti in range(c.NGT):
            t0 = ti * c.GT
            tile = self.load_tile(hin, hinB, t0, c.GT, c.KC, gain_idx=1)

            def epi(cc, ps, psB, M, sub, t0=t0):
                Tt = 512; ts0 = t0 + sub * 512
                st, stB = self.nstg()
                if cc < nq:
                    self.act(st[:, :Tt], ps[:, :Tt], AF.Copy, [psB], [stB], scale=float(c.CDK) ** -0.5)
                    self.store_chunk(self.zq, self.zqB, cc, ts0, Tt, st[:, :Tt], stB)
                elif cc < 2 * nq:
                    self.cp("act", st[:, :Tt], ps[:, :Tt], [psB], [stB])
                    self.store_chunk(self.zk, self.zkB, cc - nq, ts0, Tt, st[:, :Tt], stB)
                elif cc < 2 * nq + nv:
                    self.cp("dve", st[:, :Tt], ps[:, :Tt], [psB], [stB])
                    self.store_chunk(self.zv, self.zvB, cc - 2 * nq, ts0, Tt, st[:, :Tt], stB)
                elif cc < 2 * nq + 2 * nv:
                    self.act(st[:, :Tt], ps[:, :Tt], AF.Silu, [psB], [stB])
                    self.store_chunk(self.zu, self.zuB, cc - 2 * nq - nv, ts0, Tt, st[:, :Tt], stB)
                else:
                    self.cp("act", st[:16, :Tt], ps[:16, :Tt], [psB], [stB])
                    self.store_chunk(self.zr, self.zrB, 0, ts0, Tt, st[:16, :Tt], stB, P=16)

            self.gemm(lambda kc, sub: tile[:, kc, sub * 512:(sub + 1) * 512], self.arenaB, c.KC, c.GT // 512, self.c_win, CC, 128, epi)

    def l1_gla(self):
        c = self.cfg; T = c.T; DKC = c.DKC; DVC = c.DVC; Tt = c.TT
        NCH = Tt // 64
        ar = self.arena
        o0 = 0

        def carve(n):
            nonlocal o0
            a = ar[:, o0:o0 + n]; o0 += n
            return a
        wa2 = carve(c.CKW)
        ba = carve(c.CKW // 128)
        on = carve(DVC)
        rT = carve(Tt)
        qt = carve(DKC * Tt).rearrange("p (k t) -> p k t", k=DKC)
        kt = carve(DKC * Tt).rearrange("p (k t) -> p k t", k=DKC)
        vt = carve(DVC * Tt).rearrange("p (k t) -> p k t", k=DVC)
        gt = carve(DVC * Tt).rearrange("p (k t) -> p k t", k=DVC)
        bc = carve(DKC * Tt).rearrange("p (k t) -> p k t", k=DKC)
        ebc = carve(DKC * Tt).rearrange("p (k t) -> p k t", k=DKC)
        dec = carve(DKC * NCH).rearrange("p (k n) -> p k n", k=DKC)
        state = carve(DKC * c.CDV).rearrange("p (k e) -> p k e", k=DKC)
        ot = carve(DVC * Tt).rearrange("p (k t) -> p k t", k=DVC)
        kst = carve(DKC * 64).rearrange("p (k t) -> p k t", k=DKC)
        attm = carve(64)
        vtok = carve(c.CDV)
        ksttok = carve(c.CDK)
        assert ar.shape[1] >= o0, (ar.shape, o0)
        names = ["wa2", "ba", "on", "rT", "qt", "kt", "vt", "gt", "bc", "ebc", "dec", "state", "ot", "kst", "attm", "vtok", "ksttok"]
        B = {n: Buf("g_" + n) for n in names}
        fence = [self.arenaB]
        scanm = self.cst[:, 1728:2240]
        mask01 = self.cst[0:64, 1664:1728]
        self.fence(fence)
        self.s.dma("sp", wa2[0:16, :], self.c_wa2, [], [B["wa2"]])
        self.s.dma("sp", ba, self.c_ba, [], [B["ba"]])
        self.s.dma("sp", on, self.c_on, [], [B["on"]])
        self.ts("dve", ba, ba, -1.0, ALU.mult, [B["ba"]], [B["ba"]])
        for hd in range(c.CH):
            for dc in range(DKC):
                self.s.op("pool", lambda E, dc=dc: E.memset(state[:, dc, :], 0.0), [], [B["state"]])
            for ti in range(c.NT):
                t0 = ti * Tt
                self.s.dma("sp", rT[0:16, :], self.zr[0, 0:16, t0:t0 + Tt], [self.zrB], [B["rT"]])
                self.s.dma("sp", qt, self.zq[hd * DKC:(hd + 1) * DKC, :, t0:t0 + Tt].rearrange("k p t -> p k t"), [self.zqB], [B["qt"]])
                self.s.dma("sp", kt, self.zk[hd * DKC:(hd + 1) * DKC, :, t0:t0 + Tt].rearrange("k p t -> p k t"), [self.zkB], [B["kt"]])
                self.s.dma("sp", vt, self.zv[hd * DVC:(hd + 1) * DVC, :, t0:t0 + Tt].rearrange("k p t -> p k t"), [self.zvB], [B["vt"]])
                self.s.dma("sp", gt, self.zu[hd * DVC:(hd + 1) * DVC, :, t0:t0 + Tt].rearrange("k p t -> p k t"), [self.zuB], [B["gt"]])
                for dc in range(DKC):
                    col = hd * DKC + dc
                    ps, psB = self.psX()
                    self.mm(psB, ps[:, :Tt], wa2[0:16, col * 128:(col + 1) * 128], rT[0:16, :], True, True, [B["wa2"], B["rT"]])
                    self.act(ebc[:, dc, :], ps[:, :Tt], AF.Exp, [psB, B["ba"]], [B["ebc"]], scale=-1.0, bias=ba[:, col:col + 1])
                    one_ap = self.epsap(1.0, 128)
                    self.act(ebc[:, dc, :], ebc[:, dc, :], AF.Ln, [B["ebc"], self.epstB], [B["ebc"]], bias=one_ap)
                    self.ts("dve", ebc[:, dc, :], ebc[:, dc, :], -1.0 / 16.0, ALU.mult, [B["ebc"]], [B["ebc"]])
                    self.s.op("dve", lambda E, dc=dc: E.tensor_tensor_scan(out=bc[:, dc, :], data0=scanm[:, :Tt], data1=ebc[:, dc, :], initial=0.0,
                                                                            op0=ALU.mult, op1=ALU.add), [self.cstB, B["ebc"]], [B["bc"]])
                    self.act(dec[:, dc, :], bc[:, dc, 63:Tt:64], AF.Exp, [B["bc"]], [B["dec"]])
                    self.act(ebc[:, dc, :], bc[:, dc, :], AF.Exp, [B["bc"]], [B["ebc"]])
                    self.tt("dve", qt[:, dc, :], qt[:, dc, :], ebc[:, dc, :], ALU.mult, [B["qt"], B["ebc"]], [B["qt"]])
                    self.act(ebc[:, dc, :], bc[:, dc, :], AF.Exp, [B["bc"]], [B["ebc"]], scale=-1.0)
                    self.tt("dve", kt[:, dc, :], kt[:, dc, :], ebc[:, dc, :], ALU.mult, [B["kt"], B["ebc"]], [B["kt"]])
                for ch in range(NCH):
                    cs = slice(ch * 64, (ch + 1) * 64)
                    pa, paB = self.psX()
                    for dc in range(DKC):
                        self.mm(paB, pa[0:64, 0:64], kt[:, dc, cs], qt[:, dc, cs], dc == 0, dc == DKC - 1, [B["kt"], B["qt"]])
                    self.tt("dve", attm[0:64, :], pa[0:64, 0:64], mask01, ALU.mult, [paB, self.cstB], [B["attm"]])
                    pv, pvB = self.psX()
                    for ec in range(DVC):
                        self.tr(pvB, pv[0:64, ec * 128:(ec + 1) * 128], vt[:, ec, cs], [B["vt"]])
                    self.cp("act", vtok[0:64, :], pv[0:64, 0:c.CDV], [pvB], [B["vtok"]])
                    for dc in range(DKC):
                        self.ts("dve", kst[:, dc, :], kt[:, dc, cs], dec[:, dc, ch:ch + 1], ALU.mult, [B["kt"], B["dec"]], [B["kst"]])
                    pk, pkB = self.psX()
                    for dc in range(DKC):
                        self.tr(pkB, pk[0:64, dc * 128:(dc + 1) * 128], kst[:, dc, :], [B["kst"]])
                    self.cp("act", ksttok[0:64, :], pk[0:64, 0:c.CDK], [pkB], [B["ksttok"]])
                    po, poB = self.psG()
                    for ec in range(DVC):
                        osl = po[:, ec * 64:(ec + 1) * 64]
                        self.mm(poB, osl, vtok[0:64, ec * 128:(ec + 1) * 128], attm[0:64, :], True, False, [B["vtok"], B["attm"]])
                        for dc in range(DKC):
                            self.mm(poB, osl, state[:, dc, ec * 128:(ec + 1) * 128], qt[:, dc, cs], False, dc == DKC - 1, [B["state"], B["qt"]])
                    self.cp("act", ot[:, :, cs], po[:, 0:DVC * 64].rearrange("p (k t) -> p k t", k=DVC), [poB], [B["ot"]])
                    for dc in range(DKC):
                        pkv, pkvB = self.psG()
                        self.mm(pkvB, pkv[:, 0:c.CDV], ksttok[0:64, dc * 128:(dc + 1) * 128], vtok[0:64, :], True, True, [B["ksttok"], B["vtok"]])
                        self.stt("dve", state[:, dc, :], state[:, dc, :], dec[:, dc, ch:ch + 1], pkv[:, 0:c.CDV], ALU.mult, ALU.add,
                                 [B["state"], B["dec"], pkvB], [B["state"]])
                psq, psqB = self.psX()
                for ec in range(DVC):
                    sq, sqB = self.sq[ec % 2]
                    self.act(sq[:, :Tt], ot[:, ec, :], AF.Square, [B["ot"]], [sqB])
                    self.mm(psqB, psq[:, :Tt], self.ones[:], sq[:, :Tt], ec == 0, ec == DVC - 1, [sqB, self.onesB])
                rstd, rstdB = self.rsqrt_from(psq[:, :Tt], psqB, 1.0 / c.CDV, EPS, 128, Tt)
                for ec in range(DVC):
                    st, stB = self.nstg()
                    st2, st2B = self.nstg()
                    s16 = st2[:].bitcast(BF16)
                    self.stt("dve", st[:, :Tt], ot[:, ec, :], on[:, ec:ec + 1], rstd[:, :Tt], ALU.mult, ALU.mult, [B["ot"], B["on"], rstdB], [stB])
                    self.tt("dve", s16[:, :Tt], st[:, :Tt], gt[:, ec, :], ALU.mult, [stB, B["gt"]], [st2B])
                    self.store_chunk(self.ab, self.abB, hd * DVC + ec, t0, Tt, s16[:, :Tt], st2B)
        self._fence_release(list(B.values()), [self.arenaB])


def _wblocks(W):
    K, N = W.shape
    CC = (N + 127) // 128
    if CC * 128 != N:
        W = np.concatenate([W, np.zeros((K, CC * 128 - N), W.dtype)], axis=1)
    return np.ascontiguousarray(W.reshape(K // 128, 128, CC, 128).transpose(2, 1, 0, 3))


def _fm(v):
    return np.ascontiguousarray(v.reshape(-1, 128).T)


def _consts():
    cst = np.zeros((128, 2240), np.float32)
    cst[:, 0:128] = np.eye(128, dtype=np.float32)
    i = np.arange(128)[:, None]; j = np.arange(128)[None, :]
    cur = np.where(j >= i, (j - i).astype(np.float32), BIG).astype(np.float32)
    prev = np.where(i >= j, (j + 128 - i).astype(np.float32), BIG).astype(np.float32)
    for r in range(4):
        cst[:, 128 + r * 128:128 + (r + 1) * 128] = cur
        cst[:, 640 + r * 128:640 + (r + 1) * 128] = prev
        cst[:, 1152 + r * 128:1152 + (r + 1) * 128] = prev if r > 0 else BIG
    jj = np.arange(64)[:, None]; ii = np.arange(64)[None, :]
    cst[0:64, 1664:1728] = (ii >= jj).astype(np.float32)
    m = np.ones(512, np.float32); m[::64] = 0.0
    cst[:, 1728:2240] = m[None, :]
    return cst


def prep_shared(cfg, inp):
    c = cfg
    f = lambda a: np.asarray(a, dtype=np.float32)
    d = {}
    gains = np.zeros((128, 8, c.KC), np.float32)
    for i, (nm, l) in enumerate([("mix_norm", 0), ("mix_norm", 1), ("x_norm", 0), ("x_norm", 1), ("x_mem_norm", 0), ("x_mem_norm", 1),
                                 ("f_norm", 0), ("f_norm", 1)]):
        gains[:, i, :] = _fm(f(inp[nm])[l])
    d["gains"] = gains
    hg = np.zeros((128, 6), np.float32)
    hg[:, 0] = f(inp["ab_q_norm"])[0]; hg[:, 1] = f(inp["ab_k_norm"])[0]
    hg[:, 2] = f(inp["x_q_norm"])[0]; hg[:, 3] = f(inp["x_q_norm"])[1]
    hg[:, 4] = f(inp["x_k_norm"])[0]; hg[:, 5] = f(inp["x_k_norm"])[1]
    d["hgain"] = hg
    d["consts"] = _consts()
    t = np.arange(1, c.T + 1, dtype=np.float32)
    d["invc"] = np.stack([np.float32(1.0) / np.minimum(t, np.float32(w)) for w in (2, 4, 8, 16)]).astype(np.float32)
    d["ab_win"] = _wblocks(f(inp["ab_w_in"])[0])
    pw = f(inp["ab_pool_w"])[0]
    d["pool_w"] = np.concatenate([_wblocks(pw[g]) for g in range(4)], axis=0)
    d["pool_sc"] = _fm(f(inp["ab_pool_scale"])[0])
    d["ab_wout"] = _wblocks(f(inp["ab_w_out"])[0])
    d["c_win"] = _wblocks(f(inp["c_w_in"])[0])
    d["c_wa2"] = np.ascontiguousarray(f(inp["c_w_a2"])[0])
    d["c_ba"] = _fm(f(inp["c_b_a"])[0])
    d["c_on"] = _fm(f(inp["c_o_norm"])[0])
    d["c_wout"] = _wblocks(f(inp["c_w_out"])[0])
    wq = f(inp["x_wq"]); wkv = f(inp["x_wkv"]); wo = f(inp["x_wo"])
    d["x_wq"] = np.stack([_wblocks(wq[l]) for l in range(2)])
    d["x_wk"] = np.stack([_wblocks(wkv[l][:, :512]) for l in range(2)])
    d["x_wv"] = np.stack([np.ascontiguousarray(wkv[l][:, 512:].reshape(c.KC, 128, 512).transpose(1, 0, 2)) for l in range(2)])
    d["x_wo"] = np.stack([_wblocks(wo[l]) for l in range(2)])
    wup = f(inp["f_w_up"]); cw = f(inp["f_conv_w"]); cb = f(inp["f_conv_b"]); wdn = f(inp["f_w_down"])
    idx = np.concatenate([np.concatenate([np.arange(j * 128, (j + 1) * 128), c.DFF + np.arange(j * 128, (j + 1) * 128)]) for j in range(c.FC)])
    d["f_wup"] = np.stack([_wblocks(wup[l][:, idx]) for l in range(2)])
    fcw = np.zeros((128, 2, 2 * c.FC, 4), np.float32)
    for l in range(2):
        for k in range(3):
            fcw[:, l, :, k] = _fm(cw[l, k][idx])
        fcw[:, l, :, 3] = _fm(cb[l][idx])
    d["f_cw"] = fcw
    d["f_wdn"] = np.stack([_wblocks(wdn[l]) for l in range(2)])
    return d


def _tfm(a):
    T, D = a.shape
    return np.ascontiguousarray(a.T.reshape(D // 128, 128, T))


_PROG_CACHE = {}


def run(cfg, inputs, stop_after=None, cores=None):
    key = (cfg.D, cfg.T, cfg.CH, stop_after)
    if key not in _PROG_CACHE:
        p = Prog(cfg, stop_after=stop_after)
        p.build()
        _PROG_CACHE[key] = p
    p = _PROG_CACHE[key]
    shared = prep_shared(cfg, inputs)
    x = np.asarray(inputs["x"], dtype=np.float32); mem = np.asarray(inputs["mem"], dtype=np.float32)
    nb = x.shape[0]
    in_maps = []
    for b in range(nb):
        m = dict(shared)
        m["xT"] = _tfm(x[b]); m["memT"] = _tfm(mem[b])
        in_maps.append(m)
    res = run_bass_kernel_spmd(p.nc, in_maps, core_ids=list(range(nb)))
    outs = []
    for b in range(nb):
        o = res.results[b]["outT"]
        outs.append(np.ascontiguousarray(o.reshape(cfg.D, cfg.T).T))
    return np.stack(outs)


def kernel(**inputs):
    cfg = Cfg(4096, 4096, 8)
    return run(cfg, inputs).astype(np.float32)
```

```python
import math
import numpy as np
import concourse.bass as bass
import concourse.mybir as mybir
from concourse.bass_utils import run_bass_kernel_spmd

F32 = mybir.dt.float32
BF16 = mybir.dt.bfloat16
AF = mybir.ActivationFunctionType
ALU = mybir.AluOpType
EPS = 1e-6
BIG = 1.0e5


class Cfg:
    def __init__(self, D=4096, T=4096, CH=8):
        self.D = D; self.T = T; self.KC = D // 128
        self.AH = D // 256; self.AW = self.AH * 128
        self.BW = D - self.AW; self.BG = self.BW // 4; self.BGC = self.BG // 128
        self.EVEN_IN = 7 * self.AW + self.BW
        self.CH = CH; self.CKW = D // 2; self.CVW = D
        self.CDK = self.CKW // CH; self.CDV = self.CVW // CH
        self.DKC = self.CDK // 128; self.DVC = self.CDV // 128
        self.ODD_IN = 2 * self.CKW + 2 * self.CVW + 16
        self.DFF = ((8 * D // 3 + 255) // 256) * 256; self.FC = self.DFF // 128
        self.XH = 4; self.XW = 512; self.ML = 256
        self.TT = 512
        self.NT = T // self.TT
        self.GT = 1024
        self.NGT = T // self.GT


class Buf:
    __slots__ = ("name", "w", "r")

    def __init__(self, name):
        self.name = name; self.w = None; self.r = {}


class Sched:
    MAXC = 30000

    def __init__(self, nc):
        self.nc = nc
        self.E = {"pe": nc.tensor, "act": nc.scalar, "dve": nc.vector, "pool": nc.gpsimd, "sp": nc.sync}
        self.sems = []; self.owner = []
        self.cur = {}
        self.seen = {e: {} for e in self.E}
        self.dq = {}
        self.ninst = 0

    def newsem(self, name, owner):
        h = self.nc.alloc_semaphore(name)
        self.sems.append(h); self.owner.append(owner)
        return len(self.sems) - 1

    def _wait(self, e, si, val):
        if self.seen[e].get(si, 0) >= val:
            return
        self.E[e].wait_ge(self.sems[si], val)
        self.seen[e][si] = val

    def _sync(self, e, reads, writes):
        deps = {}
        for b in reads:
            if b.w is not None:
                si, v = b.w
                if not (e == "pe" and self.owner[si] == "pe"):
                    deps[si] = max(deps.get(si, 0), v)
        for b in writes:
            if b.w is not None:
                si, v = b.w
                if self.owner[si] != e:
                    deps[si] = max(deps.get(si, 0), v)
            for si, v in b.r.items():
                if self.owner[si] != e:
                    deps[si] = max(deps.get(si, 0), v)
        for si, v in deps.items():
            self._wait(e, si, v)

    def _tick(self, e):
        c = self.cur.get(e)
        if c is None or c[1] >= self.MAXC:
            c = [self.newsem("c_%s_%d" % (e, len(self.sems)), e), 0]
            self.cur[e] = c
        c[1] += 1
        return c[0], c[1]

    def op(self, e, emit, reads=(), writes=()):
        self._sync(e, reads, writes)
        inst = emit(self.E[e])
        si, v = self._tick(e)
        inst.then_inc(self.sems[si], 1)
        for b in reads:
            b.r[si] = v
        for b in writes:
            b.w = (si, v); b.r = {}
        self.ninst += 1
        return inst

    def dma(self, q, out, in_, reads=(), writes=()):
        self._sync(q, reads, writes)
        ring = self.dq.get(q)
        if ring is None:
            ring = {"s": [[self.newsem("d_%s_%d" % (q, i), "dma_" + q), 0] for i in range(8)], "p": 0}
            self.dq[q] = ring
        p = ring["p"]; ring["p"] = (p + 1) % 8
        ent = ring["s"][p]
        if ent[1] >= self.MAXC:
            self._wait(q, ent[0], ent[1])
            ent = [self.newsem("d_%s_%d" % (q, len(self.sems)), "dma_" + q), 0]
            ring["s"][p] = ent
        self._wait(q, ent[0], ent[1])
        inst = self.E[q].dma_start(out=out, in_=in_)
        ent[1] += 16
        inst.then_inc(self.sems[ent[0]], 16)
        for b in reads:
            b.r[ent[0]] = ent[1]
        for b in writes:
            b.w = (ent[0], ent[1]); b.r = {}
        self.ninst += 1

    def wait_all(self, e, bufs):
        self._sync(e, bufs, ())


class Prog:
    def __init__(self, cfg, stop_after=None):
        self.cfg = cfg
        self.stop_after = stop_after
        nc = bass.Bass("TRN2", target_bir_lowering=False)
        self.nc = nc
        self.s = Sched(nc)
        self.din = {}
        self._alloc()

    def dI(self, name, shape):
        ap = self.nc.dram_tensor(name, list(shape), F32, kind="ExternalInput").ap()
        self.din[name] = tuple(shape)
        return ap

    def dS(self, name, shape, dt=F32):
        return self.nc.dram_tensor(name, list(shape), dt, kind="Internal").ap(), Buf(name)

    def sb(self, name, shape):
        return self.nc.alloc_sbuf_tensor(name, list(shape), F32), Buf(name)

    def _alloc(self):
        c = self.cfg; nc = self.nc
        KC, T, FC = c.KC, c.T, c.FC
        self.xT = self.dI("xT", [KC, 128, T]); self.xTB = Buf("xT")
        self.memT = self.dI("memT", [KC, 128, c.ML])
        self.gains = self.dI("gains", [128, 8, KC])
        self.hgain = self.dI("hgain", [128, 6])
        self.consts = self.dI("consts", [128, 2240])
        self.invc = self.dI("invc", [4, T])
        self.ab_win = self.dI("ab_win", [c.EVEN_IN // 128, 128, KC, 128])
        self.pool_w = self.dI("pool_w", [4 * c.BGC, 128, c.BGC, 128])
        self.pool_sc = self.dI("pool_sc", [128, c.BW // 128])
        self.ab_wout = self.dI("ab_wout", [KC, 128, KC, 128])
        self.c_win = self.dI("c_win", [(c.ODD_IN + 127) // 128, 128, KC, 128])
        self.c_wa2 = self.dI("c_wa2", [16, c.CKW])
        self.c_ba = self.dI("c_ba", [128, c.CKW // 128])
        self.c_on = self.dI("c_on", [128, c.DVC])
        self.c_wout = self.dI("c_wout", [KC, 128, KC, 128])
        self.x_wq = self.dI("x_wq", [2, 4, 128, KC, 128])
        self.x_wk = self.dI("x_wk", [2, 4, 128, KC, 128])
        self.x_wv = self.dI("x_wv", [2, 128, KC, 512])
        self.x_wo = self.dI("x_wo", [2, KC, 128, 4, 128])
        self.f_wup = self.dI("f_wup", [2, 2 * FC, 128, KC, 128])
        self.f_cw = self.dI("f_cw", [128, 2, 2 * FC, 4])
        self.f_wdn = self.dI("f_wdn", [2, KC, 128, FC, 128])
        self.outT = nc.dram_tensor("outT", [KC, 128, T], F32, kind="ExternalOutput").ap()
        self.outTB = Buf("outT")
        self.hA, self.hAB = self.dS("hA", [KC, 128, T])
        self.hB, self.hBB = self.dS("hB", [KC, 128, T])
        nzq = max(3 * c.AH, c.CKW // 128)
        self.zq, self.zqB = self.dS("zq", [nzq, 128, T])
        self.zk, self.zkB = self.dS("zk", [nzq, 128, T])
        self.zv, self.zvB = self.dS("zv", [max(c.AH, c.CVW // 128), 128, T])
        self.zu, self.zuB = self.dS("zu", [max(c.BW // 128, c.CVW // 128), 128, T])
        self.zr, self.zrB = self.dS("zr", [1, 128, T])
        self.ab, self.abB = self.dS("ab", [KC, 128, T], BF16)
        self.ffa, self.ffaB = self.dS("ffa", [FC, 128, T], BF16)
        self.hT1, self.hT1B = self.dS("hT1", [KC, 128, T])
        self.hT2, self.hT2B = self.dS("hT2", [KC, 128, T])
        self.cst, self.cstB = self.sb("cst", [128, 2240])
        self.ones, self.onesB = self.sb("ones", [128, 128])
        self.gn, self.gnB = self.sb("gn", [128, 8, KC])
        self.hg, self.hgB = self.sb("hg", [128, 6])
        arena_elems = max(KC * 512, 4 * T, 16384)
        self.arena, self.arenaB = self.sb("arena", [128, arena_elems])
        self.KP = 16
        self.wb = []
        for i in range(2):
            self.wb.append(self.sb("wb%d" % i, [128, self.KP * 128]))
        self.wctr = 0
        self.NW16 = 6
        self.wb16 = []
        for i in range(self.NW16):
            t = self.nc.alloc_sbuf_tensor("wbh%d" % i, [128, self.KP * 128], BF16)
            self.wb16.append((t, Buf("wbh%d" % i)))
        self.w16ctr = 0
        self.ld = [self.sb("ld%d" % i, [128, 1024]) for i in range(4)]
        self.ldc = 0
        self.ones16 = self.nc.alloc_sbuf_tensor("ones16", [128, 128], BF16); self.ones16B = Buf("ones16")
        self.sq = [self.sb("sq%d" % i, [128, 1024]) for i in range(2)]
        self.t1 = [self.sb("t1_%d" % i, [128, 1024]) for i in range(2)]
        self.t2 = [self.sb("t2_%d" % i, [128, 1024]) for i in range(2)]
        self.stg = [self.sb("stg%d" % i, [128, 512]) for i in range(4)]
        self.stgc = 0
        self.res = [self.sb("res%d" % i, [128, 512]) for i in range(4)]
        self.resc = 0
        self.gp, self.gpB = self.sb("gp", [128, 8448])
        self.ps = []
        for i in range(8):
            self.ps.append((nc.alloc_psum_tensor("ps%d" % i, [128, 512], F32), Buf("ps%d" % i)))
        self.psGc = 0; self.psXc = 0

    def psG(self):
        p = self.ps[self.psGc % 4]; self.psGc += 1
        return p

    def psX(self):
        p = self.ps[4 + self.psXc % 4]; self.psXc += 1
        return p

    def nstg(self):
        p = self.stg[self.stgc % 4]; self.stgc += 1
        return p

    def nres(self):
        p = self.res[self.resc % 4]; self.resc += 1
        return p

    def act(self, out, in_, func, reads, writes, scale=None, bias=None):
        kw = {}
        if scale is not None:
            kw["scale"] = scale
        if bias is not None:
            kw["bias"] = bias
        return self.s.op("act", lambda E: E.activation(out=out, in_=in_, func=func, **kw), reads, writes)

    def mm(self, psB, out, lhsT, rhs, start, stop, reads):
        return self.s.op("pe", lambda E: E.matmul(out, lhsT=lhsT, rhs=rhs, start=start, stop=stop), reads, [psB])

    def tr(self, psB, out, in_, reads):
        ident = self.cst[:, 0:128]
        return self.s.op("pe", lambda E: E.transpose(out, in_, ident[: in_.shape[0], : in_.shape[0]]), list(reads) + [self.cstB], [psB])

    def stt(self, eng, out, in0, scalar, in1, op0, op1, reads, writes):
        return self.s.op(eng, lambda E: E.scalar_tensor_tensor(out=out, in0=in0, scalar=scalar, in1=in1, op0=op0, op1=op1), reads, writes)

    def tt(self, eng, out, in0, in1, op, reads, writes):
        return self.s.op(eng, lambda E: E.tensor_tensor(out=out, in0=in0, in1=in1, op=op), reads, writes)

    def ts(self, eng, out, in0, s1, op0, reads, writes, s2=None, op1=None):
        if op1 is None:
            return self.s.op(eng, lambda E: E.tensor_scalar(out=out, in0=in0, scalar1=s1, scalar2=None, op0=op0), reads, writes)
        return self.s.op(eng, lambda E: E.tensor_scalar(out=out, in0=in0, scalar1=s1, scalar2=s2, op0=op0, op1=op1), reads, writes)

    def cp(self, eng, out, in_, reads, writes):
        if eng == "act":
            return self.s.op("act", lambda E: E.copy(out=out, in_=in_), reads, writes)
        return self.s.op(eng, lambda E: E.tensor_copy(out=out, in_=in_), reads, writes)

    def recip(self, out, in_, reads, writes):
        return self.s.op("dve", lambda E: E.reciprocal(out=out, in_=in_), reads, writes)

    def rsqrt_from(self, ps_ap, psB, mul, add, P, N):
        t2, t2B = self.t2[0]; self.t2.reverse()
        bias_ap = self.epsap(add, P)
        self.act(t2[:P, :N], ps_ap, AF.Ln, [psB, self.epstB], [t2B], scale=mul, bias=bias_ap)
        self.act(t2[:P, :N], t2[:P, :N], AF.Exp, [t2B], [t2B], scale=-0.5)
        return t2, t2B

    def epsap(self, val, P):
        key = float(val)
        if not hasattr(self, "_epsmap"):
            self._epsmap = {}
            self.epst, self.epstB = self.sb("epst", [128, 16])
        if key not in self._epsmap:
            j = len(self._epsmap)
            self.s.op("pool", lambda E: E.memset(self.epst[:, j:j + 1], key), [], [self.epstB])
            self._epsmap[key] = j
        j = self._epsmap[key]
        return self.epst[:P, j:j + 1]

    def prologue(self):
        s = self.s
        s.dma("sp", self.cst[:], self.consts, [], [self.cstB])
        s.dma("sp", self.gn[:], self.gains, [], [self.gnB])
        s.dma("sp", self.hg[:], self.hgain, [], [self.hgB])
        s.op("pool", lambda E: E.memset(self.ones[:], 1.0), [], [self.onesB])

    def nld(self):
        p = self.ld[self.ldc % 4]; self.ldc += 1
        return p

    def load_tile(self, src, srcB, t0, Tt, KCn, gain_idx=None, src_bf16=False):
        a16 = self.arena[:].bitcast(BF16)
        tile = a16[:, 0:KCn * Tt].rearrange("p (k t) -> p k t", k=KCn)
        if src_bf16:
            step = max(1, (KCn + 3) // 4)
            for k0 in range(0, KCn, step):
                k1 = min(KCn, k0 + step)
                self.s.dma("sp", tile[:, k0:k1, :], src[k0:k1, :, t0:t0 + Tt].rearrange("k p t -> p k t"), [srcB], [self.arenaB])
            return tile
        if gain_idx is None:
            for kc in range(KCn):
                l, lB = self.nld()
                self.s.dma("sp", l[:, :Tt], src[kc, :, t0:t0 + Tt], [srcB], [lB])
                self.cp("act" if kc % 2 else "pool", tile[:, kc, :], l[:, :Tt], [lB], [self.arenaB])
            return tile
        acc, accB = self.t1[0]; self.t1.reverse()
        for kc in range(KCn):
            l, lB = self.nld()
            self.s.dma("sp", l[:, :Tt], src[kc, :, t0:t0 + Tt], [srcB], [lB])
            if kc == 0:
                self.act(acc[:, :Tt], l[:, :Tt], AF.Square, [lB], [accB])
            else:
                sq, sqB = self.sq[kc % 2]
                self.act(sq[:, :Tt], l[:, :Tt], AF.Square, [lB], [sqB])
                self.tt("pool", acc[:, :Tt], acc[:, :Tt], sq[:, :Tt], ALU.add, [accB, sqB], [accB])
        rstd, rstdB = self.t2[0]; self.t2.reverse()
        for s0 in range(0, Tt, 512):
            wd = min(512, Tt - s0)
            psq, psqB = self.psX()
            self.mm(psqB, psq[:, :wd], self.ones[:], acc[:, s0:s0 + wd], True, True, [accB, self.onesB])
            bias_ap = self.epsap(EPS, 128)
            self.act(rstd[:, s0:s0 + wd], psq[:, :wd], AF.Ln, [psqB, self.epstB], [rstdB], scale=1.0 / (KCn * 128), bias=bias_ap)
            self.act(rstd[:, s0:s0 + wd], rstd[:, s0:s0 + wd], AF.Exp, [rstdB], [rstdB], scale=-0.5)
        for kc in range(KCn):
            l, lB = self.nld()
            self.s.dma("sp", l[:, :Tt], src[kc, :, t0:t0 + Tt], [srcB], [lB])
            self.stt("dve", tile[:, kc, :], l[:, :Tt], self.gn[:, gain_idx, kc:kc + 1], rstd[:, :Tt], ALU.mult, ALU.mult,
                     [lB, self.gnB, rstdB], [self.arenaB])
        return tile

    def gemm(self, rhs_fn, rhsB, KCtot, nsub, w_dram, CC, m_last, epi, pre=None, cc_list=None, lowp=True, sw=512, epi_a=None):
        KP = self.KP
        pieces = [(k0, min(k0 + KP, KCtot)) for k0 in range(0, KCtot, KP)]
        ccs = list(range(CC)) if cc_list is None else cc_list
        blocks = [(cc, k0, k1) for cc in ccs for (k0, k1) in pieces]
        PF = (self.NW16 - 1) if lowp else 1
        loaded = {}

        def load(i):
            cc, k0, k1 = blocks[i]
            if lowp:
                w16t, w16B = self.wb16[self.w16ctr % self.NW16]; self.w16ctr += 1
                wv = w16t[:, 0:(k1 - k0) * 128].rearrange("p (k m) -> p k m", m=128)
                self.s.dma("pool", wv, w_dram[cc, :, k0:k1, :], [], [w16B])
                loaded[i] = (wv, w16B)
            else:
                wbt, wbB = self.wb[self.wctr % 2]; self.wctr += 1
                wv = wbt[:, 0:(k1 - k0) * 128].rearrange("p (k m) -> p k m", m=128)
                self.s.dma("sp", wv, w_dram[cc, :, k0:k1, :], [], [wbB])
                loaded[i] = (wv, wbB)

        for i in range(min(PF, len(blocks))):
            load(i)
        pss = None
        pend_epi = None
        for i, (cc, k0, k1) in enumerate(blocks):
            if i + PF < len(blocks):
                load(i + PF)
            if k0 == 0:
                pss = [self.psG() for _ in range(nsub)]
                if pre is not None:
                    for sub in range(nsub):
                        pre(cc, sub)
            M = m_last if cc == CC - 1 else 128
            wv, wbB = loaded.pop(i)
            for kc in range(k0, k1):
                for sub in range(nsub):
                    ps, psB = pss[sub]
                    self.mm(psB, ps[:M, :sw], wv[:, kc - k0, :M], rhs_fn(kc, sub), kc == 0, kc == KCtot - 1, [wbB, rhsB])
            if k0 == 0 and pend_epi is not None:
                pend_epi(); pend_epi = None
            if k1 == KCtot:
                if epi_a is not None:
                    for sub in range(nsub):
                        ps, psB = pss[sub]
                        epi_a(cc, ps, psB, M, sub)
                cur = (cc, list(pss), M)

                def run_epi(cur=cur):
                    cc_, pss_, M_ = cur
                    for sub in range(nsub):
                        ps, psB = pss_[sub]
                        epi(cc_, ps, psB, M_, sub)
                pend_epi = run_epi
        if pend_epi is not None:
            pend_epi()

    def store_chunk(self, dst, dstB, ci, t0, Tt, src_ap, srcB, P=128):
        self.s.dma("sp", dst[ci, 0:P, t0:t0 + Tt], src_ap, [srcB], [dstB])

    def epi_store(self, dst, dstB, t0, Tt, ci_fn=lambda cc: cc, scale=None, eng="act"):
        def epi(cc, ps, psB, M):
            st, stB = self.nstg()
            if scale is not None:
                self.act(st[:M, :Tt], ps[:M, :Tt], AF.Copy, [psB], [stB], scale=scale)
            elif eng == "act":
                self.cp("act", st[:M, :Tt], ps[:M, :Tt], [psB], [stB])
            else:
                self.cp("dve", st[:M, :Tt], ps[:M, :Tt], [psB], [stB])
            self.store_chunk(dst, dstB, ci_fn(cc), t0, Tt, st[:M, :Tt], stB, P=M)
        return epi

    def epi_residual(self, hin, hinB, hout, houtB, t0):
        pend = {}

        def pre(cc, sub):
            r, rB = self.nres()
            ts0 = t0 + sub * 512
            self.s.dma("sp", r[:, :512], hin[cc, :, ts0:ts0 + 512], [hinB], [rB])
            pend[(cc, sub)] = (r, rB)

        def epi(cc, ps, psB, M, sub):
            r, rB = pend.pop((cc, sub))
            st, stB = self.nstg()
            ts0 = t0 + sub * 512
            self.tt("dve", st[:, :512], ps[:, :512], r[:, :512], ALU.add, [psB, rB], [stB])
            self.store_chunk(hout, houtB, cc, ts0, 512, st[:, :512], stB)
        return pre, epi

    def qk_a(self, ps, psB, Tt):
        if not hasattr(self, "_sqslot"):
            self._sqslot = 0
        i = self._sqslot % 4; self._sqslot += 1
        sqt, sqB = self.sq[i // 2]
        sq = sqt[:, (i % 2) * 512:(i % 2) * 512 + Tt]
        self.act(sq, ps[:, :Tt], AF.Square, [psB], [sqB])
        return sq, sqB

    def qk_b(self, ps, psB, Tt, sq, sqB, gain_ap, gainB, fold, out_ap, outB):
        p2, p2B = self.psX()
        self.mm(p2B, p2[:, :Tt], self.ones[:], sq, True, True, [sqB, self.onesB])
        rstd, rstdB = self.rsqrt_from(p2[:, :Tt], p2B, 1.0 / (128.0 * fold * fold), EPS / (fold * fold), 128, Tt)
        self.stt("dve", out_ap, ps[:, :Tt], gain_ap, rstd[:, :Tt], ALU.mult, ALU.mult, [psB, gainB, rstdB], [outB])

    def qknorm(self, ps, psB, Tt, gain_ap, gainB, fold, out_ap, outB):
        sq, sqB = self.qk_a(ps, psB, Tt)
        self.qk_b(ps, psB, Tt, sq, sqB, gain_ap, gainB, fold, out_ap, outB)

    def build(self):
        c = self.cfg
        self.prologue()
        stages = [
            ("l0_inproj", lambda: self.l0_inproj(self.xT, self.xTB)),
            ("l0_attn", self.l0_attn),
            ("l0_pool", self.l0_pool),
            ("l0_out", lambda: self.outproj(self.ab_wout, self.xT, self.xTB, self.hA, self.hAB)),
            ("l0_x", lambda: self.xattn(0, self.hA, self.hAB, self.hB, self.hBB)),
            ("l0_f", lambda: self.ffn(0, self.hB, self.hBB, self.hA, self.hAB)),
            ("l1_inproj", lambda: self.l1_inproj(self.hA, self.hAB)),
            ("l1_gla", self.l1_gla),
            ("l1_out", lambda: self.outproj(self.c_wout, self.hA, self.hAB, self.hB, self.hBB)),
            ("l1_x", lambda: self.xattn(1, self.hB, self.hBB, self.hA, self.hAB)),
            ("l1_f", lambda: self.ffn(1, self.hA, self.hAB, self.hB, self.hBB)),
        ]
        final = (self.hB, self.hBB)
        dbg = {"l0_inproj": (self.zq, self.zqB), "l0_attn": (self.ab, self.abB), "l0_pool": (self.ab, self.abB),
               "l0_out": (self.hA, self.hAB), "l0_x": (self.hB, self.hBB), "l0_f": (self.hA, self.hAB),
               "l1_inproj": (self.zk, self.zkB), "l1_gla": (self.ab, self.abB), "l1_out": (self.hB, self.hBB),
               "l1_x": (self.hA, self.hAB), "l1_f": (self.hB, self.hBB)}
        for name, fn in stages:
            fn()
            if self.stop_after == name:
                final = dbg[name]
                break
        self.copy_out(*final)
        return self.nc

    def copy_out(self, src, srcB):
        c = self.cfg
        n = min(src.shape[0], c.KC)
        step = max(1, n // 8)
        for k0 in range(0, n, step):
            k1 = min(n, k0 + step)
            self.s.dma("sp", self.outT[k0:k1], src[k0:k1], [srcB], [self.outTB])
        for e in ("sp",):
            self.s.wait_all(e, [self.outTB])
        for q, ring in self.s.dq.items():
            for ent in ring["s"]:
                self.s._wait("sp", ent[0], ent[1])

    def l0_inproj(self, hin, hinB):
        c = self.cfg
        AH = c.AH
        nq = 3 * AH
        sqp = {}
        for ti in range(c.NGT):
            t0 = ti * c.GT
            tile = self.load_tile(hin, hinB, t0, c.GT, c.KC, gain_idx=0)

            def epi(cc, ps, psB, M, sub, t0=t0):
                ts0 = t0 + sub * 512; Tt = 512
                st, stB = self.nstg()
                if cc < nq:
                    sq, sqB = sqp.pop((cc, sub))
                    self.qk_b(ps, psB, Tt, sq, sqB, self.hg[:, 0:1], self.hgB, 128.0 ** -0.5, st[:, :Tt], stB)
                    self.store_chunk(self.zq, self.zqB, cc, ts0, Tt, st[:, :Tt], stB)
                elif cc < 2 * nq:
                    sq, sqB = sqp.pop((cc, sub))
                    self.qk_b(ps, psB, Tt, sq, sqB, self.hg[:, 1:2], self.hgB, 1.0, st[:, :Tt], stB)
                    self.store_chunk(self.zk, self.zkB, cc - nq, ts0, Tt, st[:, :Tt], stB)
                elif cc < 2 * nq + AH:
                    self.cp("act", st[:, :Tt], ps[:, :Tt], [psB], [stB])
                    self.store_chunk(self.zv, self.zvB, cc - 2 * nq, ts0, Tt, st[:, :Tt], stB)
                else:
                    self.cp("dve", st[:, :Tt], ps[:, :Tt], [psB], [stB])
                    self.store_chunk(self.zu, self.zuB, cc - 2 * nq - AH, ts0, Tt, st[:, :Tt], stB)

            def epi_a(cc, ps, psB, M, sub):
                if cc < 2 * nq:
                    sqp[(cc, sub)] = self.qk_a(ps, psB, 512)

            self.gemm(lambda kc, sub: tile[:, kc, sub * 512:(sub + 1) * 512], self.arenaB, c.KC, c.GT // 512, self.ab_win,
                      c.EVEN_IN // 128, 128, epi, epi_a=epi_a)

    def l0_attn(self):
        c = self.cfg; T = c.T; AH = c.AH
        ar = self.arena
        assert ar.shape[1] >= 4 * T
        qb = ar[:, 0:T]; kb = ar[:, T:2 * T]; vT = ar[:, 2 * T:3 * T]; accO = ar[:, 3 * T:4 * T]
        gp = self.gp
        assert gp.shape[1] >= T + 32 * 128
        accD = gp[:, 0:T]
        vtok = gp[:, T:T + 32 * 128].rearrange("p (b e) -> p b e", e=128)
        qB, kB, vB, aOB, aDB, vtB = Buf("aq"), Buf("ak"), Buf("av"), Buf("aO"), Buf("aD"), Buf("avt")
        fenceR = [self.arenaB, self.gpB]
        relcur = self.cst[:, 128:640]; relprev = self.cst[:, 640:1152]; relprevF = self.cst[:, 1152:1664]
        n_sl = 3 * AH
        slopes = [2.0 ** (-8.0 * (i + 1) / n_sl) for i in range(n_sl)]
        branches = [(128, 1), (512, 4), (2048, 16)]
        self.fence(fenceR)
        for hd in range(AH):
            self.s.dma("sp", vT, self.zv[hd, :, :], [self.zvB], [vB])
            for g, (win, d) in enumerate(branches):
                cneg = -slopes[g * AH + hd] * d
                self.s.dma("sp", qb, self.zq[g * AH + hd, :, :], [self.zqB], [qB])
                self.s.dma("sp", kb, self.zk[g * AH + hd, :, :], [self.zkB], [kB])
                NB = T // d // 128
                G = min(4, NB)
                nblk = d * NB
                for b0 in range(0, nblk, 4):
                    pt, ptB = self.psX()
                    for j in range(4):
                        blk = b0 + j; r = blk // NB; n = blk % NB
                        st0 = r + d * 128 * n
                        self.tr(ptB, pt[:, j * 128:(j + 1) * 128], vT[:, st0:st0 + d * 127 + 1:d], [vB])
                    self.cp("act", vtok[:, b0:b0 + 4, :], pt[:, :].rearrange("p (b e) -> p b e", e=128), [ptB], [vtB])
                for r in range(d):
                    for n0 in range(0, NB, G):
                        W = G * 128
                        sc, scB = self.psX(); sp_, spB = self.psX()
                        for j in range(G):
                            n = n0 + j
                            st0 = r + d * 128 * n
                            qs = qb[:, st0:st0 + d * 127 + 1:d]
                            kcur = kb[:, st0:st0 + d * 127 + 1:d]
                            self.mm(scB, sc[:, j * 128:(j + 1) * 128], kcur, qs, True, True, [kB, qB])
                            if n > 0:
                                stp = r + d * 128 * (n - 1)
                                kprev = kb[:, stp:stp + d * 127 + 1:d]
                            else:
                                kprev = kcur
                            self.mm(spB, sp_[:, j * 128:(j + 1) * 128], kprev, qs, True, True, [kB, qB])
                        pc, pcB = self.nstg(); pp, ppB = self.nstg()
                        self.stt("dve", pc[:, :W], relcur[:, :W], cneg, sc[:, :W], ALU.mult, ALU.add, [self.cstB, scB], [pcB])
                        rp = relprevF if n0 == 0 else relprev
                        self.stt("dve", pp[:, :W], rp[:, :W], cneg, sp_[:, :W], ALU.mult, ALU.add, [self.cstB, spB], [ppB])
                        self.act(pc[:, :W], pc[:, :W], AF.Exp, [pcB], [pcB])
                        self.act(pp[:, :W], pp[:, :W], AF.Exp, [ppB], [ppB])
                        po, poB = self.psG(); pd, pdB = self.psG()
                        for j in range(G):
                            n = n0 + j
                            blk = r * NB + n
                            self.mm(poB, po[:, j * 128:(j + 1) * 128], vtok[:, blk, :], pc[:, j * 128:(j + 1) * 128], True, n == 0, [vtB, pcB])
                            if n > 0:
                                self.mm(poB, po[:, j * 128:(j + 1) * 128], vtok[:, blk - 1, :], pp[:, j * 128:(j + 1) * 128], False, True, [vtB, ppB])
                            self.mm(pdB, pd[:, j * 128:(j + 1) * 128], self.ones[:], pc[:, j * 128:(j + 1) * 128], True, False, [self.onesB, pcB])
                            self.mm(pdB, pd[:, j * 128:(j + 1) * 128], self.ones[:], pp[:, j * 128:(j + 1) * 128], False, True, [self.onesB, ppB])
                        st0 = r + d * 128 * n0
                        osl = accO[:, st0:st0 + d * (W - 1) + 1:d]; dsl = accD[:, st0:st0 + d * (W - 1) + 1:d]
                        if g == 0:
                            self.cp("act", osl, po[:, :W], [poB], [aOB])
                            self.cp("dve", dsl, pd[:, :W], [pdB], [aDB])
                        else:
                            self.tt("dve", osl, po[:, :W], osl, ALU.add, [poB, aOB], [aOB])
                            self.tt("dve", dsl, pd[:, :W], dsl, ALU.add, [pdB, aDB], [aDB])
            self.recip(accD, accD, [aDB], [aDB])
            o16 = qb.bitcast(BF16)[:, 0:T]
            self.tt("dve", o16, accO, accD, ALU.mult, [aOB, aDB], [qB])
            self.s.dma("sp", self.ab[hd, :, :], o16, [qB], [self.abB])
        self._fence_release([qB, kB, vB, aOB, aDB, vtB], [self.arenaB, self.gpB])

    def fence(self, wholes):
        for e in ("act", "dve", "pool", "sp", "pe"):
            self.s._sync(e, [], wholes)

    def _fence_release(self, subs, wholes):
        for wB in wholes:
            for b in subs:
                if b.w is not None:
                    si, v = b.w
                    wB.r[si] = max(wB.r.get(si, 0), v)
                for si, v in b.r.items():
                    wB.r[si] = max(wB.r.get(si, 0), v)

    def l0_pool(self):
        c = self.cfg; T = c.T
        PADL = 16
        L = min(T, 2048)
        NH = T // L
        W = PADL + L
        ar = self.arena; gp = self.gp
        assert gp.shape[1] >= 3 * W + L
        ub = [gp[:, i * W:(i + 1) * W] for i in range(3)]
        ubB = [Buf("pu%d" % i) for i in range(3)]
        invt = gp[:, 3 * W:3 * W + L]; invB = Buf("invt")
        pooled = ar[:, 0:c.BGC * T].rearrange("p (j t) -> p j t", j=c.BGC)
        pooledB = Buf("pooled")
        psc, pscB = self.sb("psc", [128, c.BW // 128])
        self.s.dma("sp", psc[:], self.pool_sc, [], [pscB])
        self.fence([self.arenaB, self.gpB])
        for gi, win in enumerate((2, 4, 8, 16)):
            for hh in range(NH):
                t0 = hh * L
                self.s.dma("sp", invt, self.invc[gi:gi + 1, t0:t0 + L].partition_broadcast(128), [], [invB])
                for j in range(c.BGC):
                    ci = gi * c.BGC + j
                    if hh == 0:
                        self.s.op("pool", lambda E: E.memset(ub[0][:, 0:PADL], 0.0), [], [ubB[0]])
                        self.s.dma("sp", ub[0][:, PADL:], self.zu[ci, :, t0:t0 + L], [self.zuB], [ubB[0]])
                    else:
                        self.s.dma("sp", ub[0][:, :], self.zu[ci, :, t0 - PADL:t0 + L], [self.zuB], [ubB[0]])
                    src_i = 0; span = 1; dst_i = 1
                    while span < win:
                        self.tt("dve", ub[dst_i][:, span:W], ub[src_i][:, span:W], ub[src_i][:, 0:W - span], ALU.add,
                                [ubB[src_i]], [ubB[dst_i]])
                        src_i = dst_i
                        dst_i = 2 if dst_i == 1 else 1
                        span *= 2
                    self.tt("dve", ub[src_i][:, PADL:], ub[src_i][:, PADL:], invt, ALU.mult, [ubB[src_i], invB], [ubB[src_i]])
                    self.tt("dve", pooled[:, j, t0:t0 + L], ub[src_i][:, PADL:], ub[0][:, PADL:], ALU.subtract, [ubB[src_i], ubB[0]],
                            [pooledB])
            for ti in range(c.NT):
                t0 = ti * c.TT; Tt = c.TT

                def epi(cc, ps, psB, M, sub, t0=t0, Tt=Tt):
                    st, stB = self.nstg()
                    s16 = st[:].bitcast(BF16)
                    self.ts("dve", s16[:, :Tt], ps[:, :Tt], psc[:, cc:cc + 1], ALU.mult, [psB, pscB], [stB])
                    self.store_chunk(self.ab, self.abB, c.AH + cc, t0, Tt, s16[:, :Tt], stB)

                self.gemm(lambda kc, sub, t0=t0, Tt=Tt: pooled[:, kc, t0:t0 + Tt], pooledB, c.BGC, 1, self.pool_w, 4 * c.BGC, 128, epi,
                          cc_list=[gi * c.BGC + oc for oc in range(c.BGC)], lowp=False)
        self._fence_release(ubB + [pooledB, invB], [self.arenaB, self.gpB])

    def outproj(self, w, hin, hinB, hout, houtB):
        c = self.cfg
        for ti in range(c.NGT):
            t0 = ti * c.GT
            tile = self.load_tile(self.ab, self.abB, t0, c.GT, c.KC, src_bf16=True)
            pre, epi = self.epi_residual(hin, hinB, hout, houtB, t0)
            self.gemm(lambda kc, sub: tile[:, kc, sub * 512:(sub + 1) * 512], self.arenaB, c.KC, c.GT // 512, w, c.KC, 128, epi, pre=pre)

    def xattn(self, l, hin, hinB, hout, houtB):
        c = self.cfg; ML = c.ML
        gp = self.gp
        o0 = 0
        kx = gp[:, o0:o0 + 4 * ML].rearrange("p (h m) -> p h m", h=4); o0 += 4 * ML
        vx = gp[:, o0:o0 + 2 * 512].rearrange("p (b n) -> p b n", b=2); o0 += 1024
        qx = gp[:, o0:o0 + 4 * 512].rearrange("p (h t) -> p h t", h=4); o0 += 2048
        ox16 = gp[:, o0:o0 + 1024].bitcast(BF16).rearrange("p (h t) -> p h t", h=4); o0 += 1024
        pb = gp[:, o0:o0 + 2 * 512].rearrange("p (b t) -> p b t", b=2); o0 += 1024
        assert gp.shape[1] >= o0
        kxB, vxB, qxB, oxB, pbB = Buf("kx"), Buf("vx"), Buf("qx"), Buf("ox"), Buf("pb")
        fence = [self.gpB]
        mt = self.load_tile(self.memT, Buf("memT"), 0, ML, c.KC, gain_idx=4 + l)
        self.fence(fence)

        def epik(cc, ps, psB, M, sub):
            self.qknorm(ps, psB, ML, self.hg[:, 4 + l:5 + l], self.hgB, 1.0, kx[:, cc, :], kxB)
        self.gemm(lambda kc, sub: mt[:, kc, :], self.arenaB, c.KC, 1, self.x_wk[l], 4, 128, epik, sw=ML)
        for mb in range(ML // 128):
            ps, psB = self.psG()
            for k0 in range(0, c.KC, 4):
                k1 = min(c.KC, k0 + 4)
                w16t, w16B = self.wb16[self.w16ctr % self.NW16]; self.w16ctr += 1
                w16 = w16t[:, 0:(k1 - k0) * 512].rearrange("p (k n) -> p k n", n=512)
                self.s.dma("pool", w16, self.x_wv[l, :, k0:k1, :], [], [w16B])
                for kc in range(k0, k1):
                    self.mm(psB, ps[:, :], mt[:, kc, mb * 128:(mb + 1) * 128], w16[:, kc - k0, :], kc == 0, kc == c.KC - 1, [w16B, self.arenaB])
            self.cp("act", vx[:, mb, :], ps[:, :], [psB], [vxB])
        for ti in range(c.NT):
            t0 = ti * c.TT; Tt = c.TT
            tile = self.load_tile(hin, hinB, t0, Tt, c.KC, gain_idx=2 + l)

            def epiq(cc, ps, psB, M, sub):
                self.qknorm(ps, psB, Tt, self.hg[:, 2 + l:3 + l], self.hgB, 128.0 ** -0.5, qx[:, cc, :], qxB)
            self.gemm(lambda kc, sub: tile[:, kc, :], self.arenaB, c.KC, 1, self.x_wq[l], 4, 128, epiq)
            for h in range(4):
                pd, pdB = self.psX(); po, poB = self.psX()
                for mb in range(2):
                    ps, psB = self.psX()
                    self.mm(psB, ps[:, :Tt], kx[:, h, mb * 128:(mb + 1) * 128], qx[:, h, :], True, True, [kxB, qxB])
                    self.act(pb[:, mb, :], ps[:, :Tt], AF.Exp, [psB], [pbB])
                for mb in range(2):
                    self.mm(pdB, pd[:, :Tt], self.ones[:], pb[:, mb, :], mb == 0, mb == 1, [self.onesB, pbB])
                for mb in range(2):
                    self.mm(poB, po[:, :Tt], vx[:, mb, h * 128:(h + 1) * 128], pb[:, mb, :], mb == 0, mb == 1, [vxB, pbB])
                rc, rcB = self.nstg()
                self.recip(rc[:, :Tt], pd[:, :Tt], [pdB], [rcB])
                self.tt("dve", ox16[:, h, :], po[:, :Tt], rc[:, :Tt], ALU.mult, [poB, rcB], [oxB])
            pre, epi = self.epi_residual(hin, hinB, hout, houtB, t0)
            self.gemm(lambda kc, sub: ox16[:, kc, :], oxB, 4, 1, self.x_wo[l], c.KC, 128, epi, pre=pre)
        self._fence_release([kxB, vxB, qxB, oxB, pbB], [self.gpB])

    def ffn(self, l, hin, hinB, hout, houtB):
        c = self.cfg; FC = c.FC
        gp = self.gp
        o0 = 0
        cw = gp[:, o0:o0 + 2 * FC * 4].rearrange("p (c f) -> p c f", f=4); o0 += 2 * FC * 4
        tails = gp[:, o0:o0 + 2 * FC * 2].rearrange("p (c f) -> p c f", f=2); o0 += 2 * FC * 2
        ub = []
        for i in range(2):
            ub.append(gp[:, o0:o0 + 514]); o0 += 516
        cg = []
        for i in range(2):
            cg.append(gp[:, o0:o0 + 512]); o0 += 512
        cv = gp[:, o0:o0 + 512]; o0 += 512
        assert gp.shape[1] >= o0
        cwB, tlB, cvB = Buf("cw"), Buf("tails"), Buf("cv")
        cgB = [Buf("cg0"), Buf("cg1")]
        ubB = [Buf("fub0"), Buf("fub1")]
        fence = [self.gpB]
        self.fence(fence)
        self.s.dma("sp", cw, self.f_cw[:, l, :, :], [], [cwB])
        self.s.op("pool", lambda E: E.memset(tails, 0.0), [], [tlB])
        ucnt = [0]
        for ti in range(c.NGT):
            t0 = ti * c.GT
            tile = self.load_tile(hin, hinB, t0, c.GT, c.KC, gain_idx=6 + l)

            def epi(cc, ps, psB, M, sub, t0=t0):
                Tt = 512; ts0 = t0 + sub * 512
                u = ub[ucnt[0] % 2]; uB = ubB[ucnt[0] % 2]; ucnt[0] += 1
                self.cp("act", u[:, 2:2 + Tt], ps[:, :Tt], [psB], [uB])
                self.cp("dve", u[:, 0:2], tails[:, cc, :], [tlB], [uB])
                self.cp("dve", tails[:, cc, :], u[:, Tt:Tt + 2], [uB], [tlB])
                isg = (cc % 2 == 0)
                dst, dstB = (cg[sub], cgB[sub]) if isg else (cv, cvB)
                self.act(dst[:, :Tt], u[:, 2:2 + Tt], AF.Identity, [uB, cwB], [dstB], scale=cw[:, cc, 2:3], bias=cw[:, cc, 3:4])
                self.stt("dve", dst[:, :Tt], u[:, 1:1 + Tt], cw[:, cc, 1:2], dst[:, :Tt], ALU.mult, ALU.add, [uB, cwB, dstB], [dstB])
                self.stt("dve", dst[:, :Tt], u[:, 0:Tt], cw[:, cc, 0:1], dst[:, :Tt], ALU.mult, ALU.add, [uB, cwB, dstB], [dstB])
                if not isg:
                    st, stB = self.nstg()
                    s16 = st[:].bitcast(BF16)
                    self.act(cg[sub][:, :Tt], cg[sub][:, :Tt], AF.Silu, [cgB[sub]], [cgB[sub]])
                    self.tt("dve", s16[:, :Tt], cg[sub][:, :Tt], cv[:, :Tt], ALU.mult, [cgB[sub], cvB], [stB])
                    self.store_chunk(self.ffa, self.ffaB, cc // 2, ts0, Tt, s16[:, :Tt], stB)

            self.gemm(lambda kc, sub: tile[:, kc, sub * 512:(sub + 1) * 512], self.arenaB, c.KC, c.GT // 512, self.f_wup[l], 2 * FC, 128, epi)
        self._fence_release([cwB, tlB, cvB] + cgB + ubB, [self.gpB])
        n3 = (FC + 2) // 3
        parts = [(k0, min(FC, k0 + n3)) for k0 in range(0, FC, n3)]
        temps = [(self.hT1, self.hT1B), (self.hT2, self.hT2B)]
        chain = [(hin, hinB)] + temps[:len(parts) - 1] + [(hout, houtB)]
        for pi, (k0, k1) in enumerate(parts):
            src_h, src_hB = chain[pi]; dst_h, dst_hB = chain[pi + 1]
            for ti in range(c.NGT):
                t0 = ti * c.GT
                tile = self.load_tile(self.ffa[k0:k1], self.ffaB, t0, c.GT, k1 - k0, src_bf16=True)
                pre, epi = self.epi_residual(src_h, src_hB, dst_h, dst_hB, t0)
                self.gemm(lambda kc, sub: tile[:, kc, sub * 512:(sub + 1) * 512], self.arenaB, k1 - k0, c.GT // 512,
                          self.f_wdn[l][:, :, k0:k1, :], c.KC, 128, epi, pre=pre)

    def l1_inproj(self, hin, hinB):
        c = self.cfg
        nq = c.CKW // 128; nv = c.CVW // 128
        CC = (c.ODD_IN + 127) // 128
        for ti in range(c.NGT):
            t0 = ti * c.GT
            tile = self.load_tile(hin, hinB, t0, c.GT, c.KC, gain_idx=1)

            def epi(cc, ps, psB, M, sub, t0=t0):
                Tt = 512; ts0 = t0 + sub * 512
                st, stB = self.nstg()
                if cc < nq:
                    self.act(st[:, :Tt], ps[:, :Tt], AF.Copy, [psB], [stB], scale=float(c.CDK) ** -0.5)
                    self.store_chunk(self.zq, self.zqB, cc, ts0, Tt, st[:, :Tt], stB)
                elif cc < 2 * nq:
                    self.cp("act", st[:, :Tt], ps[:, :Tt], [psB], [stB])
                    self.store_chunk(self.zk, self.zkB, cc - nq, ts0, Tt, st[:, :Tt], stB)
                elif cc < 2 * nq + nv:
                    self.cp("dve", st[:, :Tt], ps[:, :Tt], [psB], [stB])
                    self.store_chunk(self.zv, self.zvB, cc - 2 * nq, ts0, Tt, st[:, :Tt], stB)
                elif cc < 2 * nq + 2 * nv:
                    self.act(st[:, :Tt], ps[:, :Tt], AF.Silu, [psB], [stB])
                    self.store_chunk(self.zu, self.zuB, cc - 2 * nq - nv, ts0, Tt, st[:, :Tt], stB)
                else:
                    self.cp("act", st[:16, :Tt], ps[:16, :Tt], [psB], [stB])
                    self.store_chunk(self.zr, self.zrB, 0, ts0, Tt, st[:16, :Tt], stB, P=16)

            self.gemm(lambda kc, sub: tile[:, kc, sub * 512:(sub + 1) * 512], self.arenaB, c.KC, c.GT // 512, self.c_win, CC, 128, epi)

    def l1_gla(self):
        c = self.cfg; T = c.T; DKC = c.DKC; DVC = c.DVC; Tt = c.TT
        NCH = Tt // 64
        ar = self.arena
        o0 = 0

        def carve(n):
            nonlocal o0
            a = ar[:, o0:o0 + n]; o0 += n
            return a
        wa2 = carve(c.CKW)
        ba = carve(c.CKW // 128)
        on = carve(DVC)
        rT = carve(Tt)
        qt = carve(DKC * Tt).rearrange("p (k t) -> p k t", k=DKC)
        kt = carve(DKC * Tt).rearrange("p (k t) -> p k t", k=DKC)
        vt = carve(DVC * Tt).rearrange("p (k t) -> p k t", k=DVC)
        gt = carve(DVC * Tt).rearrange("p (k t) -> p k t", k=DVC)
        bc = carve(DKC * Tt).rearrange("p (k t) -> p k t", k=DKC)
        ebc = carve(DKC * Tt).rearrange("p (k t) -> p k t", k=DKC)
        dec = carve(DKC * NCH).rearrange("p (k n) -> p k n", k=DKC)
        state = carve(DKC * c.CDV).rearrange("p (k e) -> p k e", k=DKC)
        ot = carve(DVC * Tt).rearrange("p (k t) -> p k t", k=DVC)
        kst = carve(DKC * 64).rearrange("p (k t) -> p k t", k=DKC)
        attm = carve(64)
        vtok = carve(c.CDV)
        ksttok = carve(c.CDK)
        assert ar.shape[1] >= o0, (ar.shape, o0)
        names = ["wa2", "ba", "on", "rT", "qt", "kt", "vt", "gt", "bc", "ebc", "dec", "state", "ot", "kst", "attm", "vtok", "ksttok"]
        B = {n: Buf("g_" + n) for n in names}
        fence = [self.arenaB]
        scanm = self.cst[:, 1728:2240]
        mask01 = self.cst[0:64, 1664:1728]
        self.fence(fence)
        self.s.dma("sp", wa2[0:16, :], self.c_wa2, [], [B["wa2"]])
        self.s.dma("sp", ba, self.c_ba, [], [B["ba"]])
        self.s.dma("sp", on, self.c_on, [], [B["on"]])
        self.ts("dve", ba, ba, -1.0, ALU.mult, [B["ba"]], [B["ba"]])
        for hd in range(c.CH):
            for dc in range(DKC):
                self.s.op("pool", lambda E, dc=dc: E.memset(state[:, dc, :], 0.0), [], [B["state"]])
            for ti in range(c.NT):
                t0 = ti * Tt
                self.s.dma("sp", rT[0:16, :], self.zr[0, 0:16, t0:t0 + Tt], [self.zrB], [B["rT"]])
                self.s.dma("sp", qt, self.zq[hd * DKC:(hd + 1) * DKC, :, t0:t0 + Tt].rearrange("k p t -> p k t"), [self.zqB], [B["qt"]])
                self.s.dma("sp", kt, self.zk[hd * DKC:(hd + 1) * DKC, :, t0:t0 + Tt].rearrange("k p t -> p k t"), [self.zkB], [B["kt"]])
                self.s.dma("sp", vt, self.zv[hd * DVC:(hd + 1) * DVC, :, t0:t0 + Tt].rearrange("k p t -> p k t"), [self.zvB], [B["vt"]])
                self.s.dma("sp", gt, self.zu[hd * DVC:(hd + 1) * DVC, :, t0:t0 + Tt].rearrange("k p t -> p k t"), [self.zuB], [B["gt"]])
                for dc in range(DKC):
                    col = hd * DKC + dc
                    ps, psB = self.psX()
                    self.mm(psB, ps[:, :Tt], wa2[0:16, col * 128:(col + 1) * 128], rT[0:16, :], True, True, [B["wa2"], B["rT"]])
                    self.act(ebc[:, dc, :], ps[:, :Tt], AF.Exp, [psB, B["ba"]], [B["ebc"]], scale=-1.0, bias=ba[:, col:col + 1])
                    one_ap = self.epsap(1.0, 128)
                    self.act(ebc[:, dc, :], ebc[:, dc, :], AF.Ln, [B["ebc"], self.epstB], [B["ebc"]], bias=one_ap)
                    self.ts("dve", ebc[:, dc, :], ebc[:, dc, :], -1.0 / 16.0, ALU.mult, [B["ebc"]], [B["ebc"]])
                    self.s.op("dve", lambda E, dc=dc: E.tensor_tensor_scan(out=bc[:, dc, :], data0=scanm[:, :Tt], data1=ebc[:, dc, :], initial=0.0,
                                                                            op0=ALU.mult, op1=ALU.add), [self.cstB, B["ebc"]], [B["bc"]])
                    self.act(dec[:, dc, :], bc[:, dc, 63:Tt:64], AF.Exp, [B["bc"]], [B["dec"]])
                    self.act(ebc[:, dc, :], bc[:, dc, :], AF.Exp, [B["bc"]], [B["ebc"]])
                    self.tt("dve", qt[:, dc, :], qt[:, dc, :], ebc[:, dc, :], ALU.mult, [B["qt"], B["ebc"]], [B["qt"]])
                    self.act(ebc[:, dc, :], bc[:, dc, :], AF.Exp, [B["bc"]], [B["ebc"]], scale=-1.0)
                    self.tt("dve", kt[:, dc, :], kt[:, dc, :], ebc[:, dc, :], ALU.mult, [B["kt"], B["ebc"]], [B["kt"]])
                for ch in range(NCH):
                    cs = slice(ch * 64, (ch + 1) * 64)
                    pa, paB = self.psX()
                    for dc in range(DKC):
                        self.mm(paB, pa[0:64, 0:64], kt[:, dc, cs], qt[:, dc, cs], dc == 0, dc == DKC - 1, [B["kt"], B["qt"]])
                    self.tt("dve", attm[0:64, :], pa[0:64, 0:64], mask01, ALU.mult, [paB, self.cstB], [B["attm"]])
                    pv, pvB = self.psX()
                    for ec in range(DVC):
                        self.tr(pvB, pv[0:64, ec * 128:(ec + 1) * 128], vt[:, ec, cs], [B["vt"]])
                    self.cp("act", vtok[0:64, :], pv[0:64, 0:c.CDV], [pvB], [B["vtok"]])
                    for dc in range(DKC):
                        self.ts("dve", kst[:, dc, :], kt[:, dc, cs], dec[:, dc, ch:ch + 1], ALU.mult, [B["kt"], B["dec"]], [B["kst"]])
                    pk, pkB = self.psX()
                    for dc in range(DKC):
                        self.tr(pkB, pk[0:64, dc * 128:(dc + 1) * 128], kst[:, dc, :], [B["kst"]])
                    self.cp("act", ksttok[0:64, :], pk[0:64, 0:c.CDK], [pkB], [B["ksttok"]])
                    po, poB = self.psG()
                    for ec in range(DVC):
                        osl = po[:, ec * 64:(ec + 1) * 64]
                        self.mm(poB, osl, vtok[0:64, ec * 128:(ec + 1) * 128], attm[0:64, :], True, False, [B["vtok"], B["attm"]])
                        for dc in range(DKC):
                            self.mm(poB, osl, state[:, dc, ec * 128:(ec + 1) * 128], qt[:, dc, cs], False, dc == DKC - 1, [B["state"], B["qt"]])
                    self.cp("act", ot[:, :, cs], po[:, 0:DVC * 64].rearrange("p (k t) -> p k t", k=DVC), [poB], [B["ot"]])
                    for dc in range(DKC):
                        pkv, pkvB = self.psG()
                        self.mm(pkvB, pkv[:, 0:c.CDV], ksttok[0:64, dc * 128:(dc + 1) * 128], vtok[0:64, :], True, True, [B["ksttok"], B["vtok"]])
                        self.stt("dve", state[:, dc, :], state[:, dc, :], dec[:, dc, ch:ch + 1], pkv[:, 0:c.CDV], ALU.mult, ALU.add,
                                 [B["state"], B["dec"], pkvB], [B["state"]])
                psq, psqB = self.psX()
                for ec in range(DVC):
                    sq, sqB = self.sq[ec % 2]
                    self.act(sq[:, :Tt], ot[:, ec, :], AF.Square, [B["ot"]], [sqB])
                    self.mm(psqB, psq[:, :Tt], self.ones[:], sq[:, :Tt], ec == 0, ec == DVC - 1, [sqB, self.onesB])
                rstd, rstdB = self.rsqrt_from(psq[:, :Tt], psqB, 1.0 / c.CDV, EPS, 128, Tt)
                for ec in range(DVC):
                    st, stB = self.nstg()
                    st2, st2B = self.nstg()
                    s16 = st2[:].bitcast(BF16)
                    self.stt("dve", st[:, :Tt], ot[:, ec, :], on[:, ec:ec + 1], rstd[:, :Tt], ALU.mult, ALU.mult, [B["ot"], B["on"], rstdB], [stB])
                    self.tt("dve", s16[:, :Tt], st[:, :Tt], gt[:, ec, :], ALU.mult, [stB, B["gt"]], [st2B])
                    self.store_chunk(self.ab, self.abB, hd * DVC + ec, t0, Tt, s16[:, :Tt], st2B)
        self._fence_release(list(B.values()), [self.arenaB])


def _wblocks(W):
    K, N = W.shape
    CC = (N + 127) // 128
    if CC * 128 != N:
        W = np.concatenate([W, np.zeros((K, CC * 128 - N), W.dtype)], axis=1)
    return np.ascontiguousarray(W.reshape(K // 128, 128, CC, 128).transpose(2, 1, 0, 3))


def _fm(v):
    return np.ascontiguousarray(v.reshape(-1, 128).T)


def _consts():
    cst = np.zeros((128, 2240), np.float32)
    cst[:, 0:128] = np.eye(128, dtype=np.float32)
    i = np.arange(128)[:, None]; j = np.arange(128)[None, :]
    cur = np.where(j >= i, (j - i).astype(np.float32), BIG).astype(np.float32)
    prev = np.where(i >= j, (j + 128 - i).astype(np.float32), BIG).astype(np.float32)
    for r in range(4):
        cst[:, 128 + r * 128:128 + (r + 1) * 128] = cur
        cst[:, 640 + r * 128:640 + (r + 1) * 128] = prev
        cst[:, 1152 + r * 128:1152 + (r + 1) * 128] = prev if r > 0 else BIG
    jj = np.arange(64)[:, None]; ii = np.arange(64)[None, :]
    cst[0:64, 1664:1728] = (ii >= jj).astype(np.float32)
    m = np.ones(512, np.float32); m[::64] = 0.0
    cst[:, 1728:2240] = m[None, :]
    return cst


def prep_shared(cfg, inp):
    c = cfg
    f = lambda a: np.asarray(a, dtype=np.float32)
    d = {}
    gains = np.zeros((128, 8, c.KC), np.float32)
    for i, (nm, l) in enumerate([("mix_norm", 0), ("mix_norm", 1), ("x_norm", 0), ("x_norm", 1), ("x_mem_norm", 0), ("x_mem_norm", 1),
                                 ("f_norm", 0), ("f_norm", 1)]):
        gains[:, i, :] = _fm(f(inp[nm])[l])
    d["gains"] = gains
    hg = np.zeros((128, 6), np.float32)
    hg[:, 0] = f(inp["ab_q_norm"])[0]; hg[:, 1] = f(inp["ab_k_norm"])[0]
    hg[:, 2] = f(inp["x_q_norm"])[0]; hg[:, 3] = f(inp["x_q_norm"])[1]
    hg[:, 4] = f(inp["x_k_norm"])[0]; hg[:, 5] = f(inp["x_k_norm"])[1]
    d["hgain"] = hg
    d["consts"] = _consts()
    t = np.arange(1, c.T + 1, dtype=np.float32)
    d["invc"] = np.stack([np.float32(1.0) / np.minimum(t, np.float32(w)) for w in (2, 4, 8, 16)]).astype(np.float32)
    d["ab_win"] = _wblocks(f(inp["ab_w_in"])[0])
    pw = f(inp["ab_pool_w"])[0]
    d["pool_w"] = np.concatenate([_wblocks(pw[g]) for g in range(4)], axis=0)
    d["pool_sc"] = _fm(f(inp["ab_pool_scale"])[0])
    d["ab_wout"] = _wblocks(f(inp["ab_w_out"])[0])
    d["c_win"] = _wblocks(f(inp["c_w_in"])[0])
    d["c_wa2"] = np.ascontiguousarray(f(inp["c_w_a2"])[0])
    d["c_ba"] = _fm(f(inp["c_b_a"])[0])
    d["c_on"] = _fm(f(inp["c_o_norm"])[0])
    d["c_wout"] = _wblocks(f(inp["c_w_out"])[0])
    wq = f(inp["x_wq"]); wkv = f(inp["x_wkv"]); wo = f(inp["x_wo"])
    d["x_wq"] = np.stack([_wblocks(wq[l]) for l in range(2)])
    d["x_wk"] = np.stack([_wblocks(wkv[l][:, :512]) for l in range(2)])
    d["x_wv"] = np.stack([np.ascontiguousarray(wkv[l][:, 512:].reshape(c.KC, 128, 512).transpose(1, 0, 2)) for l in range(2)])
    d["x_wo"] = np.stack([_wblocks(wo[l]) for l in range(2)])
    wup = f(inp["f_w_up"]); cw = f(inp["f_conv_w"]); cb = f(inp["f_conv_b"]); wdn = f(inp["f_w_down"])
    idx = np.concatenate([np.concatenate([np.arange(j * 128, (j + 1) * 128), c.DFF + np.arange(j * 128, (j + 1) * 128)]) for j in range(c.FC)])
    d["f_wup"] = np.stack([_wblocks(wup[l][:, idx]) for l in range(2)])
    fcw = np.zeros((128, 2, 2 * c.FC, 4), np.float32)
    for l in range(2):
        for k in range(3):
            fcw[:, l, :, k] = _fm(cw[l, k][idx])
        fcw[:, l, :, 3] = _fm(cb[l][idx])
    d["f_cw"] = fcw
    d["f_wdn"] = np.stack([_wblocks(wdn[l]) for l in range(2)])
    return d


def _tfm(a):
    T, D = a.shape
    return np.ascontiguousarray(a.T.reshape(D // 128, 128, T))


_PROG_CACHE = {}


def run(cfg, inputs, stop_after=None, cores=None):
    key = (cfg.D, cfg.T, cfg.CH, stop_after)
    if key not in _PROG_CACHE:
        p = Prog(cfg, stop_after=stop_after)
        p.build()
        _PROG_CACHE[key] = p
    p = _PROG_CACHE[key]
    shared = prep_shared(cfg, inputs)
    x = np.asarray(inputs["x"], dtype=np.float32); mem = np.asarray(inputs["mem"], dtype=np.float32)
    nb = x.shape[0]
    in_maps = []
    for b in range(nb):
        m = dict(shared)
        m["xT"] = _tfm(x[b]); m["memT"] = _tfm(mem[b])
        in_maps.append(m)
    res = run_bass_kernel_spmd(p.nc, in_maps, core_ids=list(range(nb)))
    outs = []
    for b in range(nb):
        o = res.results[b]["outT"]
        outs.append(np.ascontiguousarray(o.reshape(cfg.D, cfg.T).T))
    return np.stack(outs)


def kernel(**inputs):
    cfg = Cfg(4096, 4096, 8)
    return run(cfg, inputs).astype(np.float32)
```

```python
import math
import numpy as np
import concourse.bass as bass
import concourse.mybir as mybir
from concourse.bass_utils import run_bass_kernel_spmd

F32 = mybir.dt.float32
BF16 = mybir.dt.bfloat16
AF = mybir.ActivationFunctionType
ALU = mybir.AluOpType
EPS = 1e-6
BIG = 1.0e5


class Cfg:
    def __init__(self, D=4096, T=4096, CH=8):
        self.D = D; self.T = T; self.KC = D // 128
        self.AH = D // 256; self.AW = self.AH * 128
        self.BW = D - self.AW; self.BG = self.BW // 4; self.BGC = self.BG // 128
        self.EVEN_IN = 7 * self.AW + self.BW
        self.CH = CH; self.CKW = D // 2; self.CVW = D
        self.CDK = self.CKW // CH; self.CDV = self.CVW // CH
        self.DKC = self.CDK // 128; self.DVC = self.CDV // 128
        self.ODD_IN = 2 * self.CKW + 2 * self.CVW + 16
        self.DFF = ((8 * D // 3 + 255) // 256) * 256; self.FC = self.DFF // 128
        self.XH = 4; self.XW = 512; self.ML = 256
        self.TT = 512
        self.NT = T // self.TT
        self.GT = 1024
        self.NGT = T // self.GT


class Buf:
    __slots__ = ("name", "w", "r")

    def __init__(self, name):
        self.name = name; self.w = None; self.r = {}


class Sched:
    MAXC = 30000

    def __init__(self, nc):
        self.nc = nc
        self.E = {"pe": nc.tensor, "act": nc.scalar, "dve": nc.vector, "pool": nc.gpsimd, "sp": nc.sync}
        self.sems = []; self.owner = []
        self.cur = {}
        self.seen = {e: {} for e in self.E}
        self.dq = {}
        self.ninst = 0

    def newsem(self, name, owner):
        h = self.nc.alloc_semaphore(name)
        self.sems.append(h); self.owner.append(owner)
        return len(self.sems) - 1

    def _wait(self, e, si, val):
        if self.seen[e].get(si, 0) >= val:
            return
        self.E[e].wait_ge(self.sems[si], val)
        self.seen[e][si] = val

    def _sync(self, e, reads, writes):
        deps = {}
        for b in reads:
            if b.w is not None:
                si, v = b.w
                if not (e == "pe" and self.owner[si] == "pe"):
                    deps[si] = max(deps.get(si, 0), v)
        for b in writes:
            if b.w is not None:
                si, v = b.w
                if self.owner[si] != e:
                    deps[si] = max(deps.get(si, 0), v)
            for si, v in b.r.items():
                if self.owner[si] != e:
                    deps[si] = max(deps.get(si, 0), v)
        for si, v in deps.items():
            self._wait(e, si, v)

    def _tick(self, e):
        c = self.cur.get(e)
        if c is None or c[1] >= self.MAXC:
            c = [self.newsem("c_%s_%d" % (e, len(self.sems)), e), 0]
            self.cur[e] = c
        c[1] += 1
        return c[0], c[1]

    def op(self, e, emit, reads=(), writes=()):
        self._sync(e, reads, writes)
        inst = emit(self.E[e])
        si, v = self._tick(e)
        inst.then_inc(self.sems[si], 1)
        for b in reads:
            b.r[si] = v
        for b in writes:
            b.w = (si, v); b.r = {}
        self.ninst += 1
        return inst

    def dma(self, q, out, in_, reads=(), writes=()):
        self._sync(q, reads, writes)
        ring = self.dq.get(q)
        if ring is None:
            ring = {"s": [[self.newsem("d_%s_%d" % (q, i), "dma_" + q), 0] for i in range(8)], "p": 0}
            self.dq[q] = ring
        p = ring["p"]; ring["p"] = (p + 1) % 8
        ent = ring["s"][p]
        if ent[1] >= self.MAXC:
            self._wait(q, ent[0], ent[1])
            ent = [self.newsem("d_%s_%d" % (q, len(self.sems)), "dma_" + q), 0]
            ring["s"][p] = ent
        self._wait(q, ent[0], ent[1])
        inst = self.E[q].dma_start(out=out, in_=in_)
        ent[1] += 16
        inst.then_inc(self.sems[ent[0]], 16)
        for b in reads:
            b.r[ent[0]] = ent[1]
        for b in writes:
            b.w = (ent[0], ent[1]); b.r = {}
        self.ninst += 1

    def wait_all(self, e, bufs):
        self._sync(e, bufs, ())


class Prog:
    def __init__(self, cfg, stop_after=None):
        self.cfg = cfg
        self.stop_after = stop_after
        nc = bass.Bass("TRN2", target_bir_lowering=False)
        self.nc = nc
        self.s = Sched(nc)
        self.din = {}
        self._alloc()

    def dI(self, name, shape):
        ap = self.nc.dram_tensor(name, list(shape), F32, kind="ExternalInput").ap()
        self.din[name] = tuple(shape)
        return ap

    def dS(self, name, shape, dt=F32):
        return self.nc.dram_tensor(name, list(shape), dt, kind="Internal").ap(), Buf(name)

    def sb(self, name, shape):
        return self.nc.alloc_sbuf_tensor(name, list(shape), F32), Buf(name)

    def _alloc(self):
        c = self.cfg; nc = self.nc
        KC, T, FC = c.KC, c.T, c.FC
        self.xT = self.dI("xT", [KC, 128, T]); self.xTB = Buf("xT")
        self.memT = self.dI("memT", [KC, 128, c.ML])
        self.gains = self.dI("gains", [128, 8, KC])
        self.hgain = self.dI("hgain", [128, 6])
        self.consts = self.dI("consts", [128, 2240])
        self.invc = self.dI("invc", [4, T])
        self.ab_win = self.dI("ab_win", [c.EVEN_IN // 128, 128, KC, 128])
        self.pool_w = self.dI("pool_w", [4 * c.BGC, 128, c.BGC, 128])
        self.pool_sc = self.dI("pool_sc", [128, c.BW // 128])
        self.ab_wout = self.dI("ab_wout", [KC, 128, KC, 128])
        self.c_win = self.dI("c_win", [(c.ODD_IN + 127) // 128, 128, KC, 128])
        self.c_wa2 = self.dI("c_wa2", [16, c.CKW])
        self.c_ba = self.dI("c_ba", [128, c.CKW // 128])
        self.c_on = self.dI("c_on", [128, c.DVC])
        self.c_wout = self.dI("c_wout", [KC, 128, KC, 128])
        self.x_wq = self.dI("x_wq", [2, 4, 128, KC, 128])
        self.x_wk = self.dI("x_wk", [2, 4, 128, KC, 128])
        self.x_wv = self.dI("x_wv", [2, 128, KC, 512])
        self.x_wo = self.dI("x_wo", [2, KC, 128, 4, 128])
        self.f_wup = self.dI("f_wup", [2, 2 * FC, 128, KC, 128])
        self.f_cw = self.dI("f_cw", [128, 2, 2 * FC, 4])
        self.f_wdn = self.dI("f_wdn", [2, KC, 128, FC, 128])
        self.outT = nc.dram_tensor("outT", [KC, 128, T], F32, kind="ExternalOutput").ap()
        self.outTB = Buf("outT")
        self.hA, self.hAB = self.dS("hA", [KC, 128, T])
        self.hB, self.hBB = self.dS("hB", [KC, 128, T])
        nzq = max(3 * c.AH, c.CKW // 128)
        self.zq, self.zqB = self.dS("zq", [nzq, 128, T])
        self.zk, self.zkB = self.dS("zk", [nzq, 128, T])
        self.zv, self.zvB = self.dS("zv", [max(c.AH, c.CVW // 128), 128, T])
        self.zu, self.zuB = self.dS("zu", [max(c.BW // 128, c.CVW // 128), 128, T])
        self.zr, self.zrB = self.dS("zr", [1, 128, T])
        self.ab, self.abB = self.dS("ab", [KC, 128, T], BF16)
        self.ffa, self.ffaB = self.dS("ffa", [FC, 128, T], BF16)
        self.hT1, self.hT1B = self.dS("hT1", [KC, 128, T])
        self.hT2, self.hT2B = self.dS("hT2", [KC, 128, T])
        self.cst, self.cstB = self.sb("cst", [128, 2240])
        self.ones, self.onesB = self.sb("ones", [128, 128])
        self.gn, self.gnB = self.sb("gn", [128, 8, KC])
        self.hg, self.hgB = self.sb("hg", [128, 6])
        arena_elems = max(KC * 512, 4 * T, 16384)
        self.arena, self.arenaB = self.sb("arena", [128, arena_elems])
        self.KP = 16
        self.wb = []
        for i in range(2):
            self.wb.append(self.sb("wb%d" % i, [128, self.KP * 128]))
        self.wctr = 0
        self.NW16 = 6
        self.wb16 = []
        for i in range(self.NW16):
            t = self.nc.alloc_sbuf_tensor("wbh%d" % i, [128, self.KP * 128], BF16)
            self.wb16.append((t, Buf("wbh%d" % i)))
        self.w16ctr = 0
        self.ld = [self.sb("ld%d" % i, [128, 1024]) for i in range(4)]
        self.apB = [Buf("ap%d" % i) for i in range(6)]
        self.ldc = 0
        self.ones16 = self.nc.alloc_sbuf_tensor("ones16", [128, 128], BF16); self.ones16B = Buf("ones16")
        self.sq = [self.sb("sq%d" % i, [128, 1024]) for i in range(2)]
        self.t1 = [self.sb("t1_%d" % i, [128, 1024]) for i in range(2)]
        self.t2 = [self.sb("t2_%d" % i, [128, 1024]) for i in range(2)]
        self.stg = [self.sb("stg%d" % i, [128, 512]) for i in range(4)]
        self.stgc = 0
        self.res = [self.sb("res%d" % i, [128, 512]) for i in range(4)]
        self.resc = 0
        self.gp, self.gpB = self.sb("gp", [128, 8448])
        self.ps = []
        for i in range(8):
            self.ps.append((nc.alloc_psum_tensor("ps%d" % i, [128, 512], F32), Buf("ps%d" % i)))
        self.psGc = 0; self.psXc = 0

    def psG(self):
        p = self.ps[self.psGc % 4]; self.psGc += 1
        return p

    def psX(self):
        p = self.ps[4 + self.psXc % 4]; self.psXc += 1
        return p

    def nstg(self):
        p = self.stg[self.stgc % 4]; self.stgc += 1
        return p

    def nres(self):
        p = self.res[self.resc % 4]; self.resc += 1
        return p

    def act(self, out, in_, func, reads, writes, scale=None, bias=None):
        kw = {}
        if scale is not None:
            kw["scale"] = scale
        if bias is not None:
            kw["bias"] = bias
        return self.s.op("act", lambda E: E.activation(out=out, in_=in_, func=func, **kw), reads, writes)

    def mm(self, psB, out, lhsT, rhs, start, stop, reads):
        return self.s.op("pe", lambda E: E.matmul(out, lhsT=lhsT, rhs=rhs, start=start, stop=stop), reads, [psB])

    def tr(self, psB, out, in_, reads):
        ident = self.cst[:, 0:128]
        return self.s.op("pe", lambda E: E.transpose(out, in_, ident[: in_.shape[0], : in_.shape[0]]), list(reads) + [self.cstB], [psB])

    def stt(self, eng, out, in0, scalar, in1, op0, op1, reads, writes):
        return self.s.op(eng, lambda E: E.scalar_tensor_tensor(out=out, in0=in0, scalar=scalar, in1=in1, op0=op0, op1=op1), reads, writes)

    def tt(self, eng, out, in0, in1, op, reads, writes):
        return self.s.op(eng, lambda E: E.tensor_tensor(out=out, in0=in0, in1=in1, op=op), reads, writes)

    def ts(self, eng, out, in0, s1, op0, reads, writes, s2=None, op1=None):
        if op1 is None:
            return self.s.op(eng, lambda E: E.tensor_scalar(out=out, in0=in0, scalar1=s1, scalar2=None, op0=op0), reads, writes)
        return self.s.op(eng, lambda E: E.tensor_scalar(out=out, in0=in0, scalar1=s1, scalar2=s2, op0=op0, op1=op1), reads, writes)

    def cp(self, eng, out, in_, reads, writes):
        if eng == "act":
            return self.s.op("act", lambda E: E.copy(out=out, in_=in_), reads, writes)
        return self.s.op(eng, lambda E: E.tensor_copy(out=out, in_=in_), reads, writes)

    def recip(self, out, in_, reads, writes):
        return self.s.op("dve", lambda E: E.reciprocal(out=out, in_=in_), reads, writes)

    def rsqrt_from(self, ps_ap, psB, mul, add, P, N):
        t2, t2B = self.t2[0]; self.t2.reverse()
        bias_ap = self.epsap(add, P)
        self.act(t2[:P, :N], ps_ap, AF.Ln, [psB, self.epstB], [t2B], scale=mul, bias=bias_ap)
        self.act(t2[:P, :N], t2[:P, :N], AF.Exp, [t2B], [t2B], scale=-0.5)
        return t2, t2B

    def epsap(self, val, P):
        key = float(val)
        if not hasattr(self, "_epsmap"):
            self._epsmap = {}
            self.epst, self.epstB = self.sb("epst", [128, 16])
        if key not in self._epsmap:
            j = len(self._epsmap)
            self.s.op("pool", lambda E: E.memset(self.epst[:, j:j + 1], key), [], [self.epstB])
            self._epsmap[key] = j
        j = self._epsmap[key]
        return self.epst[:P, j:j + 1]

    def prologue(self):
        s = self.s
        s.dma("sp", self.cst[:], self.consts, [], [self.cstB])
        s.dma("sp", self.gn[:], self.gains, [], [self.gnB])
        s.dma("sp", self.hg[:], self.hgain, [], [self.hgB])
        s.op("pool", lambda E: E.memset(self.ones[:], 1.0), [], [self.onesB])

    def nld(self):
        p = self.ld[self.ldc % 4]; self.ldc += 1
        return p

    def load_tile(self, src, srcB, t0, Tt, KCn, gain_idx=None, src_bf16=False):
        a16 = self.arena[:].bitcast(BF16)
        tile = a16[:, 0:KCn * Tt].rearrange("p (k t) -> p k t", k=KCn)
        QS = ("sp", "act", "pool")
        self._fence_release(self.apB, [self.arenaB])
        self.tile_bufs = [self.arenaB]
        if src_bf16:
            step = max(1, (KCn + 5) // 6)
            for q in QS:
                self.s._sync(q, [], [self.arenaB])
            used = []
            for qi, k0 in enumerate(range(0, KCn, step)):
                k1 = min(KCn, k0 + step)
                self.s.dma(QS[qi % 3], tile[:, k0:k1, :], src[k0:k1, :, t0:t0 + Tt].rearrange("k p t -> p k t"), [srcB], [self.apB[qi]])
                used.append(self.apB[qi])
            self.tile_bufs = used
            return tile
        if gain_idx is None:
            for kc in range(KCn):
                l, lB = self.nld()
                self.s.dma("sp", l[:, :Tt], src[kc, :, t0:t0 + Tt], [srcB], [lB])
                self.cp("act" if kc % 2 else "pool", tile[:, kc, :], l[:, :Tt], [lB], [self.arenaB])
            return tile
        accs = [self.t1[0], self.t1[1]]
        for kc in range(KCn):
            l, lB = self.nld()
            self.s.dma("sp", l[:, :Tt], src[kc, :, t0:t0 + Tt], [srcB], [lB])
            acc, accB = accs[kc % 2]
            if kc < 2:
                self.act(acc[:, :Tt], l[:, :Tt], AF.Square, [lB], [accB])
            else:
                sq, sqB = self.sq[kc % 2]
                self.act(sq[:, :Tt], l[:, :Tt], AF.Square, [lB], [sqB])
                self.tt("pool" if kc % 2 == 0 else "dve", acc[:, :Tt], acc[:, :Tt], sq[:, :Tt], ALU.add, [accB, sqB], [accB])
        nacc = min(2, KCn)
        rstd, rstdB = self.t2[0]; self.t2.reverse()
        for s0 in range(0, Tt, 512):
            wd = min(512, Tt - s0)
            psq, psqB = self.psX()
            for ai in range(nacc):
                acc, accB = accs[ai]
                self.mm(psqB, psq[:, :wd], self.ones[:], acc[:, s0:s0 + wd], ai == 0, ai == nacc - 1, [accB, self.onesB])
            bias_ap = self.epsap(EPS, 128)
            self.act(rstd[:, s0:s0 + wd], psq[:, :wd], AF.Ln, [psqB, self.epstB], [rstdB], scale=1.0 / (KCn * 128), bias=bias_ap)
            self.act(rstd[:, s0:s0 + wd], rstd[:, s0:s0 + wd], AF.Exp, [rstdB], [rstdB], scale=-0.5)
        for kc in range(KCn):
            l, lB = self.nld()
            self.s.dma(QS[kc % 3], l[:, :Tt], src[kc, :, t0:t0 + Tt], [srcB], [lB])
            self.stt("dve", tile[:, kc, :], l[:, :Tt], self.gn[:, gain_idx, kc:kc + 1], rstd[:, :Tt], ALU.mult, ALU.mult,
                     [lB, self.gnB, rstdB], [self.arenaB])
        return tile

    def gemm(self, rhs_fn, rhsB, KCtot, nsub, w_dram, CC, m_last, epi, pre=None, cc_list=None, lowp=True, sw=512, epi_a=None):
        rhsL = list(rhsB) if isinstance(rhsB, (list, tuple)) else [rhsB]
        KP = self.KP
        pieces = [(k0, min(k0 + KP, KCtot)) for k0 in range(0, KCtot, KP)]
        ccs = list(range(CC)) if cc_list is None else cc_list
        blocks = [(cc, k0, k1) for cc in ccs for (k0, k1) in pieces]
        PF = (self.NW16 - 1) if lowp else 1
        loaded = {}

        def load(i):
            cc, k0, k1 = blocks[i]
            if lowp:
                w16t, w16B = self.wb16[self.w16ctr % self.NW16]; self.w16ctr += 1
                wv = w16t[:, 0:(k1 - k0) * 128].rearrange("p (k m) -> p k m", m=128)
                self.s.dma("pool", wv, w_dram[cc, :, k0:k1, :], [], [w16B])
                loaded[i] = (wv, w16B)
            else:
                wbt, wbB = self.wb[self.wctr % 2]; self.wctr += 1
                wv = wbt[:, 0:(k1 - k0) * 128].rearrange("p (k m) -> p k m", m=128)
                self.s.dma("sp", wv, w_dram[cc, :, k0:k1, :], [], [wbB])
                loaded[i] = (wv, wbB)

        for i in range(min(PF, len(blocks))):
            load(i)
        pss = None
        pend_epi = None
        for i, (cc, k0, k1) in enumerate(blocks):
            if i + PF < len(blocks):
                load(i + PF)
            if k0 == 0:
                pss = [self.psG() for _ in range(nsub)]
                if pre is not None:
                    for sub in range(nsub):
                        pre(cc, sub)
            M = m_last if cc == CC - 1 else 128
            wv, wbB = loaded.pop(i)
            for kc in range(k0, k1):
                for sub in range(nsub):
                    ps, psB = pss[sub]
                    self.mm(psB, ps[:M, :sw], wv[:, kc - k0, :M], rhs_fn(kc, sub), kc == 0, kc == KCtot - 1, [wbB] + rhsL)
            if k0 == 0 and pend_epi is not None:
                pend_epi(); pend_epi = None
            if k1 == KCtot:
                if epi_a is not None:
                    for sub in range(nsub):
                        ps, psB = pss[sub]
                        epi_a(cc, ps, psB, M, sub)
                cur = (cc, list(pss), M)

                def run_epi(cur=cur):
                    cc_, pss_, M_ = cur
                    for sub in range(nsub):
                        ps, psB = pss_[sub]
                        epi(cc_, ps, psB, M_, sub)
                pend_epi = run_epi
        if pend_epi is not None:
            pend_epi()

    def store_chunk(self, dst, dstB, ci, t0, Tt, src_ap, srcB, P=128):
        self.s.dma("sp", dst[ci, 0:P, t0:t0 + Tt], src_ap, [srcB], [dstB])

    def epi_store(self, dst, dstB, t0, Tt, ci_fn=lambda cc: cc, scale=None, eng="act"):
        def epi(cc, ps, psB, M):
            st, stB = self.nstg()
            if scale is not None:
                self.act(st[:M, :Tt], ps[:M, :Tt], AF.Copy, [psB], [stB], scale=scale)
            elif eng == "act":
                self.cp("act", st[:M, :Tt], ps[:M, :Tt], [psB], [stB])
            else:
                self.cp("dve", st[:M, :Tt], ps[:M, :Tt], [psB], [stB])
            self.store_chunk(dst, dstB, ci_fn(cc), t0, Tt, st[:M, :Tt], stB, P=M)
        return epi

    def epi_residual(self, hin, hinB, hout, houtB, t0):
        pend = {}

        def pre(cc, sub):
            r, rB = self.nres()
            ts0 = t0 + sub * 512
            self.s.dma("sp", r[:, :512], hin[cc, :, ts0:ts0 + 512], [hinB], [rB])
            pend[(cc, sub)] = (r, rB)

        def epi(cc, ps, psB, M, sub):
            r, rB = pend.pop((cc, sub))
            st, stB = self.nstg()
            ts0 = t0 + sub * 512
            self.tt("dve", st[:, :512], ps[:, :512], r[:, :512], ALU.add, [psB, rB], [stB])
            self.store_chunk(hout, houtB, cc, ts0, 512, st[:, :512], stB)
        return pre, epi

    def qk_a(self, ps, psB, Tt):
        if not hasattr(self, "_sqslot"):
            self._sqslot = 0
        i = self._sqslot % 4; self._sqslot += 1
        sqt, sqB = self.sq[i // 2]
        sq = sqt[:, (i % 2) * 512:(i % 2) * 512 + Tt]
        self.act(sq, ps[:, :Tt], AF.Square, [psB], [sqB])
        return sq, sqB

    def qk_b(self, ps, psB, Tt, sq, sqB, gain_ap, gainB, fold, out_ap, outB):
        p2, p2B = self.psX()
        self.mm(p2B, p2[:, :Tt], self.ones[:], sq, True, True, [sqB, self.onesB])
        rstd, rstdB = self.rsqrt_from(p2[:, :Tt], p2B, 1.0 / (128.0 * fold * fold), EPS / (fold * fold), 128, Tt)
        self.stt("dve", out_ap, ps[:, :Tt], gain_ap, rstd[:, :Tt], ALU.mult, ALU.mult, [psB, gainB, rstdB], [outB])

    def qknorm(self, ps, psB, Tt, gain_ap, gainB, fold, out_ap, outB):
        sq, sqB = self.qk_a(ps, psB, Tt)
        self.qk_b(ps, psB, Tt, sq, sqB, gain_ap, gainB, fold, out_ap, outB)

    def build(self):
        c = self.cfg
        self.prologue()
        stages = [
            ("l0_inproj", lambda: self.l0_inproj(self.xT, self.xTB)),
            ("l0_attn", self.l0_attn),
            ("l0_pool", self.l0_pool),
            ("l0_out", lambda: self.outproj(self.ab_wout, self.xT, self.xTB, self.hA, self.hAB)),
            ("l0_x", lambda: self.xattn(0, self.hA, self.hAB, self.hB, self.hBB)),
            ("l0_f", lambda: self.ffn(0, self.hB, self.hBB, self.hA, self.hAB)),
            ("l1_inproj", lambda: self.l1_inproj(self.hA, self.hAB)),
            ("l1_gla", self.l1_gla),
            ("l1_out", lambda: self.outproj(self.c_wout, self.hA, self.hAB, self.hB, self.hBB)),
            ("l1_x", lambda: self.xattn(1, self.hB, self.hBB, self.hA, self.hAB)),
            ("l1_f", lambda: self.ffn(1, self.hA, self.hAB, self.hB, self.hBB)),
        ]
        final = (self.hB, self.hBB)
        dbg = {"l0_inproj": (self.zq, self.zqB), "l0_attn": (self.ab, self.abB), "l0_pool": (self.ab, self.abB),
               "l0_out": (self.hA, self.hAB), "l0_x": (self.hB, self.hBB), "l0_f": (self.hA, self.hAB),
               "l1_inproj": (self.zk, self.zkB), "l1_gla": (self.ab, self.abB), "l1_out": (self.hB, self.hBB),
               "l1_x": (self.hA, self.hAB), "l1_f": (self.hB, self.hBB)}
        for name, fn in stages:
            fn()
            if self.stop_after == name:
                final = dbg[name]
                break
        self.copy_out(*final)
        return self.nc

    def copy_out(self, src, srcB):
        c = self.cfg
        n = min(src.shape[0], c.KC)
        step = max(1, n // 8)
        for k0 in range(0, n, step):
            k1 = min(n, k0 + step)
            self.s.dma("sp", self.outT[k0:k1], src[k0:k1], [srcB], [self.outTB])
        for e in ("sp",):
            self.s.wait_all(e, [self.outTB])
        for q, ring in self.s.dq.items():
            for ent in ring["s"]:
                self.s._wait("sp", ent[0], ent[1])

    def l0_inproj(self, hin, hinB):
        c = self.cfg
        AH = c.AH
        nq = 3 * AH
        sqp = {}
        for ti in range(c.NGT):
            t0 = ti * c.GT
            tile = self.load_tile(hin, hinB, t0, c.GT, c.KC, gain_idx=0)

            def epi(cc, ps, psB, M, sub, t0=t0):
                ts0 = t0 + sub * 512; Tt = 512
                st, stB = self.nstg()
                if cc < nq:
                    sq, sqB = sqp.pop((cc, sub))
                    self.qk_b(ps, psB, Tt, sq, sqB, self.hg[:, 0:1], self.hgB, 128.0 ** -0.5, st[:, :Tt], stB)
                    self.store_chunk(self.zq, self.zqB, cc, ts0, Tt, st[:, :Tt], stB)
                elif cc < 2 * nq:
                    sq, sqB = sqp.pop((cc, sub))
                    self.qk_b(ps, psB, Tt, sq, sqB, self.hg[:, 1:2], self.hgB, 1.0, st[:, :Tt], stB)
                    self.store_chunk(self.zk, self.zkB, cc - nq, ts0, Tt, st[:, :Tt], stB)
                elif cc < 2 * nq + AH:
                    self.cp("act", st[:, :Tt], ps[:, :Tt], [psB], [stB])
                    self.store_chunk(self.zv, self.zvB, cc - 2 * nq, ts0, Tt, st[:, :Tt], stB)
                else:
                    self.cp("dve", st[:, :Tt], ps[:, :Tt], [psB], [stB])
                    self.store_chunk(self.zu, self.zuB, cc - 2 * nq - AH, ts0, Tt, st[:, :Tt], stB)

            def epi_a(cc, ps, psB, M, sub):
                if cc < 2 * nq:
                    sqp[(cc, sub)] = self.qk_a(ps, psB, 512)

            self.gemm(lambda kc, sub: tile[:, kc, sub * 512:(sub + 1) * 512], self.arenaB, c.KC, c.GT // 512, self.ab_win,
                      c.EVEN_IN // 128, 128, epi, epi_a=epi_a)

    def l0_attn(self):
        c = self.cfg; T = c.T; AH = c.AH
        ar = self.arena
        assert ar.shape[1] >= 4 * T
        qb = ar[:, 0:T]; kb = ar[:, T:2 * T]; vT = ar[:, 2 * T:3 * T]; accO = ar[:, 3 * T:4 * T]
        gp = self.gp
        assert gp.shape[1] >= T + 32 * 128
        accD = gp[:, 0:T]
        vtok = gp[:, T:T + 32 * 128].rearrange("p (b e) -> p b e", e=128)
        qB, kB, vB, aOB, aDB, vtB = Buf("aq"), Buf("ak"), Buf("av"), Buf("aO"), Buf("aD"), Buf("avt")
        fenceR = [self.arenaB, self.gpB]
        relcur = self.cst[:, 128:640]; relprev = self.cst[:, 640:1152]; relprevF = self.cst[:, 1152:1664]
        n_sl = 3 * AH
        slopes = [2.0 ** (-8.0 * (i + 1) / n_sl) for i in range(n_sl)]
        branches = [(128, 1), (512, 4), (2048, 16)]
        self.fence(fenceR)
        for hd in range(AH):
            self.s.dma("sp", vT, self.zv[hd, :, :], [self.zvB], [vB])
            for g, (win, d) in enumerate(branches):
                cneg = -slopes[g * AH + hd] * d
                self.s.dma("sp", qb, self.zq[g * AH + hd, :, :], [self.zqB], [qB])
                self.s.dma("sp", kb, self.zk[g * AH + hd, :, :], [self.zkB], [kB])
                NB = T // d // 128
                G = min(4, NB)
                nblk = d * NB
                for b0 in range(0, nblk, 4):
                    pt, ptB = self.psX()
                    for j in range(4):
                        blk = b0 + j; r = blk // NB; n = blk % NB
                        st0 = r + d * 128 * n
                        self.tr(ptB, pt[:, j * 128:(j + 1) * 128], vT[:, st0:st0 + d * 127 + 1:d], [vB])
                    self.cp("act", vtok[:, b0:b0 + 4, :], pt[:, :].rearrange("p (b e) -> p b e", e=128), [ptB], [vtB])
                for r in range(d):
                    for n0 in range(0, NB, G):
                        W = G * 128
                        sc, scB = self.psX(); sp_, spB = self.psX()
                        for j in range(G):
                            n = n0 + j
                            st0 = r + d * 128 * n
                            qs = qb[:, st0:st0 + d * 127 + 1:d]
                            kcur = kb[:, st0:st0 + d * 127 + 1:d]
                            self.mm(scB, sc[:, j * 128:(j + 1) * 128], kcur, qs, True, True, [kB, qB])
                            if n > 0:
                                stp = r + d * 128 * (n - 1)
                                kprev = kb[:, stp:stp + d * 127 + 1:d]
                            else:
                                kprev = kcur
                            self.mm(spB, sp_[:, j * 128:(j + 1) * 128], kprev, qs, True, True, [kB, qB])
                        pc, pcB = self.nstg(); pp, ppB = self.nstg()
                        self.stt("dve", pc[:, :W], relcur[:, :W], cneg, sc[:, :W], ALU.mult, ALU.add, [self.cstB, scB], [pcB])
                        rp = relprevF if n0 == 0 else relprev
                        self.stt("dve", pp[:, :W], rp[:, :W], cneg, sp_[:, :W], ALU.mult, ALU.add, [self.cstB, spB], [ppB])
                        self.act(pc[:, :W], pc[:, :W], AF.Exp, [pcB], [pcB])
                        self.act(pp[:, :W], pp[:, :W], AF.Exp, [ppB], [ppB])
                        po, poB = self.psG(); pd, pdB = self.psG()
                        for j in range(G):
                            n = n0 + j
                            blk = r * NB + n
                            self.mm(poB, po[:, j * 128:(j + 1) * 128], vtok[:, blk, :], pc[:, j * 128:(j + 1) * 128], True, n == 0, [vtB, pcB])
                            if n > 0:
                                self.mm(poB, po[:, j * 128:(j + 1) * 128], vtok[:, blk - 1, :], pp[:, j * 128:(j + 1) * 128], False, True, [vtB, ppB])
                            self.mm(pdB, pd[:, j * 128:(j + 1) * 128], self.ones[:], pc[:, j * 128:(j + 1) * 128], True, False, [self.onesB, pcB])
                            self.mm(pdB, pd[:, j * 128:(j + 1) * 128], self.ones[:], pp[:, j * 128:(j + 1) * 128], False, True, [self.onesB, ppB])
                        st0 = r + d * 128 * n0
                        osl = accO[:, st0:st0 + d * (W - 1) + 1:d]; dsl = accD[:, st0:st0 + d * (W - 1) + 1:d]
                        if g == 0:
                            self.cp("act", osl, po[:, :W], [poB], [aOB])
                            self.cp("dve", dsl, pd[:, :W], [pdB], [aDB])
                        else:
                            self.tt("dve", osl, po[:, :W], osl, ALU.add, [poB, aOB], [aOB])
                            self.tt("dve", dsl, pd[:, :W], dsl, ALU.add, [pdB, aDB], [aDB])
            self.recip(accD, accD, [aDB], [aDB])
            o16 = qb.bitcast(BF16)[:, 0:T]
            self.tt("dve", o16, accO, accD, ALU.mult, [aOB, aDB], [qB])
            self.s.dma("sp", self.ab[hd, :, :], o16, [qB], [self.abB])
        self._fence_release([qB, kB, vB, aOB, aDB, vtB], [self.arenaB, self.gpB])

    def fence(self, wholes):
        self._fence_release(self.apB, [self.arenaB])
        for e in ("act", "dve", "pool", "sp", "pe"):
            self.s._sync(e, [], wholes)

    def _fence_release(self, subs, wholes):
        for wB in wholes:
            for b in subs:
                if b.w is not None:
                    si, v = b.w
                    wB.r[si] = max(wB.r.get(si, 0), v)
                for si, v in b.r.items():
                    wB.r[si] = max(wB.r.get(si, 0), v)

    def l0_pool(self):
        c = self.cfg; T = c.T
        PADL = 16
        L = min(T, 2048)
        NH = T // L
        W = PADL + L
        ar = self.arena; gp = self.gp
        assert gp.shape[1] >= 3 * W + L
        ub = [gp[:, i * W:(i + 1) * W] for i in range(3)]
        ubB = [Buf("pu%d" % i) for i in range(3)]
        invt = gp[:, 3 * W:3 * W + L]; invB = Buf("invt")
        pooled = ar[:, 0:c.BGC * T].rearrange("p (j t) -> p j t", j=c.BGC)
        pooledB = Buf("pooled")
        psc, pscB = self.sb("psc", [128, c.BW // 128])
        self.s.dma("sp", psc[:], self.pool_sc, [], [pscB])
        self.fence([self.arenaB, self.gpB])
        for gi, win in enumerate((2, 4, 8, 16)):
            for hh in range(NH):
                t0 = hh * L
                self.s.dma("sp", invt, self.invc[gi:gi + 1, t0:t0 + L].partition_broadcast(128), [], [invB])
                for j in range(c.BGC):
                    ci = gi * c.BGC + j
                    if hh == 0:
                        self.s.op("pool", lambda E: E.memset(ub[0][:, 0:PADL], 0.0), [], [ubB[0]])
                        self.s.dma("sp", ub[0][:, PADL:], self.zu[ci, :, t0:t0 + L], [self.zuB], [ubB[0]])
                    else:
                        self.s.dma("sp", ub[0][:, :], self.zu[ci, :, t0 - PADL:t0 + L], [self.zuB], [ubB[0]])
                    src_i = 0; span = 1; dst_i = 1
                    while span < win:
                        self.tt("dve", ub[dst_i][:, span:W], ub[src_i][:, span:W], ub[src_i][:, 0:W - span], ALU.add,
                                [ubB[src_i]], [ubB[dst_i]])
                        src_i = dst_i
                        dst_i = 2 if dst_i == 1 else 1
                        span *= 2
                    self.tt("dve", ub[src_i][:, PADL:], ub[src_i][:, PADL:], invt, ALU.mult, [ubB[src_i], invB], [ubB[src_i]])
                    self.tt("dve", pooled[:, j, t0:t0 + L], ub[src_i][:, PADL:], ub[0][:, PADL:], ALU.subtract, [ubB[src_i], ubB[0]],
                            [pooledB])
            for ti in range(c.NT):
                t0 = ti * c.TT; Tt = c.TT

                def epi(cc, ps, psB, M, sub, t0=t0, Tt=Tt):
                    st, stB = self.nstg()
                    s16 = st[:].bitcast(BF16)
                    self.ts("dve", s16[:, :Tt], ps[:, :Tt], psc[:, cc:cc + 1], ALU.mult, [psB, pscB], [stB])
                    self.store_chunk(self.ab, self.abB, c.AH + cc, t0, Tt, s16[:, :Tt], stB)

                self.gemm(lambda kc, sub, t0=t0, Tt=Tt: pooled[:, kc, t0:t0 + Tt], pooledB, c.BGC, 1, self.pool_w, 4 * c.BGC, 128, epi,
                          cc_list=[gi * c.BGC + oc for oc in range(c.BGC)], lowp=False)
        self._fence_release(ubB + [pooledB, invB], [self.arenaB, self.gpB])

    def outproj(self, w, hin, hinB, hout, houtB):
        c = self.cfg
        for ti in range(c.NGT):
            t0 = ti * c.GT
            tile = self.load_tile(self.ab, self.abB, t0, c.GT, c.KC, src_bf16=True)
            pre, epi = self.epi_residual(hin, hinB, hout, houtB, t0)
            self.gemm(lambda kc, sub: tile[:, kc, sub * 512:(sub + 1) * 512], self.tile_bufs, c.KC, c.GT // 512, w, c.KC, 128, epi, pre=pre)

    def xattn(self, l, hin, hinB, hout, houtB):
        c = self.cfg; ML = c.ML
        gp = self.gp
        o0 = 0
        kx = gp[:, o0:o0 + 4 * ML].rearrange("p (h m) -> p h m", h=4); o0 += 4 * ML
        vx = gp[:, o0:o0 + 2 * 512].rearrange("p (b n) -> p b n", b=2); o0 += 1024
        qx = gp[:, o0:o0 + 4 * 512].rearrange("p (h t) -> p h t", h=4); o0 += 2048
        ox16 = gp[:, o0:o0 + 1024].bitcast(BF16).rearrange("p (h t) -> p h t", h=4); o0 += 1024
        pb = gp[:, o0:o0 + 2 * 512].rearrange("p (b t) -> p b t", b=2); o0 += 1024
        assert gp.shape[1] >= o0
        kxB, vxB, qxB, oxB, pbB = Buf("kx"), Buf("vx"), Buf("qx"), Buf("ox"), Buf("pb")
        fence = [self.gpB]
        mt = self.load_tile(self.memT, Buf("memT"), 0, ML, c.KC, gain_idx=4 + l)
        self.fence(fence)

        def epik(cc, ps, psB, M, sub):
            self.qknorm(ps, psB, ML, self.hg[:, 4 + l:5 + l], self.hgB, 1.0, kx[:, cc, :], kxB)
        self.gemm(lambda kc, sub: mt[:, kc, :], self.arenaB, c.KC, 1, self.x_wk[l], 4, 128, epik, sw=ML)
        for mb in range(ML // 128):
            ps, psB = self.psG()
            for k0 in range(0, c.KC, 4):
                k1 = min(c.KC, k0 + 4)
                w16t, w16B = self.wb16[self.w16ctr % self.NW16]; self.w16ctr += 1
                w16 = w16t[:, 0:(k1 - k0) * 512].rearrange("p (k n) -> p k n", n=512)
                self.s.dma("pool", w16, self.x_wv[l, :, k0:k1, :], [], [w16B])
                for kc in range(k0, k1):
                    self.mm(psB, ps[:, :], mt[:, kc, mb * 128:(mb + 1) * 128], w16[:, kc - k0, :], kc == 0, kc == c.KC - 1, [w16B, self.arenaB])
            self.cp("act", vx[:, mb, :], ps[:, :], [psB], [vxB])
        for ti in range(c.NT):
            t0 = ti * c.TT; Tt = c.TT
            tile = self.load_tile(hin, hinB, t0, Tt, c.KC, gain_idx=2 + l)

            def epiq(cc, ps, psB, M, sub):
                self.qknorm(ps, psB, Tt, self.hg[:, 2 + l:3 + l], self.hgB, 128.0 ** -0.5, qx[:, cc, :], qxB)
            self.gemm(lambda kc, sub: tile[:, kc, :], self.arenaB, c.KC, 1, self.x_wq[l], 4, 128, epiq)
            for h in range(4):
                pd, pdB = self.psX(); po, poB = self.psX()
                for mb in range(2):
                    ps, psB = self.psX()
                    self.mm(psB, ps[:, :Tt], kx[:, h, mb * 128:(mb + 1) * 128], qx[:, h, :], True, True, [kxB, qxB])
                    self.act(pb[:, mb, :], ps[:, :Tt], AF.Exp, [psB], [pbB])
                for mb in range(2):
                    self.mm(pdB, pd[:, :Tt], self.ones[:], pb[:, mb, :], mb == 0, mb == 1, [self.onesB, pbB])
                for mb in range(2):
                    self.mm(poB, po[:, :Tt], vx[:, mb, h * 128:(h + 1) * 128], pb[:, mb, :], mb == 0, mb == 1, [vxB, pbB])
                rc, rcB = self.nstg()
                self.recip(rc[:, :Tt], pd[:, :Tt], [pdB], [rcB])
                self.tt("dve", ox16[:, h, :], po[:, :Tt], rc[:, :Tt], ALU.mult, [poB, rcB], [oxB])
            pre, epi = self.epi_residual(hin, hinB, hout, houtB, t0)
            self.gemm(lambda kc, sub: ox16[:, kc, :], oxB, 4, 1, self.x_wo[l], c.KC, 128, epi, pre=pre)
        self._fence_release([kxB, vxB, qxB, oxB, pbB], [self.gpB])

    def ffn(self, l, hin, hinB, hout, houtB):
        c = self.cfg; FC = c.FC
        gp = self.gp
        o0 = 0
        cw = gp[:, o0:o0 + 2 * FC * 4].rearrange("p (c f) -> p c f", f=4); o0 += 2 * FC * 4
        tails = gp[:, o0:o0 + 2 * FC * 2].rearrange("p (c f) -> p c f", f=2); o0 += 2 * FC * 2
        ub = []
        for i in range(2):
            ub.append(gp[:, o0:o0 + 514]); o0 += 516
        cg = []
        for i in range(2):
            cg.append(gp[:, o0:o0 + 512]); o0 += 512
        cv = gp[:, o0:o0 + 512]; o0 += 512
        assert gp.shape[1] >= o0
        cwB, tlB, cvB = Buf("cw"), Buf("tails"), Buf("cv")
        cgB = [Buf("cg0"), Buf("cg1")]
        ubB = [Buf("fub0"), Buf("fub1")]
        fence = [self.gpB]
        self.fence(fence)
        self.s.dma("sp", cw, self.f_cw[:, l, :, :], [], [cwB])
        self.s.op("pool", lambda E: E.memset(tails, 0.0), [], [tlB])
        ucnt = [0]
        for ti in range(c.NGT):
            t0 = ti * c.GT
            tile = self.load_tile(hin, hinB, t0, c.GT, c.KC, gain_idx=6 + l)

            def epi(cc, ps, psB, M, sub, t0=t0):
                Tt = 512; ts0 = t0 + sub * 512
                u = ub[ucnt[0] % 2]; uB = ubB[ucnt[0] % 2]; ucnt[0] += 1
                self.cp("act", u[:, 2:2 + Tt], ps[:, :Tt], [psB], [uB])
                self.cp("dve", u[:, 0:2], tails[:, cc, :], [tlB], [uB])
                self.cp("dve", tails[:, cc, :], u[:, Tt:Tt + 2], [uB], [tlB])
                isg = (cc % 2 == 0)
                dst, dstB = (cg[sub], cgB[sub]) if isg else (cv, cvB)
                self.act(dst[:, :Tt], u[:, 2:2 + Tt], AF.Identity, [uB, cwB], [dstB], scale=cw[:, cc, 2:3], bias=cw[:, cc, 3:4])
                self.stt("dve", dst[:, :Tt], u[:, 1:1 + Tt], cw[:, cc, 1:2], dst[:, :Tt], ALU.mult, ALU.add, [uB, cwB, dstB], [dstB])
                self.stt("dve", dst[:, :Tt], u[:, 0:Tt], cw[:, cc, 0:1], dst[:, :Tt], ALU.mult, ALU.add, [uB, cwB, dstB], [dstB])
                if not isg:
                    st, stB = self.nstg()
                    s16 = st[:].bitcast(BF16)
                    self.act(cg[sub][:, :Tt], cg[sub][:, :Tt], AF.Silu, [cgB[sub]], [cgB[sub]])
                    self.tt("dve", s16[:, :Tt], cg[sub][:, :Tt], cv[:, :Tt], ALU.mult, [cgB[sub], cvB], [stB])
                    self.store_chunk(self.ffa, self.ffaB, cc // 2, ts0, Tt, s16[:, :Tt], stB)

            self.gemm(lambda kc, sub: tile[:, kc, sub * 512:(sub + 1) * 512], self.arenaB, c.KC, c.GT // 512, self.f_wup[l], 2 * FC, 128, epi)
        self._fence_release([cwB, tlB, cvB] + cgB + ubB, [self.gpB])
        n3 = (FC + 2) // 3
        parts = [(k0, min(FC, k0 + n3)) for k0 in range(0, FC, n3)]
        temps = [(self.hT1, self.hT1B), (self.hT2, self.hT2B)]
        chain = [(hin, hinB)] + temps[:len(parts) - 1] + [(hout, houtB)]
        for pi, (k0, k1) in enumerate(parts):
            src_h, src_hB = chain[pi]; dst_h, dst_hB = chain[pi + 1]
            for ti in range(c.NGT):
                t0 = ti * c.GT
                tile = self.load_tile(self.ffa[k0:k1], self.ffaB, t0, c.GT, k1 - k0, src_bf16=True)
                pre, epi = self.epi_residual(src_h, src_hB, dst_h, dst_hB, t0)
                self.gemm(lambda kc, sub: tile[:, kc, sub * 512:(sub + 1) * 512], self.tile_bufs, k1 - k0, c.GT // 512,
                          self.f_wdn[l][:, :, k0:k1, :], c.KC, 128, epi, pre=pre)

    def l1_inproj(self, hin, hinB):
        c = self.cfg
        nq = c.CKW // 128; nv = c.CVW // 128
        CC = (c.ODD_IN + 127) // 128
        for ti in range(c.NGT):
            t0 = ti * c.GT
            tile = self.load_tile(hin, hinB, t0, c.GT, c.KC, gain_idx=1)

            def epi(cc, ps, psB, M, sub, t0=t0):
                Tt = 512; ts0 = t0 + sub * 512
                st, stB = self.nstg()
                if cc < nq:
                    self.act(st[:, :Tt], ps[:, :Tt], AF.Copy, [psB], [stB], scale=float(c.CDK) ** -0.5)
                    self.store_chunk(self.zq, self.zqB, cc, ts0, Tt, st[:, :Tt], stB)
                elif cc < 2 * nq:
                    self.cp("act", st[:, :Tt], ps[:, :Tt], [psB], [stB])
                    self.store_chunk(self.zk, self.zkB, cc - nq, ts0, Tt, st[:, :Tt], stB)
                elif cc < 2 * nq + nv:
                    self.cp("dve", st[:, :Tt], ps[:, :Tt], [psB], [stB])
                    self.store_chunk(self.zv, self.zvB, cc - 2 * nq, ts0, Tt, st[:, :Tt], stB)
                elif cc < 2 * nq + 2 * nv:
                    self.act(st[:, :Tt], ps[:, :Tt], AF.Silu, [psB], [stB])
                    self.store_chunk(self.zu, self.zuB, cc - 2 * nq - nv, ts0, Tt, st[:, :Tt], stB)
                else:
                    self.cp("act", st[:16, :Tt], ps[:16, :Tt], [psB], [stB])
                    self.store_chunk(self.zr, self.zrB, 0, ts0, Tt, st[:16, :Tt], stB, P=16)

            self.gemm(lambda kc, sub: tile[:, kc, sub * 512:(sub + 1) * 512], self.arenaB, c.KC, c.GT // 512, self.c_win, CC, 128, epi)

    def l1_gla(self):
        c = self.cfg; T = c.T; DKC = c.DKC; DVC = c.DVC; Tt = c.TT
        NCH = Tt // 64
        ar = self.arena
        o0 = 0

        def carve(n):
            nonlocal o0
            a = ar[:, o0:o0 + n]; o0 += n
            return a
        wa2 = carve(c.CKW)
        ba = carve(c.CKW // 128)
        on = carve(DVC)
        rT = carve(Tt)
        qt = carve(DKC * Tt).rearrange("p (k t) -> p k t", k=DKC)
        kt = carve(DKC * Tt).rearrange("p (k t) -> p k t", k=DKC)
        vt = carve(DVC * Tt).rearrange("p (k t) -> p k t", k=DVC)
        gt = carve(DVC * Tt).rearrange("p (k t) -> p k t", k=DVC)
        bc = carve(DKC * Tt).rearrange("p (k t) -> p k t", k=DKC)
        ebc = carve(DKC * Tt).rearrange("p (k t) -> p k t", k=DKC)
        dec = carve(DKC * NCH).rearrange("p (k n) -> p k n", k=DKC)
        state = carve(DKC * c.CDV).rearrange("p (k e) -> p k e", k=DKC)
        ot = carve(DVC * Tt).rearrange("p (k t) -> p k t", k=DVC)
        kst = carve(DKC * 64).rearrange("p (k t) -> p k t", k=DKC)
        attm = carve(64)
        vtok = carve(c.CDV)
        ksttok = carve(c.CDK)
        assert ar.shape[1] >= o0, (ar.shape, o0)
        names = ["wa2", "ba", "on", "rT", "qt", "kt", "vt", "gt", "bc", "ebc", "dec", "state", "ot", "kst", "attm", "vtok", "ksttok"]
        B = {n: Buf("g_" + n) for n in names}
        fence = [self.arenaB]
        scanm = self.cst[:, 1728:2240]
        mask01 = self.cst[0:64, 1664:1728]
        self.fence(fence)
        self.s.dma("sp", wa2[0:16, :], self.c_wa2, [], [B["wa2"]])
        self.s.dma("sp", ba, self.c_ba, [], [B["ba"]])
        self.s.dma("sp", on, self.c_on, [], [B["on"]])
        self.ts("dve", ba, ba, -1.0, ALU.mult, [B["ba"]], [B["ba"]])
        for hd in range(c.CH):
            for dc in range(DKC):
                self.s.op("pool", lambda E, dc=dc: E.memset(state[:, dc, :], 0.0), [], [B["state"]])
            for ti in range(c.NT):
                t0 = ti * Tt
                self.s.dma("sp", rT[0:16, :], self.zr[0, 0:16, t0:t0 + Tt], [self.zrB], [B["rT"]])
                self.s.dma("sp", qt, self.zq[hd * DKC:(hd + 1) * DKC, :, t0:t0 + Tt].rearrange("k p t -> p k t"), [self.zqB], [B["qt"]])
                self.s.dma("sp", kt, self.zk[hd * DKC:(hd + 1) * DKC, :, t0:t0 + Tt].rearrange("k p t -> p k t"), [self.zkB], [B["kt"]])
                self.s.dma("sp", vt, self.zv[hd * DVC:(hd + 1) * DVC, :, t0:t0 + Tt].rearrange("k p t -> p k t"), [self.zvB], [B["vt"]])
                self.s.dma("sp", gt, self.zu[hd * DVC:(hd + 1) * DVC, :, t0:t0 + Tt].rearrange("k p t -> p k t"), [self.zuB], [B["gt"]])
                for dc in range(DKC):
                    col = hd * DKC + dc
                    ps, psB = self.psX()
                    self.mm(psB, ps[:, :Tt], wa2[0:16, col * 128:(col + 1) * 128], rT[0:16, :], True, True, [B["wa2"], B["rT"]])
                    self.act(ebc[:, dc, :], ps[:, :Tt], AF.Exp, [psB, B["ba"]], [B["ebc"]], scale=-1.0, bias=ba[:, col:col + 1])
                    one_ap = self.epsap(1.0, 128)
                    self.act(ebc[:, dc, :], ebc[:, dc, :], AF.Ln, [B["ebc"], self.epstB], [B["ebc"]], bias=one_ap)
                    self.ts("dve", ebc[:, dc, :], ebc[:, dc, :], -1.0 / 16.0, ALU.mult, [B["ebc"]], [B["ebc"]])
                    self.s.op("dve", lambda E, dc=dc: E.tensor_tensor_scan(out=bc[:, dc, :], data0=scanm[:, :Tt], data1=ebc[:, dc, :], initial=0.0,
                                                                            op0=ALU.mult, op1=ALU.add), [self.cstB, B["ebc"]], [B["bc"]])
                    self.act(dec[:, dc, :], bc[:, dc, 63:Tt:64], AF.Exp, [B["bc"]], [B["dec"]])
                    self.act(ebc[:, dc, :], bc[:, dc, :], AF.Exp, [B["bc"]], [B["ebc"]])
                    self.tt("dve", qt[:, dc, :], qt[:, dc, :], ebc[:, dc, :], ALU.mult, [B["qt"], B["ebc"]], [B["qt"]])
                    self.act(ebc[:, dc, :], bc[:, dc, :], AF.Exp, [B["bc"]], [B["ebc"]], scale=-1.0)
                    self.tt("dve", kt[:, dc, :], kt[:, dc, :], ebc[:, dc, :], ALU.mult, [B["kt"], B["ebc"]], [B["kt"]])
                for ch in range(NCH):
                    cs = slice(ch * 64, (ch + 1) * 64)
                    pa, paB = self.psX()
                    for dc in range(DKC):
                        self.mm(paB, pa[0:64, 0:64], kt[:, dc, cs], qt[:, dc, cs], dc == 0, dc == DKC - 1, [B["kt"], B["qt"]])
                    self.tt("dve", attm[0:64, :], pa[0:64, 0:64], mask01, ALU.mult, [paB, self.cstB], [B["attm"]])
                    pv, pvB = self.psX()
                    for ec in range(DVC):
                        self.tr(pvB, pv[0:64, ec * 128:(ec + 1) * 128], vt[:, ec, cs], [B["vt"]])
                    self.cp("act", vtok[0:64, :], pv[0:64, 0:c.CDV], [pvB], [B["vtok"]])
                    for dc in range(DKC):
                        self.ts("dve", kst[:, dc, :], kt[:, dc, cs], dec[:, dc, ch:ch + 1], ALU.mult, [B["kt"], B["dec"]], [B["kst"]])
                    pk, pkB = self.psX()
                    for dc in range(DKC):
                        self.tr(pkB, pk[0:64, dc * 128:(dc + 1) * 128], kst[:, dc, :], [B["kst"]])
                    self.cp("act", ksttok[0:64, :], pk[0:64, 0:c.CDK], [pkB], [B["ksttok"]])
                    po, poB = self.psG()
                    for ec in range(DVC):
                        osl = po[:, ec * 64:(ec + 1) * 64]
                        self.mm(poB, osl, vtok[0:64, ec * 128:(ec + 1) * 128], attm[0:64, :], True, False, [B["vtok"], B["attm"]])
                        for dc in range(DKC):
                            self.mm(poB, osl, state[:, dc, ec * 128:(ec + 1) * 128], qt[:, dc, cs], False, dc == DKC - 1, [B["state"], B["qt"]])
                    self.cp("act", ot[:, :, cs], po[:, 0:DVC * 64].rearrange("p (k t) -> p k t", k=DVC), [poB], [B["ot"]])
                    for dc in range(DKC):
                        pkv, pkvB = self.psG()
                        self.mm(pkvB, pkv[:, 0:c.CDV], ksttok[0:64, dc * 128:(dc + 1) * 128], vtok[0:64, :], True, True, [B["ksttok"], B["vtok"]])
                        self.stt("dve", state[:, dc, :], state[:, dc, :], dec[:, dc, ch:ch + 1], pkv[:, 0:c.CDV], ALU.mult, ALU.add,
                                 [B["state"], B["dec"], pkvB], [B["state"]])
                psq, psqB = self.psX()
                for ec in range(DVC):
                    sq, sqB = self.sq[ec % 2]
                    self.act(sq[:, :Tt], ot[:, ec, :], AF.Square, [B["ot"]], [sqB])
                    self.mm(psqB, psq[:, :Tt], self.ones[:], sq[:, :Tt], ec == 0, ec == DVC - 1, [sqB, self.onesB])
                rstd, rstdB = self.rsqrt_from(psq[:, :Tt], psqB, 1.0 / c.CDV, EPS, 128, Tt)
                for ec in range(DVC):
                    st, stB = self.nstg()
                    st2, st2B = self.nstg()
                    s16 = st2[:].bitcast(BF16)
                    self.stt("dve", st[:, :Tt], ot[:, ec, :], on[:, ec:ec + 1], rstd[:, :Tt], ALU.mult, ALU.mult, [B["ot"], B["on"], rstdB], [stB])
                    self.tt("dve", s16[:, :Tt], st[:, :Tt], gt[:, ec, :], ALU.mult, [stB, B["gt"]], [st2B])
                    self.store_chunk(self.ab, self.abB, hd * DVC + ec, t0, Tt, s16[:, :Tt], st2B)
        self._fence_release(list(B.values()), [self.arenaB])


def _wblocks(W):
    K, N = W.shape
    CC = (N + 127) // 128
    if CC * 128 != N:
        W = np.concatenate([W, np.zeros((K, CC * 128 - N), W.dtype)], axis=1)
    return np.ascontiguousarray(W.reshape(K // 128, 128, CC, 128).transpose(2, 1, 0, 3))


def _fm(v):
    return np.ascontiguousarray(v.reshape(-1, 128).T)


def _consts():
    cst = np.zeros((128, 2240), np.float32)
    cst[:, 0:128] = np.eye(128, dtype=np.float32)
    i = np.arange(128)[:, None]; j = np.arange(128)[None, :]
    cur = np.where(j >= i, (j - i).astype(np.float32), BIG).astype(np.float32)
    prev = np.where(i >= j, (j + 128 - i).astype(np.float32), BIG).astype(np.float32)
    for r in range(4):
        cst[:, 128 + r * 128:128 + (r + 1) * 128] = cur
        cst[:, 640 + r * 128:640 + (r + 1) * 128] = prev
        cst[:, 1152 + r * 128:1152 + (r + 1) * 128] = prev if r > 0 else BIG
    jj = np.arange(64)[:, None]; ii = np.arange(64)[None, :]
    cst[0:64, 1664:1728] = (ii >= jj).astype(np.float32)
    m = np.ones(512, np.float32); m[::64] = 0.0
    cst[:, 1728:2240] = m[None, :]
    return cst


def prep_shared(cfg, inp):
    c = cfg
    f = lambda a: np.asarray(a, dtype=np.float32)
    d = {}
    gains = np.zeros((128, 8, c.KC), np.float32)
    for i, (nm, l) in enumerate([("mix_norm", 0), ("mix_norm", 1), ("x_norm", 0), ("x_norm", 1), ("x_mem_norm", 0), ("x_mem_norm", 1),
                                 ("f_norm", 0), ("f_norm", 1)]):
        gains[:, i, :] = _fm(f(inp[nm])[l])
    d["gains"] = gains
    hg = np.zeros((128, 6), np.float32)
    hg[:, 0] = f(inp["ab_q_norm"])[0]; hg[:, 1] = f(inp["ab_k_norm"])[0]
    hg[:, 2] = f(inp["x_q_norm"])[0]; hg[:, 3] = f(inp["x_q_norm"])[1]
    hg[:, 4] = f(inp["x_k_norm"])[0]; hg[:, 5] = f(inp["x_k_norm"])[1]
    d["hgain"] = hg
    d["consts"] = _consts()
    t = np.arange(1, c.T + 1, dtype=np.float32)
    d["invc"] = np.stack([np.float32(1.0) / np.minimum(t, np.float32(w)) for w in (2, 4, 8, 16)]).astype(np.float32)
    d["ab_win"] = _wblocks(f(inp["ab_w_in"])[0])
    pw = f(inp["ab_pool_w"])[0]
    d["pool_w"] = np.concatenate([_wblocks(pw[g]) for g in range(4)], axis=0)
    d["pool_sc"] = _fm(f(inp["ab_pool_scale"])[0])
    d["ab_wout"] = _wblocks(f(inp["ab_w_out"])[0])
    d["c_win"] = _wblocks(f(inp["c_w_in"])[0])
    d["c_wa2"] = np.ascontiguousarray(f(inp["c_w_a2"])[0])
    d["c_ba"] = _fm(f(inp["c_b_a"])[0])
    d["c_on"] = _fm(f(inp["c_o_norm"])[0])
    d["c_wout"] = _wblocks(f(inp["c_w_out"])[0])
    wq = f(inp["x_wq"]); wkv = f(inp["x_wkv"]); wo = f(inp["x_wo"])
    d["x_wq"] = np.stack([_wblocks(wq[l]) for l in range(2)])
    d["x_wk"] = np.stack([_wblocks(wkv[l][:, :512]) for l in range(2)])
    d["x_wv"] = np.stack([np.ascontiguousarray(wkv[l][:, 512:].reshape(c.KC, 128, 512).transpose(1, 0, 2)) for l in range(2)])
    d["x_wo"] = np.stack([_wblocks(wo[l]) for l in range(2)])
    wup = f(inp["f_w_up"]); cw = f(inp["f_conv_w"]); cb = f(inp["f_conv_b"]); wdn = f(inp["f_w_down"])
    idx = np.concatenate([np.concatenate([np.arange(j * 128, (j + 1) * 128), c.DFF + np.arange(j * 128, (j + 1) * 128)]) for j in range(c.FC)])
    d["f_wup"] = np.stack([_wblocks(wup[l][:, idx]) for l in range(2)])
    fcw = np.zeros((128, 2, 2 * c.FC, 4), np.float32)
    for l in range(2):
        for k in range(3):
            fcw[:, l, :, k] = _fm(cw[l, k][idx])
        fcw[:, l, :, 3] = _fm(cb[l][idx])
    d["f_cw"] = fcw
    d["f_wdn"] = np.stack([_wblocks(wdn[l]) for l in range(2)])
    return d


def _tfm(a):
    T, D = a.shape
    return np.ascontiguousarray(a.T.reshape(D // 128, 128, T))


_PROG_CACHE = {}


def run(cfg, inputs, stop_after=None, cores=None):
    key = (cfg.D, cfg.T, cfg.CH, stop_after)
    if key not in _PROG_CACHE:
        p = Prog(cfg, stop_after=stop_after)
        p.build()
        _PROG_CACHE[key] = p
    p = _PROG_CACHE[key]
    shared = prep_shared(cfg, inputs)
    x = np.asarray(inputs["x"], dtype=np.float32); mem = np.asarray(inputs["mem"], dtype=np.float32)
    nb = x.shape[0]
    in_maps = []
    for b in range(nb):
        m = dict(shared)
        m["xT"] = _tfm(x[b]); m["memT"] = _tfm(mem[b])
        in_maps.append(m)
    res = run_bass_kernel_spmd(p.nc, in_maps, core_ids=list(range(nb)))
    outs = []
    for b in range(nb):
        o = res.results[b]["outT"]
        outs.append(np.ascontiguousarray(o.reshape(cfg.D, cfg.T).T))
    return np.stack(outs)


def kernel(**inputs):
    cfg = Cfg(4096, 4096, 8)
    return run(cfg, inputs).astype(np.float32)
```

```python
import math
import numpy as np
import concourse.bass as bass
import concourse.mybir as mybir
from concourse.bass_utils import run_bass_kernel_spmd

F32 = mybir.dt.float32
BF16 = mybir.dt.bfloat16
AF = mybir.ActivationFunctionType
ALU = mybir.AluOpType
EPS = 1e-6
BIG = 1.0e5


class Cfg:
    def __init__(self, D=4096, T=4096, CH=8):
        self.D = D; self.T = T; self.KC = D // 128
        self.AH = D // 256; self.AW = self.AH * 128
        self.BW = D - self.AW; self.BG = self.BW // 4; self.BGC = self.BG // 128
        self.EVEN_IN = 7 * self.AW + self.BW
        self.CH = CH; self.CKW = D // 2; self.CVW = D
        self.CDK = self.CKW // CH; self.CDV = self.CVW // CH
        self.DKC = self.CDK // 128; self.DVC = self.CDV // 128
        self.ODD_IN = 2 * self.CKW + 2 * self.CVW + 16
        self.DFF = ((8 * D // 3 + 255) // 256) * 256; self.FC = self.DFF // 128
        self.XH = 4; self.XW = 512; self.ML = 256
        self.TT = 512
        self.NT = T // self.TT
        self.GT = 1024
        self.NGT = T // self.GT


class Buf:
    __slots__ = ("name", "w", "r")

    def __init__(self, name):
        self.name = name; self.w = None; self.r = {}


class Sched:
    MAXC = 30000

    def __init__(self, nc):
        self.nc = nc
        self.E = {"pe": nc.tensor, "act": nc.scalar, "dve": nc.vector, "pool": nc.gpsimd, "sp": nc.sync}
        self.sems = []; self.owner = []
        self.cur = {}
        self.seen = {e: {} for e in self.E}
        self.dq = {}
        self.ninst = 0

    def newsem(self, name, owner):
        h = self.nc.alloc_semaphore(name)
        self.sems.append(h); self.owner.append(owner)
        return len(self.sems) - 1

    def _wait(self, e, si, val):
        if self.seen[e].get(si, 0) >= val:
            return
        self.E[e].wait_ge(self.sems[si], val)
        self.seen[e][si] = val

    def _sync(self, e, reads, writes):
        deps = {}
        for b in reads:
            if b.w is not None:
                si, v = b.w
                if not (e == "pe" and self.owner[si] == "pe"):
                    deps[si] = max(deps.get(si, 0), v)
        for b in writes:
            if b.w is not None:
                si, v = b.w
                if self.owner[si] != e:
                    deps[si] = max(deps.get(si, 0), v)
            for si, v in b.r.items():
                if self.owner[si] != e:
                    deps[si] = max(deps.get(si, 0), v)
        for si, v in deps.items():
            self._wait(e, si, v)

    def _tick(self, e):
        c = self.cur.get(e)
        if c is None or c[1] >= self.MAXC:
            c = [self.newsem("c_%s_%d" % (e, len(self.sems)), e), 0]
            self.cur[e] = c
        c[1] += 1
        return c[0], c[1]

    def op(self, e, emit, reads=(), writes=()):
        self._sync(e, reads, writes)
        inst = emit(self.E[e])
        si, v = self._tick(e)
        inst.then_inc(self.sems[si], 1)
        for b in reads:
            b.r[si] = v
        for b in writes:
            b.w = (si, v); b.r = {}
        self.ninst += 1
        return inst

    def dma(self, q, out, in_, reads=(), writes=()):
        self._sync(q, reads, writes)
        ring = self.dq.get(q)
        if ring is None:
            ring = {"s": [[self.newsem("d_%s_%d" % (q, i), "dma_" + q), 0] for i in range(8)], "p": 0}
            self.dq[q] = ring
        p = ring["p"]; ring["p"] = (p + 1) % 8
        ent = ring["s"][p]
        if ent[1] >= self.MAXC:
            self._wait(q, ent[0], ent[1])
            ent = [self.newsem("d_%s_%d" % (q, len(self.sems)), "dma_" + q), 0]
            ring["s"][p] = ent
        self._wait(q, ent[0], ent[1])
        inst = self.E[q].dma_start(out=out, in_=in_)
        ent[1] += 16
        inst.then_inc(self.sems[ent[0]], 16)
        for b in reads:
            b.r[ent[0]] = ent[1]
        for b in writes:
            b.w = (ent[0], ent[1]); b.r = {}
        self.ninst += 1

    def wait_all(self, e, bufs):
        self._sync(e, bufs, ())


class Prog:
    def __init__(self, cfg, stop_after=None):
        self.cfg = cfg
        self.stop_after = stop_after
        nc = bass.Bass("TRN2", target_bir_lowering=False)
        self.nc = nc
        self.s = Sched(nc)
        self.din = {}
        self._alloc()

    def dI(self, name, shape):
        ap = self.nc.dram_tensor(name, list(shape), F32, kind="ExternalInput").ap()
        self.din[name] = tuple(shape)
        return ap

    def dS(self, name, shape, dt=F32):
        return self.nc.dram_tensor(name, list(shape), dt, kind="Internal").ap(), Buf(name)

    def sb(self, name, shape):
        return self.nc.alloc_sbuf_tensor(name, list(shape), F32), Buf(name)

    def _alloc(self):
        c = self.cfg; nc = self.nc
        KC, T, FC = c.KC, c.T, c.FC
        self.xT = self.dI("xT", [KC, 128, T]); self.xTB = Buf("xT")
        self.memT = self.dI("memT", [KC, 128, c.ML])
        self.gains = self.dI("gains", [128, 8, KC])
        self.hgain = self.dI("hgain", [128, 6])
        self.consts = self.dI("consts", [128, 2240])
        self.invc = self.dI("invc", [4, T])
        self.ab_win = self.dI("ab_win", [c.EVEN_IN // 128, 128, KC, 128])
        self.pool_w = self.dI("pool_w", [4 * c.BGC, 128, c.BGC, 128])
        self.pool_sc = self.dI("pool_sc", [128, c.BW // 128])
        self.ab_wout = self.dI("ab_wout", [KC, 128, KC, 128])
        self.c_win = self.dI("c_win", [(c.ODD_IN + 127) // 128, 128, KC, 128])
        self.c_wa2 = self.dI("c_wa2", [16, c.CKW])
        self.c_ba = self.dI("c_ba", [128, c.CKW // 128])
        self.c_on = self.dI("c_on", [128, c.DVC])
        self.c_wout = self.dI("c_wout", [KC, 128, KC, 128])
        self.x_wq = self.dI("x_wq", [2, 4, 128, KC, 128])
        self.x_wk = self.dI("x_wk", [2, 4, 128, KC, 128])
        self.x_wv = self.dI("x_wv", [2, 128, KC, 512])
        self.x_wo = self.dI("x_wo", [2, KC, 128, 4, 128])
        self.f_wup = self.dI("f_wup", [2, 2 * FC, 128, KC, 128])
        self.f_cw = self.dI("f_cw", [128, 2, 2 * FC, 4])
        self.f_wdn = self.dI("f_wdn", [2, KC, 128, FC, 128])
        self.outT = nc.dram_tensor("outT", [KC, 128, T], F32, kind="ExternalOutput").ap()
        self.outTB = Buf("outT")
        self.hA, self.hAB = self.dS("hA", [KC, 128, T])
        self.hB, self.hBB = self.dS("hB", [KC, 128, T])
        nzq = max(3 * c.AH, c.CKW // 128)
        self.zq, self.zqB = self.dS("zq", [nzq, 128, T])
        self.zk, self.zkB = self.dS("zk", [nzq, 128, T])
        self.zv, self.zvB = self.dS("zv", [max(c.AH, c.CVW // 128), 128, T])
        self.zu, self.zuB = self.dS("zu", [max(c.BW // 128, c.CVW // 128), 128, T])
        self.zr, self.zrB = self.dS("zr", [1, 128, T])
        self.ab, self.abB = self.dS("ab", [KC, 128, T], BF16)
        self.ffa, self.ffaB = self.dS("ffa", [FC, 128, T], BF16)
        self.hT1, self.hT1B = self.dS("hT1", [KC, 128, T])
        self.hT2, self.hT2B = self.dS("hT2", [KC, 128, T])
        self.cst, self.cstB = self.sb("cst", [128, 2240])
        self.ones, self.onesB = self.sb("ones", [128, 128])
        self.gn, self.gnB = self.sb("gn", [128, 8, KC])
        self.hg, self.hgB = self.sb("hg", [128, 6])
        arena_elems = max(KC * 512, 4 * T, 16384)
        self.arena, self.arenaB = self.sb("arena", [128, arena_elems])
        self.KP = 16
        self.wb = []
        for i in range(1):
            self.wb.append(self.sb("wb%d" % i, [128, self.KP * 128]))
        self.wctr = 0
        self.rs = [self.sb("rs%d" % i, [128, 1024]) for i in range(2)]
        self.rsc = 0
        self.NW16 = 6
        self.wb16 = []
        for i in range(self.NW16):
            t = self.nc.alloc_sbuf_tensor("wbh%d" % i, [128, self.KP * 128], BF16)
            self.wb16.append((t, Buf("wbh%d" % i)))
        self.w16ctr = 0
        self.ld = [self.sb("ld%d" % i, [128, 1024]) for i in range(4)]
        self.apB = [Buf("ap%d" % i) for i in range(6)]
        self.ldc = 0
        self.ones16 = self.nc.alloc_sbuf_tensor("ones16", [128, 128], BF16); self.ones16B = Buf("ones16")
        self.sq = [self.sb("sq%d" % i, [128, 1024]) for i in range(2)]
        self.t1 = [self.sb("t1_%d" % i, [128, 1024]) for i in range(2)]
        self.t2 = [self.sb("t2_%d" % i, [128, 1024]) for i in range(2)]
        self.stg = [self.sb("stg%d" % i, [128, 512]) for i in range(4)]
        self.stgc = 0
        self.res = [self.sb("res%d" % i, [128, 512]) for i in range(4)]
        self.resc = 0
        self.gp, self.gpB = self.sb("gp", [128, 8448])
        self.ps = []
        for i in range(8):
            self.ps.append((nc.alloc_psum_tensor("ps%d" % i, [128, 512], F32), Buf("ps%d" % i)))
        self.psGc = 0; self.psXc = 0

    def psG(self):
        p = self.ps[self.psGc % 4]; self.psGc += 1
        return p

    def psX(self):
        p = self.ps[4 + self.psXc % 4]; self.psXc += 1
        return p

    def nstg(self):
        p = self.stg[self.stgc % 4]; self.stgc += 1
        return p

    def nres(self):
        p = self.res[self.resc % 4]; self.resc += 1
        return p

    def act(self, out, in_, func, reads, writes, scale=None, bias=None):
        kw = {}
        if scale is not None:
            kw["scale"] = scale
        if bias is not None:
            kw["bias"] = bias
        return self.s.op("act", lambda E: E.activation(out=out, in_=in_, func=func, **kw), reads, writes)

    def mm(self, psB, out, lhsT, rhs, start, stop, reads):
        return self.s.op("pe", lambda E: E.matmul(out, lhsT=lhsT, rhs=rhs, start=start, stop=stop), reads, [psB])

    def tr(self, psB, out, in_, reads):
        ident = self.cst[:, 0:128]
        return self.s.op("pe", lambda E: E.transpose(out, in_, ident[: in_.shape[0], : in_.shape[0]]), list(reads) + [self.cstB], [psB])

    def stt(self, eng, out, in0, scalar, in1, op0, op1, reads, writes):
        return self.s.op(eng, lambda E: E.scalar_tensor_tensor(out=out, in0=in0, scalar=scalar, in1=in1, op0=op0, op1=op1), reads, writes)

    def tt(self, eng, out, in0, in1, op, reads, writes):
        return self.s.op(eng, lambda E: E.tensor_tensor(out=out, in0=in0, in1=in1, op=op), reads, writes)

    def ts(self, eng, out, in0, s1, op0, reads, writes, s2=None, op1=None):
        if op1 is None:
            return self.s.op(eng, lambda E: E.tensor_scalar(out=out, in0=in0, scalar1=s1, scalar2=None, op0=op0), reads, writes)
        return self.s.op(eng, lambda E: E.tensor_scalar(out=out, in0=in0, scalar1=s1, scalar2=s2, op0=op0, op1=op1), reads, writes)

    def cp(self, eng, out, in_, reads, writes):
        if eng == "act":
            return self.s.op("act", lambda E: E.copy(out=out, in_=in_), reads, writes)
        return self.s.op(eng, lambda E: E.tensor_copy(out=out, in_=in_), reads, writes)

    def recip(self, out, in_, reads, writes):
        return self.s.op("dve", lambda E: E.reciprocal(out=out, in_=in_), reads, writes)

    def rsqrt_from(self, ps_ap, psB, mul, add, P, N):
        t2, t2B = self.t2[0]; self.t2.reverse()
        bias_ap = self.epsap(add, P)
        self.act(t2[:P, :N], ps_ap, AF.Ln, [psB, self.epstB], [t2B], scale=mul, bias=bias_ap)
        self.act(t2[:P, :N], t2[:P, :N], AF.Exp, [t2B], [t2B], scale=-0.5)
        return t2, t2B

    def epsap(self, val, P):
        key = float(val)
        if not hasattr(self, "_epsmap"):
            self._epsmap = {}
            self.epst, self.epstB = self.sb("epst", [128, 16])
        if key not in self._epsmap:
            j = len(self._epsmap)
            self.s.op("pool", lambda E: E.memset(self.epst[:, j:j + 1], key), [], [self.epstB])
            self._epsmap[key] = j
        j = self._epsmap[key]
        return self.epst[:P, j:j + 1]

    def prologue(self):
        s = self.s
        s.dma("sp", self.cst[:], self.consts, [], [self.cstB])
        s.dma("sp", self.gn[:], self.gains, [], [self.gnB])
        s.dma("sp", self.hg[:], self.hgain, [], [self.hgB])
        s.op("pool", lambda E: E.memset(self.ones[:], 1.0), [], [self.onesB])

    def nld(self):
        p = self.ld[self.ldc % 4]; self.ldc += 1
        return p

    def load_tile(self, src, srcB, t0, Tt, KCn, gain_idx=None, src_bf16=False, pre1=None):
        a16 = self.arena[:].bitcast(BF16)
        tile = a16[:, 0:KCn * Tt].rearrange("p (k t) -> p k t", k=KCn)
        QS = ("sp", "act", "pool")
        self._fence_release(self.apB, [self.arenaB])
        self.tile_bufs = [self.arenaB]
        if src_bf16:
            step = max(1, (KCn + 5) // 6)
            for q in QS:
                self.s._sync(q, [], [self.arenaB])
            used = []
            for qi, k0 in enumerate(range(0, KCn, step)):
                k1 = min(KCn, k0 + step)
                self.s.dma(QS[qi % 3], tile[:, k0:k1, :], src[k0:k1, :, t0:t0 + Tt].rearrange("k p t -> p k t"), [srcB], [self.apB[qi]])
                used.append(self.apB[qi])
            self.tile_bufs = used
            return tile
        if gain_idx is None:
            for kc in range(KCn):
                l, lB = self.nld()
                self.s.dma("sp", l[:, :Tt], src[kc, :, t0:t0 + Tt], [srcB], [lB])
                self.cp("act" if kc % 2 else "pool", tile[:, kc, :], l[:, :Tt], [lB], [self.arenaB])
            return tile
        if pre1 is None:
            pre1 = self.pass1(src, srcB, t0, Tt, KCn)
            for _ in pre1["gen"]:
                pass
        rstd, rstdB = pre1["rstd"]
        for kc in range(KCn):
            l, lB = self.nld()
            self.s.dma(QS[kc % 3], l[:, :Tt], src[kc, :, t0:t0 + Tt], [srcB], [lB])
            self.stt("dve", tile[:, kc, :], l[:, :Tt], self.gn[:, gain_idx, kc:kc + 1], rstd[:, :Tt], ALU.mult, ALU.mult,
                     [lB, self.gnB, rstdB], [self.arenaB])
        return tile

    def pass1(self, src, srcB, t0, Tt, KCn):
        rstd, rstdB = self.rs[self.rsc % 2]; self.rsc += 1

        def gen():
            accs = [self.t1[0], self.t1[1]]
            lds = {}

            def issue(kc):
                l, lB = self.nld()
                self.s.dma("sp", l[:, :Tt], src[kc, :, t0:t0 + Tt], [srcB], [lB])
                lds[kc] = (l, lB)
            issue(0)
            if KCn > 1:
                issue(1)
            for kc in range(KCn):
                if kc + 2 < KCn:
                    issue(kc + 2)
                l, lB = lds.pop(kc)
                acc, accB = accs[kc % 2]
                if kc < 2:
                    self.act(acc[:, :Tt], l[:, :Tt], AF.Square, [lB], [accB])
                else:
                    self.act(l[:, :Tt], l[:, :Tt], AF.Square, [lB], [lB])
                    self.tt("pool" if kc % 2 == 0 else "dve", acc[:, :Tt], acc[:, :Tt], l[:, :Tt], ALU.add, [accB, lB], [accB])
                yield
            nacc = min(2, KCn)
            for s0 in range(0, Tt, 512):
                wd = min(512, Tt - s0)
                psq, psqB = self.psX()
                for ai in range(nacc):
                    acc, accB = accs[ai]
                    self.mm(psqB, psq[:, :wd], self.ones[:], acc[:, s0:s0 + wd], ai == 0, ai == nacc - 1, [accB, self.onesB])
                bias_ap = self.epsap(EPS, 128)
                self.act(rstd[:, s0:s0 + wd], psq[:, :wd], AF.Ln, [psqB, self.epstB], [rstdB], scale=1.0 / (KCn * 128), bias=bias_ap)
                self.act(rstd[:, s0:s0 + wd], rstd[:, s0:s0 + wd], AF.Exp, [rstdB], [rstdB], scale=-0.5)
            yield
        return {"gen": gen(), "rstd": (rstd, rstdB)}

    def gemm(self, rhs_fn, rhsB, KCtot, nsub, w_dram, CC, m_last, epi, pre=None, cc_list=None, lowp=True, sw=512, epi_a=None, side=None):
        rhsL = list(rhsB) if isinstance(rhsB, (list, tuple)) else [rhsB]
        KP = self.KP
        pieces = [(k0, min(k0 + KP, KCtot)) for k0 in range(0, KCtot, KP)]
        ccs = list(range(CC)) if cc_list is None else cc_list
        blocks = [(cc, k0, k1) for cc in ccs for (k0, k1) in pieces]
        PF = (self.NW16 - 1) if lowp else 0
        loaded = {}

        def load(i):
            cc, k0, k1 = blocks[i]
            if lowp:
                w16t, w16B = self.wb16[self.w16ctr % self.NW16]; self.w16ctr += 1
                wv = w16t[:, 0:(k1 - k0) * 128].rearrange("p (k m) -> p k m", m=128)
                self.s.dma("pool", wv, w_dram[cc, :, k0:k1, :], [], [w16B])
                loaded[i] = (wv, w16B)
            else:
                wbt, wbB = self.wb[0]; self.wctr += 1
                wv = wbt[:, 0:(k1 - k0) * 128].rearrange("p (k m) -> p k m", m=128)
                self.s.dma("sp", wv, w_dram[cc, :, k0:k1, :], [], [wbB])
                loaded[i] = (wv, wbB)

        for i in range(min(PF, len(blocks))):
            load(i)
        pss = None
        pend_epi = None
        for i, (cc, k0, k1) in enumerate(blocks):
            if i + PF < len(blocks):
                load(i + PF)
            if side is not None and k0 == 0:
                next(side, None)
            if k0 == 0:
                pss = [self.psG() for _ in range(nsub)]
                if pre is not None:
                    for sub in range(nsub):
                        pre(cc, sub)
            M = m_last if cc == CC - 1 else 128
            wv, wbB = loaded.pop(i)
            for kc in range(k0, k1):
                for sub in range(nsub):
                    ps, psB = pss[sub]
                    self.mm(psB, ps[:M, :sw], wv[:, kc - k0, :M], rhs_fn(kc, sub), kc == 0, kc == KCtot - 1, [wbB] + rhsL)
            if k0 == 0 and pend_epi is not None:
                pend_epi(); pend_epi = None
            if k1 == KCtot:
                if epi_a is not None:
                    for sub in range(nsub):
                        ps, psB = pss[sub]
                        epi_a(cc, ps, psB, M, sub)
                cur = (cc, list(pss), M)

                def run_epi(cur=cur):
                    cc_, pss_, M_ = cur
                    for sub in range(nsub):
                        ps, psB = pss_[sub]
                        epi(cc_, ps, psB, M_, sub)
                pend_epi = run_epi
        if pend_epi is not None:
            pend_epi()
        if side is not None:
            for _ in side:
                pass

    def store_chunk(self, dst, dstB, ci, t0, Tt, src_ap, srcB, P=128):
        self.s.dma("sp", dst[ci, 0:P, t0:t0 + Tt], src_ap, [srcB], [dstB])

    def epi_store(self, dst, dstB, t0, Tt, ci_fn=lambda cc: cc, scale=None, eng="act"):
        def epi(cc, ps, psB, M):
            st, stB = self.nstg()
            if scale is not None:
                self.act(st[:M, :Tt], ps[:M, :Tt], AF.Copy, [psB], [stB], scale=scale)
            elif eng == "act":
                self.cp("act", st[:M, :Tt], ps[:M, :Tt], [psB], [stB])
            else:
                self.cp("dve", st[:M, :Tt], ps[:M, :Tt], [psB], [stB])
            self.store_chunk(dst, dstB, ci_fn(cc), t0, Tt, st[:M, :Tt], stB, P=M)
        return epi

    def epi_residual(self, hin, hinB, hout, houtB, t0):
        pend = {}

        def pre(cc, sub):
            r, rB = self.nres()
            ts0 = t0 + sub * 512
            self.s.dma("sp", r[:, :512], hin[cc, :, ts0:ts0 + 512], [hinB], [rB])
            pend[(cc, sub)] = (r, rB)

        def epi(cc, ps, psB, M, sub):
            r, rB = pend.pop((cc, sub))
            st, stB = self.nstg()
            ts0 = t0 + sub * 512
            self.tt("dve", st[:, :512], ps[:, :512], r[:, :512], ALU.add, [psB, rB], [stB])
            self.store_chunk(hout, houtB, cc, ts0, 512, st[:, :512], stB)
        return pre, epi

    def qk_a(self, ps, psB, Tt):
        if not hasattr(self, "_sqslot"):
            self._sqslot = 0
        i = self._sqslot % 4; self._sqslot += 1
        sqt, sqB = self.sq[i // 2]
        sq = sqt[:, (i % 2) * 512:(i % 2) * 512 + Tt]
        self.act(sq, ps[:, :Tt], AF.Square, [psB], [sqB])
        return sq, sqB

    def qk_b(self, ps, psB, Tt, sq, sqB, gain_ap, gainB, fold, out_ap, outB):
        p2, p2B = self.psX()
        self.mm(p2B, p2[:, :Tt], self.ones[:], sq, True, True, [sqB, self.onesB])
        rstd, rstdB = self.rsqrt_from(p2[:, :Tt], p2B, 1.0 / (128.0 * fold * fold), EPS / (fold * fold), 128, Tt)
        self.stt("dve", out_ap, ps[:, :Tt], gain_ap, rstd[:, :Tt], ALU.mult, ALU.mult, [psB, gainB, rstdB], [outB])

    def qknorm(self, ps, psB, Tt, gain_ap, gainB, fold, out_ap, outB):
        sq, sqB = self.qk_a(ps, psB, Tt)
        self.qk_b(ps, psB, Tt, sq, sqB, gain_ap, gainB, fold, out_ap, outB)

    def build(self):
        c = self.cfg
        self.prologue()
        stages = [
            ("l0_inproj", lambda: self.l0_inproj(self.xT, self.xTB)),
            ("l0_attn", self.l0_attn),
            ("l0_pool", self.l0_pool),
            ("l0_out", lambda: self.outproj(self.ab_wout, self.xT, self.xTB, self.hA, self.hAB)),
            ("l0_x", lambda: self.xattn(0, self.hA, self.hAB, self.hB, self.hBB)),
            ("l0_f", lambda: self.ffn(0, self.hB, self.hBB, self.hA, self.hAB)),
            ("l1_inproj", lambda: self.l1_inproj(self.hA, self.hAB)),
            ("l1_gla", self.l1_gla),
            ("l1_out", lambda: self.outproj(self.c_wout, self.hA, self.hAB, self.hB, self.hBB)),
            ("l1_x", lambda: self.xattn(1, self.hB, self.hBB, self.hA, self.hAB)),
            ("l1_f", lambda: self.ffn(1, self.hA, self.hAB, self.hB, self.hBB)),
        ]
        final = (self.hB, self.hBB)
        dbg = {"l0_inproj": (self.zq, self.zqB), "l0_attn": (self.ab, self.abB), "l0_pool": (self.ab, self.abB),
               "l0_out": (self.hA, self.hAB), "l0_x": (self.hB, self.hBB), "l0_f": (self.hA, self.hAB),
               "l1_inproj": (self.zk, self.zkB), "l1_gla": (self.ab, self.abB), "l1_out": (self.hB, self.hBB),
               "l1_x": (self.hA, self.hAB), "l1_f": (self.hB, self.hBB)}
        for name, fn in stages:
            fn()
            if self.stop_after == name:
                final = dbg[name]
                break
        self.copy_out(*final)
        return self.nc

    def copy_out(self, src, srcB):
        c = self.cfg
        n = min(src.shape[0], c.KC)
        step = max(1, n // 8)
        for k0 in range(0, n, step):
            k1 = min(n, k0 + step)
            self.s.dma("sp", self.outT[k0:k1], src[k0:k1], [srcB], [self.outTB])
        for e in ("sp",):
            self.s.wait_all(e, [self.outTB])
        for q, ring in self.s.dq.items():
            for ent in ring["s"]:
                self.s._wait("sp", ent[0], ent[1])

    def l0_inproj(self, hin, hinB):
        c = self.cfg
        AH = c.AH
        nq = 3 * AH
        sqp = {}
        nxt = None
        for ti in range(c.NGT):
            t0 = ti * c.GT
            tile = self.load_tile(hin, hinB, t0, c.GT, c.KC, gain_idx=0, pre1=nxt)
            nxt = self.pass1(hin, hinB, t0 + c.GT, c.GT, c.KC) if ti + 1 < c.NGT else None

            def epi(cc, ps, psB, M, sub, t0=t0):
                ts0 = t0 + sub * 512; Tt = 512
                st, stB = self.nstg()
                if cc < nq:
                    sq, sqB = sqp.pop((cc, sub))
                    self.qk_b(ps, psB, Tt, sq, sqB, self.hg[:, 0:1], self.hgB, 128.0 ** -0.5, st[:, :Tt], stB)
                    self.store_chunk(self.zq, self.zqB, cc, ts0, Tt, st[:, :Tt], stB)
                elif cc < 2 * nq:
                    sq, sqB = sqp.pop((cc, sub))
                    self.qk_b(ps, psB, Tt, sq, sqB, self.hg[:, 1:2], self.hgB, 1.0, st[:, :Tt], stB)
                    self.store_chunk(self.zk, self.zkB, cc - nq, ts0, Tt, st[:, :Tt], stB)
                elif cc < 2 * nq + AH:
                    self.cp("act", st[:, :Tt], ps[:, :Tt], [psB], [stB])
                    self.store_chunk(self.zv, self.zvB, cc - 2 * nq, ts0, Tt, st[:, :Tt], stB)
                else:
                    self.cp("dve", st[:, :Tt], ps[:, :Tt], [psB], [stB])
                    self.store_chunk(self.zu, self.zuB, cc - 2 * nq - AH, ts0, Tt, st[:, :Tt], stB)

            def epi_a(cc, ps, psB, M, sub):
                if cc < 2 * nq:
                    sqp[(cc, sub)] = self.qk_a(ps, psB, 512)

            self.gemm(lambda kc, sub: tile[:, kc, sub * 512:(sub + 1) * 512], self.arenaB, c.KC, c.GT // 512, self.ab_win,
                      c.EVEN_IN // 128, 128, epi, epi_a=epi_a, side=(nxt["gen"] if nxt else None))

    def l0_attn(self):
        c = self.cfg; T = c.T; AH = c.AH
        ar = self.arena
        assert ar.shape[1] >= 4 * T
        qb = ar[:, 0:T]; kb = ar[:, T:2 * T]; vT = ar[:, 2 * T:3 * T]; accO = ar[:, 3 * T:4 * T]
        gp = self.gp
        assert gp.shape[1] >= T + 32 * 128
        accD = gp[:, 0:T]
        vtok = gp[:, T:T + 32 * 128].rearrange("p (b e) -> p b e", e=128)
        qB, kB, vB, aOB, aDB, vtB = Buf("aq"), Buf("ak"), Buf("av"), Buf("aO"), Buf("aD"), Buf("avt")
        fenceR = [self.arenaB, self.gpB]
        relcur = self.cst[:, 128:640]; relprev = self.cst[:, 640:1152]; relprevF = self.cst[:, 1152:1664]
        n_sl = 3 * AH
        slopes = [2.0 ** (-8.0 * (i + 1) / n_sl) for i in range(n_sl)]
        branches = [(128, 1), (512, 4), (2048, 16)]
        self.fence(fenceR)
        for hd in range(AH):
            self.s.dma("sp", vT, self.zv[hd, :, :], [self.zvB], [vB])
            for g, (win, d) in enumerate(branches):
                cneg = -slopes[g * AH + hd] * d
                self.s.dma("sp", qb, self.zq[g * AH + hd, :, :], [self.zqB], [qB])
                self.s.dma("sp", kb, self.zk[g * AH + hd, :, :], [self.zkB], [kB])
                NB = T // d // 128
                G = min(4, NB)
                nblk = d * NB
                for b0 in range(0, nblk, 4):
                    pt, ptB = self.psX()
                    for j in range(4):
                        blk = b0 + j; r = blk // NB; n = blk % NB
                        st0 = r + d * 128 * n
                        self.tr(ptB, pt[:, j * 128:(j + 1) * 128], vT[:, st0:st0 + d * 127 + 1:d], [vB])
                    self.cp("act", vtok[:, b0:b0 + 4, :], pt[:, :].rearrange("p (b e) -> p b e", e=128), [ptB], [vtB])
                for r in range(d):
                    for n0 in range(0, NB, G):
                        W = G * 128
                        sc, scB = self.psX(); sp_, spB = self.psX()
                        for j in range(G):
                            n = n0 + j
                            st0 = r + d * 128 * n
                            qs = qb[:, st0:st0 + d * 127 + 1:d]
                            kcur = kb[:, st0:st0 + d * 127 + 1:d]
                            self.mm(scB, sc[:, j * 128:(j + 1) * 128], kcur, qs, True, True, [kB, qB])
                            if n > 0:
                                stp = r + d * 128 * (n - 1)
                                kprev = kb[:, stp:stp + d * 127 + 1:d]
                            else:
                                kprev = kcur
                            self.mm(spB, sp_[:, j * 128:(j + 1) * 128], kprev, qs, True, True, [kB, qB])
                        pc, pcB = self.nstg(); pp, ppB = self.nstg()
                        self.stt("dve", pc[:, :W], relcur[:, :W], cneg, sc[:, :W], ALU.mult, ALU.add, [self.cstB, scB], [pcB])
                        rp = relprevF if n0 == 0 else relprev
                        self.stt("dve", pp[:, :W], rp[:, :W], cneg, sp_[:, :W], ALU.mult, ALU.add, [self.cstB, spB], [ppB])
                        self.act(pc[:, :W], pc[:, :W], AF.Exp, [pcB], [pcB])
                        self.act(pp[:, :W], pp[:, :W], AF.Exp, [ppB], [ppB])
                        po, poB = self.psG(); pd, pdB = self.psG()
                        for j in range(G):
                            n = n0 + j
                            blk = r * NB + n
                            self.mm(poB, po[:, j * 128:(j + 1) * 128], vtok[:, blk, :], pc[:, j * 128:(j + 1) * 128], True, n == 0, [vtB, pcB])
                            if n > 0:
                                self.mm(poB, po[:, j * 128:(j + 1) * 128], vtok[:, blk - 1, :], pp[:, j * 128:(j + 1) * 128], False, True, [vtB, ppB])
                            self.mm(pdB, pd[:, j * 128:(j + 1) * 128], self.ones[:], pc[:, j * 128:(j + 1) * 128], True, False, [self.onesB, pcB])
                            self.mm(pdB, pd[:, j * 128:(j + 1) * 128], self.ones[:], pp[:, j * 128:(j + 1) * 128], False, True, [self.onesB, ppB])
                        st0 = r + d * 128 * n0
                        osl = accO[:, st0:st0 + d * (W - 1) + 1:d]; dsl = accD[:, st0:st0 + d * (W - 1) + 1:d]
                        if g == 0:
                            self.cp("act", osl, po[:, :W], [poB], [aOB])
                            self.cp("dve", dsl, pd[:, :W], [pdB], [aDB])
                        else:
                            self.tt("dve", osl, po[:, :W], osl, ALU.add, [poB, aOB], [aOB])
                            self.tt("dve", dsl, pd[:, :W], dsl, ALU.add, [pdB, aDB], [aDB])
            self.recip(accD, accD, [aDB], [aDB])
            o16 = qb.bitcast(BF16)[:, 0:T]
            self.tt("dve", o16, accO, accD, ALU.mult, [aOB, aDB], [qB])
            self.s.dma("sp", self.ab[hd, :, :], o16, [qB], [self.abB])
        self._fence_release([qB, kB, vB, aOB, aDB, vtB], [self.arenaB, self.gpB])

    def fence(self, wholes):
        self._fence_release(self.apB, [self.arenaB])
        for e in ("act", "dve", "pool", "sp", "pe"):
            self.s._sync(e, [], wholes)

    def _fence_release(self, subs, wholes):
        for wB in wholes:
            for b in subs:
                if b.w is not None:
                    si, v = b.w
                    wB.r[si] = max(wB.r.get(si, 0), v)
                for si, v in b.r.items():
                    wB.r[si] = max(wB.r.get(si, 0), v)

    def l0_pool(self):
        c = self.cfg; T = c.T
        PADL = 16
        L = min(T, 2048)
        NH = T // L
        W = PADL + L
        ar = self.arena; gp = self.gp
        assert gp.shape[1] >= 3 * W + L
        ub = [gp[:, i * W:(i + 1) * W] for i in range(3)]
        ubB = [Buf("pu%d" % i) for i in range(3)]
        invt = gp[:, 3 * W:3 * W + L]; invB = Buf("invt")
        pooled = ar[:, 0:c.BGC * T].rearrange("p (j t) -> p j t", j=c.BGC)
        pooledB = Buf("pooled")
        psc, pscB = self.sb("psc", [128, c.BW // 128])
        self.s.dma("sp", psc[:], self.pool_sc, [], [pscB])
        self.fence([self.arenaB, self.gpB])
        for gi, win in enumerate((2, 4, 8, 16)):
            for hh in range(NH):
                t0 = hh * L
                self.s.dma("sp", invt, self.invc[gi:gi + 1, t0:t0 + L].partition_broadcast(128), [], [invB])
                for j in range(c.BGC):
                    ci = gi * c.BGC + j
                    if hh == 0:
                        self.s.op("pool", lambda E: E.memset(ub[0][:, 0:PADL], 0.0), [], [ubB[0]])
                        self.s.dma("sp", ub[0][:, PADL:], self.zu[ci, :, t0:t0 + L], [self.zuB], [ubB[0]])
                    else:
                        self.s.dma("sp", ub[0][:, :], self.zu[ci, :, t0 - PADL:t0 + L], [self.zuB], [ubB[0]])
                    src_i = 0; span = 1; dst_i = 1
                    while span < win:
                        self.tt("dve", ub[dst_i][:, span:W], ub[src_i][:, span:W], ub[src_i][:, 0:W - span], ALU.add,
                                [ubB[src_i]], [ubB[dst_i]])
                        src_i = dst_i
                        dst_i = 2 if dst_i == 1 else 1
                        span *= 2
                    self.tt("dve", ub[src_i][:, PADL:], ub[src_i][:, PADL:], invt, ALU.mult, [ubB[src_i], invB], [ubB[src_i]])
                    self.tt("dve", pooled[:, j, t0:t0 + L], ub[src_i][:, PADL:], ub[0][:, PADL:], ALU.subtract, [ubB[src_i], ubB[0]],
                            [pooledB])
            for ti in range(c.NT):
                t0 = ti * c.TT; Tt = c.TT

                def epi(cc, ps, psB, M, sub, t0=t0, Tt=Tt):
                    st, stB = self.nstg()
                    s16 = st[:].bitcast(BF16)
                    self.ts("dve", s16[:, :Tt], ps[:, :Tt], psc[:, cc:cc + 1], ALU.mult, [psB, pscB], [stB])
                    self.store_chunk(self.ab, self.abB, c.AH + cc, t0, Tt, s16[:, :Tt], stB)

                self.gemm(lambda kc, sub, t0=t0, Tt=Tt: pooled[:, kc, t0:t0 + Tt], pooledB, c.BGC, 1, self.pool_w, 4 * c.BGC, 128, epi,
                          cc_list=[gi * c.BGC + oc for oc in range(c.BGC)], lowp=False)
        self._fence_release(ubB + [pooledB, invB], [self.arenaB, self.gpB])

    def outproj(self, w, hin, hinB, hout, houtB):
        c = self.cfg
        for ti in range(c.NGT):
            t0 = ti * c.GT
            tile = self.load_tile(self.ab, self.abB, t0, c.GT, c.KC, src_bf16=True)
            pre, epi = self.epi_residual(hin, hinB, hout, houtB, t0)
            self.gemm(lambda kc, sub: tile[:, kc, sub * 512:(sub + 1) * 512], self.tile_bufs, c.KC, c.GT // 512, w, c.KC, 128, epi, pre=pre)

    def xattn(self, l, hin, hinB, hout, houtB):
        c = self.cfg; ML = c.ML
        gp = self.gp
        o0 = 0
        kx = gp[:, o0:o0 + 4 * ML].rearrange("p (h m) -> p h m", h=4); o0 += 4 * ML
        vx = gp[:, o0:o0 + 2 * 512].rearrange("p (b n) -> p b n", b=2); o0 += 1024
        qx = gp[:, o0:o0 + 4 * 512].rearrange("p (h t) -> p h t", h=4); o0 += 2048
        ox16 = gp[:, o0:o0 + 1024].bitcast(BF16).rearrange("p (h t) -> p h t", h=4); o0 += 1024
        pb = gp[:, o0:o0 + 2 * 512].rearrange("p (b t) -> p b t", b=2); o0 += 1024
        assert gp.shape[1] >= o0
        kxB, vxB, qxB, oxB, pbB = Buf("kx"), Buf("vx"), Buf("qx"), Buf("ox"), Buf("pb")
        fence = [self.gpB]
        mt = self.load_tile(self.memT, Buf("memT"), 0, ML, c.KC, gain_idx=4 + l)
        self.fence(fence)

        def epik(cc, ps, psB, M, sub):
            self.qknorm(ps, psB, ML, self.hg[:, 4 + l:5 + l], self.hgB, 1.0, kx[:, cc, :], kxB)
        self.gemm(lambda kc, sub: mt[:, kc, :], self.arenaB, c.KC, 1, self.x_wk[l], 4, 128, epik, sw=ML)
        for mb in range(ML // 128):
            ps, psB = self.psG()
            for k0 in range(0, c.KC, 4):
                k1 = min(c.KC, k0 + 4)
                w16t, w16B = self.wb16[self.w16ctr % self.NW16]; self.w16ctr += 1
                w16 = w16t[:, 0:(k1 - k0) * 512].rearrange("p (k n) -> p k n", n=512)
                self.s.dma("pool", w16, self.x_wv[l, :, k0:k1, :], [], [w16B])
                for kc in range(k0, k1):
                    self.mm(psB, ps[:, :], mt[:, kc, mb * 128:(mb + 1) * 128], w16[:, kc - k0, :], kc == 0, kc == c.KC - 1, [w16B, self.arenaB])
            self.cp("act", vx[:, mb, :], ps[:, :], [psB], [vxB])
        nxt = None
        for ti in range(c.NT):
            t0 = ti * c.TT; Tt = c.TT
            tile = self.load_tile(hin, hinB, t0, Tt, c.KC, gain_idx=2 + l, pre1=nxt)
            nxt = self.pass1(hin, hinB, t0 + Tt, Tt, c.KC) if ti + 1 < c.NT else None

            def epiq(cc, ps, psB, M, sub):
                self.qknorm(ps, psB, Tt, self.hg[:, 2 + l:3 + l], self.hgB, 128.0 ** -0.5, qx[:, cc, :], qxB)
            self.gemm(lambda kc, sub: tile[:, kc, :], self.arenaB, c.KC, 1, self.x_wq[l], 4, 128, epiq)
            for h in range(4):
                pd, pdB = self.psX(); po, poB = self.psX()
                for mb in range(2):
                    ps, psB = self.psX()
                    self.mm(psB, ps[:, :Tt], kx[:, h, mb * 128:(mb + 1) * 128], qx[:, h, :], True, True, [kxB, qxB])
                    self.act(pb[:, mb, :], ps[:, :Tt], AF.Exp, [psB], [pbB])
                for mb in range(2):
                    self.mm(pdB, pd[:, :Tt], self.ones[:], pb[:, mb, :], mb == 0, mb == 1, [self.onesB, pbB])
                for mb in range(2):
                    self.mm(poB, po[:, :Tt], vx[:, mb, h * 128:(h + 1) * 128], pb[:, mb, :], mb == 0, mb == 1, [vxB, pbB])
                rc, rcB = self.nstg()
                self.recip(rc[:, :Tt], pd[:, :Tt], [pdB], [rcB])
                self.tt("dve", ox16[:, h, :], po[:, :Tt], rc[:, :Tt], ALU.mult, [poB, rcB], [oxB])
            pre, epi = self.epi_residual(hin, hinB, hout, houtB, t0)
            self.gemm(lambda kc, sub: ox16[:, kc, :], oxB, 4, 1, self.x_wo[l], c.KC, 128, epi, pre=pre, side=(nxt["gen"] if nxt else None))
        self._fence_release([kxB, vxB, qxB, oxB, pbB], [self.gpB])

    def ffn(self, l, hin, hinB, hout, houtB):
        c = self.cfg; FC = c.FC
        gp = self.gp
        o0 = 0
        cw = gp[:, o0:o0 + 2 * FC * 4].rearrange("p (c f) -> p c f", f=4); o0 += 2 * FC * 4
        tails = gp[:, o0:o0 + 2 * FC * 2].rearrange("p (c f) -> p c f", f=2); o0 += 2 * FC * 2
        ub = []
        for i in range(2):
            ub.append(gp[:, o0:o0 + 514]); o0 += 516
        cg = []
        for i in range(2):
            cg.append(gp[:, o0:o0 + 512]); o0 += 512
        cv = gp[:, o0:o0 + 512]; o0 += 512
        assert gp.shape[1] >= o0
        cwB, tlB, cvB = Buf("cw"), Buf("tails"), Buf("cv")
        cgB = [Buf("cg0"), Buf("cg1")]
        ubB = [Buf("fub0"), Buf("fub1")]
        fence = [self.gpB]
        self.fence(fence)
        self.s.dma("sp", cw, self.f_cw[:, l, :, :], [], [cwB])
        self.s.op("pool", lambda E: E.memset(tails, 0.0), [], [tlB])
        ucnt = [0]
        nxt = None
        for ti in range(c.NGT):
            t0 = ti * c.GT
            tile = self.load_tile(hin, hinB, t0, c.GT, c.KC, gain_idx=6 + l, pre1=nxt)
            nxt = self.pass1(hin, hinB, t0 + c.GT, c.GT, c.KC) if ti + 1 < c.NGT else None

            def epi(cc, ps, psB, M, sub, t0=t0):
                Tt = 512; ts0 = t0 + sub * 512
                u = ub[ucnt[0] % 2]; uB = ubB[ucnt[0] % 2]; ucnt[0] += 1
                self.cp("act", u[:, 2:2 + Tt], ps[:, :Tt], [psB], [uB])
                self.cp("dve", u[:, 0:2], tails[:, cc, :], [tlB], [uB])
                self.cp("dve", tails[:, cc, :], u[:, Tt:Tt + 2], [uB], [tlB])
                isg = (cc % 2 == 0)
                dst, dstB = (cg[sub], cgB[sub]) if isg else (cv, cvB)
                self.act(dst[:, :Tt], u[:, 2:2 + Tt], AF.Identity, [uB, cwB], [dstB], scale=cw[:, cc, 2:3], bias=cw[:, cc, 3:4])
                self.stt("dve", dst[:, :Tt], u[:, 1:1 + Tt], cw[:, cc, 1:2], dst[:, :Tt], ALU.mult, ALU.add, [uB, cwB, dstB], [dstB])
                self.stt("dve", dst[:, :Tt], u[:, 0:Tt], cw[:, cc, 0:1], dst[:, :Tt], ALU.mult, ALU.add, [uB, cwB, dstB], [dstB])
                if not isg:
                    st, stB = self.nstg()
                    s16 = st[:].bitcast(BF16)
                    self.act(cg[sub][:, :Tt], cg[sub][:, :Tt], AF.Silu, [cgB[sub]], [cgB[sub]])
                    self.tt("dve", s16[:, :Tt], cg[sub][:, :Tt], cv[:, :Tt], ALU.mult, [cgB[sub], cvB], [stB])
                    self.store_chunk(self.ffa, self.ffaB, cc // 2, ts0, Tt, s16[:, :Tt], stB)

            self.gemm(lambda kc, sub: tile[:, kc, sub * 512:(sub + 1) * 512], self.arenaB, c.KC, c.GT // 512, self.f_wup[l], 2 * FC, 128, epi,
                      side=(nxt["gen"] if nxt else None))
        self._fence_release([cwB, tlB, cvB] + cgB + ubB, [self.gpB])
        n3 = (FC + 2) // 3
        parts = [(k0, min(FC, k0 + n3)) for k0 in range(0, FC, n3)]
        temps = [(self.hT1, self.hT1B), (self.hT2, self.hT2B)]
        chain = [(hin, hinB)] + temps[:len(parts) - 1] + [(hout, houtB)]
        for pi, (k0, k1) in enumerate(parts):
            src_h, src_hB = chain[pi]; dst_h, dst_hB = chain[pi + 1]
            for ti in range(c.NGT):
                t0 = ti * c.GT
                tile = self.load_tile(self.ffa[k0:k1], self.ffaB, t0, c.GT, k1 - k0, src_bf16=True)
                pre, epi = self.epi_residual(src_h, src_hB, dst_h, dst_hB, t0)
                self.gemm(lambda kc, sub: tile[:, kc, sub * 512:(sub + 1) * 512], self.tile_bufs, k1 - k0, c.GT // 512,
                          self.f_wdn[l][:, :, k0:k1, :], c.KC, 128, epi, pre=pre)

    def l1_inproj(self, hin, hinB):
        c = self.cfg
        nq = c.CKW // 128; nv = c.CVW // 128
        CC = (c.ODD_IN + 127) // 128
        nxt = None
        for ti in range(c.NGT):
            t0 = ti * c.GT
            tile = self.load_tile(hin, hinB, t0, c.GT, c.KC, gain_idx=1, pre1=nxt)
            nxt = self.pass1(hin, hinB, t0 + c.GT, c.GT, c.KC) if ti + 1 < c.NGT else None

            def epi(cc, ps, psB, M, sub, t0=t0):
                Tt = 512; ts0 = t0 + sub * 512
                st, stB = self.nstg()
                if cc < nq:
                    self.act(st[:, :Tt], ps[:, :Tt], AF.Copy, [psB], [stB], scale=float(c.CDK) ** -0.5)
                    self.store_chunk(self.zq, self.zqB, cc, ts0, Tt, st[:, :Tt], stB)
                elif cc < 2 * nq:
                    self.cp("act", st[:, :Tt], ps[:, :Tt], [psB], [stB])
                    self.store_chunk(self.zk, self.zkB, cc - nq, ts0, Tt, st[:, :Tt], stB)
                elif cc < 2 * nq + nv:
                    self.cp("dve", st[:, :Tt], ps[:, :Tt], [psB], [stB])
                    self.store_chunk(self.zv, self.zvB, cc - 2 * nq, ts0, Tt, st[:, :Tt], stB)
                elif cc < 2 * nq + 2 * nv:
                    self.act(st[:, :Tt], ps[:, :Tt], AF.Silu, [psB], [stB])
                    self.store_chunk(self.zu, self.zuB, cc - 2 * nq - nv, ts0, Tt, st[:, :Tt], stB)
                else:
                    self.cp("act", st[:16, :Tt], ps[:16, :Tt], [psB], [stB])
                    self.store_chunk(self.zr, self.zrB, 0, ts0, Tt, st[:16, :Tt], stB, P=16)

            self.gemm(lambda kc, sub: tile[:, kc, sub * 512:(sub + 1) * 512], self.arenaB, c.KC, c.GT // 512, self.c_win, CC, 128, epi,
                      side=(nxt["gen"] if nxt else None))

    def l1_gla(self):
        c = self.cfg; T = c.T; DKC = c.DKC; DVC = c.DVC; Tt = c.TT
        NCH = Tt // 64
        ar = self.arena
        o0 = 0

        def carve(n):
            nonlocal o0
            a = ar[:, o0:o0 + n]; o0 += n
            return a
        wa2 = carve(c.CKW)
        ba = carve(c.CKW // 128)
        on = carve(DVC)
        rT = carve(Tt)
        qt = carve(DKC * Tt).rearrange("p (k t) -> p k t", k=DKC)
        kt = carve(DKC * Tt).rearrange("p (k t) -> p k t", k=DKC)
        vt = carve(DVC * Tt).rearrange("p (k t) -> p k t", k=DVC)
        gt = carve(DVC * Tt).rearrange("p (k t) -> p k t", k=DVC)
        bc = carve(DKC * Tt).rearrange("p (k t) -> p k t", k=DKC)
        ebc = carve(DKC * Tt).rearrange("p (k t) -> p k t", k=DKC)
        dec = carve(DKC * NCH).rearrange("p (k n) -> p k n", k=DKC)
        state = carve(DKC * c.CDV).rearrange("p (k e) -> p k e", k=DKC)
        ot = carve(DVC * Tt).rearrange("p (k t) -> p k t", k=DVC)
        kst = carve(DKC * 64).rearrange("p (k t) -> p k t", k=DKC)
        attm = carve(64)
        vtok = carve(c.CDV)
        ksttok = carve(c.CDK)
        assert ar.shape[1] >= o0, (ar.shape, o0)
        names = ["wa2", "ba", "on", "rT", "qt", "kt", "vt", "gt", "bc", "ebc", "dec", "state", "ot", "kst", "attm", "vtok", "ksttok"]
        B = {n: Buf("g_" + n) for n in names}
        fence = [self.arenaB]
        scanm = self.cst[:, 1728:2240]
        mask01 = self.cst[0:64, 1664:1728]
        self.fence(fence)
        self.s.dma("sp", wa2[0:16, :], self.c_wa2, [], [B["wa2"]])
        self.s.dma("sp", ba, self.c_ba, [], [B["ba"]])
        self.s.dma("sp", on, self.c_on, [], [B["on"]])
        self.ts("dve", ba, ba, -1.0, ALU.mult, [B["ba"]], [B["ba"]])
        for hd in range(c.CH):
            for dc in range(DKC):
                self.s.op("pool", lambda E, dc=dc: E.memset(state[:, dc, :], 0.0), [], [B["state"]])
            for ti in range(c.NT):
                t0 = ti * Tt
                self.s.dma("sp", rT[0:16, :], self.zr[0, 0:16, t0:t0 + Tt], [self.zrB], [B["rT"]])
                self.s.dma("sp", qt, self.zq[hd * DKC:(hd + 1) * DKC, :, t0:t0 + Tt].rearrange("k p t -> p k t"), [self.zqB], [B["qt"]])
                self.s.dma("sp", kt, self.zk[hd * DKC:(hd + 1) * DKC, :, t0:t0 + Tt].rearrange("k p t -> p k t"), [self.zkB], [B["kt"]])
                self.s.dma("sp", vt, self.zv[hd * DVC:(hd + 1) * DVC, :, t0:t0 + Tt].rearrange("k p t -> p k t"), [self.zvB], [B["vt"]])
                self.s.dma("sp", gt, self.zu[hd * DVC:(hd + 1) * DVC, :, t0:t0 + Tt].rearrange("k p t -> p k t"), [self.zuB], [B["gt"]])
                for dc in range(DKC):
                    col = hd * DKC + dc
                    ps, psB = self.psX()
                    self.mm(psB, ps[:, :Tt], wa2[0:16, col * 128:(col + 1) * 128], rT[0:16, :], True, True, [B["wa2"], B["rT"]])
                    self.act(ebc[:, dc, :], ps[:, :Tt], AF.Exp, [psB, B["ba"]], [B["ebc"]], scale=-1.0, bias=ba[:, col:col + 1])
                    one_ap = self.epsap(1.0, 128)
                    self.act(ebc[:, dc, :], ebc[:, dc, :], AF.Ln, [B["ebc"], self.epstB], [B["ebc"]], bias=one_ap)
                    self.ts("dve", ebc[:, dc, :], ebc[:, dc, :], -1.0 / 16.0, ALU.mult, [B["ebc"]], [B["ebc"]])
                    self.s.op("dve", lambda E, dc=dc: E.tensor_tensor_scan(out=bc[:, dc, :], data0=scanm[:, :Tt], data1=ebc[:, dc, :], initial=0.0,
                                                                            op0=ALU.mult, op1=ALU.add), [self.cstB, B["ebc"]], [B["bc"]])
                    self.act(dec[:, dc, :], bc[:, dc, 63:Tt:64], AF.Exp, [B["bc"]], [B["dec"]])
                    self.act(ebc[:, dc, :], bc[:, dc, :], AF.Exp, [B["bc"]], [B["ebc"]])
                    self.tt("dve", qt[:, dc, :], qt[:, dc, :], ebc[:, dc, :], ALU.mult, [B["qt"], B["ebc"]], [B["qt"]])
                    self.act(ebc[:, dc, :], bc[:, dc, :], AF.Exp, [B["bc"]], [B["ebc"]], scale=-1.0)
                    self.tt("dve", kt[:, dc, :], kt[:, dc, :], ebc[:, dc, :], ALU.mult, [B["kt"], B["ebc"]], [B["kt"]])
                for ch in range(NCH):
                    cs = slice(ch * 64, (ch + 1) * 64)
                    pa, paB = self.psX()
                    for dc in range(DKC):
                        self.mm(paB, pa[0:64, 0:64], kt[:, dc, cs], qt[:, dc, cs], dc == 0, dc == DKC - 1, [B["kt"], B["qt"]])
                    self.tt("dve", attm[0:64, :], pa[0:64, 0:64], mask01, ALU.mult, [paB, self.cstB], [B["attm"]])
                    pv, pvB = self.psX()
                    for ec in range(DVC):
                        self.tr(pvB, pv[0:64, ec * 128:(ec + 1) * 128], vt[:, ec, cs], [B["vt"]])
                    self.cp("act", vtok[0:64, :], pv[0:64, 0:c.CDV], [pvB], [B["vtok"]])
                    for dc in range(DKC):
                        self.ts("dve", kst[:, dc, :], kt[:, dc, cs], dec[:, dc, ch:ch + 1], ALU.mult, [B["kt"], B["dec"]], [B["kst"]])
                    pk, pkB = self.psX()
                    for dc in range(DKC):
                        self.tr(pkB, pk[0:64, dc * 128:(dc + 1) * 128], kst[:, dc, :], [B["kst"]])
                    self.cp("act", ksttok[0:64, :], pk[0:64, 0:c.CDK], [pkB], [B["ksttok"]])
                    po, poB = self.psG()
                    for ec in range(DVC):
                        osl = po[:, ec * 64:(ec + 1) * 64]
                        self.mm(poB, osl, vtok[0:64, ec * 128:(ec + 1) * 128], attm[0:64, :], True, False, [B["vtok"], B["attm"]])
                        for dc in range(DKC):
                            self.mm(poB, osl, state[:, dc, ec * 128:(ec + 1) * 128], qt[:, dc, cs], False, dc == DKC - 1, [B["state"], B["qt"]])
                    self.cp("act", ot[:, :, cs], po[:, 0:DVC * 64].rearrange("p (k t) -> p k t", k=DVC), [poB], [B["ot"]])
                    for dc in range(DKC):
                        pkv, pkvB = self.psG()
                        self.mm(pkvB, pkv[:, 0:c.CDV], ksttok[0:64, dc * 128:(dc + 1) * 128], vtok[0:64, :], True, True, [B["ksttok"], B["vtok"]])
                        self.stt("dve", state[:, dc, :], state[:, dc, :], dec[:, dc, ch:ch + 1], pkv[:, 0:c.CDV], ALU.mult, ALU.add,
                                 [B["state"], B["dec"], pkvB], [B["state"]])
                psq, psqB = self.psX()
                for ec in range(DVC):
                    sq, sqB = self.sq[ec % 2]
                    self.act(sq[:, :Tt], ot[:, ec, :], AF.Square, [B["ot"]], [sqB])
                    self.mm(psqB, psq[:, :Tt], self.ones[:], sq[:, :Tt], ec == 0, ec == DVC - 1, [sqB, self.onesB])
                rstd, rstdB = self.rsqrt_from(psq[:, :Tt], psqB, 1.0 / c.CDV, EPS, 128, Tt)
                for ec in range(DVC):
                    st, stB = self.nstg()
                    st2, st2B = self.nstg()
                    s16 = st2[:].bitcast(BF16)
                    self.stt("dve", st[:, :Tt], ot[:, ec, :], on[:, ec:ec + 1], rstd[:, :Tt], ALU.mult, ALU.mult, [B["ot"], B["on"], rstdB], [stB])
                    self.tt("dve", s16[:, :Tt], st[:, :Tt], gt[:, ec, :], ALU.mult, [stB, B["gt"]], [st2B])
                    self.store_chunk(self.ab, self.abB, hd * DVC + ec, t0, Tt, s16[:, :Tt], st2B)
        self._fence_release(list(B.values()), [self.arenaB])


def _wblocks(W):
    K, N = W.shape
    CC = (N + 127) // 128
    if CC * 128 != N:
        W = np.concatenate([W, np.zeros((K, CC * 128 - N), W.dtype)], axis=1)
    return np.ascontiguousarray(W.reshape(K // 128, 128, CC, 128).transpose(2, 1, 0, 3))


def _fm(v):
    return np.ascontiguousarray(v.reshape(-1, 128).T)


def _consts():
    cst = np.zeros((128, 2240), np.float32)
    cst[:, 0:128] = np.eye(128, dtype=np.float32)
    i = np.arange(128)[:, None]; j = np.arange(128)[None, :]
    cur = np.where(j >= i, (j - i).astype(np.float32), BIG).astype(np.float32)
    prev = np.where(i >= j, (j + 128 - i).astype(np.float32), BIG).astype(np.float32)
    for r in range(4):
        cst[:, 128 + r * 128:128 + (r + 1) * 128] = cur
        cst[:, 640 + r * 128:640 + (r + 1) * 128] = prev
        cst[:, 1152 + r * 128:1152 + (r + 1) * 128] = prev if r > 0 else BIG
    jj = np.arange(64)[:, None]; ii = np.arange(64)[None, :]
    cst[0:64, 1664:1728] = (ii >= jj).astype(np.float32)
    m = np.ones(512, np.float32); m[::64] = 0.0
    cst[:, 1728:2240] = m[None, :]
    return cst


def prep_shared(cfg, inp):
    c = cfg
    f = lambda a: np.asarray(a, dtype=np.float32)
    d = {}
    gains = np.zeros((128, 8, c.KC), np.float32)
    for i, (nm, l) in enumerate([("mix_norm", 0), ("mix_norm", 1), ("x_norm", 0), ("x_norm", 1), ("x_mem_norm", 0), ("x_mem_norm", 1),
                                 ("f_norm", 0), ("f_norm", 1)]):
        gains[:, i, :] = _fm(f(inp[nm])[l])
    d["gains"] = gains
    hg = np.zeros((128, 6), np.float32)
    hg[:, 0] = f(inp["ab_q_norm"])[0]; hg[:, 1] = f(inp["ab_k_norm"])[0]
    hg[:, 2] = f(inp["x_q_norm"])[0]; hg[:, 3] = f(inp["x_q_norm"])[1]
    hg[:, 4] = f(inp["x_k_norm"])[0]; hg[:, 5] = f(inp["x_k_norm"])[1]
    d["hgain"] = hg
    d["consts"] = _consts()
    t = np.arange(1, c.T + 1, dtype=np.float32)
    d["invc"] = np.stack([np.float32(1.0) / np.minimum(t, np.float32(w)) for w in (2, 4, 8, 16)]).astype(np.float32)
    d["ab_win"] = _wblocks(f(inp["ab_w_in"])[0])
    pw = f(inp["ab_pool_w"])[0]
    d["pool_w"] = np.concatenate([_wblocks(pw[g]) for g in range(4)], axis=0)
    d["pool_sc"] = _fm(f(inp["ab_pool_scale"])[0])
    d["ab_wout"] = _wblocks(f(inp["ab_w_out"])[0])
    d["c_win"] = _wblocks(f(inp["c_w_in"])[0])
    d["c_wa2"] = np.ascontiguousarray(f(inp["c_w_a2"])[0])
    d["c_ba"] = _fm(f(inp["c_b_a"])[0])
    d["c_on"] = _fm(f(inp["c_o_norm"])[0])
    d["c_wout"] = _wblocks(f(inp["c_w_out"])[0])
    wq = f(inp["x_wq"]); wkv = f(inp["x_wkv"]); wo = f(inp["x_wo"])
    d["x_wq"] = np.stack([_wblocks(wq[l]) for l in range(2)])
    d["x_wk"] = np.stack([_wblocks(wkv[l][:, :512]) for l in range(2)])
    d["x_wv"] = np.stack([np.ascontiguousarray(wkv[l][:, 512:].reshape(c.KC, 128, 512).transpose(1, 0, 2)) for l in range(2)])
    d["x_wo"] = np.stack([_wblocks(wo[l]) for l in range(2)])
    wup = f(inp["f_w_up"]); cw = f(inp["f_conv_w"]); cb = f(inp["f_conv_b"]); wdn = f(inp["f_w_down"])
    idx = np.concatenate([np.concatenate([np.arange(j * 128, (j + 1) * 128), c.DFF + np.arange(j * 128, (j + 1) * 128)]) for j in range(c.FC)])
    d["f_wup"] = np.stack([_wblocks(wup[l][:, idx]) for l in range(2)])
    fcw = np.zeros((128, 2, 2 * c.FC, 4), np.float32)
    for l in range(2):
        for k in range(3):
            fcw[:, l, :, k] = _fm(cw[l, k][idx])
        fcw[:, l, :, 3] = _fm(cb[l][idx])
    d["f_cw"] = fcw
    d["f_wdn"] = np.stack([_wblocks(wdn[l]) for l in range(2)])
    return d


def _tfm(a):
    T, D = a.shape
    return np.ascontiguousarray(a.T.reshape(D // 128, 128, T))


_PROG_CACHE = {}


def run(cfg, inputs, stop_after=None, cores=None):
    key = (cfg.D, cfg.T, cfg.CH, stop_after)
    if key not in _PROG_CACHE:
        p = Prog(cfg, stop_after=stop_after)
        p.build()
        _PROG_CACHE[key] = p
    p = _PROG_CACHE[key]
    shared = prep_shared(cfg, inputs)
    x = np.asarray(inputs["x"], dtype=np.float32); mem = np.asarray(inputs["mem"], dtype=np.float32)
    nb = x.shape[0]
    in_maps = []
    for b in range(nb):
        m = dict(shared)
        m["xT"] = _tfm(x[b]); m["memT"] = _tfm(mem[b])
        in_maps.append(m)
    res = run_bass_kernel_spmd(p.nc, in_maps, core_ids=list(range(nb)))
    outs = []
    for b in range(nb):
        o = res.results[b]["outT"]
        outs.append(np.ascontiguousarray(o.reshape(cfg.D, cfg.T).T))
    return np.stack(outs)


def kernel(**inputs):
    cfg = Cfg(4096, 4096, 8)
    return run(cfg, inputs).astype(np.float32)
```

```python
import math
import numpy as np
import concourse.bass as bass
import concourse.mybir as mybir
from concourse.bass_utils import run_bass_kernel_spmd

F32 = mybir.dt.float32
BF16 = mybir.dt.bfloat16
AF = mybir.ActivationFunctionType
ALU = mybir.AluOpType
EPS = 1e-6
BIG = 1.0e5


class Cfg:
    def __init__(self, D=4096, T=4096, CH=8):
        self.D = D; self.T = T; self.KC = D // 128
        self.AH = D // 256; self.AW = self.AH * 128
        self.BW = D - self.AW; self.BG = self.BW // 4; self.BGC = self.BG // 128
        self.EVEN_IN = 7 * self.AW + self.BW
        self.CH = CH; self.CKW = D // 2; self.CVW = D
        self.CDK = self.CKW // CH; self.CDV = self.CVW // CH
        self.DKC = self.CDK // 128; self.DVC = self.CDV // 128
        self.ODD_IN = 2 * self.CKW + 2 * self.CVW + 16
        self.DFF = ((8 * D // 3 + 255) // 256) * 256; self.FC = self.DFF // 128
        self.XH = 4; self.XW = 512; self.ML = 256
        self.TT = 512
        self.NT = T // self.TT
        self.GT = 1024
        self.NGT = T // self.GT


class Buf:
    __slots__ = ("name", "w", "r")

    def __init__(self, name):
        self.name = name; self.w = None; self.r = {}


class Sched:
    MAXC = 30000

    def __init__(self, nc):
        self.nc = nc
        self.E = {"pe": nc.tensor, "act": nc.scalar, "dve": nc.vector, "pool": nc.gpsimd, "sp": nc.sync}
        self.sems = []; self.owner = []
        self.cur = {}
        self.seen = {e: {} for e in self.E}
        self.dq = {}
        self.ninst = 0

    def newsem(self, name, owner):
        h = self.nc.alloc_semaphore(name)
        self.sems.append(h); self.owner.append(owner)
        return len(self.sems) - 1

    def _wait(self, e, si, val):
        if self.seen[e].get(si, 0) >= val:
            return
        self.E[e].wait_ge(self.sems[si], val)
        self.seen[e][si] = val

    def _sync(self, e, reads, writes):
        deps = {}
        for b in reads:
            if b.w is not None:
                si, v = b.w
                if not (e == "pe" and self.owner[si] == "pe"):
                    deps[si] = max(deps.get(si, 0), v)
        for b in writes:
            if b.w is not None:
                si, v = b.w
                if self.owner[si] != e:
                    deps[si] = max(deps.get(si, 0), v)
            for si, v in b.r.items():
                if self.owner[si] != e:
                    deps[si] = max(deps.get(si, 0), v)
        for si, v in deps.items():
            self._wait(e, si, v)

    def _tick(self, e):
        c = self.cur.get(e)
        if c is None or c[1] >= self.MAXC:
            c = [self.newsem("c_%s_%d" % (e, len(self.sems)), e), 0]
            self.cur[e] = c
        c[1] += 1
        return c[0], c[1]

    def op(self, e, emit, reads=(), writes=()):
        self._sync(e, reads, writes)
        inst = emit(self.E[e])
        si, v = self._tick(e)
        inst.then_inc(self.sems[si], 1)
        for b in reads:
            b.r[si] = v
        for b in writes:
            b.w = (si, v); b.r = {}
        self.ninst += 1
        return inst

    def dma(self, q, out, in_, reads=(), writes=()):
        self._sync(q, reads, writes)
        ring = self.dq.get(q)
        if ring is None:
            ring = {"s": [[self.newsem("d_%s_%d" % (q, i), "dma_" + q), 0] for i in range(8)], "p": 0}
            self.dq[q] = ring
        p = ring["p"]; ring["p"] = (p + 1) % 8
        ent = ring["s"][p]
        if ent[1] >= self.MAXC:
            self._wait(q, ent[0], ent[1])
            ent = [self.newsem("d_%s_%d" % (q, len(self.sems)), "dma_" + q), 0]
            ring["s"][p] = ent
        self._wait(q, ent[0], ent[1])
        inst = self.E[q].dma_start(out=out, in_=in_)
        ent[1] += 16
        inst.then_inc(self.sems[ent[0]], 16)
        for b in reads:
            b.r[ent[0]] = ent[1]
        for b in writes:
            b.w = (ent[0], ent[1]); b.r = {}
        self.ninst += 1

    def wait_all(self, e, bufs):
        self._sync(e, bufs, ())


class Prog:
    def __init__(self, cfg, stop_after=None):
        self.cfg = cfg
        self.stop_after = stop_after
        nc = bass.Bass("TRN2", target_bir_lowering=False)
        self.nc = nc
        self.s = Sched(nc)
        self.din = {}
        self._alloc()

    def dI(self, name, shape):
        ap = self.nc.dram_tensor(name, list(shape), F32, kind="ExternalInput").ap()
        self.din[name] = tuple(shape)
        return ap

    def dS(self, name, shape, dt=F32):
        return self.nc.dram_tensor(name, list(shape), dt, kind="Internal").ap(), Buf(name)

    def sb(self, name, shape):
        return self.nc.alloc_sbuf_tensor(name, list(shape), F32), Buf(name)

    def _alloc(self):
        c = self.cfg; nc = self.nc
        KC, T, FC = c.KC, c.T, c.FC
        self.xT = self.dI("xT", [KC, 128, T]); self.xTB = Buf("xT")
        self.memT = self.dI("memT", [KC, 128, c.ML])
        self.gains = self.dI("gains", [128, 8, KC])
        self.hgain = self.dI("hgain", [128, 6])
        self.consts = self.dI("consts", [128, 2240])
        self.invc = self.dI("invc", [4, T])
        self.ab_win = self.dI("ab_win", [c.EVEN_IN // 128, 128, KC, 128])
        self.pool_w = self.dI("pool_w", [4 * c.BGC, 128, c.BGC, 128])
        self.pool_sc = self.dI("pool_sc", [128, c.BW // 128])
        self.ab_wout = self.dI("ab_wout", [KC, 128, KC, 128])
        self.c_win = self.dI("c_win", [(c.ODD_IN + 127) // 128, 128, KC, 128])
        self.c_wa2 = self.dI("c_wa2", [16, c.CKW])
        self.c_ba = self.dI("c_ba", [128, c.CKW // 128])
        self.c_on = self.dI("c_on", [128, c.DVC])
        self.c_wout = self.dI("c_wout", [KC, 128, KC, 128])
        self.x_wq = self.dI("x_wq", [2, 4, 128, KC, 128])
        self.x_wk = self.dI("x_wk", [2, 4, 128, KC, 128])
        self.x_wv = self.dI("x_wv", [2, 128, KC, 512])
        self.x_wo = self.dI("x_wo", [2, KC, 128, 4, 128])
        self.f_wup = self.dI("f_wup", [2, 2 * FC, 128, KC, 128])
        self.f_cw = self.dI("f_cw", [128, 2, 2 * FC, 4])
        self.f_wdn = self.dI("f_wdn", [2, KC, 128, FC, 128])
        self.outT = nc.dram_tensor("outT", [KC, 128, T], F32, kind="ExternalOutput").ap()
        self.outTB = Buf("outT")
        self.hA, self.hAB = self.dS("hA", [KC, 128, T])
        self.hB, self.hBB = self.dS("hB", [KC, 128, T])
        nzq = max(3 * c.AH, c.CKW // 128)
        self.zq, self.zqB = self.dS("zq", [nzq, 128, T])
        self.zk, self.zkB = self.dS("zk", [nzq, 128, T])
        self.zv, self.zvB = self.dS("zv", [max(c.AH, c.CVW // 128), 128, T])
        self.zu, self.zuB = self.dS("zu", [max(c.BW // 128, c.CVW // 128), 128, T])
        self.zr, self.zrB = self.dS("zr", [1, 128, T])
        self.ab, self.abB = self.dS("ab", [KC, 128, T], BF16)
        self.ffa, self.ffaB = self.dS("ffa", [FC, 128, T], BF16)
        self.hT1, self.hT1B = self.dS("hT1", [KC, 128, T])
        self.hT2, self.hT2B = self.dS("hT2", [KC, 128, T])
        self.cst, self.cstB = self.sb("cst", [128, 2240])
        self.ones, self.onesB = self.sb("ones", [128, 128])
        self.gn, self.gnB = self.sb("gn", [128, 8, KC])
        self.hg, self.hgB = self.sb("hg", [128, 6])
        arena_elems = max(KC * 512, 4 * T, 16384)
        self.arena, self.arenaB = self.sb("arena", [128, arena_elems])
        self.KP = 16
        self.wb = []
        for i in range(1):
            self.wb.append(self.sb("wb%d" % i, [128, self.KP * 128]))
        self.wctr = 0
        self.rs = [self.sb("rs%d" % i, [128, 1024]) for i in range(2)]
        self.rsc = 0
        self.NW16 = 6
        self.wb16 = []
        for i in range(self.NW16):
            t = self.nc.alloc_sbuf_tensor("wbh%d" % i, [128, self.KP * 128], BF16)
            self.wb16.append((t, Buf("wbh%d" % i)))
        self.w16ctr = 0
        self.ld = [self.sb("ld%d" % i, [128, 1024]) for i in range(5)]
        self.apB = [Buf("ap%d" % i) for i in range(6)]
        self.ldc = 0
        self.ones16 = self.nc.alloc_sbuf_tensor("ones16", [128, 128], BF16); self.ones16B = Buf("ones16")
        self.sq = [self.sb("sq%d" % i, [128, 1024]) for i in range(2)]
        self.t1 = [self.sb("t1_%d" % i, [128, 1024]) for i in range(2)]
        self.t2 = [self.sb("t2_%d" % i, [128, 1024]) for i in range(2)]
        self.stg = [self.sb("stg%d" % i, [128, 512]) for i in range(4)]
        self.stgc = 0
        self.res = [self.sb("res%d" % i, [128, 512]) for i in range(4)]
        self.resc = 0
        self.gp, self.gpB = self.sb("gp", [128, 8448])
        self.ps = []
        for i in range(8):
            self.ps.append((nc.alloc_psum_tensor("ps%d" % i, [128, 512], F32), Buf("ps%d" % i)))
        self.psGc = 0; self.psXc = 0

    def psG(self):
        p = self.ps[self.psGc % 4]; self.psGc += 1
        return p

    def psX(self):
        p = self.ps[4 + self.psXc % 4]; self.psXc += 1
        return p

    def nstg(self):
        p = self.stg[self.stgc % 4]; self.stgc += 1
        return p

    def nres(self):
        p = self.res[self.resc % 4]; self.resc += 1
        return p

    def act(self, out, in_, func, reads, writes, scale=None, bias=None):
        kw = {}
        if scale is not None:
            kw["scale"] = scale
        if bias is not None:
            kw["bias"] = bias
        return self.s.op("act", lambda E: E.activation(out=out, in_=in_, func=func, **kw), reads, writes)

    def mm(self, psB, out, lhsT, rhs, start, stop, reads):
        return self.s.op("pe", lambda E: E.matmul(out, lhsT=lhsT, rhs=rhs, start=start, stop=stop), reads, [psB])

    def tr(self, psB, out, in_, reads):
        ident = self.cst[:, 0:128]
        return self.s.op("pe", lambda E: E.transpose(out, in_, ident[: in_.shape[0], : in_.shape[0]]), list(reads) + [self.cstB], [psB])

    def stt(self, eng, out, in0, scalar, in1, op0, op1, reads, writes):
        return self.s.op(eng, lambda E: E.scalar_tensor_tensor(out=out, in0=in0, scalar=scalar, in1=in1, op0=op0, op1=op1), reads, writes)

    def tt(self, eng, out, in0, in1, op, reads, writes):
        return self.s.op(eng, lambda E: E.tensor_tensor(out=out, in0=in0, in1=in1, op=op), reads, writes)

    def ts(self, eng, out, in0, s1, op0, reads, writes, s2=None, op1=None):
        if op1 is None:
            return self.s.op(eng, lambda E: E.tensor_scalar(out=out, in0=in0, scalar1=s1, scalar2=None, op0=op0), reads, writes)
        return self.s.op(eng, lambda E: E.tensor_scalar(out=out, in0=in0, scalar1=s1, scalar2=s2, op0=op0, op1=op1), reads, writes)

    def cp(self, eng, out, in_, reads, writes):
        if eng == "act":
            return self.s.op("act", lambda E: E.copy(out=out, in_=in_), reads, writes)
        return self.s.op(eng, lambda E: E.tensor_copy(out=out, in_=in_), reads, writes)

    def recip(self, out, in_, reads, writes):
        return self.s.op("dve", lambda E: E.reciprocal(out=out, in_=in_), reads, writes)

    def rsqrt_from(self, ps_ap, psB, mul, add, P, N):
        t2, t2B = self.t2[0]; self.t2.reverse()
        bias_ap = self.epsap(add, P)
        self.act(t2[:P, :N], ps_ap, AF.Ln, [psB, self.epstB], [t2B], scale=mul, bias=bias_ap)
        self.act(t2[:P, :N], t2[:P, :N], AF.Exp, [t2B], [t2B], scale=-0.5)
        return t2, t2B

    def epsap(self, val, P):
        key = float(val)
        if not hasattr(self, "_epsmap"):
            self._epsmap = {}
            self.epst, self.epstB = self.sb("epst", [128, 16])
        if key not in self._epsmap:
            j = len(self._epsmap)
            self.s.op("pool", lambda E: E.memset(self.epst[:, j:j + 1], key), [], [self.epstB])
            self._epsmap[key] = j
        j = self._epsmap[key]
        return self.epst[:P, j:j + 1]

    def prologue(self):
        s = self.s
        s.dma("sp", self.cst[:], self.consts, [], [self.cstB])
        s.dma("sp", self.gn[:], self.gains, [], [self.gnB])
        s.dma("sp", self.hg[:], self.hgain, [], [self.hgB])
        s.op("pool", lambda E: E.memset(self.ones[:], 1.0), [], [self.onesB])

    def nld(self):
        p = self.ld[self.ldc % 5]; self.ldc += 1
        return p

    def load_tile(self, src, srcB, t0, Tt, KCn, gain_idx=None, src_bf16=False, pre1=None):
        a16 = self.arena[:].bitcast(BF16)
        tile = a16[:, 0:KCn * Tt].rearrange("p (k t) -> p k t", k=KCn)
        QS = ("sp", "act", "pool")
        self._fence_release(self.apB, [self.arenaB])
        self.tile_bufs = [self.arenaB]
        if src_bf16:
            step = max(1, (KCn + 5) // 6)
            for q in QS:
                self.s._sync(q, [], [self.arenaB])
            used = []
            for qi, k0 in enumerate(range(0, KCn, step)):
                k1 = min(KCn, k0 + step)
                self.s.dma(QS[qi % 3], tile[:, k0:k1, :], src[k0:k1, :, t0:t0 + Tt].rearrange("k p t -> p k t"), [srcB], [self.apB[qi]])
                used.append(self.apB[qi])
            self.tile_bufs = used
            return tile
        if gain_idx is None:
            for kc in range(KCn):
                l, lB = self.nld()
                self.s.dma("sp", l[:, :Tt], src[kc, :, t0:t0 + Tt], [srcB], [lB])
                self.cp("act" if kc % 2 else "pool", tile[:, kc, :], l[:, :Tt], [lB], [self.arenaB])
            return tile
        if pre1 is None:
            pre1 = self.pass1(src, srcB, t0, Tt, KCn)
            for _ in pre1["gen"]:
                pass
        rstd, rstdB = pre1["rstd"]
        for kc in range(KCn):
            l, lB = self.nld()
            self.s.dma(QS[kc % 3], l[:, :Tt], src[kc, :, t0:t0 + Tt], [srcB], [lB])
            self.stt("dve", tile[:, kc, :], l[:, :Tt], self.gn[:, gain_idx, kc:kc + 1], rstd[:, :Tt], ALU.mult, ALU.mult,
                     [lB, self.gnB, rstdB], [self.arenaB])
        return tile

    def pass1(self, src, srcB, t0, Tt, KCn):
        rstd, rstdB = self.rs[self.rsc % 2]; self.rsc += 1

        def gen():
            accs = [self.t1[0], self.t1[1]]
            lds = {}

            def issue(kc):
                l, lB = self.nld()
                self.s.dma("sp", l[:, :Tt], src[kc, :, t0:t0 + Tt], [srcB], [lB])
                lds[kc] = (l, lB)
            issue(0)
            if KCn > 1:
                issue(1)
            for kc in range(KCn):
                if kc + 2 < KCn:
                    issue(kc + 2)
                l, lB = lds.pop(kc)
                acc, accB = accs[kc % 2]
                if kc < 2:
                    self.act(acc[:, :Tt], l[:, :Tt], AF.Square, [lB], [accB])
                else:
                    self.act(l[:, :Tt], l[:, :Tt], AF.Square, [lB], [lB])
                    self.tt("pool" if kc % 2 == 0 else "dve", acc[:, :Tt], acc[:, :Tt], l[:, :Tt], ALU.add, [accB, lB], [accB])
                yield
            nacc = min(2, KCn)
            for s0 in range(0, Tt, 512):
                wd = min(512, Tt - s0)
                psq, psqB = self.psX()
                for ai in range(nacc):
                    acc, accB = accs[ai]
                    self.mm(psqB, psq[:, :wd], self.ones[:], acc[:, s0:s0 + wd], ai == 0, ai == nacc - 1, [accB, self.onesB])
                bias_ap = self.epsap(EPS, 128)
                self.act(rstd[:, s0:s0 + wd], psq[:, :wd], AF.Ln, [psqB, self.epstB], [rstdB], scale=1.0 / (KCn * 128), bias=bias_ap)
                self.act(rstd[:, s0:s0 + wd], rstd[:, s0:s0 + wd], AF.Exp, [rstdB], [rstdB], scale=-0.5)
            yield
        return {"gen": gen(), "rstd": (rstd, rstdB)}

    def gemm_begin(self, KCtot, w_dram, CC, cc_list=None, lowp=True):
        KP = self.KP
        pieces = [(k0, min(k0 + KP, KCtot)) for k0 in range(0, KCtot, KP)]
        ccs = list(range(CC)) if cc_list is None else cc_list
        blocks = [(cc, k0, k1) for cc in ccs for (k0, k1) in pieces]
        PF = (self.NW16 - 1) if lowp else 0
        loaded = {}

        def load(i):
            cc, k0, k1 = blocks[i]
            if lowp:
                w16t, w16B = self.wb16[self.w16ctr % self.NW16]; self.w16ctr += 1
                wv = w16t[:, 0:(k1 - k0) * 128].rearrange("p (k m) -> p k m", m=128)
                self.s.dma("pool", wv, w_dram[cc, :, k0:k1, :], [], [w16B])
                loaded[i] = (wv, w16B)
            else:
                wbt, wbB = self.wb[0]; self.wctr += 1
                wv = wbt[:, 0:(k1 - k0) * 128].rearrange("p (k m) -> p k m", m=128)
                self.s.dma("sp", wv, w_dram[cc, :, k0:k1, :], [], [wbB])
                loaded[i] = (wv, wbB)

        for i in range(min(PF, len(blocks))):
            load(i)
        return {"blocks": blocks, "loaded": loaded, "load": load, "PF": PF}

    def gemm(self, rhs_fn, rhsB, KCtot, nsub, w_dram, CC, m_last, epi, pre=None, cc_list=None, lowp=True, sw=512, epi_a=None, side=None, begun=None):
        rhsL = list(rhsB) if isinstance(rhsB, (list, tuple)) else [rhsB]
        if begun is None:
            begun = self.gemm_begin(KCtot, w_dram, CC, cc_list, lowp)
        blocks = begun["blocks"]; loaded = begun["loaded"]; load = begun["load"]; PF = begun["PF"]
        pss = None
        pend_epi = None
        for i, (cc, k0, k1) in enumerate(blocks):
            if i + PF < len(blocks):
                load(i + PF)
            if side is not None and k0 == 0:
                next(side, None)
            if k0 == 0:
                pss = [self.psG() for _ in range(nsub)]
                if pre is not None:
                    for sub in range(nsub):
                        pre(cc, sub)
            M = m_last if cc == CC - 1 else 128
            wv, wbB = loaded.pop(i)
            for kc in range(k0, k1):
                for sub in range(nsub):
                    ps, psB = pss[sub]
                    self.mm(psB, ps[:M, :sw], wv[:, kc - k0, :M], rhs_fn(kc, sub), kc == 0, kc == KCtot - 1, [wbB] + rhsL)
            if k0 == 0 and pend_epi is not None:
                pend_epi(); pend_epi = None
            if k1 == KCtot:
                if epi_a is not None:
                    for sub in range(nsub):
                        ps, psB = pss[sub]
                        epi_a(cc, ps, psB, M, sub)
                cur = (cc, list(pss), M)

                def run_epi(cur=cur):
                    cc_, pss_, M_ = cur
                    for sub in range(nsub):
                        ps, psB = pss_[sub]
                        epi(cc_, ps, psB, M_, sub)
                pend_epi = run_epi
        if pend_epi is not None:
            pend_epi()
        if side is not None:
            for _ in side:
                pass

    def store_chunk(self, dst, dstB, ci, t0, Tt, src_ap, srcB, P=128):
        self.s.dma("sp", dst[ci, 0:P, t0:t0 + Tt], src_ap, [srcB], [dstB])

    def epi_store(self, dst, dstB, t0, Tt, ci_fn=lambda cc: cc, scale=None, eng="act"):
        def epi(cc, ps, psB, M):
            st, stB = self.nstg()
            if scale is not None:
                self.act(st[:M, :Tt], ps[:M, :Tt], AF.Copy, [psB], [stB], scale=scale)
            elif eng == "act":
                self.cp("act", st[:M, :Tt], ps[:M, :Tt], [psB], [stB])
            else:
                self.cp("dve", st[:M, :Tt], ps[:M, :Tt], [psB], [stB])
            self.store_chunk(dst, dstB, ci_fn(cc), t0, Tt, st[:M, :Tt], stB, P=M)
        return epi

    def epi_residual(self, hin, hinB, hout, houtB, t0):
        pend = {}

        def pre(cc, sub):
            r, rB = self.nres()
            ts0 = t0 + sub * 512
            self.s.dma("sp", r[:, :512], hin[cc, :, ts0:ts0 + 512], [hinB], [rB])
            pend[(cc, sub)] = (r, rB)

        def epi(cc, ps, psB, M, sub):
            r, rB = pend.pop((cc, sub))
            st, stB = self.nstg()
            ts0 = t0 + sub * 512
            self.tt("dve", st[:, :512], ps[:, :512], r[:, :512], ALU.add, [psB, rB], [stB])
            self.store_chunk(hout, houtB, cc, ts0, 512, st[:, :512], stB)
        return pre, epi

    def qk_a(self, ps, psB, Tt):
        if not hasattr(self, "_sqslot"):
            self._sqslot = 0
        i = self._sqslot % 4; self._sqslot += 1
        sqt, sqB = self.sq[i // 2]
        sq = sqt[:, (i % 2) * 512:(i % 2) * 512 + Tt]
        self.act(sq, ps[:, :Tt], AF.Square, [psB], [sqB])
        return sq, sqB

    def qk_b(self, ps, psB, Tt, sq, sqB, gain_ap, gainB, fold, out_ap, outB):
        p2, p2B = self.psX()
        self.mm(p2B, p2[:, :Tt], self.ones[:], sq, True, True, [sqB, self.onesB])
        rstd, rstdB = self.rsqrt_from(p2[:, :Tt], p2B, 1.0 / (128.0 * fold * fold), EPS / (fold * fold), 128, Tt)
        self.stt("dve", out_ap, ps[:, :Tt], gain_ap, rstd[:, :Tt], ALU.mult, ALU.mult, [psB, gainB, rstdB], [outB])

    def qknorm(self, ps, psB, Tt, gain_ap, gainB, fold, out_ap, outB):
        sq, sqB = self.qk_a(ps, psB, Tt)
        self.qk_b(ps, psB, Tt, sq, sqB, gain_ap, gainB, fold, out_ap, outB)

    def build(self):
        c = self.cfg
        self.prologue()
        stages = [
            ("l0_inproj", lambda: self.l0_inproj(self.xT, self.xTB)),
            ("l0_attn", self.l0_attn),
            ("l0_pool", self.l0_pool),
            ("l0_out", lambda: self.outproj(self.ab_wout, self.xT, self.xTB, self.hA, self.hAB)),
            ("l0_x", lambda: self.xattn(0, self.hA, self.hAB, self.hB, self.hBB)),
            ("l0_f", lambda: self.ffn(0, self.hB, self.hBB, self.hA, self.hAB)),
            ("l1_inproj", lambda: self.l1_inproj(self.hA, self.hAB)),
            ("l1_gla", self.l1_gla),
            ("l1_out", lambda: self.outproj(self.c_wout, self.hA, self.hAB, self.hB, self.hBB)),
            ("l1_x", lambda: self.xattn(1, self.hB, self.hBB, self.hA, self.hAB)),
            ("l1_f", lambda: self.ffn(1, self.hA, self.hAB, self.hB, self.hBB)),
        ]
        final = (self.hB, self.hBB)
        dbg = {"l0_inproj": (self.zq, self.zqB), "l0_attn": (self.ab, self.abB), "l0_pool": (self.ab, self.abB),
               "l0_out": (self.hA, self.hAB), "l0_x": (self.hB, self.hBB), "l0_f": (self.hA, self.hAB),
               "l1_inproj": (self.zk, self.zkB), "l1_gla": (self.ab, self.abB), "l1_out": (self.hB, self.hBB),
               "l1_x": (self.hA, self.hAB), "l1_f": (self.hB, self.hBB)}
        for name, fn in stages:
            fn()
            if self.stop_after == name:
                final = dbg[name]
                break
        self.copy_out(*final)
        return self.nc

    def copy_out(self, src, srcB):
        c = self.cfg
        n = min(src.shape[0], c.KC)
        step = max(1, n // 8)
        for k0 in range(0, n, step):
            k1 = min(n, k0 + step)
            self.s.dma("sp", self.outT[k0:k1], src[k0:k1], [srcB], [self.outTB])
        for e in ("sp",):
            self.s.wait_all(e, [self.outTB])
        for q, ring in self.s.dq.items():
            for ent in ring["s"]:
                self.s._wait("sp", ent[0], ent[1])

    def l0_inproj(self, hin, hinB):
        c = self.cfg
        AH = c.AH
        nq = 3 * AH
        sqp = {}
        nxt = None
        for ti in range(c.NGT):
            t0 = ti * c.GT
            bg = self.gemm_begin(c.KC, self.ab_win, c.EVEN_IN // 128)
            tile = self.load_tile(hin, hinB, t0, c.GT, c.KC, gain_idx=0, pre1=nxt)
            nxt = self.pass1(hin, hinB, t0 + c.GT, c.GT, c.KC) if ti + 1 < c.NGT else None

            def epi(cc, ps, psB, M, sub, t0=t0):
                ts0 = t0 + sub * 512; Tt = 512
                st, stB = self.nstg()
                if cc < nq:
                    sq, sqB = sqp.pop((cc, sub))
                    self.qk_b(ps, psB, Tt, sq, sqB, self.hg[:, 0:1], self.hgB, 128.0 ** -0.5, st[:, :Tt], stB)
                    self.store_chunk(self.zq, self.zqB, cc, ts0, Tt, st[:, :Tt], stB)
                elif cc < 2 * nq:
                    sq, sqB = sqp.pop((cc, sub))
                    self.qk_b(ps, psB, Tt, sq, sqB, self.hg[:, 1:2], self.hgB, 1.0, st[:, :Tt], stB)
                    self.store_chunk(self.zk, self.zkB, cc - nq, ts0, Tt, st[:, :Tt], stB)
                elif cc < 2 * nq + AH:
                    self.cp("act", st[:, :Tt], ps[:, :Tt], [psB], [stB])
                    self.store_chunk(self.zv, self.zvB, cc - 2 * nq, ts0, Tt, st[:, :Tt], stB)
                else:
                    self.cp("dve", st[:, :Tt], ps[:, :Tt], [psB], [stB])
                    self.store_chunk(self.zu, self.zuB, cc - 2 * nq - AH, ts0, Tt, st[:, :Tt], stB)

            def epi_a(cc, ps, psB, M, sub):
                if cc < 2 * nq:
                    sqp[(cc, sub)] = self.qk_a(ps, psB, 512)

            self.gemm(lambda kc, sub: tile[:, kc, sub * 512:(sub + 1) * 512], self.arenaB, c.KC, c.GT // 512, self.ab_win,
                      c.EVEN_IN // 128, 128, epi, epi_a=epi_a, side=(nxt["gen"] if nxt else None), begun=bg)

    def l0_attn(self):
        c = self.cfg; T = c.T; AH = c.AH
        ar = self.arena
        assert ar.shape[1] >= 4 * T
        qb = ar[:, 0:T]; kb = ar[:, T:2 * T]; vT = ar[:, 2 * T:3 * T]; accO = ar[:, 3 * T:4 * T]
        gp = self.gp
        assert gp.shape[1] >= T + 32 * 128
        accD = gp[:, 0:T]
        vtok = gp[:, T:T + 32 * 128].rearrange("p (b e) -> p b e", e=128)
        qB, kB, vB, aOB, aDB, vtB = Buf("aq"), Buf("ak"), Buf("av"), Buf("aO"), Buf("aD"), Buf("avt")
        fenceR = [self.arenaB, self.gpB]
        relcur = self.cst[:, 128:640]; relprev = self.cst[:, 640:1152]; relprevF = self.cst[:, 1152:1664]
        n_sl = 3 * AH
        slopes = [2.0 ** (-8.0 * (i + 1) / n_sl) for i in range(n_sl)]
        branches = [(128, 1), (512, 4), (2048, 16)]
        self.fence(fenceR)
        for hd in range(AH):
            self.s.dma("sp", vT, self.zv[hd, :, :], [self.zvB], [vB])
            for g, (win, d) in enumerate(branches):
                cneg = -slopes[g * AH + hd] * d
                self.s.dma("sp", qb, self.zq[g * AH + hd, :, :], [self.zqB], [qB])
                self.s.dma("sp", kb, self.zk[g * AH + hd, :, :], [self.zkB], [kB])
                NB = T // d // 128
                G = min(4, NB)
                nblk = d * NB
                for b0 in range(0, nblk, 4):
                    pt, ptB = self.psX()
                    for j in range(4):
                        blk = b0 + j; r = blk // NB; n = blk % NB
                        st0 = r + d * 128 * n
                        self.tr(ptB, pt[:, j * 128:(j + 1) * 128], vT[:, st0:st0 + d * 127 + 1:d], [vB])
                    self.cp("act", vtok[:, b0:b0 + 4, :], pt[:, :].rearrange("p (b e) -> p b e", e=128), [ptB], [vtB])
                for r in range(d):
                    for n0 in range(0, NB, G):
                        W = G * 128
                        sc, scB = self.psX(); sp_, spB = self.psX()
                        for j in range(G):
                            n = n0 + j
                            st0 = r + d * 128 * n
                            qs = qb[:, st0:st0 + d * 127 + 1:d]
                            kcur = kb[:, st0:st0 + d * 127 + 1:d]
                            self.mm(scB, sc[:, j * 128:(j + 1) * 128], kcur, qs, True, True, [kB, qB])
                            if n > 0:
                                stp = r + d * 128 * (n - 1)
                                kprev = kb[:, stp:stp + d * 127 + 1:d]
                            else:
                                kprev = kcur
                            self.mm(spB, sp_[:, j * 128:(j + 1) * 128], kprev, qs, True, True, [kB, qB])
                        pc, pcB = self.nstg(); pp, ppB = self.nstg()
                        self.stt("dve", pc[:, :W], relcur[:, :W], cneg, sc[:, :W], ALU.mult, ALU.add, [self.cstB, scB], [pcB])
                        rp = relprevF if n0 == 0 else relprev
                        self.stt("dve", pp[:, :W], rp[:, :W], cneg, sp_[:, :W], ALU.mult, ALU.add, [self.cstB, spB], [ppB])
                        self.act(pc[:, :W], pc[:, :W], AF.Exp, [pcB], [pcB])
                        self.act(pp[:, :W], pp[:, :W], AF.Exp, [ppB], [ppB])
                        po, poB = self.psG(); pd, pdB = self.psG()
                        for j in range(G):
                            n = n0 + j
                            blk = r * NB + n
                            self.mm(poB, po[:, j * 128:(j + 1) * 128], vtok[:, blk, :], pc[:, j * 128:(j + 1) * 128], True, n == 0, [vtB, pcB])
                            if n > 0:
                                self.mm(poB, po[:, j * 128:(j + 1) * 128], vtok[:, blk - 1, :], pp[:, j * 128:(j + 1) * 128], False, True, [vtB, ppB])
                            self.mm(pdB, pd[:, j * 128:(j + 1) * 128], self.ones[:], pc[:, j * 128:(j + 1) * 128], True, False, [self.onesB, pcB])
                            self.mm(pdB, pd[:, j * 128:(j + 1) * 128], self.ones[:], pp[:, j * 128:(j + 1) * 128], False, True, [self.onesB, ppB])
                        st0 = r + d * 128 * n0
                        osl = accO[:, st0:st0 + d * (W - 1) + 1:d]; dsl = accD[:, st0:st0 + d * (W - 1) + 1:d]
                        if g == 0:
                            self.cp("act", osl, po[:, :W], [poB], [aOB])
                            self.cp("dve", dsl, pd[:, :W], [pdB], [aDB])
                        else:
                            self.tt("dve", osl, po[:, :W], osl, ALU.add, [poB, aOB], [aOB])
                            self.tt("dve", dsl, pd[:, :W], dsl, ALU.add, [pdB, aDB], [aDB])
            self.recip(accD, accD, [aDB], [aDB])
            o16 = qb.bitcast(BF16)[:, 0:T]
            self.tt("dve", o16, accO, accD, ALU.mult, [aOB, aDB], [qB])
            self.s.dma("sp", self.ab[hd, :, :], o16, [qB], [self.abB])
        self._fence_release([qB, kB, vB, aOB, aDB, vtB], [self.arenaB, self.gpB])

    def fence(self, wholes):
        self._fence_release(self.apB, [self.arenaB])
        for e in ("act", "dve", "pool", "sp", "pe"):
            self.s._sync(e, [], wholes)

    def _fence_release(self, subs, wholes):
        for wB in wholes:
            for b in subs:
                if b.w is not None:
                    si, v = b.w
                    wB.r[si] = max(wB.r.get(si, 0), v)
                for si, v in b.r.items():
                    wB.r[si] = max(wB.r.get(si, 0), v)

    def l0_pool(self):
        c = self.cfg; T = c.T
        PADL = 16
        L = min(T, 2048)
        NH = T // L
        W = PADL + L
        ar = self.arena; gp = self.gp
        assert gp.shape[1] >= 3 * W + L
        ub = [gp[:, i * W:(i + 1) * W] for i in range(3)]
        ubB = [Buf("pu%d" % i) for i in range(3)]
        invt = gp[:, 3 * W:3 * W + L]; invB = Buf("invt")
        pooled = ar[:, 0:c.BGC * T].rearrange("p (j t) -> p j t", j=c.BGC)
        pooledB = Buf("pooled")
        psc, pscB = self.sb("psc", [128, c.BW // 128])
        self.s.dma("sp", psc[:], self.pool_sc, [], [pscB])
        self.fence([self.arenaB, self.gpB])
        for gi, win in enumerate((2, 4, 8, 16)):
            for hh in range(NH):
                t0 = hh * L
                self.s.dma("sp", invt, self.invc[gi:gi + 1, t0:t0 + L].partition_broadcast(128), [], [invB])
                for j in range(c.BGC):
                    ci = gi * c.BGC + j
                    if hh == 0:
                        self.s.op("pool", lambda E: E.memset(ub[0][:, 0:PADL], 0.0), [], [ubB[0]])
                        self.s.dma("sp", ub[0][:, PADL:], self.zu[ci, :, t0:t0 + L], [self.zuB], [ubB[0]])
                    else:
                        self.s.dma("sp", ub[0][:, :], self.zu[ci, :, t0 - PADL:t0 + L], [self.zuB], [ubB[0]])
                    src_i = 0; span = 1; dst_i = 1
                    while span < win:
                        self.tt("dve", ub[dst_i][:, span:W], ub[src_i][:, span:W], ub[src_i][:, 0:W - span], ALU.add,
                                [ubB[src_i]], [ubB[dst_i]])
                        src_i = dst_i
                        dst_i = 2 if dst_i == 1 else 1
                        span *= 2
                    self.tt("dve", ub[src_i][:, PADL:], ub[src_i][:, PADL:], invt, ALU.mult, [ubB[src_i], invB], [ubB[src_i]])
                    self.tt("dve", pooled[:, j, t0:t0 + L], ub[src_i][:, PADL:], ub[0][:, PADL:], ALU.subtract, [ubB[src_i], ubB[0]],
                            [pooledB])
            for ti in range(c.NT):
                t0 = ti * c.TT; Tt = c.TT

                def epi(cc, ps, psB, M, sub, t0=t0, Tt=Tt):
                    st, stB = self.nstg()
                    s16 = st[:].bitcast(BF16)
                    self.ts("dve", s16[:, :Tt], ps[:, :Tt], psc[:, cc:cc + 1], ALU.mult, [psB, pscB], [stB])
                    self.store_chunk(self.ab, self.abB, c.AH + cc, t0, Tt, s16[:, :Tt], stB)

                self.gemm(lambda kc, sub, t0=t0, Tt=Tt: pooled[:, kc, t0:t0 + Tt], pooledB, c.BGC, 1, self.pool_w, 4 * c.BGC, 128, epi,
                          cc_list=[gi * c.BGC + oc for oc in range(c.BGC)], lowp=False)
        self._fence_release(ubB + [pooledB, invB], [self.arenaB, self.gpB])

    def outproj(self, w, hin, hinB, hout, houtB):
        c = self.cfg
        for ti in range(c.NGT):
            t0 = ti * c.GT
            bg = self.gemm_begin(c.KC, w, c.KC)
            tile = self.load_tile(self.ab, self.abB, t0, c.GT, c.KC, src_bf16=True)
            pre, epi = self.epi_residual(hin, hinB, hout, houtB, t0)
            self.gemm(lambda kc, sub: tile[:, kc, sub * 512:(sub + 1) * 512], self.tile_bufs, c.KC, c.GT // 512, w, c.KC, 128, epi, pre=pre, begun=bg)

    def xattn(self, l, hin, hinB, hout, houtB):
        c = self.cfg; ML = c.ML
        gp = self.gp
        o0 = 0
        kx = gp[:, o0:o0 + 4 * ML].rearrange("p (h m) -> p h m", h=4); o0 += 4 * ML
        vx = gp[:, o0:o0 + 2 * 512].rearrange("p (b n) -> p b n", b=2); o0 += 1024
        qx = gp[:, o0:o0 + 4 * 512].rearrange("p (h t) -> p h t", h=4); o0 += 2048
        ox16 = gp[:, o0:o0 + 1024].bitcast(BF16).rearrange("p (h t) -> p h t", h=4); o0 += 1024
        pb = gp[:, o0:o0 + 2 * 512].rearrange("p (b t) -> p b t", b=2); o0 += 1024
        assert gp.shape[1] >= o0
        kxB, vxB, qxB, oxB, pbB = Buf("kx"), Buf("vx"), Buf("qx"), Buf("ox"), Buf("pb")
        fence = [self.gpB]
        mt = self.load_tile(self.memT, Buf("memT"), 0, ML, c.KC, gain_idx=4 + l)
        self.fence(fence)

        def epik(cc, ps, psB, M, sub):
            self.qknorm(ps, psB, ML, self.hg[:, 4 + l:5 + l], self.hgB, 1.0, kx[:, cc, :], kxB)
        self.gemm(lambda kc, sub: mt[:, kc, :], self.arenaB, c.KC, 1, self.x_wk[l], 4, 128, epik, sw=ML)
        for mb in range(ML // 128):
            ps, psB = self.psG()
            for k0 in range(0, c.KC, 4):
                k1 = min(c.KC, k0 + 4)
                w16t, w16B = self.wb16[self.w16ctr % self.NW16]; self.w16ctr += 1
                w16 = w16t[:, 0:(k1 - k0) * 512].rearrange("p (k n) -> p k n", n=512)
                self.s.dma("pool", w16, self.x_wv[l, :, k0:k1, :], [], [w16B])
                for kc in range(k0, k1):
                    self.mm(psB, ps[:, :], mt[:, kc, mb * 128:(mb + 1) * 128], w16[:, kc - k0, :], kc == 0, kc == c.KC - 1, [w16B, self.arenaB])
            self.cp("act", vx[:, mb, :], ps[:, :], [psB], [vxB])
        nxt = None
        for ti in range(c.NT):
            t0 = ti * c.TT; Tt = c.TT
            tile = self.load_tile(hin, hinB, t0, Tt, c.KC, gain_idx=2 + l, pre1=nxt)
            nxt = self.pass1(hin, hinB, t0 + Tt, Tt, c.KC) if ti + 1 < c.NT else None

            def epiq(cc, ps, psB, M, sub):
                self.qknorm(ps, psB, Tt, self.hg[:, 2 + l:3 + l], self.hgB, 128.0 ** -0.5, qx[:, cc, :], qxB)
            self.gemm(lambda kc, sub: tile[:, kc, :], self.arenaB, c.KC, 1, self.x_wq[l], 4, 128, epiq)
            for h in range(4):
                pd, pdB = self.psX(); po, poB = self.psX()
                for mb in range(2):
                    ps, psB = self.psX()
                    self.mm(psB, ps[:, :Tt], kx[:, h, mb * 128:(mb + 1) * 128], qx[:, h, :], True, True, [kxB, qxB])
                    self.act(pb[:, mb, :], ps[:, :Tt], AF.Exp, [psB], [pbB])
                for mb in range(2):
                    self.mm(pdB, pd[:, :Tt], self.ones[:], pb[:, mb, :], mb == 0, mb == 1, [self.onesB, pbB])
                for mb in range(2):
                    self.mm(poB, po[:, :Tt], vx[:, mb, h * 128:(h + 1) * 128], pb[:, mb, :], mb == 0, mb == 1, [vxB, pbB])
                rc, rcB = self.nstg()
                self.recip(rc[:, :Tt], pd[:, :Tt], [pdB], [rcB])
                self.tt("dve", ox16[:, h, :], po[:, :Tt], rc[:, :Tt], ALU.mult, [poB, rcB], [oxB])
            pre, epi = self.epi_residual(hin, hinB, hout, houtB, t0)
            self.gemm(lambda kc, sub: ox16[:, kc, :], oxB, 4, 1, self.x_wo[l], c.KC, 128, epi, pre=pre, side=(nxt["gen"] if nxt else None))
        self._fence_release([kxB, vxB, qxB, oxB, pbB], [self.gpB])

    def ffn(self, l, hin, hinB, hout, houtB):
        c = self.cfg; FC = c.FC
        gp = self.gp
        o0 = 0
        cw = gp[:, o0:o0 + 2 * FC * 4].rearrange("p (c f) -> p c f", f=4); o0 += 2 * FC * 4
        tails = gp[:, o0:o0 + 2 * FC * 2].rearrange("p (c f) -> p c f", f=2); o0 += 2 * FC * 2
        ub = []
        for i in range(2):
            ub.append(gp[:, o0:o0 + 514]); o0 += 516
        cg = []
        for i in range(2):
            cg.append(gp[:, o0:o0 + 512]); o0 += 512
        cv = gp[:, o0:o0 + 512]; o0 += 512
        assert gp.shape[1] >= o0
        cwB, tlB, cvB = Buf("cw"), Buf("tails"), Buf("cv")
        cgB = [Buf("cg0"), Buf("cg1")]
        ubB = [Buf("fub0"), Buf("fub1")]
        fence = [self.gpB]
        self.fence(fence)
        self.s.dma("sp", cw, self.f_cw[:, l, :, :], [], [cwB])
        self.s.op("pool", lambda E: E.memset(tails, 0.0), [], [tlB])
        ucnt = [0]
        nxt = None
        for ti in range(c.NGT):
            t0 = ti * c.GT
            bg = self.gemm_begin(c.KC, self.f_wup[l], 2 * FC)
            tile = self.load_tile(hin, hinB, t0, c.GT, c.KC, gain_idx=6 + l, pre1=nxt)
            nxt = self.pass1(hin, hinB, t0 + c.GT, c.GT, c.KC) if ti + 1 < c.NGT else None

            def epi(cc, ps, psB, M, sub, t0=t0):
                Tt = 512; ts0 = t0 + sub * 512
                u = ub[ucnt[0] % 2]; uB = ubB[ucnt[0] % 2]; ucnt[0] += 1
                self.cp("act", u[:, 2:2 + Tt], ps[:, :Tt], [psB], [uB])
                self.cp("dve", u[:, 0:2], tails[:, cc, :], [tlB], [uB])
                self.cp("dve", tails[:, cc, :], u[:, Tt:Tt + 2], [uB], [tlB])
                isg = (cc % 2 == 0)
                dst, dstB = (cg[sub], cgB[sub]) if isg else (cv, cvB)
                self.act(dst[:, :Tt], u[:, 2:2 + Tt], AF.Identity, [uB, cwB], [dstB], scale=cw[:, cc, 2:3], bias=cw[:, cc, 3:4])
                self.stt("dve", dst[:, :Tt], u[:, 1:1 + Tt], cw[:, cc, 1:2], dst[:, :Tt], ALU.mult, ALU.add, [uB, cwB, dstB], [dstB])
                self.stt("dve", dst[:, :Tt], u[:, 0:Tt], cw[:, cc, 0:1], dst[:, :Tt], ALU.mult, ALU.add, [uB, cwB, dstB], [dstB])
                if not isg:
                    st, stB = self.nstg()
                    s16 = st[:].bitcast(BF16)
                    self.act(cg[sub][:, :Tt], cg[sub][:, :Tt], AF.Silu, [cgB[sub]], [cgB[sub]])
                    self.tt("dve", s16[:, :Tt], cg[sub][:, :Tt], cv[:, :Tt], ALU.mult, [cgB[sub], cvB], [stB])
                    self.store_chunk(self.ffa, self.ffaB, cc // 2, ts0, Tt, s16[:, :Tt], stB)

            self.gemm(lambda kc, sub: tile[:, kc, sub * 512:(sub + 1) * 512], self.arenaB, c.KC, c.GT // 512, self.f_wup[l], 2 * FC, 128, epi,
                      side=(nxt["gen"] if nxt else None), begun=bg)
        self._fence_release([cwB, tlB, cvB] + cgB + ubB, [self.gpB])
        n3 = (FC + 2) // 3
        parts = [(k0, min(FC, k0 + n3)) for k0 in range(0, FC, n3)]
        temps = [(self.hT1, self.hT1B), (self.hT2, self.hT2B)]
        chain = [(hin, hinB)] + temps[:len(parts) - 1] + [(hout, houtB)]
        for pi, (k0, k1) in enumerate(parts):
            src_h, src_hB = chain[pi]; dst_h, dst_hB = chain[pi + 1]
            for ti in range(c.NGT):
                t0 = ti * c.GT
                bg = self.gemm_begin(k1 - k0, self.f_wdn[l][:, :, k0:k1, :], c.KC)
                tile = self.load_tile(self.ffa[k0:k1], self.ffaB, t0, c.GT, k1 - k0, src_bf16=True)
                pre, epi = self.epi_residual(src_h, src_hB, dst_h, dst_hB, t0)
                self.gemm(lambda kc, sub: tile[:, kc, sub * 512:(sub + 1) * 512], self.tile_bufs, k1 - k0, c.GT // 512,
                          self.f_wdn[l][:, :, k0:k1, :], c.KC, 128, epi, pre=pre, begun=bg)

    def l1_inproj(self, hin, hinB):
        c = self.cfg
        nq = c.CKW // 128; nv = c.CVW // 128
        CC = (c.ODD_IN + 127) // 128
        nxt = None
        for ti in range(c.NGT):
            t0 = ti * c.GT
            bg = self.gemm_begin(c.KC, self.c_win, CC)
            tile = self.load_tile(hin, hinB, t0, c.GT, c.KC, gain_idx=1, pre1=nxt)
            nxt = self.pass1(hin, hinB, t0 + c.GT, c.GT, c.KC) if ti + 1 < c.NGT else None

            def epi(cc, ps, psB, M, sub, t0=t0):
                Tt = 512; ts0 = t0 + sub * 512
                st, stB = self.nstg()
                if cc < nq:
                    self.act(st[:, :Tt], ps[:, :Tt], AF.Copy, [psB], [stB], scale=float(c.CDK) ** -0.5)
                    self.store_chunk(self.zq, self.zqB, cc, ts0, Tt, st[:, :Tt], stB)
                elif cc < 2 * nq:
                    self.cp("act", st[:, :Tt], ps[:, :Tt], [psB], [stB])
                    self.store_chunk(self.zk, self.zkB, cc - nq, ts0, Tt, st[:, :Tt], stB)
                elif cc < 2 * nq + nv:
                    self.cp("dve", st[:, :Tt], ps[:, :Tt], [psB], [stB])
                    self.store_chunk(self.zv, self.zvB, cc - 2 * nq, ts0, Tt, st[:, :Tt], stB)
                elif cc < 2 * nq + 2 * nv:
                    self.act(st[:, :Tt], ps[:, :Tt], AF.Silu, [psB], [stB])
                    self.store_chunk(self.zu, self.zuB, cc - 2 * nq - nv, ts0, Tt, st[:, :Tt], stB)
                else:
                    self.cp("act", st[:16, :Tt], ps[:16, :Tt], [psB], [stB])
                    self.store_chunk(self.zr, self.zrB, 0, ts0, Tt, st[:16, :Tt], stB, P=16)

            self.gemm(lambda kc, sub: tile[:, kc, sub * 512:(sub + 1) * 512], self.arenaB, c.KC, c.GT // 512, self.c_win, CC, 128, epi,
                      side=(nxt["gen"] if nxt else None), begun=bg)

    def l1_gla(self):
        c = self.cfg; T = c.T; DKC = c.DKC; DVC = c.DVC; Tt = c.TT
        NCH = Tt // 64
        ar = self.arena
        o0 = 0

        def carve(n):
            nonlocal o0
            a = ar[:, o0:o0 + n]; o0 += n
            return a
        wa2 = carve(c.CKW)
        ba = carve(c.CKW // 128)
        on = carve(DVC)
        rT = carve(Tt)
        qt = carve(DKC * Tt).rearrange("p (k t) -> p k t", k=DKC)
        kt = carve(DKC * Tt).rearrange("p (k t) -> p k t", k=DKC)
        vt = carve(DVC * Tt).rearrange("p (k t) -> p k t", k=DVC)
        gt = carve(DVC * Tt).rearrange("p (k t) -> p k t", k=DVC)
        bc = carve(DKC * Tt).rearrange("p (k t) -> p k t", k=DKC)
        ebc = carve(DKC * Tt).rearrange("p (k t) -> p k t", k=DKC)
        dec = carve(DKC * NCH).rearrange("p (k n) -> p k n", k=DKC)
        state = carve(DKC * c.CDV).rearrange("p (k e) -> p k e", k=DKC)
        ot = carve(DVC * Tt).rearrange("p (k t) -> p k t", k=DVC)
        kst = carve(DKC * 64).rearrange("p (k t) -> p k t", k=DKC)
        attm = carve(64)
        vtok = carve(c.CDV)
        ksttok = carve(c.CDK)
        assert ar.shape[1] >= o0, (ar.shape, o0)
        names = ["wa2", "ba", "on", "rT", "qt", "kt", "vt", "gt", "bc", "ebc", "dec", "state", "ot", "kst", "attm", "vtok", "ksttok"]
        B = {n: Buf("g_" + n) for n in names}
        fence = [self.arenaB]
        scanm = self.cst[:, 1728:2240]
        mask01 = self.cst[0:64, 1664:1728]
        self.fence(fence)
        self.s.dma("sp", wa2[0:16, :], self.c_wa2, [], [B["wa2"]])
        self.s.dma("sp", ba, self.c_ba, [], [B["ba"]])
        self.s.dma("sp", on, self.c_on, [], [B["on"]])
        self.ts("dve", ba, ba, -1.0, ALU.mult, [B["ba"]], [B["ba"]])
        for hd in range(c.CH):
            for dc in range(DKC):
                self.s.op("pool", lambda E, dc=dc: E.memset(state[:, dc, :], 0.0), [], [B["state"]])
            for ti in range(c.NT):
                t0 = ti * Tt
                self.s.dma("sp", rT[0:16, :], self.zr[0, 0:16, t0:t0 + Tt], [self.zrB], [B["rT"]])
                self.s.dma("sp", qt, self.zq[hd * DKC:(hd + 1) * DKC, :, t0:t0 + Tt].rearrange("k p t -> p k t"), [self.zqB], [B["qt"]])
                self.s.dma("sp", kt, self.zk[hd * DKC:(hd + 1) * DKC, :, t0:t0 + Tt].rearrange("k p t -> p k t"), [self.zkB], [B["kt"]])
                self.s.dma("sp", vt, self.zv[hd * DVC:(hd + 1) * DVC, :, t0:t0 + Tt].rearrange("k p t -> p k t"), [self.zvB], [B["vt"]])
                self.s.dma("sp", gt, self.zu[hd * DVC:(hd + 1) * DVC, :, t0:t0 + Tt].rearrange("k p t -> p k t"), [self.zuB], [B["gt"]])
                for dc in range(DKC):
                    col = hd * DKC + dc
                    ps, psB = self.psX()
                    self.mm(psB, ps[:, :Tt], wa2[0:16, col * 128:(col + 1) * 128], rT[0:16, :], True, True, [B["wa2"], B["rT"]])
                    self.act(ebc[:, dc, :], ps[:, :Tt], AF.Exp, [psB, B["ba"]], [B["ebc"]], scale=-1.0, bias=ba[:, col:col + 1])
                    one_ap = self.epsap(1.0, 128)
                    self.act(ebc[:, dc, :], ebc[:, dc, :], AF.Ln, [B["ebc"], self.epstB], [B["ebc"]], bias=one_ap)
                    self.ts("dve", ebc[:, dc, :], ebc[:, dc, :], -1.0 / 16.0, ALU.mult, [B["ebc"]], [B["ebc"]])
                    self.s.op("dve", lambda E, dc=dc: E.tensor_tensor_scan(out=bc[:, dc, :], data0=scanm[:, :Tt], data1=ebc[:, dc, :], initial=0.0,
                                                                            op0=ALU.mult, op1=ALU.add), [self.cstB, B["ebc"]], [B["bc"]])
                    self.act(dec[:, dc, :], bc[:, dc, 63:Tt:64], AF.Exp, [B["bc"]], [B["dec"]])
                    self.act(ebc[:, dc, :], bc[:, dc, :], AF.Exp, [B["bc"]], [B["ebc"]])
                    self.tt("dve", qt[:, dc, :], qt[:, dc, :], ebc[:, dc, :], ALU.mult, [B["qt"], B["ebc"]], [B["qt"]])
                    self.act(ebc[:, dc, :], bc[:, dc, :], AF.Exp, [B["bc"]], [B["ebc"]], scale=-1.0)
                    self.tt("dve", kt[:, dc, :], kt[:, dc, :], ebc[:, dc, :], ALU.mult, [B["kt"], B["ebc"]], [B["kt"]])
                for ch in range(NCH):
                    cs = slice(ch * 64, (ch + 1) * 64)
                    pa, paB = self.psX()
                    for dc in range(DKC):
                        self.mm(paB, pa[0:64, 0:64], kt[:, dc, cs], qt[:, dc, cs], dc == 0, dc == DKC - 1, [B["kt"], B["qt"]])
                    self.tt("dve", attm[0:64, :], pa[0:64, 0:64], mask01, ALU.mult, [paB, self.cstB], [B["attm"]])
                    pv, pvB = self.psX()
                    for ec in range(DVC):
                        self.tr(pvB, pv[0:64, ec * 128:(ec + 1) * 128], vt[:, ec, cs], [B["vt"]])
                    self.cp("act", vtok[0:64, :], pv[0:64, 0:c.CDV], [pvB], [B["vtok"]])
                    for dc in range(DKC):
                        self.ts("dve", kst[:, dc, :], kt[:, dc, cs], dec[:, dc, ch:ch + 1], ALU.mult, [B["kt"], B["dec"]], [B["kst"]])
                    pk, pkB = self.psX()
                    for dc in range(DKC):
                        self.tr(pkB, pk[0:64, dc * 128:(dc + 1) * 128], kst[:, dc, :], [B["kst"]])
                    self.cp("act", ksttok[0:64, :], pk[0:64, 0:c.CDK], [pkB], [B["ksttok"]])
                    po, poB = self.psG()
                    for ec in range(DVC):
                        osl = po[:, ec * 64:(ec + 1) * 64]
                        self.mm(poB, osl, vtok[0:64, ec * 128:(ec + 1) * 128], attm[0:64, :], True, False, [B["vtok"], B["attm"]])
                        for dc in range(DKC):
                            self.mm(poB, osl, state[:, dc, ec * 128:(ec + 1) * 128], qt[:, dc, cs], False, dc == DKC - 1, [B["state"], B["qt"]])
                    self.cp("act", ot[:, :, cs], po[:, 0:DVC * 64].rearrange("p (k t) -> p k t", k=DVC), [poB], [B["ot"]])
                    for dc in range(DKC):
                        pkv, pkvB = self.psG()
                        self.mm(pkvB, pkv[:, 0:c.CDV], ksttok[0:64, dc * 128:(dc + 1) * 128], vtok[0:64, :], True, True, [B["ksttok"], B["vtok"]])
                        self.stt("dve", state[:, dc, :], state[:, dc, :], dec[:, dc, ch:ch + 1], pkv[:, 0:c.CDV], ALU.mult, ALU.add,
                                 [B["state"], B["dec"], pkvB], [B["state"]])
                psq, psqB = self.psX()
                for ec in range(DVC):
                    sq, sqB = self.sq[ec % 2]
                    self.act(sq[:, :Tt], ot[:, ec, :], AF.Square, [B["ot"]], [sqB])
                    self.mm(psqB, psq[:, :Tt], self.ones[:], sq[:, :Tt], ec == 0, ec == DVC - 1, [sqB, self.onesB])
                rstd, rstdB = self.rsqrt_from(psq[:, :Tt], psqB, 1.0 / c.CDV, EPS, 128, Tt)
                for ec in range(DVC):
                    st, stB = self.nstg()
                    st2, st2B = self.nstg()
                    s16 = st2[:].bitcast(BF16)
                    self.stt("dve", st[:, :Tt], ot[:, ec, :], on[:, ec:ec + 1], rstd[:, :Tt], ALU.mult, ALU.mult, [B["ot"], B["on"], rstdB], [stB])
                    self.tt("dve", s16[:, :Tt], st[:, :Tt], gt[:, ec, :], ALU.mult, [stB, B["gt"]], [st2B])
                    self.store_chunk(self.ab, self.abB, hd * DVC + ec, t0, Tt, s16[:, :Tt], st2B)
        self._fence_release(list(B.values()), [self.arenaB])


def _wblocks(W):
    K, N = W.shape
    CC = (N + 127) // 128
    if CC * 128 != N:
        W = np.concatenate([W, np.zeros((K, CC * 128 - N), W.dtype)], axis=1)
    return np.ascontiguousarray(W.reshape(K // 128, 128, CC, 128).transpose(2, 1, 0, 3))


def _fm(v):
    return np.ascontiguousarray(v.reshape(-1, 128).T)


def _consts():
    cst = np.zeros((128, 2240), np.float32)
    cst[:, 0:128] = np.eye(128, dtype=np.float32)
    i = np.arange(128)[:, None]; j = np.arange(128)[None, :]
    cur = np.where(j >= i, (j - i).astype(np.float32), BIG).astype(np.float32)
    prev = np.where(i >= j, (j + 128 - i).astype(np.float32), BIG).astype(np.float32)
    for r in range(4):
        cst[:, 128 + r * 128:128 + (r + 1) * 128] = cur
        cst[:, 640 + r * 128:640 + (r + 1) * 128] = prev
        cst[:, 1152 + r * 128:1152 + (r + 1) * 128] = prev if r > 0 else BIG
    jj = np.arange(64)[:, None]; ii = np.arange(64)[None, :]
    cst[0:64, 1664:1728] = (ii >= jj).astype(np.float32)
    m = np.ones(512, np.float32); m[::64] = 0.0
    cst[:, 1728:2240] = m[None, :]
    return cst


def prep_shared(cfg, inp):
    c = cfg
    f = lambda a: np.asarray(a, dtype=np.float32)
    d = {}
    gains = np.zeros((128, 8, c.KC), np.float32)
    for i, (nm, l) in enumerate([("mix_norm", 0), ("mix_norm", 1), ("x_norm", 0), ("x_norm", 1), ("x_mem_norm", 0), ("x_mem_norm", 1),
                                 ("f_norm", 0), ("f_norm", 1)]):
        gains[:, i, :] = _fm(f(inp[nm])[l])
    d["gains"] = gains
    hg = np.zeros((128, 6), np.float32)
    hg[:, 0] = f(inp["ab_q_norm"])[0]; hg[:, 1] = f(inp["ab_k_norm"])[0]
    hg[:, 2] = f(inp["x_q_norm"])[0]; hg[:, 3] = f(inp["x_q_norm"])[1]
    hg[:, 4] = f(inp["x_k_norm"])[0]; hg[:, 5] = f(inp["x_k_norm"])[1]
    d["hgain"] = hg
    d["consts"] = _consts()
    t = np.arange(1, c.T + 1, dtype=np.float32)
    d["invc"] = np.stack([np.float32(1.0) / np.minimum(t, np.float32(w)) for w in (2, 4, 8, 16)]).astype(np.float32)
    d["ab_win"] = _wblocks(f(inp["ab_w_in"])[0])
    pw = f(inp["ab_pool_w"])[0]
    d["pool_w"] = np.concatenate([_wblocks(pw[g]) for g in range(4)], axis=0)
    d["pool_sc"] = _fm(f(inp["ab_pool_scale"])[0])
    d["ab_wout"] = _wblocks(f(inp["ab_w_out"])[0])
    d["c_win"] = _wblocks(f(inp["c_w_in"])[0])
    d["c_wa2"] = np.ascontiguousarray(f(inp["c_w_a2"])[0])
    d["c_ba"] = _fm(f(inp["c_b_a"])[0])
    d["c_on"] = _fm(f(inp["c_o_norm"])[0])
    d["c_wout"] = _wblocks(f(inp["c_w_out"])[0])
    wq = f(inp["x_wq"]); wkv = f(inp["x_wkv"]); wo = f(inp["x_wo"])
    d["x_wq"] = np.stack([_wblocks(wq[l]) for l in range(2)])
    d["x_wk"] = np.stack([_wblocks(wkv[l][:, :512]) for l in range(2)])
    d["x_wv"] = np.stack([np.ascontiguousarray(wkv[l][:, 512:].reshape(c.KC, 128, 512).transpose(1, 0, 2)) for l in range(2)])
    d["x_wo"] = np.stack([_wblocks(wo[l]) for l in range(2)])
    wup = f(inp["f_w_up"]); cw = f(inp["f_conv_w"]); cb = f(inp["f_conv_b"]); wdn = f(inp["f_w_down"])
    idx = np.concatenate([np.concatenate([np.arange(j * 128, (j + 1) * 128), c.DFF + np.arange(j * 128, (j + 1) * 128)]) for j in range(c.FC)])
    d["f_wup"] = np.stack([_wblocks(wup[l][:, idx]) for l in range(2)])
    fcw = np.zeros((128, 2, 2 * c.FC, 4), np.float32)
    for l in range(2):
        for k in range(3):
            fcw[:, l, :, k] = _fm(cw[l, k][idx])
        fcw[:, l, :, 3] = _fm(cb[l][idx])
    d["f_cw"] = fcw
    d["f_wdn"] = np.stack([_wblocks(wdn[l]) for l in range(2)])
    return d


def _tfm(a):
    T, D = a.shape
    return np.ascontiguousarray(a.T.reshape(D // 128, 128, T))


_PROG_CACHE = {}


def run(cfg, inputs, stop_after=None, cores=None):
    key = (cfg.D, cfg.T, cfg.CH, stop_after)
    if key not in _PROG_CACHE:
        p = Prog(cfg, stop_after=stop_after)
        p.build()
        _PROG_CACHE[key] = p
    p = _PROG_CACHE[key]
    shared = prep_shared(cfg, inputs)
    x = np.asarray(inputs["x"], dtype=np.float32); mem = np.asarray(inputs["mem"], dtype=np.float32)
    nb = x.shape[0]
    in_maps = []
    for b in range(nb):
        m = dict(shared)
        m["xT"] = _tfm(x[b]); m["memT"] = _tfm(mem[b])
        in_maps.append(m)
    res = run_bass_kernel_spmd(p.nc, in_maps, core_ids=list(range(nb)))
    outs = []
    for b in range(nb):
        o = res.results[b]["outT"]
        outs.append(np.ascontiguousarray(o.reshape(cfg.D, cfg.T).T))
    return np.stack(outs)


def kernel(**inputs):
    cfg = Cfg(4096, 4096, 8)
    return run(cfg, inputs).astype(np.float32)
```
